# Optimizing a Trainium2 kernel written in Bass

```python
import jax
import jax.numpy as jnp
from jax import lax
import numpy as np

D_MODEL = 2048
BATCH = 1
SEQ = 16384
DEPTH = 1

HEAD_DIM = 128
NSA_HEADS = 8
NSA_GROUPS = 2
NSA_HPG = NSA_HEADS // NSA_GROUPS
SB_HEADS = 4
MEM_HEADS = 4
MEM_LEN = 256
CMP_LEN = 32
CMP_STRIDE = 16
CMP_HIDDEN = 2 * HEAD_DIM
SEL_BLOCK = 64
SEL_TOPK = 16
WINDOW = 512
Q_BLOCK = 128
ROPE_THETA = 10000.0
NORM_EPS = 1e-6
NEG_BIG = -1e30
N_BRANCH = 3
FFN_HIDDEN = ((8 * D_MODEL + 3 * 256 - 1) // (3 * 256)) * 256

NSA_Q_W = NSA_HEADS * HEAD_DIM
NSA_KV_W = NSA_GROUPS * HEAD_DIM
SB_W = SB_HEADS * HEAD_DIM
MEM_W = MEM_HEADS * HEAD_DIM
IN_SPLITS = (NSA_Q_W, 6 * NSA_KV_W, 3 * NSA_HEADS, 3 * SB_W, MEM_W, N_BRANCH * D_MODEL)
IN_WIDTH = sum(IN_SPLITS)

kernel_name = 'hybrid_nsa_stickbreak_memxattn_swiglu'


def _split_cols(a):
    outs, start = [], 0
    for w in IN_SPLITS:
        outs.append(a[..., start:start + w])
        start += w
    return outs


def rms_norm(x, g):
    xf = x.astype(jnp.float32)
    y = xf * lax.rsqrt(jnp.mean(xf * xf, axis=-1, keepdims=True) + NORM_EPS)
    return (y * g.astype(jnp.float32)).astype(x.dtype)


def rope_tables(pos):
    inv = 1.0 / (ROPE_THETA ** (jnp.arange(0, HEAD_DIM, 2, dtype=jnp.float32) / HEAD_DIM))
    ang = pos.astype(jnp.float32)[..., None] * inv
    return jnp.cos(ang), jnp.sin(ang)


def apply_rope(x, cos, sin):
    x1, x2 = jnp.split(x.astype(jnp.float32), 2, axis=-1)
    c, s = cos[:, :, None, :], sin[:, :, None, :]
    return jnp.concatenate([x1 * c - x2 * s, x2 * c + x1 * s], axis=-1).astype(x.dtype)


def compress_blocks(t, pe, w1, w2):
    B, T, G, dh = t.shape
    chunks = t.reshape(B, T // CMP_STRIDE, CMP_STRIDE, G, dh)
    blocks = jnp.concatenate([chunks[:, :-1], chunks[:, 1:]], axis=2) + pe[:, None, :]
    hid = jax.nn.gelu(jnp.einsum('bnlgd,ldf->bngf', blocks, w1))
    return jnp.einsum('bngf,fd->bngd', hid, w2)


def masked_softmax(s, mask):
    return jax.nn.softmax(jnp.where(mask, s, NEG_BIG), axis=-1)


def nsa_mixer(q_flat, kv_flat, gate_flat, positions, cos, sin, q_norm, kc_norm, ks_norm, kw_norm,
              ck_pe, ck_w1, ck_w2, cv_pe, cv_w1, cv_w2):
    B, T, _ = q_flat.shape
    G, Hg, dh = NSA_GROUPS, NSA_HPG, HEAD_DIM
    n_cmp = T // CMP_STRIDE - 1
    n_sel = T // SEL_BLOCK
    top_k = min(SEL_TOPK, n_sel)
    scale = HEAD_DIM ** -0.5
    dtype = q_flat.dtype
    f32 = jnp.float32

    q = apply_rope(rms_norm(q_flat.reshape(B, T, NSA_HEADS, dh), q_norm), cos, sin)
    qg = q.reshape(B, T, G, Hg, dh).transpose(0, 2, 3, 1, 4)
    gg = jax.nn.sigmoid(gate_flat.reshape(B, T, G, Hg, 3)).transpose(0, 2, 3, 1, 4)
    kc, vc, ks, vs, kw, vw = [a.reshape(B, T, G, dh) for a in jnp.split(kv_flat, 6, axis=-1)]

    cos_c, sin_c = rope_tables(positions[:, CMP_LEN - 1::CMP_STRIDE])
    k_c = apply_rope(rms_norm(compress_blocks(kc, ck_pe, ck_w1, ck_w2), kc_norm), cos_c, sin_c)
    k_c = k_c.transpose(0, 2, 1, 3)
    v_c = compress_blocks(vc, cv_pe, cv_w1, cv_w2).transpose(0, 2, 1, 3)
    k_s = apply_rope(rms_norm(ks, ks_norm), cos, sin).transpose(0, 2, 1, 3).reshape(B, G, n_sel, SEL_BLOCK, dh)
    v_s = vs.transpose(0, 2, 1, 3).reshape(B, G, n_sel, SEL_BLOCK, dh)
    pad = ((0, 0), (0, 0), (WINDOW, 0), (0, 0))
    k_w = jnp.pad(apply_rope(rms_norm(kw, kw_norm), cos, sin).transpose(0, 2, 1, 3), pad)
    v_w = jnp.pad(vw.transpose(0, 2, 1, 3), pad)

    cmp_start = jnp.arange(n_cmp) * CMP_STRIDE
    cmp_end = cmp_start + (CMP_LEN - 1)
    sel_start = jnp.arange(n_sel) * SEL_BLOCK
    overlap = ((cmp_start[:, None] < sel_start[None, :] + SEL_BLOCK)
               & (cmp_end[:, None] >= sel_start[None, :])).astype(f32)
    blk_ids = jnp.arange(n_sel)
    b_ix = jnp.arange(B)[:, None, None, None]
    g_ix = jnp.arange(G)[None, :, None, None]
    sel_off = jnp.arange(SEL_BLOCK)
    win_off = jnp.arange(WINDOW + Q_BLOCK) - WINDOW

    def block(i):
        q0 = i * Q_BLOCK
        qb = lax.dynamic_slice_in_dim(qg, q0, Q_BLOCK, axis=3)
        gb = lax.dynamic_slice_in_dim(gg, q0, Q_BLOCK, axis=3)
        t = q0 + jnp.arange(Q_BLOCK)
        s_c = jnp.einsum('bghqd,bgnd->bghqn', qb, k_c, preferred_element_type=f32) * scale
        valid_c = cmp_end[None, :] <= t[:, None]
        p_c = masked_softmax(s_c, valid_c) * valid_c
        o_c = jnp.einsum('bghqn,bgnd->bghqd', p_c.astype(dtype), v_c)
        imp = jnp.einsum('bgqn,ns->bgqs', p_c.sum(axis=2), overlap)
        cur = t // SEL_BLOCK
        forced = ((blk_ids[None, :] == 0) | (blk_ids[None, :] == cur[:, None])
                  | (blk_ids[None, :] == cur[:, None] - 1))
        future = sel_start[None, :] > t[:, None]
        imp = jnp.where(forced, jnp.inf, jnp.where(future, -jnp.inf, imp))
        _, idx = lax.top_k(imp, top_k)
        k_sel = k_s[b_ix, g_ix, idx].reshape(B, G, Q_BLOCK, top_k * SEL_BLOCK, dh)
        v_sel = v_s[b_ix, g_ix, idx].reshape(B, G, Q_BLOCK, top_k * SEL_BLOCK, dh)
        tok = (idx[..., None] * SEL_BLOCK + sel_off).reshape(B, G, Q_BLOCK, top_k * SEL_BLOCK)
        valid_s = (tok <= t[:, None])[:, :, None]
        s_s = jnp.einsum('bghqd,bgqkd->bghqk', qb, k_sel, preferred_element_type=f32) * scale
        o_s = jnp.einsum('bghqk,bgqkd->bghqd', masked_softmax(s_s, valid_s).astype(dtype), v_sel)
        k_wb = lax.dynamic_slice_in_dim(k_w, q0, WINDOW + Q_BLOCK, axis=2)
        v_wb = lax.dynamic_slice_in_dim(v_w, q0, WINDOW + Q_BLOCK, axis=2)
        kpos = q0 + win_off
        valid_w = ((kpos[None, :] <= t[:, None]) & (kpos[None, :] > t[:, None] - WINDOW)
                   & (kpos[None, :] >= 0))
        s_w = jnp.einsum('bghqd,bgkd->bghqk', qb, k_wb, preferred_element_type=f32) * scale
        o_w = jnp.einsum('bghqk,bgkd->bghqd', masked_softmax(s_w, valid_w).astype(dtype), v_wb)
        return gb[..., 0:1] * o_c + gb[..., 1:2] * o_s + gb[..., 2:3] * o_w

    out = lax.map(block, jnp.arange(T // Q_BLOCK))
    return out.transpose(1, 0, 4, 2, 3, 5).reshape(B, T, NSA_Q_W)


def stick_breaking_mixer(qkv_flat):
    B, T, _ = qkv_flat.shape
    q, k, v = [a.reshape(B, T, SB_HEADS, HEAD_DIM).transpose(0, 2, 1, 3)
               for a in jnp.split(qkv_flat, 3, axis=-1)]
    scale = HEAD_DIM ** -0.5
    key_idx = jnp.arange(T)

    def block(i):
        q0 = i * Q_BLOCK
        qb = lax.dynamic_slice_in_dim(q, q0, Q_BLOCK, axis=2)
        t = q0 + jnp.arange(Q_BLOCK)
        z = jnp.einsum('bhqd,bhkd->bhqk', qb, k, preferred_element_type=jnp.float32) * scale
        past = key_idx[None, :] < t[:, None]
        log_keep = jnp.where(past, jax.nn.log_sigmoid(-z), 0.0)
        log_between = lax.cumsum(log_keep, axis=3, reverse=True) - log_keep
        weight = jnp.where(past, jnp.exp(jax.nn.log_sigmoid(z) + log_between), 0.0)
        return jnp.einsum('bhqk,bhkd->bhqd', weight.astype(v.dtype), v)

    out = lax.map(block, jnp.arange(T // Q_BLOCK))
    return out.transpose(1, 0, 3, 2, 4).reshape(B, T, SB_W)


def memory_cross_attention(q_flat, mem, mem_norm, w_mem_kv, q_norm, k_norm):
    B, T, _ = q_flat.shape
    M = mem.shape[1]
    kv = rms_norm(mem, mem_norm) @ w_mem_kv
    k, v = [a.reshape(B, M, MEM_HEADS, HEAD_DIM) for a in jnp.split(kv, 2, axis=-1)]
    k = rms_norm(k, k_norm)
    q = rms_norm(q_flat.reshape(B, T, MEM_HEADS, HEAD_DIM), q_norm)
    s = jnp.einsum('bthd,bmhd->bhtm', q, k, preferred_element_type=jnp.float32) * (HEAD_DIM ** -0.5)
    p = jax.nn.softmax(s, axis=-1)
    return jnp.einsum('bhtm,bmhd->bthd', p.astype(v.dtype), v).reshape(B, T, MEM_W)


def setup_inputs(seed: int = 0) -> dict:
    key = jax.random.key(seed)
    ks = jax.random.split(key, 27)
    f32 = jnp.float32

    def dense(k, shape, fan_in):
        return jax.random.normal(k, (DEPTH,) + shape, f32) * (fan_in ** -0.5)

    def gain(k, n):
        return 1.0 + 0.01 * jax.random.normal(k, (DEPTH, n), f32)

    x = jax.random.normal(ks[0], (BATCH, SEQ, D_MODEL), f32)
    mem = jax.random.normal(ks[1], (BATCH, MEM_LEN, D_MODEL), f32)
    positions = (jax.random.randint(ks[2], (BATCH, 1), 0, 1024, dtype=jnp.int32)
                 + jnp.arange(SEQ, dtype=jnp.int32)[None, :])
    return {
        'x': x,
        'mem': mem,
        'positions': positions,
        'attn_norm': gain(ks[3], D_MODEL),
        'w_in': dense(ks[4], (D_MODEL, IN_WIDTH), D_MODEL),
        'nsa_q_norm': gain(ks[5], HEAD_DIM),
        'nsa_kc_norm': gain(ks[6], HEAD_DIM),
        'nsa_ks_norm': gain(ks[7], HEAD_DIM),
        'nsa_kw_norm': gain(ks[8], HEAD_DIM),
        'cmp_k_pe': 0.02 * jax.random.normal(ks[9], (DEPTH, CMP_LEN, HEAD_DIM), f32),
        'cmp_k_w1': dense(ks[10], (CMP_LEN, HEAD_DIM, CMP_HIDDEN), CMP_LEN * HEAD_DIM),
        'cmp_k_w2': dense(ks[11], (CMP_HIDDEN, HEAD_DIM), CMP_HIDDEN),
        'cmp_v_pe': 0.02 * jax.random.normal(ks[12], (DEPTH, CMP_LEN, HEAD_DIM), f32),
        'cmp_v_w1': dense(ks[13], (CMP_LEN, HEAD_DIM, CMP_HIDDEN), CMP_LEN * HEAD_DIM),
        'cmp_v_w2': dense(ks[14], (CMP_HIDDEN, HEAD_DIM), CMP_HIDDEN),
        'mem_norm': gain(ks[15], D_MODEL),
        'w_mem_kv': dense(ks[16], (D_MODEL, 2 * MEM_W), D_MODEL),
        'mem_q_norm': gain(ks[17], HEAD_DIM),
        'mem_k_norm': gain(ks[18], HEAD_DIM),
        'w_o_nsa': dense(ks[19], (NSA_Q_W, D_MODEL), NSA_Q_W),
        'w_o_sb': dense(ks[20], (SB_W, D_MODEL), SB_W),
        'w_o_mem': dense(ks[21], (MEM_W, D_MODEL), MEM_W),
        'w_out': dense(ks[22], (D_MODEL, D_MODEL), D_MODEL),
        'ffn_norm': gain(ks[23], D_MODEL),
        'w_ffn_gate': dense(ks[24], (D_MODEL, FFN_HIDDEN), D_MODEL),
        'w_ffn_up': dense(ks[25], (D_MODEL, FFN_HIDDEN), D_MODEL),
        'w_ffn_down': dense(ks[26], (FFN_HIDDEN, D_MODEL), FFN_HIDDEN),
    }


def reference(x, mem, positions, attn_norm, w_in, nsa_q_norm, nsa_kc_norm, nsa_ks_norm, nsa_kw_norm,
              cmp_k_pe, cmp_k_w1, cmp_k_w2, cmp_v_pe, cmp_v_w1, cmp_v_w2, mem_norm, w_mem_kv,
              mem_q_norm, mem_k_norm, w_o_nsa, w_o_sb, w_o_mem, w_out, ffn_norm,
              w_ffn_gate, w_ffn_up, w_ffn_down):
    B, T, D = x.shape
    cos, sin = rope_tables(positions)
    for l in range(DEPTH):
        h = rms_norm(x, attn_norm[l])
        q_nsa, kv_nsa, g_nsa, qkv_sb, q_mem, g_merge = _split_cols(h @ w_in[l])
        y_nsa = nsa_mixer(q_nsa, kv_nsa, g_nsa, positions, cos, sin, nsa_q_norm[l], nsa_kc_norm[l],
                          nsa_ks_norm[l], nsa_kw_norm[l], cmp_k_pe[l], cmp_k_w1[l], cmp_k_w2[l],
                          cmp_v_pe[l], cmp_v_w1[l], cmp_v_w2[l])
        y_sb = stick_breaking_mixer(qkv_sb)
        y_mem = memory_cross_attention(q_mem, mem, mem_norm[l], w_mem_kv[l], mem_q_norm[l], mem_k_norm[l])
        g = jax.nn.sigmoid(g_merge).reshape(B, T, N_BRANCH, D)
        mixed = (g[:, :, 0] * (y_nsa @ w_o_nsa[l]) + g[:, :, 1] * (y_sb @ w_o_sb[l])
                 + g[:, :, 2] * (y_mem @ w_o_mem[l]))
        x = x + mixed @ w_out[l]
        h = rms_norm(x, ffn_norm[l])
        x = x + (jax.nn.silu(h @ w_ffn_gate[l]) * (h @ w_ffn_up[l])) @ w_ffn_down[l]
    return x
```

```python
import math
from contextlib import ExitStack
import numpy as np
import concourse.bass as bass
import concourse.mybir as mybir
from concourse.bass_utils import run_bass_kernel_spmd

F32 = mybir.dt.float32
BF16 = mybir.dt.bfloat16
I32 = mybir.dt.int32
ALU = mybir.AluOpType
AF = mybir.ActivationFunctionType

D = 2048
KC = 16
NCORE = 8
TQ = 512
FFN = 5632
NEG = -30000.0
SCALE = 128 ** -0.5
MAGIC = 12582912.0
C1 = 6.28125
C2 = float(2 * np.pi - 6.28125)
PI_LO = 3.1415925

COMPUTE_Q = ("pe", "act", "dve", "pool")
ALLQ = ("pe", "act", "dve", "pool", "sp")


class Sched:
    def __init__(self, nc, es):
        self.nc = nc
        self.es = es
        self.q = {k: [] for k in ALLQ}
        self.cnt = {k: 0 for k in COMPUTE_Q}
        self.sems = {k: es.enter_context(nc.semaphore("s_" + k)) for k in COMPUTE_Q}
        self.dcnt = {}
        self.lastw = {}
        self.readers = {}
        self.seen = {k: {} for k in ALLQ}
        self.nops = 0

    def _deps(self, q, reads, writes):
        need = {}

        def add(tok):
            if tok is None:
                return
            k, v = tok
            if need.get(k, 0) < v:
                need[k] = v
        for b in reads:
            add(self.lastw.get(b))
        for b in writes:
            add(self.lastw.get(b))
            for t in self.readers.get(b, ()):
                add(t)
        out = []
        for k, v in need.items():
            if k == q and q == "pe":
                continue
            if self.seen[q].get(k, 0) >= v:
                continue
            self.seen[q][k] = v
            out.append((k, v))
        return out

    def _commit(self, tok, reads, writes):
        for b in reads:
            self.readers.setdefault(b, []).append(tok)
        for b in writes:
            self.lastw[b] = tok
            self.readers[b] = []

    def _rec(self, fn):
        class _R:
            def __getattr__(s, name):
                def f(*a, **kw):
                    s.call = (name, a, kw)
                    return None
                return f
        r = _R()
        fn(r)
        name, a, kw = r.call
        return lambda eng: getattr(eng, name)(*a, **kw)

    def op(self, q, fn, reads=(), writes=()):
        fn = self._rec(fn)
        waits = self._deps(q, reads, writes)
        self.cnt[q] += 1
        tok = (q, self.cnt[q])
        self.q[q].append((waits, fn, q, 1))
        self._commit(tok, reads, writes)
        self.nops += 1
        return tok

    def dma(self, q, key, fn, reads=(), writes=()):
        fn = self._rec(fn)
        waits = self._deps(q, reads, writes)
        k = "dma:" + key
        if k not in self.sems:
            self.sems[k] = self.es.enter_context(self.nc.semaphore("d_" + key))
            self.dcnt[k] = 0
        self.dcnt[k] += 16
        tok = (k, self.dcnt[k])
        self.q[q].append((waits, fn, k, 16))
        self._commit(tok, reads, writes)
        self.nops += 1
        return tok

    def barrier(self):
        allw = [(k, v) for k, v in self.cnt.items() if v > 0] + [(k, v) for k, v in self.dcnt.items() if v > 0]
        for q in ALLQ:
            w = [(k, v) for k, v in allw if self.seen[q].get(k, 0) < v]
            for k, v in w:
                self.seen[q][k] = v
            if w:
                self.q[q].append((w, None, None, 0))
        self.lastw = {}
        self.readers = {}

    def flush(self):
        nc = self.nc
        engs = {"pe": "tensor", "act": "scalar", "dve": "vector", "pool": "gpsimd", "sp": "sync"}
        if not any(self.q.values()):
            return
        with nc.Block() as block:
            def mk(items):
                def body(eng):
                    for waits, fn, semk, inc in items:
                        for k, v in waits:
                            eng.wait_ge(self.sems[k], v)
                        if fn is not None:
                            fn(eng).then_inc(self.sems[semk], inc)
                return body
            for qn, attr in engs.items():
                if self.q[qn]:
                    getattr(block, attr)(mk(self.q[qn]))
        self.q = {k: [] for k in ALLQ}


def cst_layout(NT, NCC):
    o = {}
    c = 0
    for name, n in [("attn", 16), ("ffn", 16), ("memn", 16), ("qn", 1), ("kcn", 1), ("ksn", 1), ("kwn", 1),
                    ("mqn", 1), ("mkn", 1), ("inv", 1), ("eps", 1), ("one", 1), ("zero", 1), ("halfpi", 1),
                    ("cw", 8), ("pv", NT), ("cur", NT * 4), ("cthr", NT * NCC), ("pek", 32), ("pev", 32),
                    ("tiny", 1), ("_om", NT * 2)]:
        o[name] = c
        c += n
    o["_n"] = c
    return o


def cm_layout(NT, NCC):
    o = {}
    c = 0
    for name, n in [("ident", 128), ("negI", 128), ("ones", 128), ("onesneg", 128), ("UIneg", 128), ("prot", 128),
                    ("E64", 8192), ("OV", NCC * 257), ("DiagN", 2048), ("DiagS", 2048), ("DiagW", 2048),
                    ("Eown", 512), ("OwnSel", NT * 16)]:
        o[name] = c
        c += n
    o["_n"] = c
    return o


def build(T):
    NT = T // (NCORE * TQ)
    NGT = T // TQ
    NCC = T // 2048
    NCP = NCC * 128
    NB = min(512, NCP)
    CL = cst_layout(NT, NCC)
    ML = cm_layout(NT, NCC)
    TO = NT * TQ

    nc = bass.Bass("TRN2", target_bir_lowering=False)

    def din(name, shape, dt=F32):
        return nc.dram_tensor(name, list(shape), dt, kind="ExternalInput").ap()

    xT_all = din("xT_all", [D, T])
    xT_own = din("xT_own", [D, TO])
    xT_prev = din("xT_prev", [D, TO])
    memT = din("memT", [D, 256])
    pos_all = din("pos_all", [1, T], I32)
    pos_own = din("pos_own", [1, TO], I32)
    pos_prev = din("pos_prev", [1, TO], I32)
    pos_cmp = din("pos_cmp", [1, NCP], I32)
    w_in = din("w_in", [D, 10776])
    w1k = din("w1k", [32, 128, 256])
    w2k = din("w2k", [256, 128])
    w1v = din("w1v", [32, 128, 256])
    w2v = din("w2v", [256, 128])
    w_mem = din("w_mem", [D, 1024])
    wo_nsa = din("wo_nsa", [1024, D])
    wo_sb = din("wo_sb", [512, D])
    wo_mem = din("wo_mem", [512, D])
    w_out = din("w_out", [D, D])
    w_gate = din("w_gate", [D, FFN])
    w_up = din("w_up", [D, FFN])
    w_down = din("w_down", [FFN, D])
    cst_d = din("cst", [128, CL["_n"]])
    rowt_d = din("rowt", [128, 256])
    j2_d = din("j2", [128, 512])
    cmat_d = din("cmat", [128, ML["_n"]])
    outT = nc.dram_tensor("outT", [D, TO], F32, kind="ExternalOutput").ap()

    kcT_d = nc.dram_tensor("kcT_d", [2, 128, T], BF16).ap()
    vcT_d = nc.dram_tensor("vcT_d", [2, 128, T], BF16).ap()
    ksT_d = nc.dram_tensor("ksT_d", [2, 128, T], BF16).ap()
    sbkT_d = nc.dram_tensor("sbkT_d", [4, 128, T], BF16).ap()
    vtok_d = nc.dram_tensor("vtok_d", [T, 768], BF16).ap()
    gates_d = nc.dram_tensor("gates_d", [24, TQ], F32).ap()
    hn_d = nc.dram_tensor("hn_d", [128, KC * TQ], BF16).ap()

    with ExitStack() as es_all:
        S = Sched(nc, es_all)
        uid = [0]

        def sbt(es, name, shape, dt):
            uid[0] += 1
            return es.enter_context(nc.sbuf_tensor("%s_%d" % (name, uid[0]), list(shape), dt))

        PSB = [es_all.enter_context(nc.psum_tensor("ps%d" % i, [128, 512], F32)) for i in range(8)]
        rot = {"banks": list(range(8)), "i": 0}

        def set_rot(banks):
            rot["banks"] = list(banks)
            rot["i"] = 0

        def nextps():
            b = rot["banks"][rot["i"] % len(rot["banks"])]
            rot["i"] += 1
            return b

        def PS(b):
            return PSB[b]

        def pn(b):
            return "ps%d" % b

        cst = sbt(es_all, "cst", [128, CL["_n"]], F32)
        rowt = sbt(es_all, "rowt", [128, 256], F32)
        j2 = sbt(es_all, "j2", [128, 512], F32)
        cm = sbt(es_all, "cm", [128, ML["_n"]], BF16)
        kcTs = sbt(es_all, "kcTs", [128, 2, NCP], BF16)
        vcs = sbt(es_all, "vcs", [128, 2, NCC, 128], BF16)
        kmem = sbt(es_all, "kmem", [128, 4, 256], BF16)
        vmem = sbt(es_all, "vmem", [128, 2, 512], BF16)

        def ccol(name, i=0):
            return cst[:, CL[name] + i:CL[name] + i + 1]

        def cmv(name, off=0, n=128):
            return cm[:, ML[name] + off:ML[name] + off + n]

        S.dma("sp", "c0a", lambda e: e.dma_start(out=cst[:], in_=cst_d), writes=["cst"])
        S.dma("sp", "c0b", lambda e: e.dma_start(out=rowt[:], in_=rowt_d), writes=["rowt"])
        S.dma("sp", "c0c", lambda e: e.dma_start(out=j2[:], in_=j2_d), writes=["j2"])
        S.dma("pool", "c1", lambda e: e.dma_start(out=cm[:], in_=cmat_d), writes=["cm"])

        class Rope:
            def __init__(self, es, n, tag):
                self.n, self.tag = n, tag
                self.posi = sbt(es, "posi", [128, n], I32)
                self.ang = sbt(es, "ang", [128, n], F32)
                self.t1 = sbt(es, "rt1", [128, n], F32)
                self.kk = sbt(es, "rkk", [128, n], F32)

            def run(self, pos_ap, cosT, sinT, csname):
                n, tag = self.n, self.tag
                posi, ang, t1, kk = self.posi, self.ang, self.t1, self.kk
                S.dma("sp", tag + "pos", lambda e: e.dma_start(out=posi[:], in_=pos_ap.to_broadcast([128, n])), writes=[tag + "posi"])
                S.op("dve", lambda e: e.tensor_copy(out=t1[:], in_=posi[:]), reads=[tag + "posi"], writes=[tag + "t1"])
                S.op("dve", lambda e: e.tensor_scalar(out=ang[:], in0=t1[:], scalar1=ccol("inv"), scalar2=None, op0=ALU.mult),
                     reads=[tag + "t1", "cst"], writes=[tag + "ang"])
                S.op("dve", lambda e: e.tensor_scalar(out=t1[:], in0=ang[:], scalar1=float(1.0 / (2 * np.pi)), scalar2=MAGIC, op0=ALU.mult, op1=ALU.add),
                     reads=[tag + "ang"], writes=[tag + "t1"])
                S.op("dve", lambda e: e.tensor_scalar(out=kk[:], in0=t1[:], scalar1=MAGIC, scalar2=None, op0=ALU.subtract),
                     reads=[tag + "t1"], writes=[tag + "kk"])
                S.op("dve", lambda e: e.scalar_tensor_tensor(out=t1[:], in0=kk[:], scalar=-C1, in1=ang[:], op0=ALU.mult, op1=ALU.add),
                     reads=[tag + "kk", tag + "ang"], writes=[tag + "t1"])
                S.op("dve", lambda e: e.scalar_tensor_tensor(out=ang[:], in0=kk[:], scalar=-C2, in1=t1[:], op0=ALU.mult, op1=ALU.add),
                     reads=[tag + "kk", tag + "t1"], writes=[tag + "ang"])
                S.op("dve", lambda e: e.tensor_scalar(out=ang[:], in0=ang[:], scalar1=PI_LO, scalar2=-PI_LO, op0=ALU.min, op1=ALU.max),
                     reads=[tag + "ang"], writes=[tag + "ang"])
                S.op("dve", lambda e: e.scalar_tensor_tensor(out=t1[:], in0=ang[:], scalar=-1.0, in1=ang[:], op0=ALU.mult, op1=ALU.max),
                     reads=[tag + "ang"], writes=[tag + "t1"])
                S.op("act", lambda e: e.activation(out=sinT, in_=ang[:], func=AF.Sin), reads=[tag + "ang"], writes=[csname + "sin"])
                S.op("act", lambda e: e.activation(out=cosT, in_=t1[:], func=AF.Sin, scale=-1.0, bias=ccol("halfpi")),
                     reads=[tag + "t1", "cst"], writes=[csname + "cos"])

        class RMS:
            def __init__(self, es, n, tag):
                self.n, self.tag = n, tag
                self.xb = [sbt(es, "xch", [128, n], F32) for _ in range(3)]
                self.sq = [sbt(es, "sqc", [128, n], BF16) for _ in range(2)]
                self.rs = sbt(es, "rstd", [128, n], F32)

            def run(self, src_ap, gname, hn, hn_name):
                n, tag, xb, sq, rs = self.n, self.tag, self.xb, self.sq, self.rs
                pb = nextps()
                for c in range(KC):
                    b = xb[c % 3]
                    S.dma("sp", tag + "x%d" % (c % 3), lambda e, b=b, c=c: e.dma_start(out=b[:], in_=src_ap[c * 128:(c + 1) * 128, :]),
                          writes=[tag + "xb%d" % (c % 3)])
                    S.op("act", lambda e, b=b, c=c: e.activation(out=sq[c % 2][:], in_=b[:], func=AF.Square),
                         reads=[tag + "xb%d" % (c % 3)], writes=[tag + "sq%d" % (c % 2)])
                    S.op("pe", lambda e, c=c: e.matmul(PS(pb)[:, 0:n], lhsT=cmv("ones"), rhs=sq[c % 2][:], start=(c == 0), stop=(c == KC - 1)),
                         reads=[tag + "sq%d" % (c % 2), "cm"], writes=[pn(pb)])
                S.op("act", lambda e: e.activation(out=rs[:], in_=PS(pb)[:, 0:n], func=AF.Sqrt, scale=1.0 / D, bias=ccol("eps")),
                     reads=[pn(pb), "cst"], writes=[tag + "rs"])
                S.op("dve", lambda e: e.reciprocal(out=rs[:], in_=rs[:]), reads=[tag + "rs"], writes=[tag + "rs"])
                for c in range(KC):
                    b = xb[c % 3]
                    S.dma("sp", tag + "x%d" % (c % 3), lambda e, b=b, c=c: e.dma_start(out=b[:], in_=src_ap[c * 128:(c + 1) * 128, :]),
                          writes=[tag + "xb%d" % (c % 3)])
                    S.op("dve", lambda e, b=b, c=c: e.scalar_tensor_tensor(out=hn[:, c, :], in0=b[:], scalar=ccol(gname, c), in1=rs[:],
                                                                          op0=ALU.mult, op1=ALU.mult),
                         reads=[tag + "xb%d" % (c % 3), tag + "rs", "cst"], writes=[hn_name])

        class NR:
            def __init__(self, es, n, tag):
                self.n = n
                self.tag = tag
                self.sq = sbt(es, "nrsq", [128, n], BF16)
                self.r = sbt(es, "nrr", [128, n], F32)
                self.kb = sbt(es, "nrkb", [128, n], BF16)
                self.t1 = sbt(es, "nrt1", [128, n], F32)
                self.t2 = sbt(es, "nrt2", [128, n], F32)

            def run(self, pin, gname, out_ap, out_name, cosT=None, sinT=None, csname=None):
                n, tag = self.n, self.tag
                p2 = nextps()
                S.op("act", lambda e: e.activation(out=self.sq[:], in_=PS(pin)[:, 0:n], func=AF.Square), reads=[pn(pin)], writes=[tag + "sq"])
                S.op("pe", lambda e: e.matmul(PS(p2)[:, 0:n], lhsT=cmv("ones"), rhs=self.sq[:], start=True, stop=True),
                     reads=[tag + "sq", "cm"], writes=[pn(p2)])
                S.op("act", lambda e: e.activation(out=self.r[:], in_=PS(p2)[:, 0:n], func=AF.Sqrt, scale=1.0 / 128, bias=ccol("eps")),
                     reads=[pn(p2), "cst"], writes=[tag + "r"])
                S.op("dve", lambda e: e.reciprocal(out=self.r[:], in_=self.r[:]), reads=[tag + "r"], writes=[tag + "r"])
                if cosT is None:
                    S.op("dve", lambda e: e.scalar_tensor_tensor(out=out_ap, in0=PS(pin)[:, 0:n], scalar=ccol(gname), in1=self.r[:], op0=ALU.mult, op1=ALU.mult),
                         reads=[pn(pin), tag + "r", "cst"], writes=[out_name])
                    return
                S.op("dve", lambda e: e.scalar_tensor_tensor(out=self.kb[:], in0=PS(pin)[:, 0:n], scalar=ccol(gname), in1=self.r[:], op0=ALU.mult, op1=ALU.mult),
                     reads=[pn(pin), tag + "r", "cst"], writes=[tag + "kb"])
                p3 = nextps()
                S.op("pe", lambda e: e.matmul(PS(p3)[:, 0:n], lhsT=cmv("prot"), rhs=self.kb[:], start=True, stop=True),
                     reads=[tag + "kb", "cm"], writes=[pn(p3)])
                S.op("dve", lambda e: e.tensor_tensor(out=self.t1[:], in0=self.kb[:], in1=cosT, op=ALU.mult),
                     reads=[tag + "kb", csname + "cos"], writes=[tag + "t1"])
                S.op("dve", lambda e: e.tensor_tensor(out=self.t2[:], in0=PS(p3)[:, 0:n], in1=sinT, op=ALU.mult),
                     reads=[pn(p3), csname + "sin"], writes=[tag + "t2"])
                S.op("dve", lambda e: e.tensor_tensor(out=out_ap, in0=self.t1[:], in1=self.t2[:], op=ALU.add),
                     reads=[tag + "t1", tag + "t2"], writes=[out_name])

        class WStream:
            def __init__(self, es, tag, nbuf=2, size=5632):
                self.bufs = [sbt(es, "wbuf", [128, size], BF16) for _ in range(nbuf)]
                self.tag = tag
                self.i = 0
                self.size = size

            def load(self, w_ap, K, col0, gc):
                kc = K // 128
                assert kc * gc <= self.size
                i = self.i % len(self.bufs)
                self.i += 1
                view = self.bufs[i][:, 0:kc * gc].rearrange("p (c n) -> p c n", c=kc)
                name = self.tag + "w%d" % i
                S.dma("pool", name, lambda e: e.dma_start(out=view, in_=w_ap[:, col0:col0 + gc].rearrange("(c p) n -> p c n", p=128)),
                      writes=[name])
                return view, name

        def proj_fm(ws, w_ap, K, col0, ncols, rhs_fn, rhs_names, n, evac, gcmax=None):
            kc = K // 128
            gc_full = min(ncols, (ws.size // kc) // 128 * 128)
            if gcmax:
                gc_full = min(gc_full, gcmax)
            j = 0
            g0 = 0
            while g0 < ncols:
                gc = min(gc_full, ncols - g0)
                view, wname = ws.load(w_ap, K, col0 + g0, gc)
                for jj in range((gc + 127) // 128):
                    m = min(128, gc - jj * 128)
                    pb = nextps()
                    for c in range(kc):
                        S.op("pe", lambda e, c=c, jj=jj, pb=pb, view=view, m=m: e.matmul(PS(pb)[0:m, 0:n], lhsT=view[:, c, jj * 128:jj * 128 + m], rhs=rhs_fn(c),
                                                                                start=(c == 0), stop=(c == kc - 1)),
                             reads=[wname] + rhs_names, writes=[pn(pb)])
                    evac(j, pb)
                    j += 1
                g0 += gc

        def proj_tm(ws, w_ap, K, col0, ncols, lhs_fn, lhs_names, nsub, evac):
            kc = K // 128
            gc_full = min(ncols, (ws.size // kc) // 128 * 128, 512)
            g0 = 0
            while g0 < ncols:
                gc = min(gc_full, ncols - g0)
                view, wname = ws.load(w_ap, K, col0 + g0, gc)
                for ts in range(nsub):
                    pb = nextps()
                    for c in range(kc):
                        S.op("pe", lambda e, c=c, ts=ts, pb=pb, view=view, gc=gc: e.matmul(PS(pb)[:, 0:gc], lhsT=lhs_fn(c, ts), rhs=view[:, c, 0:gc],
                                                                                  start=(c == 0), stop=(c == kc - 1)),
                             reads=[wname] + lhs_names, writes=[pn(pb)])
                    evac(ts, g0, gc, pb)
                g0 += gc

        cpy_i = [0]

        def copy_out(out_ap, out_name, in_ap, in_names):
            cpy_i[0] += 1
            if cpy_i[0] % 2:
                S.op("act", lambda e: e.activation(out=out_ap, in_=in_ap, func=AF.Copy), reads=in_names, writes=[out_name])
            else:
                S.op("dve", lambda e: e.tensor_copy(out=out_ap, in_=in_ap), reads=in_names, writes=[out_name])

        def stage_end():
            S.barrier()
            S.flush()

        with ExitStack() as es:
            set_rot(range(8))
            wkv = sbt(es, "wkv", [128, KC, 2048], BF16)
            for (c0, n, d0) in [(1024, 768, 0), (3096, 512, 768), (1792, 256, 1280), (3608, 512, 1536)]:
                for half in range(2):
                    S.dma("pool", "wkv", lambda e, c0=c0, n=n, d0=d0, half=half: e.dma_start(
                        out=wkv[:, half * 8:(half + 1) * 8, d0:d0 + n],
                        in_=w_in[half * 1024:(half + 1) * 1024, c0:c0 + n].rearrange("(c p) n -> p c n", p=128)), writes=["wkv"])
            hnb = [sbt(es, "hn1", [128, KC, TQ], BF16) for _ in range(2)]
            cosT = sbt(es, "cosT", [128, TQ], F32)
            sinT = sbt(es, "sinT", [128, TQ], F32)
            nr = NR(es, TQ, "p1nr")
            rms = RMS(es, TQ, "p1rms")
            rope = Rope(es, TQ, "p1rope")
            stg = [sbt(es, "stg", [128, TQ], BF16) for _ in range(3)]
            vst = [sbt(es, "vst", [128, 768], BF16) for _ in range(2)]
            si = 0
            for gt in range(NGT):
                hn = hnb[gt % 2]
                hname = "hn1_%d" % (gt % 2)
                t0 = gt * TQ
                rms.run(xT_all[:, t0:t0 + TQ], "attn", hn, hname)
                rope.run(pos_all[0:1, t0:t0 + TQ], cosT[:], sinT[:], "p1cs")
                for j in range(10):
                    pb = nextps()
                    for c in range(KC):
                        S.op("pe", lambda e, c=c, j=j, pb=pb, hn=hn: e.matmul(PS(pb)[:, :], lhsT=wkv[:, c, j * 128:(j + 1) * 128], rhs=hn[:, c, :],
                                                                          start=(c == 0), stop=(c == KC - 1)),
                             reads=["wkv", hname], writes=[pn(pb)])
                    sb_ = stg[si % 3]
                    sname = "stg%d" % (si % 3)
                    si += 1
                    if j in (4, 5):
                        nr.run(pb, "ksn", sb_[:], sname, cosT[:], sinT[:], "p1cs")
                        dst = ksT_d[j - 4, :, t0:t0 + TQ]
                    else:
                        copy_out(sb_[:], sname, PS(pb)[:, :], [pn(pb)])
                        if j < 2:
                            dst = kcT_d[j, :, t0:t0 + TQ]
                        elif j < 4:
                            dst = vcT_d[j - 2, :, t0:t0 + TQ]
                        else:
                            dst = sbkT_d[j - 6, :, t0:t0 + TQ]
                    S.dma("sp", "p1st_" + sname, lambda e, dst=dst, sb_=sb_: e.dma_start(out=dst, in_=sb_[:]), reads=[sname], writes=["kvdram"])
                for ts in range(4):
                    vs_ = vst[ts % 2]
                    vname = "vst%d" % (ts % 2)
                    for (c0, n) in [(1280, 256), (1536, 512)]:
                        pb = nextps()
                        for c in range(KC):
                            S.op("pe", lambda e, c=c, pb=pb, hn=hn, ts=ts, c0=c0, n=n: e.matmul(PS(pb)[:, 0:n], lhsT=hn[:, c, ts * 128:(ts + 1) * 128],
                                                                                         rhs=wkv[:, c, c0:c0 + n], start=(c == 0), stop=(c == KC - 1)),
                                 reads=["wkv", hname], writes=[pn(pb)])
                        copy_out(vs_[:, c0 - 1280:c0 - 1280 + n], vname, PS(pb)[:, 0:n], [pn(pb)])
                    S.dma("sp", "p1sv_" + vname, lambda e, vs_=vs_, ts=ts, t0=t0: e.dma_start(out=vtok_d[t0 + ts * 128:t0 + (ts + 1) * 128, :], in_=vs_[:]),
                          reads=[vname], writes=["kvdram"])
            stage_end()

        with ExitStack() as es:
            set_rot(range(8))
            kcs = sbt(es, "kcs", [128, T + 16], BF16)
            w1b = sbt(es, "w1b", [128, 32, 256], BF16)
            w2b = sbt(es, "w2b", [128, 2, 128], BF16)
            peb = sbt(es, "peb", [128, 32], BF16)
            pebias = sbt(es, "pebias", [128, 2], F32)
            hf = sbt(es, "hf", [128, NB], F32)
            h2 = sbt(es, "h2", [128, NB], F32)
            sg = sbt(es, "sg", [128, NB], F32)
            hid = sbt(es, "hid", [128, 2, NB], BF16)
            cosC = sbt(es, "cosC", [128, NCP], F32)
            sinC = sbt(es, "sinC", [128, NCP], F32)
            nrc = NR(es, NB, "cnr")
            ropec = Rope(es, NCP, "crope")
            ropec.run(pos_cmp[0:1, :], cosC[:], sinC[:], "ccs")
            S.op("pool", lambda e: e.memset(kcs[:, T:T + 16], 0.0), writes=["kcs_tail"])
            for kv in range(2):
                w1d, w2d, pename = (w1k, w2k, "pek") if kv == 0 else (w1v, w2v, "pev")
                S.dma("pool", "w1b", lambda e, w1d=w1d: e.dma_start(out=w1b[:], in_=w1d.rearrange("l d f -> d l f")), writes=["w1b"])
                S.dma("pool", "w2b", lambda e, w2d=w2d: e.dma_start(out=w2b[:], in_=w2d.rearrange("(c p) d -> p c d", p=128)), writes=["w2b"])
                S.op("dve", lambda e, pename=pename: e.tensor_copy(out=peb[:], in_=cst[:, CL[pename]:CL[pename] + 32]), reads=["cst"], writes=["peb"])
                for fc in range(2):
                    pb = nextps()
                    for l in range(32):
                        S.op("pe", lambda e, l=l, fc=fc, pb=pb: e.matmul(PS(pb)[:, 0:1], lhsT=w1b[:, l, fc * 128:(fc + 1) * 128], rhs=peb[:, l:l + 1],
                                                                     start=(l == 0), stop=(l == 31)), reads=["w1b", "peb"], writes=[pn(pb)])
                    S.op("dve", lambda e, fc=fc, pb=pb: e.tensor_copy(out=pebias[:, fc:fc + 1], in_=PS(pb)[:, 0:1]), reads=[pn(pb)], writes=["pebias"])
                for g in range(2):
                    src = kcT_d if kv == 0 else vcT_d
                    S.dma("sp", "kcs", lambda e, src=src, g=g: e.dma_start(out=kcs[:, 0:T], in_=src[g, :, :]), reads=["kvdram"], writes=["kcs"])
                    for nt in range(NCP // NB):
                        n0 = nt * NB
                        for fc in range(2):
                            pb = nextps()
                            for l in range(32):
                                a0 = 16 * n0 + l
                                S.op("pe", lambda e, l=l, fc=fc, pb=pb, a0=a0: e.matmul(PS(pb)[:, 0:NB], lhsT=w1b[:, l, fc * 128:(fc + 1) * 128],
                                                                                 rhs=kcs[:, a0:a0 + 16 * (NB - 1) + 1:16], start=(l == 0), stop=(l == 31)),
                                     reads=["w1b", "kcs", "kcs_tail"], writes=[pn(pb)])
                            S.op("act", lambda e, pb=pb, fc=fc: e.activation(out=hf[:], in_=PS(pb)[:, 0:NB], func=AF.Identity, bias=pebias[:, fc:fc + 1]),
                                 reads=[pn(pb), "pebias"], writes=["hf"])
                            S.op("dve", lambda e: e.tensor_tensor(out=h2[:], in0=hf[:], in1=hf[:], op=ALU.mult), reads=["hf"], writes=["h2"])
                            S.op("dve", lambda e: e.tensor_scalar(out=h2[:], in0=h2[:], scalar1=0.044715, scalar2=1.0, op0=ALU.mult, op1=ALU.add),
                                 reads=["h2"], writes=["h2"])
                            S.op("dve", lambda e: e.tensor_tensor(out=h2[:], in0=h2[:], in1=hf[:], op=ALU.mult), reads=["h2", "hf"], writes=["h2"])
                            S.op("act", lambda e: e.activation(out=sg[:], in_=h2[:], func=AF.Sigmoid, scale=float(2.0 * math.sqrt(2.0 / math.pi))),
                                 reads=["h2"], writes=["sg"])
                            S.op("dve", lambda e, fc=fc: e.tensor_tensor(out=hid[:, fc, :], in0=hf[:], in1=sg[:], op=ALU.mult), reads=["hf", "sg"], writes=["hid"])
                        if kv == 0:
                            pb = nextps()
                            for fc in range(2):
                                S.op("pe", lambda e, fc=fc, pb=pb: e.matmul(PS(pb)[:, 0:NB], lhsT=w2b[:, fc, :], rhs=hid[:, fc, :], start=(fc == 0), stop=(fc == 1)),
                                     reads=["w2b", "hid"], writes=[pn(pb)])
                            nrc.run(pb, "kcn", kcTs[:, g, n0:n0 + NB], "kcTs", cosC[:, n0:n0 + NB], sinC[:, n0:n0 + NB], "ccs")
                        else:
                            for ns in range(NB // 128):
                                pb = nextps()
                                for fc in range(2):
                                    S.op("pe", lambda e, fc=fc, pb=pb, ns=ns: e.matmul(PS(pb)[:, 0:128], lhsT=hid[:, fc, ns * 128:(ns + 1) * 128], rhs=w2b[:, fc, :],
                                                                                  start=(fc == 0), stop=(fc == 1)), reads=["w2b", "hid"], writes=[pn(pb)])
                                copy_out(vcs[:, g, n0 // 128 + ns, :], "vcs", PS(pb)[:, 0:128], [pn(pb)])
            stage_end()

        with ExitStack() as es:
            set_rot(range(8))
            hm = sbt(es, "hm", [128, KC, 256], BF16)
            rmsm = RMS(es, 256, "mrms")
            nrm = NR(es, 256, "mnr")
            wsm = WStream(es, "wsm", 2, 4096)
            rmsm.run(memT, "memn", hm, "hm")

            def ev_k(j, pb):
                nrm.run(pb, "mkn", kmem[:, j, :], "kmem")
            proj_fm(wsm, w_mem, D, 0, 512, lambda c: hm[:, c, :], ["hm"], 256, ev_k, gcmax=256)

            def ev_v(ts, g0, gc, pb):
                copy_out(vmem[:, ts, g0:g0 + gc], "vmem", PS(pb)[:, 0:gc], [pn(pb)])
            proj_tm(wsm, w_mem, D, 512, 512, lambda c, ts: hm[:, c, ts * 128:(ts + 1) * 128], ["hm"], 2, ev_v)
            stage_end()

        with ExitStack() as es_p2:
            ynsa_b = sbt(es_p2, "ynsa_b", [128, 8, TQ], BF16)
            ysb = sbt(es_p2, "ysb", [128, 4, TQ], BF16)
            ymem = sbt(es_p2, "ymem", [128, 4, TQ], BF16)
            for k in range(NT):
                o0 = k * TQ
                with ExitStack() as es_a:
                    qn = sbt(es_a, "qn", [128, 8, TQ], BF16)
                    qs = sbt(es_a, "qs", [128, 4, TQ], BF16)
                    qm = sbt(es_a, "qm", [128, 4, TQ], BF16)
                    ksTo = sbt(es_a, "ksTo", [128, 2, TQ], BF16)
                    kwTo = sbt(es_a, "kwTo", [128, 2, TQ], BF16)
                    kwTp = sbt(es_a, "kwTp", [128, 2, TQ], BF16)
                    sbkTo = sbt(es_a, "sbkTo", [128, 4, TQ], BF16)
                    vown = sbt(es_a, "vown", [128, 4, 1024], BF16)
                    vprev = sbt(es_a, "vprev", [128, 4, 256], BF16)
                    with ExitStack() as es:
                        set_rot(range(8))
                        hp = sbt(es, "hp", [128, KC, TQ], BF16)
                        hn = sbt(es, "hn", [128, KC, TQ], BF16)
                        cosO = sbt(es, "cosO", [128, TQ], F32)
                        sinO = sbt(es, "sinO", [128, TQ], F32)
                        cosP = sbt(es, "cosP", [128, TQ], F32)
                        sinP = sbt(es, "sinP", [128, TQ], F32)
                        g32 = sbt(es, "g32", [24, TQ], F32)
                        rms2 = RMS(es, TQ, "a1rms")
                        rope2 = Rope(es, TQ, "a1rope")
                        nr2 = NR(es, TQ, "a1nr")
                        ws = WStream(es, "a1ws", 2, 4096)
                        rms2.run(xT_own[:, o0:o0 + TQ], "attn", hn, "hn")
                        rms2.run(xT_prev[:, o0:o0 + TQ], "attn", hp, "hp")
                        rope2.run(pos_own[0:1, o0:o0 + TQ], cosO[:], sinO[:], "cso")
                        rope2.run(pos_prev[0:1, o0:o0 + TQ], cosP[:], sinP[:], "csp")
                        rh = lambda c: hn[:, c, :]
                        rp = lambda c: hp[:, c, :]
                        proj_fm(ws, w_in, D, 0, 1024, rh, ["hn"], TQ, lambda j, pb: nr2.run(pb, "qn", qn[:, j, :], "qn", cosO[:], sinO[:], "cso"), gcmax=256)
                        proj_fm(ws, w_in, D, 1536, 256, rh, ["hn"], TQ, lambda j, pb: nr2.run(pb, "ksn", ksTo[:, j, :], "ksTo", cosO[:], sinO[:], "cso"), gcmax=256)
                        proj_fm(ws, w_in, D, 2048, 256, rh, ["hn"], TQ, lambda j, pb: nr2.run(pb, "kwn", kwTo[:, j, :], "kwTo", cosO[:], sinO[:], "cso"), gcmax=256)
                        proj_fm(ws, w_in, D, 2048, 256, rp, ["hp"], TQ, lambda j, pb: nr2.run(pb, "kwn", kwTp[:, j, :], "kwTp", cosP[:], sinP[:], "csp"), gcmax=256)
                        proj_fm(ws, w_in, D, 2584, 512, rh, ["hn"], TQ, lambda j, pb: copy_out(qs[:, j, :], "qs", PS(pb)[:, :], [pn(pb)]), gcmax=256)
                        proj_fm(ws, w_in, D, 3096, 512, rh, ["hn"], TQ, lambda j, pb: copy_out(sbkTo[:, j, :], "sbkTo", PS(pb)[:, :], [pn(pb)]), gcmax=256)
                        proj_fm(ws, w_in, D, 4120, 512, rh, ["hn"], TQ, lambda j, pb: nr2.run(pb, "mqn", qm[:, j, :], "qm"), gcmax=256)

                        def ev_g(j, pb):
                            S.op("act", lambda e: e.activation(out=g32[:], in_=PS(pb)[0:24, :], func=AF.Sigmoid), reads=[pn(pb)], writes=["g32"])
                            S.dma("sp", "gst", lambda e: e.dma_start(out=gates_d, in_=g32[:]), reads=["g32"], writes=["gates_d"])
                        proj_fm(ws, w_in, D, 2560, 24, rh, ["hn"], TQ, ev_g)
                        lh = lambda c, ts: hn[:, c, ts * 128:(ts + 1) * 128]
                        lp = lambda c, ts: hp[:, c, ts * 128:(ts + 1) * 128]
                        proj_tm(ws, w_in, D, 1792, 256, lh, ["hn"], 4, lambda ts, g0, gc, pb: copy_out(vown[:, ts, 0:256], "vown", PS(pb)[:, 0:256], [pn(pb)]))
                        proj_tm(ws, w_in, D, 2304, 256, lh, ["hn"], 4, lambda ts, g0, gc, pb: copy_out(vown[:, ts, 256:512], "vown", PS(pb)[:, 0:256], [pn(pb)]))
                        proj_tm(ws, w_in, D, 3608, 512, lh, ["hn"], 4, lambda ts, g0, gc, pb: copy_out(vown[:, ts, 512 + g0:512 + g0 + gc], "vown", PS(pb)[:, 0:gc], [pn(pb)]))
                        proj_tm(ws, w_in, D, 2304, 256, lp, ["hp"], 4, lambda ts, g0, gc, pb: copy_out(vprev[:, ts, :], "vprev", PS(pb)[:, 0:256], [pn(pb)]))
                        S.dma("sp", "hnst", lambda e: e.dma_start(out=hn_d, in_=hn[:].rearrange("p c t -> p (c t)")), reads=["hn"], writes=["hn_d"])
                        stage_end()

                    with ExitStack() as es:
                        O_B, SUM_B, U0_B, U1_B = 0, 1, 2, 3
                        set_rot([4, 5, 6, 7])
                        ynsa = sbt(es, "ynsa", [128, 4, TQ], F32)
                        impacc = sbt(es, "impacc", [128, 2, 4, 256], F32)
                        selbT = sbt(es, "selbT", [128, 2, 2, TQ], BF16)
                        ownb = sbt(es, "ownb", [8, 2, TQ], BF16)
                        gbc = [sbt(es, "gbc", [128, 3, TQ], F32) for _ in range(1)]
                        Pb = [sbt(es, "Pb", [128, TQ], BF16) for _ in range(3)]
                        rec = sbt(es, "rec", [128, TQ], F32)
                        tmpf = sbt(es, "tmpf", [128, TQ], F32)
                        kbuf = [sbt(es, "kbuf", [128, TQ], BF16) for _ in range(3)]
                        vbuf = [sbt(es, "vbuf", [128, 4, 128], BF16) for _ in range(3)]
                        cmask = sbt(es, "cmask", [128, NCC, TQ], BF16)
                        rtok = sbt(es, "rtok", [128, 4], F32)
                        tk1 = sbt(es, "tk1", [128, 256], F32)
                        tk2 = sbt(es, "tk2", [128, 256], F32)
                        tk3 = sbt(es, "tk3", [128, 256], F32)
                        tk4 = sbt(es, "tk4", [128, 256], F32)
                        m8 = sbt(es, "m8", [128, 16], F32)
                        alb = sbt(es, "alb", [128, 256], BF16)
                        alT = sbt(es, "alT", [128, 2, TQ], BF16)
                        e32 = [sbt(es, "e32", [128, TQ], F32) for _ in range(2)]
                        ec32 = [sbt(es, "ec32", [128, TQ], F32) for _ in range(2)]
                        Lp = [sbt(es, "Lp", [128, TQ], BF16) for _ in range(3)]
                        Wb = [sbt(es, "Wb", [128, TQ], BF16) for _ in range(2)]
                        Rb = [sbt(es, "Rb", [128, TQ], BF16) for _ in range(2)]
                        ncmp = min(NCC, 2 * (k + 1))
                        gi = [0]

                        def load_gates(h):
                            b = gbc[0]
                            name = "gbc0"
                            gi[0] += 1
                            S.dma("sp", name, lambda e: e.dma_start(out=b[:], in_=gates_d[3 * h:3 * h + 3, :].rearrange("(o r) t -> o r t", o=1).to_broadcast([128, 3, TQ])),
                                  reads=["gates_d"], writes=[name])
                            return b, name

                        for j in range(ncmp):
                            S.op("dve", lambda e, j=j: e.tensor_scalar(out=cmask[:, j, :], in0=j2[:], scalar1=ccol("cthr", k * NCC + j), scalar2=None, op0=ALU.is_gt),
                                 reads=["j2", "cst"], writes=["cmask"])

                        pi = [0]

                        def softmax_chunks(chunks, q_ap, q_names, extra=None, i0=0, n_total=None):
                            n = len(chunks) if n_total is None else n_total
                            for i_, ch in enumerate(chunks):
                                i = i0 + i_
                                sb_ = nextps()
                                masks = ch.get("masks", [])
                                S.op("pe", lambda e, ch=ch, sb_=sb_, masks=masks: e.matmul(PS(sb_)[:, :], lhsT=ch["kT"], rhs=q_ap, start=True, stop=(len(masks) == 0)),
                                     reads=ch["kn"] + q_names, writes=[pn(sb_)])
                                for mi, (ml, mr, mn) in enumerate(masks):
                                    S.op("pe", lambda e, ml=ml, mr=mr, sb_=sb_, mi=mi, masks=masks: e.matmul(PS(sb_)[:, :], lhsT=ml, rhs=mr, start=False, stop=(mi == len(masks) - 1)),
                                         reads=mn + ["cm"], writes=[pn(sb_)])
                                P = Pb[pi[0] % 3]
                                pname = "Pb%d" % (pi[0] % 3)
                                pi[0] += 1
                                bias = ch.get("bias")
                                if bias is None:
                                    bias = ccol("zero")
                                S.op("act", lambda e, P=P, sb_=sb_, bias=bias: e.activation(out=P[:], in_=PS(sb_)[:, :], func=AF.Exp, scale=SCALE, bias=bias),
                                     reads=[pn(sb_), "cst"], writes=[pname])
                                S.op("pe", lambda e, ch=ch, P=P, i=i: e.matmul(PS(O_B)[:, :], lhsT=ch["v"], rhs=P[:], start=(i == 0), stop=(i == n - 1)),
                                     reads=ch["vn"] + [pname], writes=[pn(O_B)])
                                S.op("pe", lambda e, P=P, i=i: e.matmul(PS(SUM_B)[:, :], lhsT=cmv("ones"), rhs=P[:], start=(i == 0), stop=(i == n - 1)),
                                     reads=["cm", pname], writes=[pn(SUM_B)])
                                if extra:
                                    extra(i, P, pname, n)

                        def finalize(out_ap, out_name, gate_ap=None, gate_name=None, accumulate=False):
                            S.op("dve", lambda e: e.tensor_scalar(out=rec[:], in0=PS(SUM_B)[:, :], scalar1=1e-30, scalar2=None, op0=ALU.max), reads=[pn(SUM_B)], writes=["rec"])
                            S.op("dve", lambda e: e.reciprocal(out=rec[:], in_=rec[:]), reads=["rec"], writes=["rec"])
                            if gate_ap is not None:
                                S.op("dve", lambda e: e.tensor_tensor(out=rec[:], in0=rec[:], in1=gate_ap, op=ALU.mult), reads=["rec", gate_name], writes=["rec"])
                            if accumulate:
                                S.op("dve", lambda e: e.tensor_tensor(out=tmpf[:], in0=PS(O_B)[:, :], in1=rec[:], op=ALU.mult), reads=[pn(O_B), "rec"], writes=["tmpf"])
                                S.op("dve", lambda e: e.tensor_tensor(out=out_ap, in0=out_ap, in1=tmpf[:], op=ALU.add), reads=["tmpf", out_name], writes=[out_name])
                            else:
                                S.op("dve", lambda e: e.tensor_tensor(out=out_ap, in0=PS(O_B)[:, :], in1=rec[:], op=ALU.mult), reads=[pn(O_B), "rec"], writes=[out_name])

                        si2 = [0]

                        def stream_tile(kT_src, v_c0, gt):
                            i = si2[0] % 3
                            si2[0] += 1
                            kb, vb = kbuf[i], vbuf[i]
                            S.dma("sp", "kb%d" % i, lambda e: e.dma_start(out=kb[:], in_=kT_src[:, gt * TQ:(gt + 1) * TQ]), reads=["kvdram"], writes=["kbuf%d" % i])
                            S.dma("sp", "vb%d" % i, lambda e: e.dma_start(out=vb[:], in_=vtok_d[gt * TQ:(gt + 1) * TQ, v_c0:v_c0 + 128].rearrange("(ts p) d -> p ts d", p=128)),
                                  reads=["kvdram"], writes=["vbuf%d" % i])
                            return kb, vb, "kbuf%d" % i, "vbuf%d" % i

                        for g in range(2):
                            set_rot([6, 7])
                            for hg in range(4):
                                h = g * 4 + hg
                                gb, gname = load_gates(h)
                                chunks = []
                                for j in range(ncmp):
                                    chunks.append(dict(kT=kcTs[:, g, j * 128:(j + 1) * 128], kn=["kcTs"], v=vcs[:, g, j, :], vn=["vcs"],
                                                       masks=[(cmv("negI"), cmask[:, j, :], ["cmask"])]))

                                def extra(i, P, pname, n):
                                    for ts in range(4):
                                        ub = 2 + ts
                                        o_ = 0
                                        S.op("pe", lambda e, P=P, ts=ts, ub=ub, o_=o_, i=i, n=n: e.matmul(PS(ub)[:, o_:o_ + 257], lhsT=P[:, ts * 128:(ts + 1) * 128],
                                                                                                    rhs=cm[:, ML["OV"] + i * 257:ML["OV"] + (i + 1) * 257],
                                                                                                    start=(i == 0), stop=(i == n - 1)),
                                             reads=[pname, "cm"], writes=[pn(ub)])
                                softmax_chunks(chunks, qn[:, h, :], ["qn"], extra)
                                finalize(ynsa[:, hg, :], "ynsa", gb[:, 0, :], gname)
                                for ts in range(4):
                                    ub = 2 + ts
                                    o_ = 0
                                    S.op("dve", lambda e, ts=ts, ub=ub, o_=o_: e.tensor_scalar(out=rtok[:, ts:ts + 1], in0=PS(ub)[:, o_ + 256:o_ + 257], scalar1=1e-30, scalar2=None, op0=ALU.max),
                                         reads=[pn(ub)], writes=["rtok"])
                                    S.op("dve", lambda e, ts=ts: e.reciprocal(out=rtok[:, ts:ts + 1], in_=rtok[:, ts:ts + 1]), reads=["rtok"], writes=["rtok"])
                                    if hg == 0:
                                        S.op("dve", lambda e, ts=ts, ub=ub, o_=o_: e.tensor_scalar(out=impacc[:, g, ts, :], in0=PS(ub)[:, o_:o_ + 256], scalar1=rtok[:, ts:ts + 1], scalar2=None, op0=ALU.mult),
                                             reads=[pn(ub), "rtok"], writes=["impacc"])
                                    else:
                                        S.op("dve", lambda e, ts=ts, ub=ub, o_=o_: e.scalar_tensor_tensor(out=impacc[:, g, ts, :], in0=PS(ub)[:, o_:o_ + 256], scalar=rtok[:, ts:ts + 1],
                                                                                                       in1=impacc[:, g, ts, :], op0=ALU.mult, op1=ALU.add),
                                             reads=[pn(ub), "rtok", "impacc"], writes=["impacc"])
                            set_rot([2, 3, 4, 5, 6, 7])
                            for ts in range(4):
                                curc = ccol("cur", k * 4 + ts)
                                blk = rowt[:, 0:256]
                                S.op("dve", lambda e, curc=curc: e.tensor_scalar(out=tk1[:], in0=blk, scalar1=curc, scalar2=None, op0=ALU.subtract), reads=["rowt", "cst"], writes=["tk1"])
                                S.op("dve", lambda e: e.tensor_scalar(out=tk2[:], in0=tk1[:], scalar1=0.0, scalar2=None, op0=ALU.is_gt), reads=["tk1"], writes=["tk2"])
                                S.op("dve", lambda e: e.tensor_scalar(out=tk3[:], in0=tk1[:], scalar1=-1.0, scalar2=None, op0=ALU.is_ge), reads=["tk1"], writes=["tk3"])
                                S.op("dve", lambda e: e.tensor_tensor(out=tk3[:], in0=tk3[:], in1=tk2[:], op=ALU.subtract), reads=["tk3", "tk2"], writes=["tk3"])
                                S.op("dve", lambda e: e.tensor_scalar(out=tk4[:], in0=blk, scalar1=0.0, scalar2=None, op0=ALU.is_equal), reads=["rowt"], writes=["tk4"])
                                S.op("dve", lambda e: e.tensor_tensor(out=tk3[:], in0=tk3[:], in1=tk4[:], op=ALU.max), reads=["tk3", "tk4"], writes=["tk3"])
                                S.op("dve", lambda e: e.tensor_tensor(out=tk3[:], in0=tk3[:], in1=tk2[:], op=ALU.subtract), reads=["tk3", "tk2"], writes=["tk3"])
                                S.op("dve", lambda e, ts=ts: e.scalar_tensor_tensor(out=tk1[:], in0=tk3[:], scalar=1.0e4, in1=impacc[:, g, ts, :], op0=ALU.mult, op1=ALU.add),
                                     reads=["tk3", "impacc"], writes=["tk1"])
                                S.op("dve", lambda e: e.max(out=m8[:, 0:8], in_=tk1[:]), reads=["tk1"], writes=["m8a"])
                                S.op("dve", lambda e: e.match_replace(out=tk4[:], in_to_replace=m8[:, 0:8], in_values=tk1[:], imm_value=-3.0e4), reads=["tk1", "m8a"], writes=["tk4"])
                                S.op("dve", lambda e: e.max(out=m8[:, 8:16], in_=tk4[:]), reads=["tk4"], writes=["m8b"])
                                S.op("dve", lambda e: e.tensor_scalar(out=tk4[:], in0=tk1[:], scalar1=m8[:, 15:16], scalar2=None, op0=ALU.is_ge), reads=["tk1", "m8b"], writes=["tk4"])
                                S.op("dve", lambda e: e.tensor_scalar(out=tk2[:], in0=tk2[:], scalar1=-1.0, scalar2=1.0, op0=ALU.mult, op1=ALU.add), reads=["tk2"], writes=["tk2"])
                                S.op("dve", lambda e: e.tensor_tensor(out=alb[:], in0=tk4[:], in1=tk2[:], op=ALU.mult), reads=["tk4", "tk2"], writes=["alb"])
                                for half in range(2):
                                    pb = nextps()
                                    S.op("pe", lambda e, half=half, pb=pb: e.matmul(PS(pb)[:, 0:128], lhsT=alb[:, half * 128:(half + 1) * 128], rhs=cmv("ident"), start=True, stop=True),
                                         reads=["alb", "cm"], writes=[pn(pb)])
                                    copy_out(alT[:, half, ts * 128:(ts + 1) * 128], "alT", PS(pb)[:, 0:128], [pn(pb)])
                            pb = nextps()
                            for half in range(2):
                                osel = cm[:, ML["OwnSel"] + k * 16 + half * 8:ML["OwnSel"] + k * 16 + half * 8 + 8]
                                S.op("pe", lambda e, half=half, pb=pb, osel=osel: e.matmul(PS(pb)[0:8, :], lhsT=osel, rhs=alT[:, half, :], start=(half == 0), stop=(half == 1)),
                                     reads=["alT", "cm"], writes=[pn(pb)])
                            S.op("dve", lambda e, pb=pb: e.tensor_scalar(out=ownb[:, g, :], in0=PS(pb)[0:8, :], scalar1=-1.0, scalar2=-NEG, op0=ALU.add, op1=ALU.mult),
                                 reads=[pn(pb)], writes=["ownb"])
                            for half in range(2):
                                S.op("dve", lambda e, half=half: e.tensor_scalar(out=selbT[:, g, half, :], in0=alT[:, half, :], scalar1=cst[:, CL["_om"] + k * 2 + half:CL["_om"] + k * 2 + half + 1],
                                                                               scalar2=None, op0=ALU.mult), reads=["alT", "cst"], writes=["selbT"])
                                S.op("dve", lambda e, half=half: e.tensor_scalar(out=selbT[:, g, half, :], in0=selbT[:, g, half, :], scalar1=-1.0, scalar2=-NEG, op0=ALU.add, op1=ALU.mult),
                                     reads=["selbT"], writes=["selbT"])
                            for hg in range(4):
                                h = g * 4 + hg
                                gb, gname = load_gates(h)
                                chunks = []
                                for c_ in range(4):
                                    chunks.append(dict(kT=ksTo[:, g, c_ * 128:(c_ + 1) * 128], kn=["ksTo"], v=vown[:, c_, g * 128:(g + 1) * 128], vn=["vown"],
                                                       masks=[(cm[0:8, ML["Eown"] + c_ * 128:ML["Eown"] + (c_ + 1) * 128], ownb[:, g, :], ["ownb"]),
                                                              (cmv("negI"), cm[:, ML["DiagN"] + c_ * 512:ML["DiagN"] + (c_ + 1) * 512], [])]))
                                ntot = 4 + 4 * (8 * k + 8)
                                softmax_chunks(chunks, qn[:, h, :], ["qn"], None, 0, ntot)
                                for gt in range(8 * k + 8):
                                    kb, vb, kname, vname = stream_tile(ksT_d[g], g * 128, gt)
                                    chunks = []
                                    for c_ in range(4):
                                        cg = gt * 4 + c_
                                        chunks.append(dict(kT=kb[:, c_ * 128:(c_ + 1) * 128], kn=[kname], v=vb[:, c_, :], vn=[vname],
                                                           masks=[(cm[:, ML["E64"] + (cg % 64) * 128:ML["E64"] + (cg % 64 + 1) * 128], selbT[:, g, cg // 64, :], ["selbT"])]))
                                    softmax_chunks(chunks, qn[:, h, :], ["qn"], None, 4 + 4 * gt, ntot)
                                finalize(ynsa[:, hg, :], "ynsa", gb[:, 1, :], gname, accumulate=True)
                            for hg in range(4):
                                h = g * 4 + hg
                                gb, gname = load_gates(h)
                                chunks = []
                                for c_ in range(4):
                                    chunks.append(dict(kT=kwTp[:, g, c_ * 128:(c_ + 1) * 128], kn=["kwTp"], v=vprev[:, c_, g * 128:(g + 1) * 128], vn=["vprev"],
                                                       masks=[(cmv("negI"), cm[:, ML["DiagW"] + c_ * 512:ML["DiagW"] + (c_ + 1) * 512], [])], bias=ccol("pv", k)))
                                for c_ in range(4):
                                    chunks.append(dict(kT=kwTo[:, g, c_ * 128:(c_ + 1) * 128], kn=["kwTo"], v=vown[:, c_, 256 + g * 128:256 + (g + 1) * 128], vn=["vown"],
                                                       masks=[(cmv("negI"), cm[:, ML["DiagN"] + c_ * 512:ML["DiagN"] + (c_ + 1) * 512], [])]))
                                softmax_chunks(chunks, qn[:, h, :], ["qn"])
                                finalize(ynsa[:, hg, :], "ynsa", gb[:, 2, :], gname, accumulate=True)
                                copy_out(ynsa_b[:, h, :], "ynsa_b", ynsa[:, hg, :], ["ynsa"])
                        for h in range(4):
                            chunks = [dict(kT=kmem[:, h, ms * 128:(ms + 1) * 128], kn=["kmem"], v=vmem[:, ms, h * 128:(h + 1) * 128], vn=["vmem"]) for ms in range(2)]
                            softmax_chunks(chunks, qm[:, h, :], ["qm"])
                            finalize(ymem[:, h, :], "ymem")
                        for h in range(4):
                            chunks = []
                            for c_ in (3, 2, 1, 0):
                                chunks.append(dict(kT=sbkTo[:, h, c_ * 128:(c_ + 1) * 128], kn=["sbkTo"], v=vown[:, c_, 512 + h * 128:512 + (h + 1) * 128], vn=["vown"],
                                                   diag=cm[:, ML["DiagS"] + c_ * 512:ML["DiagS"] + (c_ + 1) * 512], bias=ccol("zero")))
                            nch = 4 + (8 * k + 8) * 4
                            ci = 0
                            Rprev = None
                            tiles = [None] + list(range(8 * k + 7, -1, -1))
                            for tl in tiles:
                                if tl is None:
                                    cl = chunks
                                else:
                                    kb, vb, kname, vname = stream_tile(sbkT_d[h], 256 + h * 128, tl)
                                    j_ = tl - 8 * k
                                    bias = ccol("cw", j_) if j_ >= 0 else ccol("zero")
                                    cl = [dict(kT=kb[:, c_ * 128:(c_ + 1) * 128], kn=[kname], v=vb[:, c_, :], vn=[vname], bias=bias) for c_ in (3, 2, 1, 0)]
                                for ch in cl:
                                    zb = nextps()
                                    dg = ch.get("diag")
                                    S.op("pe", lambda e, ch=ch, zb=zb, dg=dg: e.matmul(PS(zb)[:, :], lhsT=ch["kT"], rhs=qs[:, h, :], start=True, stop=(dg is None)),
                                         reads=ch["kn"] + ["qs"], writes=[pn(zb)])
                                    if dg is not None:
                                        S.op("pe", lambda e, zb=zb, dg=dg: e.matmul(PS(zb)[:, :], lhsT=cmv("negI"), rhs=dg, start=False, stop=True), reads=["cm"], writes=[pn(zb)])
                                    e_ = e32[ci % 2]
                                    en = "e32_%d" % (ci % 2)
                                    S.op("act", lambda e, e_=e_, zb=zb, ch=ch: e.activation(out=e_[:], in_=PS(zb)[:, :], func=AF.Exp, scale=SCALE, bias=ch["bias"]),
                                         reads=[pn(zb), "cst"], writes=[en])
                                    L_ = Lp[ci % 3]
                                    ln_ = "Lp%d" % (ci % 3)
                                    S.op("act", lambda e, e_=e_, L_=L_: e.activation(out=L_[:], in_=e_[:], func=AF.Ln, bias=ccol("one")), reads=[en, "cst"], writes=[ln_])
                                    cb = nextps()
                                    S.op("pe", lambda e, cb=cb, L_=L_, Rprev=Rprev: e.matmul(PS(cb)[:, :], lhsT=cmv("UIneg"), rhs=L_[:], start=True, stop=(Rprev is None)),
                                         reads=[ln_, "cm"], writes=[pn(cb)])
                                    if Rprev is not None:
                                        S.op("pe", lambda e, cb=cb, Rprev=Rprev: e.matmul(PS(cb)[:, :], lhsT=cmv("onesneg"), rhs=Rprev[0][:], start=False, stop=True),
                                             reads=[Rprev[1], "cm"], writes=[pn(cb)])
                                    ec_ = ec32[ci % 2]
                                    ecn = "ec32_%d" % (ci % 2)
                                    S.op("act", lambda e, ec_=ec_, cb=cb: e.activation(out=ec_[:], in_=PS(cb)[:, :], func=AF.Exp), reads=[pn(cb)], writes=[ecn])
                                    W_ = Wb[ci % 2]
                                    wn_ = "Wb%d" % (ci % 2)
                                    S.op("dve", lambda e, W_=W_, e_=e_, ec_=ec_: e.tensor_tensor(out=W_[:], in0=e_[:], in1=ec_[:], op=ALU.mult), reads=[en, ecn], writes=[wn_])
                                    Rn = Rb[ci % 2]
                                    rn_ = "Rb%d" % (ci % 2)
                                    if Rprev is None:
                                        S.op("dve", lambda e, Rn=Rn, L_=L_: e.tensor_copy(out=Rn[:], in_=L_[:]), reads=[ln_], writes=[rn_])
                                    else:
                                        S.op("dve", lambda e, Rn=Rn, L_=L_, Rprev=Rprev: e.tensor_tensor(out=Rn[:], in0=Rprev[0][:], in1=L_[:], op=ALU.add), reads=[ln_, Rprev[1]], writes=[rn_])
                                    Rprev = (Rn, rn_)
                                    S.op("pe", lambda e, ch=ch, W_=W_, ci=ci: e.matmul(PS(O_B)[:, :], lhsT=ch["v"], rhs=W_[:], start=(ci == 0), stop=(ci == nch - 1)),
                                         reads=ch["vn"] + [wn_], writes=[pn(O_B)])
                                    ci += 1
                            copy_out(ysb[:, h, :], "ysb", PS(O_B)[:, :], [pn(O_B)])
                        stage_end()

                with ExitStack() as es_b:
                    x1 = sbt(es_b, "x1", [128, KC, TQ], F32)
                    hn = sbt(es_b, "hnB", [128, KC, TQ], BF16)
                    S.dma("sp", "hnld", lambda e: e.dma_start(out=hn[:].rearrange("p c t -> p (c t)"), in_=hn_d), reads=["hn_d"], writes=["hn"])
                    with ExitStack() as es:
                        set_rot(range(8))
                        mixed = sbt(es, "mixed", [128, KC, TQ], BF16)
                        sig = sbt(es, "sig", [128, 3, TQ], F32)
                        acc = sbt(es, "acc", [128, TQ], F32)
                        tmpm = sbt(es, "tmpm", [128, TQ], F32)
                        wsg = WStream(es, "b1wg", 2, 2048)
                        wso = WStream(es, "b1wo", 2, 2048)
                        for dc in range(KC):
                            for br in range(3):
                                def ev_s(j, pb, br=br):
                                    S.op("act", lambda e: e.activation(out=sig[:, br, :], in_=PS(pb)[:, :], func=AF.Sigmoid), reads=[pn(pb)], writes=["sig"])
                                proj_fm(wsg, w_in, D, 4632 + br * D + dc * 128, 128, lambda c: hn[:, c, :], ["hn"], TQ, ev_s)
                            for br, (wo, K_, y_, yn_) in enumerate([(wo_nsa, 1024, ynsa_b, "ynsa_b"), (wo_sb, 512, ysb, "ysb"), (wo_mem, 512, ymem, "ymem")]):
                                def ev_a(j, pb, br=br):
                                    if br == 0:
                                        S.op("dve", lambda e: e.tensor_tensor(out=acc[:], in0=PS(pb)[:, :], in1=sig[:, 0, :], op=ALU.mult), reads=[pn(pb), "sig"], writes=["acc"])
                                    else:
                                        S.op("dve", lambda e: e.tensor_tensor(out=tmpm[:], in0=PS(pb)[:, :], in1=sig[:, br, :], op=ALU.mult), reads=[pn(pb), "sig"], writes=["tmpm"])
                                        if br == 1:
                                            S.op("dve", lambda e: e.tensor_tensor(out=acc[:], in0=acc[:], in1=tmpm[:], op=ALU.add), reads=["acc", "tmpm"], writes=["acc"])
                                        else:
                                            S.op("dve", lambda e: e.tensor_tensor(out=mixed[:, dc, :], in0=acc[:], in1=tmpm[:], op=ALU.add), reads=["acc", "tmpm"], writes=["mixed"])
                                proj_fm(wso, wo, K_, dc * 128, 128, lambda c, y_=y_: y_[:, c, :], [yn_], TQ, ev_a)
                        xr = [sbt(es, "xr", [128, TQ], F32) for _ in range(2)]

                        def ev_o(j, pb):
                            b = xr[j % 2]
                            S.dma("sp", "xr%d" % (j % 2), lambda e: e.dma_start(out=b[:], in_=xT_own[j * 128:(j + 1) * 128, o0:o0 + TQ]), writes=["xr%d" % (j % 2)])
                            S.op("dve", lambda e: e.tensor_tensor(out=x1[:, j, :], in0=PS(pb)[:, :], in1=b[:], op=ALU.add), reads=[pn(pb), "xr%d" % (j % 2)], writes=["x1"])
                        proj_fm(wsg, w_out, D, 0, D, lambda c: mixed[:, c, :], ["mixed"], TQ, ev_o, gcmax=128)
                        stage_end()
                    with ExitStack() as es:
                        set_rot(range(8))
                        act_all = sbt(es, "act_all", [128, FFN // 256, TQ], BF16)
                        sq2 = [sbt(es, "sq2", [128, TQ], BF16) for _ in range(2)]
                        rs2 = sbt(es, "rs2", [128, TQ], F32)
                        sgl = sbt(es, "sgl", [128, TQ], F32)
                        ost = [sbt(es, "ost", [128, TQ], F32) for _ in range(2)]
                        wsf = WStream(es, "b2wf", 2, 5632)
                        pb0 = nextps()
                        for c in range(KC):
                            S.op("act", lambda e, c=c: e.activation(out=sq2[c % 2][:], in_=x1[:, c, :], func=AF.Square), reads=["x1"], writes=["sq2_%d" % (c % 2)])
                            S.op("pe", lambda e, c=c: e.matmul(PS(pb0)[:, :], lhsT=cmv("ones"), rhs=sq2[c % 2][:], start=(c == 0), stop=(c == KC - 1)),
                                 reads=["sq2_%d" % (c % 2), "cm"], writes=[pn(pb0)])
                        S.op("act", lambda e: e.activation(out=rs2[:], in_=PS(pb0)[:, :], func=AF.Sqrt, scale=1.0 / D, bias=ccol("eps")), reads=[pn(pb0), "cst"], writes=["rs2"])
                        S.op("dve", lambda e: e.reciprocal(out=rs2[:], in_=rs2[:]), reads=["rs2"], writes=["rs2"])
                        for c in range(KC):
                            S.op("dve", lambda e, c=c: e.scalar_tensor_tensor(out=hn[:, c, :], in0=x1[:, c, :], scalar=ccol("ffn", c), in1=rs2[:], op0=ALU.mult, op1=ALU.mult),
                                 reads=["x1", "rs2", "cst"], writes=["hn"])
                        for hf_ in range(2):
                          for fi in range(FFN // 256):
                            def ev_g2(j, pb):
                                S.op("act", lambda e: e.activation(out=sgl[:], in_=PS(pb)[:, :], func=AF.Silu), reads=[pn(pb)], writes=["sgl"])
                            proj_fm(wsf, w_gate, D, hf_ * 2816 + fi * 128, 128, lambda c: hn[:, c, :], ["hn"], TQ, ev_g2)

                            def ev_u(j, pb, fi=fi):
                                S.op("dve", lambda e: e.tensor_tensor(out=act_all[:, fi, :], in0=PS(pb)[:, :], in1=sgl[:], op=ALU.mult), reads=[pn(pb), "sgl"], writes=["act_all"])
                            proj_fm(wsf, w_up, D, hf_ * 2816 + fi * 128, 128, lambda c: hn[:, c, :], ["hn"], TQ, ev_u)

                          def ev_d(j, pb, hf_=hf_):
                            if hf_ == 0:
                                S.op("dve", lambda e: e.tensor_tensor(out=x1[:, j, :], in0=PS(pb)[:, :], in1=x1[:, j, :], op=ALU.add), reads=[pn(pb), "x1"], writes=["x1"])
                                return
                            b = ost[j % 2]
                            S.op("dve", lambda e: e.tensor_tensor(out=b[:], in0=PS(pb)[:, :], in1=x1[:, j, :], op=ALU.add), reads=[pn(pb), "x1"], writes=["ost%d" % (j % 2)])
                            S.dma("sp", "ost%d" % (j % 2), lambda e: e.dma_start(out=outT[j * 128:(j + 1) * 128, o0:o0 + TQ], in_=b[:]), reads=["ost%d" % (j % 2)], writes=["outT"])
                          proj_fm(wsf, w_down[hf_ * 2816:(hf_ + 1) * 2816, :], 2816, 0, D, lambda c: act_all[:, c, :], ["act_all"], TQ, ev_d, gcmax=128)

                        stage_end()
        S.barrier()
        S.flush()
    return nc


def host_consts(T, core):
    NT = T // (NCORE * TQ)
    NCC = T // 2048
    CL = cst_layout(NT, NCC)
    ML = cm_layout(NT, NCC)
    p = np.arange(128)
    cm = np.zeros((128, ML["_n"]), np.float32)
    cm[p, ML["ident"] + p] = 1.0
    cm[p, ML["negI"] + p] = NEG
    cm[:, ML["ones"]:ML["ones"] + 128] = 1.0
    cm[:, ML["onesneg"]:ML["onesneg"] + 128] = -1.0
    cm[:, ML["UIneg"]:ML["UIneg"] + 128] = -(p[:, None] >= p[None, :]).astype(np.float32)
    for m in range(128):
        if m < 64:
            cm[m + 64, ML["prot"] + m] = -1.0
        else:
            cm[m - 64, ML["prot"] + m] = 1.0
    u = np.arange(8192)
    cm[:, ML["E64"]:ML["E64"] + 8192] = (p[:, None] == (u[None, :] // 64)).astype(np.float32)
    for j in range(NCC):
        n = 128 * j + p
        s = np.arange(256)
        ov = ((n[:, None] >= 4 * s[None, :] - 1) & (n[:, None] <= 4 * s[None, :] + 3) & (n[:, None] < T // 16 - 1)).astype(np.float32)
        cm[:, ML["OV"] + j * 257:ML["OV"] + j * 257 + 256] = ov
        cm[:, ML["OV"] + j * 257 + 256] = (n < T // 16 - 1).astype(np.float32)
    t = np.arange(512)
    for c_ in range(4):
        sg_ = 128 * c_ + p
        cm[:, ML["DiagN"] + c_ * 512:ML["DiagN"] + (c_ + 1) * 512] = (sg_[:, None] > t[None, :]).astype(np.float32)
        cm[:, ML["DiagS"] + c_ * 512:ML["DiagS"] + (c_ + 1) * 512] = (sg_[:, None] >= t[None, :]).astype(np.float32)
        cm[:, ML["DiagW"] + c_ * 512:ML["DiagW"] + (c_ + 1) * 512] = (sg_[:, None] <= t[None, :]).astype(np.float32)
        s_ = np.arange(128)
        for r in range(2):
            cm[2 * c_ + r, ML["Eown"] + c_ * 128 + s_] = (s_ // 64 == r).astype(np.float32)
    for k in range(NT):
        ob0 = 8 * (8 * k + core)
        for half in range(2):
            for b in range(8):
                blk = ob0 + b
                if blk // 128 == half:
                    cm[blk % 128, ML["OwnSel"] + k * 16 + half * 8 + b] = 1.0
    cst = np.zeros((128, CL["_n"]), np.float32)
    cst[:, CL["eps"]] = 1e-6
    cst[:, CL["one"]] = 1.0
    cst[:, CL["halfpi"]] = np.float32(np.pi / 2)
    cst[:, CL["tiny"]] = 1e-30
    inv = (np.float32(1.0) / np.power(np.float32(10000.0), np.arange(0, 128, 2, dtype=np.float32) / np.float32(128))).astype(np.float32)
    cst[:, CL["inv"]] = inv[p % 64]
    for j in range(8):
        cst[:, CL["cw"] + j] = 0.0 if j < core else NEG
    for k in range(NT):
        gt = 8 * k + core
        cst[:, CL["pv"] + k] = NEG if gt == 0 else 0.0
        for ts in range(4):
            cst[:, CL["cur"] + k * 4 + ts] = (gt * 512 + ts * 128 + p) // 64
        for j in range(NCC):
            cst[:, CL["cthr"] + k * NCC + j] = gt * 512 - 2048 * j
        ob0 = 8 * gt
        for half in range(2):
            cst[:, CL["_om"] + k * 2 + half] = ((half * 128 + p) < ob0).astype(np.float32)
    rowt = np.zeros((128, 256), np.float32)
    rowt[:, 0:256] = np.arange(256)[None, :]
    j2 = (16 * p[:, None] + 31 - t[None, :]).astype(np.float32)
    return cst, cm, rowt, j2, CL


_NC_CACHE = {}


def kernel(x, mem, positions, attn_norm, w_in, nsa_q_norm, nsa_kc_norm, nsa_ks_norm, nsa_kw_norm,
           cmp_k_pe, cmp_k_w1, cmp_k_w2, cmp_v_pe, cmp_v_w1, cmp_v_w2, mem_norm, w_mem_kv,
           mem_q_norm, mem_k_norm, w_o_nsa, w_o_sb, w_o_mem, w_out, ffn_norm,
           w_ffn_gate, w_ffn_up, w_ffn_down):
    x = np.asarray(x)
    T = x.shape[1]
    NT = T // (NCORE * TQ)
    NCC = T // 2048
    NCP = NCC * 128
    if T not in _NC_CACHE:
        _NC_CACHE[T] = build(T)
    nc = _NC_CACHE[T]
    f = lambda a: np.ascontiguousarray(np.asarray(a, dtype=np.float32))
    xT = np.ascontiguousarray(np.asarray(x)[0].T)
    pos = np.asarray(positions).astype(np.int32)
    pos_cmp = np.zeros((1, NCP), np.int32)
    pc = pos[0, 31::16]
    pos_cmp[0, :len(pc)] = pc
    common = {
        "xT_all": xT, "memT": np.ascontiguousarray(np.asarray(mem)[0].T), "pos_all": pos, "pos_cmp": pos_cmp,
        "w_in": f(w_in[0]), "w1k": f(cmp_k_w1[0]), "w2k": f(cmp_k_w2[0]), "w1v": f(cmp_v_w1[0]), "w2v": f(cmp_v_w2[0]),
        "w_mem": f(w_mem_kv[0]), "wo_nsa": f(w_o_nsa[0]), "wo_sb": f(w_o_sb[0]), "wo_mem": f(w_o_mem[0]), "w_out": f(w_out[0]),
        "w_gate": f(w_ffn_gate[0]), "w_up": f(w_ffn_up[0]), "w_down": f(w_ffn_down[0]),
    }
    in_maps = []
    for core in range(NCORE):
        cst, cm, rowt, j2, CL = host_consts(T, core)
        for name, arr in [("attn", attn_norm), ("ffn", ffn_norm), ("memn", mem_norm)]:
            cst[:, CL[name]:CL[name] + 16] = np.asarray(arr)[0].reshape(16, 128).T
        for name, arr in [("qn", nsa_q_norm), ("kcn", nsa_kc_norm), ("ksn", nsa_ks_norm), ("kwn", nsa_kw_norm), ("mqn", mem_q_norm), ("mkn", mem_k_norm)]:
            cst[:, CL[name]] = np.asarray(arr)[0]
        cst[:, CL["pek"]:CL["pek"] + 32] = np.asarray(cmp_k_pe)[0].T
        cst[:, CL["pev"]:CL["pev"] + 32] = np.asarray(cmp_v_pe)[0].T
        own = np.zeros((D, NT * TQ), np.float32)
        prev = np.zeros((D, NT * TQ), np.float32)
        pown = np.zeros((1, NT * TQ), np.int32)
        pprev = np.zeros((1, NT * TQ), np.int32)
        for k in range(NT):
            gt = 8 * k + core
            own[:, k * TQ:(k + 1) * TQ] = xT[:, gt * TQ:(gt + 1) * TQ]
            pown[0, k * TQ:(k + 1) * TQ] = pos[0, gt * TQ:(gt + 1) * TQ]
            if gt > 0:
                prev[:, k * TQ:(k + 1) * TQ] = xT[:, (gt - 1) * TQ:gt * TQ]
                pprev[0, k * TQ:(k + 1) * TQ] = pos[0, (gt - 1) * TQ:gt * TQ]
        m = dict(common)
        m.update({"xT_own": own, "xT_prev": prev, "pos_own": pown, "pos_prev": pprev, "cst": cst, "cmat": cm, "rowt": rowt, "j2": j2})
        in_maps.append(m)
    res = run_bass_kernel_spmd(nc, in_maps, core_ids=list(range(NCORE)))
    out = np.zeros((1, T, D), np.float32)
    for core in range(NCORE):
        oT = np.asarray(res.results[core]["outT"])
        for k in range(NT):
            gt = 8 * k + core
            out[0, gt * TQ:(gt + 1) * TQ, :] = oT[:, k * TQ:(k + 1) * TQ].T
    return out
```

```python
import math
from contextlib import ExitStack
import numpy as np
import concourse.bass as bass
import concourse.mybir as mybir
from concourse.bass_utils import run_bass_kernel_spmd

F32 = mybir.dt.float32
BF16 = mybir.dt.bfloat16
I32 = mybir.dt.int32
ALU = mybir.AluOpType
AF = mybir.ActivationFunctionType

D = 2048
KC = 16
NCORE = 8
TQ = 512
FFN = 5632
NEG = -30000.0
SCALE = 128 ** -0.5
MAGIC = 12582912.0
C1 = 6.28125
C2 = float(2 * np.pi - 6.28125)
PI_LO = 3.1415925

COMPUTE_Q = ("pe", "act", "dve", "pool")
ALLQ = ("pe", "act", "dve", "pool", "sp")


class Sched:
    def __init__(self, nc, es):
        self.nc = nc
        self.es = es
        self.q = {k: [] for k in ALLQ}
        self.cnt = {k: 0 for k in COMPUTE_Q}
        self.sems = {k: es.enter_context(nc.semaphore("s_" + k)) for k in COMPUTE_Q}
        self.dcnt = {}
        self.lastw = {}
        self.readers = {}
        self.seen = {k: {} for k in ALLQ}
        self.nops = 0

    def _deps(self, q, reads, writes):
        need = {}

        def add(tok):
            if tok is None:
                return
            k, v = tok
            if need.get(k, 0) < v:
                need[k] = v
        for b in reads:
            add(self.lastw.get(b))
        for b in writes:
            add(self.lastw.get(b))
            for t in self.readers.get(b, ()):
                add(t)
        out = []
        for k, v in need.items():
            if k == q and q == "pe":
                continue
            if self.seen[q].get(k, 0) >= v:
                continue
            self.seen[q][k] = v
            out.append((k, v))
        return out

    def _commit(self, tok, reads, writes):
        for b in reads:
            self.readers.setdefault(b, []).append(tok)
        for b in writes:
            self.lastw[b] = tok
            self.readers[b] = []

    def _rec(self, fn):
        class _R:
            def __getattr__(s, name):
                def f(*a, **kw):
                    s.call = (name, a, kw)
                    return None
                return f
        r = _R()
        fn(r)
        name, a, kw = r.call
        return lambda eng: getattr(eng, name)(*a, **kw)

    def op(self, q, fn, reads=(), writes=()):
        fn = self._rec(fn)
        waits = self._deps(q, reads, writes)
        self.cnt[q] += 1
        tok = (q, self.cnt[q])
        self.q[q].append((waits, fn, q, 1))
        self._commit(tok, reads, writes)
        self.nops += 1
        return tok

    def dma(self, q, key, fn, reads=(), writes=()):
        fn = self._rec(fn)
        waits = self._deps(q, reads, writes)
        k = "dma:" + key
        if k not in self.sems:
            self.sems[k] = self.es.enter_context(self.nc.semaphore("d_" + key))
            self.dcnt[k] = 0
        self.dcnt[k] += 16
        tok = (k, self.dcnt[k])
        self.q[q].append((waits, fn, k, 16))
        self._commit(tok, reads, writes)
        self.nops += 1
        return tok

    def barrier(self):
        allw = [(k, v) for k, v in self.cnt.items() if v > 0] + [(k, v) for k, v in self.dcnt.items() if v > 0]
        for q in ALLQ:
            w = [(k, v) for k, v in allw if self.seen[q].get(k, 0) < v]
            for k, v in w:
                self.seen[q][k] = v
            if w:
                self.q[q].append((w, None, None, 0))
        self.lastw = {}
        self.readers = {}

    def flush(self):
        nc = self.nc
        engs = {"pe": "tensor", "act": "scalar", "dve": "vector", "pool": "gpsimd", "sp": "sync"}
        if not any(self.q.values()):
            return
        with nc.Block() as block:
            def mk(items):
                def body(eng):
                    for waits, fn, semk, inc in items:
                        for k, v in waits:
                            eng.wait_ge(self.sems[k], v)
                        if fn is not None:
                            fn(eng).then_inc(self.sems[semk], inc)
                return body
            for qn, attr in engs.items():
                if self.q[qn]:
                    getattr(block, attr)(mk(self.q[qn]))
        self.q = {k: [] for k in ALLQ}


def cst_layout(NT, NCC):
    o = {}
    c = 0
    for name, n in [("attn", 16), ("ffn", 16), ("memn", 16), ("qn", 1), ("kcn", 1), ("ksn", 1), ("kwn", 1),
                    ("mqn", 1), ("mkn", 1), ("inv", 1), ("eps", 1), ("one", 1), ("zero", 1), ("halfpi", 1),
                    ("cw", 8), ("pv", NT), ("cur", NT * 4), ("cthr", NT * NCC), ("pek", 32), ("pev", 32),
                    ("tiny", 1), ("_om", NT * 2)]:
        o[name] = c
        c += n
    o["_n"] = c
    return o


def cm_layout(NT, NCC):
    o = {}
    c = 0
    for name, n in [("ident", 128), ("negI", 128), ("ones", 128), ("onesneg", 128), ("UIneg", 128), ("prot", 128),
                    ("E64", 8192), ("OV", NCC * 257), ("DiagN", 2048), ("DiagS", 2048), ("DiagW", 2048),
                    ("Eown", 512), ("OwnSel", NT * 16)]:
        o[name] = c
        c += n
    o["_n"] = c
    return o


def build(T):
    NT = T // (NCORE * TQ)
    NGT = T // TQ
    NCC = T // 2048
    NCP = NCC * 128
    NB = min(512, NCP)
    CL = cst_layout(NT, NCC)
    ML = cm_layout(NT, NCC)
    TO = NT * TQ

    nc = bass.Bass("TRN2", target_bir_lowering=False)

    def din(name, shape, dt=F32):
        return nc.dram_tensor(name, list(shape), dt, kind="ExternalInput").ap()

    xT_all = din("xT_all", [D, T])
    xT_own = din("xT_own", [D, TO])
    xT_prev = din("xT_prev", [D, TO])
    memT = din("memT", [D, 256])
    pos_all = din("pos_all", [1, T], I32)
    pos_own = din("pos_own", [1, TO], I32)
    pos_prev = din("pos_prev", [1, TO], I32)
    pos_cmp = din("pos_cmp", [1, NCP], I32)
    w_in = din("w_in", [D, 10776])
    w1k = din("w1k", [32, 128, 256])
    w2k = din("w2k", [256, 128])
    w1v = din("w1v", [32, 128, 256])
    w2v = din("w2v", [256, 128])
    w_mem = din("w_mem", [D, 1024])
    wo_nsa = din("wo_nsa", [1024, D])
    wo_sb = din("wo_sb", [512, D])
    wo_mem = din("wo_mem", [512, D])
    w_out = din("w_out", [D, D])
    w_gate = din("w_gate", [D, FFN])
    w_up = din("w_up", [D, FFN])
    w_down = din("w_down", [FFN, D])
    cst_d = din("cst", [128, CL["_n"]])
    rowt_d = din("rowt", [128, 256])
    j2_d = din("j2", [128, 512])
    cmat_d = din("cmat", [128, ML["_n"]])
    outT = nc.dram_tensor("outT", [D, TO], F32, kind="ExternalOutput").ap()

    kcT_d = nc.dram_tensor("kcT_d", [2, 128, T], BF16).ap()
    vcT_d = nc.dram_tensor("vcT_d", [2, 128, T], BF16).ap()
    ksT_d = nc.dram_tensor("ksT_d", [2, 128, T], BF16).ap()
    sbkT_d = nc.dram_tensor("sbkT_d", [4, 128, T], BF16).ap()
    vtok_d = nc.dram_tensor("vtok_d", [T, 768], BF16).ap()
    gates_d = nc.dram_tensor("gates_d", [24, TQ], F32).ap()
    hn_d = nc.dram_tensor("hn_d", [128, KC * TQ], BF16).ap()
    w_in_b = nc.dram_tensor("w_in_b", [D, 10776], BF16).ap()
    wo_nsa_b = nc.dram_tensor("wo_nsa_b", [1024, D], BF16).ap()
    wo_sb_b = nc.dram_tensor("wo_sb_b", [512, D], BF16).ap()
    wo_mem_b = nc.dram_tensor("wo_mem_b", [512, D], BF16).ap()
    w_out_b = nc.dram_tensor("w_out_b", [D, D], BF16).ap()
    w_gate_b = nc.dram_tensor("w_gate_b", [D, FFN], BF16).ap()
    w_up_b = nc.dram_tensor("w_up_b", [D, FFN], BF16).ap()
    w_down_b = nc.dram_tensor("w_down_b", [FFN, D], BF16).ap()

    with ExitStack() as es_all:
        S = Sched(nc, es_all)
        uid = [0]

        def sbt(es, name, shape, dt):
            uid[0] += 1
            return es.enter_context(nc.sbuf_tensor("%s_%d" % (name, uid[0]), list(shape), dt))

        PSB = [es_all.enter_context(nc.psum_tensor("ps%d" % i, [128, 512], F32)) for i in range(8)]
        rot = {"banks": list(range(8)), "i": 0}

        def set_rot(banks):
            rot["banks"] = list(banks)
            rot["i"] = 0

        def nextps():
            b = rot["banks"][rot["i"] % len(rot["banks"])]
            rot["i"] += 1
            return b

        def PS(b):
            return PSB[b]

        def pn(b):
            return "ps%d" % b

        cst = sbt(es_all, "cst", [128, CL["_n"]], F32)
        rowt = sbt(es_all, "rowt", [128, 256], F32)
        j2 = sbt(es_all, "j2", [128, 512], F32)
        cm = sbt(es_all, "cm", [128, ML["_n"]], BF16)
        kcTs = sbt(es_all, "kcTs", [128, 2, NCP], BF16)
        vcs = sbt(es_all, "vcs", [128, 2, NCC, 128], BF16)
        kmem = sbt(es_all, "kmem", [128, 4, 256], BF16)
        vmem = sbt(es_all, "vmem", [128, 2, 512], BF16)

        def ccol(name, i=0):
            return cst[:, CL[name] + i:CL[name] + i + 1]

        def cmv(name, off=0, n=128):
            return cm[:, ML[name] + off:ML[name] + off + n]

        S.dma("sp", "c0a", lambda e: e.dma_start(out=cst[:], in_=cst_d), writes=["cst"])
        S.dma("sp", "c0b", lambda e: e.dma_start(out=rowt[:], in_=rowt_d), writes=["rowt"])
        S.dma("sp", "c0c", lambda e: e.dma_start(out=j2[:], in_=j2_d), writes=["j2"])
        S.dma("pool", "c1", lambda e: e.dma_start(out=cm[:], in_=cmat_d), writes=["cm"])

        class Rope:
            def __init__(self, es, n, tag):
                self.n, self.tag = n, tag
                self.posi = sbt(es, "posi", [128, n], I32)
                self.ang = sbt(es, "ang", [128, n], F32)
                self.t1 = sbt(es, "rt1", [128, n], F32)
                self.kk = sbt(es, "rkk", [128, n], F32)

            def run(self, pos_ap, cosT, sinT, csname):
                n, tag = self.n, self.tag
                posi, ang, t1, kk = self.posi, self.ang, self.t1, self.kk
                S.dma("sp", tag + "pos", lambda e: e.dma_start(out=posi[:], in_=pos_ap.to_broadcast([128, n])), writes=[tag + "posi"])
                S.op("dve", lambda e: e.tensor_copy(out=t1[:], in_=posi[:]), reads=[tag + "posi"], writes=[tag + "t1"])
                S.op("dve", lambda e: e.tensor_scalar(out=ang[:], in0=t1[:], scalar1=ccol("inv"), scalar2=None, op0=ALU.mult),
                     reads=[tag + "t1", "cst"], writes=[tag + "ang"])
                S.op("dve", lambda e: e.tensor_scalar(out=t1[:], in0=ang[:], scalar1=float(1.0 / (2 * np.pi)), scalar2=MAGIC, op0=ALU.mult, op1=ALU.add),
                     reads=[tag + "ang"], writes=[tag + "t1"])
                S.op("dve", lambda e: e.tensor_scalar(out=kk[:], in0=t1[:], scalar1=MAGIC, scalar2=None, op0=ALU.subtract),
                     reads=[tag + "t1"], writes=[tag + "kk"])
                S.op("dve", lambda e: e.scalar_tensor_tensor(out=t1[:], in0=kk[:], scalar=-C1, in1=ang[:], op0=ALU.mult, op1=ALU.add),
                     reads=[tag + "kk", tag + "ang"], writes=[tag + "t1"])
                S.op("dve", lambda e: e.scalar_tensor_tensor(out=ang[:], in0=kk[:], scalar=-C2, in1=t1[:], op0=ALU.mult, op1=ALU.add),
                     reads=[tag + "kk", tag + "t1"], writes=[tag + "ang"])
                S.op("dve", lambda e: e.tensor_scalar(out=ang[:], in0=ang[:], scalar1=PI_LO, scalar2=-PI_LO, op0=ALU.min, op1=ALU.max),
                     reads=[tag + "ang"], writes=[tag + "ang"])
                S.op("dve", lambda e: e.scalar_tensor_tensor(out=t1[:], in0=ang[:], scalar=-1.0, in1=ang[:], op0=ALU.mult, op1=ALU.max),
                     reads=[tag + "ang"], writes=[tag + "t1"])
                S.op("act", lambda e: e.activation(out=sinT, in_=ang[:], func=AF.Sin), reads=[tag + "ang"], writes=[csname + "sin"])
                S.op("act", lambda e: e.activation(out=cosT, in_=t1[:], func=AF.Sin, scale=-1.0, bias=ccol("halfpi")),
                     reads=[tag + "t1", "cst"], writes=[csname + "cos"])

        class RMS:
            def __init__(self, es, n, tag):
                self.n, self.tag = n, tag
                self.xb = [sbt(es, "xch", [128, n], F32) for _ in range(3)]
                self.sq = [sbt(es, "sqc", [128, n], BF16) for _ in range(2)]
                self.rs = sbt(es, "rstd", [128, n], F32)

            def run(self, src_ap, gname, hn, hn_name):
                n, tag, xb, sq, rs = self.n, self.tag, self.xb, self.sq, self.rs
                pb = nextps()
                for c in range(KC):
                    b = xb[c % 3]
                    S.dma("sp", tag + "x%d" % (c % 3), lambda e, b=b, c=c: e.dma_start(out=b[:], in_=src_ap[c * 128:(c + 1) * 128, :]),
                          writes=[tag + "xb%d" % (c % 3)])
                    S.op("act", lambda e, b=b, c=c: e.activation(out=sq[c % 2][:], in_=b[:], func=AF.Square),
                         reads=[tag + "xb%d" % (c % 3)], writes=[tag + "sq%d" % (c % 2)])
                    S.op("pe", lambda e, c=c: e.matmul(PS(pb)[:, 0:n], lhsT=cmv("ones"), rhs=sq[c % 2][:], start=(c == 0), stop=(c == KC - 1)),
                         reads=[tag + "sq%d" % (c % 2), "cm"], writes=[pn(pb)])
                S.op("act", lambda e: e.activation(out=rs[:], in_=PS(pb)[:, 0:n], func=AF.Sqrt, scale=1.0 / D, bias=ccol("eps")),
                     reads=[pn(pb), "cst"], writes=[tag + "rs"])
                S.op("dve", lambda e: e.reciprocal(out=rs[:], in_=rs[:]), reads=[tag + "rs"], writes=[tag + "rs"])
                for c in range(KC):
                    b = xb[c % 3]
                    S.dma("sp", tag + "x%d" % (c % 3), lambda e, b=b, c=c: e.dma_start(out=b[:], in_=src_ap[c * 128:(c + 1) * 128, :]),
                          writes=[tag + "xb%d" % (c % 3)])
                    S.op("dve", lambda e, b=b, c=c: e.scalar_tensor_tensor(out=hn[:, c, :], in0=b[:], scalar=ccol(gname, c), in1=rs[:],
                                                                          op0=ALU.mult, op1=ALU.mult),
                         reads=[tag + "xb%d" % (c % 3), tag + "rs", "cst"], writes=[hn_name])

        class NR:
            def __init__(self, es, n, tag):
                self.n = n
                self.tag = tag
                self.sq = sbt(es, "nrsq", [128, n], BF16)
                self.r = sbt(es, "nrr", [128, n], F32)
                self.kb = sbt(es, "nrkb", [128, n], BF16)
                self.t1 = sbt(es, "nrt1", [128, n], F32)
                self.t2 = sbt(es, "nrt2", [128, n], F32)

            def run(self, pin, gname, out_ap, out_name, cosT=None, sinT=None, csname=None):
                n, tag = self.n, self.tag
                p2 = nextps()
                S.op("act", lambda e: e.activation(out=self.sq[:], in_=PS(pin)[:, 0:n], func=AF.Square), reads=[pn(pin)], writes=[tag + "sq"])
                S.op("pe", lambda e: e.matmul(PS(p2)[:, 0:n], lhsT=cmv("ones"), rhs=self.sq[:], start=True, stop=True),
                     reads=[tag + "sq", "cm"], writes=[pn(p2)])
                S.op("act", lambda e: e.activation(out=self.r[:], in_=PS(p2)[:, 0:n], func=AF.Sqrt, scale=1.0 / 128, bias=ccol("eps")),
                     reads=[pn(p2), "cst"], writes=[tag + "r"])
                S.op("dve", lambda e: e.reciprocal(out=self.r[:], in_=self.r[:]), reads=[tag + "r"], writes=[tag + "r"])
                if cosT is None:
                    S.op("dve", lambda e: e.scalar_tensor_tensor(out=out_ap, in0=PS(pin)[:, 0:n], scalar=ccol(gname), in1=self.r[:], op0=ALU.mult, op1=ALU.mult),
                         reads=[pn(pin), tag + "r", "cst"], writes=[out_name])
                    return
                S.op("dve", lambda e: e.scalar_tensor_tensor(out=self.kb[:], in0=PS(pin)[:, 0:n], scalar=ccol(gname), in1=self.r[:], op0=ALU.mult, op1=ALU.mult),
                     reads=[pn(pin), tag + "r", "cst"], writes=[tag + "kb"])
                p3 = nextps()
                S.op("pe", lambda e: e.matmul(PS(p3)[:, 0:n], lhsT=cmv("prot"), rhs=self.kb[:], start=True, stop=True),
                     reads=[tag + "kb", "cm"], writes=[pn(p3)])
                S.op("dve", lambda e: e.tensor_tensor(out=self.t1[:], in0=self.kb[:], in1=cosT, op=ALU.mult),
                     reads=[tag + "kb", csname + "cos"], writes=[tag + "t1"])
                S.op("dve", lambda e: e.tensor_tensor(out=self.t2[:], in0=PS(p3)[:, 0:n], in1=sinT, op=ALU.mult),
                     reads=[pn(p3), csname + "sin"], writes=[tag + "t2"])
                S.op("dve", lambda e: e.tensor_tensor(out=out_ap, in0=self.t1[:], in1=self.t2[:], op=ALU.add),
                     reads=[tag + "t1", tag + "t2"], writes=[out_name])

        class WStream:
            def __init__(self, es, tag, nbuf=2, size=5632, cast=False):
                self.bufs = [sbt(es, "wbuf", [128, size], BF16) for _ in range(nbuf)]
                self.tag = tag
                self.i = 0
                self.size = size
                self.cast = cast

            def load(self, w_ap, K, col0, gc):
                kc = K // 128
                assert kc * gc <= self.size
                i = self.i % len(self.bufs)
                self.i += 1
                view = self.bufs[i][:, 0:kc * gc].rearrange("p (c n) -> p c n", c=kc)
                name = self.tag + "w%d" % i
                S.dma("pool" if self.cast else "sp", name, lambda e: e.dma_start(out=view, in_=w_ap[:, col0:col0 + gc].rearrange("(c p) n -> p c n", p=128)),
                      reads=[] if self.cast else ["wb16"], writes=[name])
                return view, name

        def proj_fm(ws, w_ap, K, col0, ncols, rhs_fn, rhs_names, n, evac, gcmax=None):
            kc = K // 128
            gc_full = min(ncols, (ws.size // kc) // 128 * 128)
            if gcmax:
                gc_full = min(gc_full, gcmax)
            j = 0
            g0 = 0
            while g0 < ncols:
                gc = min(gc_full, ncols - g0)
                view, wname = ws.load(w_ap, K, col0 + g0, gc)
                for jj in range((gc + 127) // 128):
                    m = min(128, gc - jj * 128)
                    pb = nextps()
                    for c in range(kc):
                        S.op("pe", lambda e, c=c, jj=jj, pb=pb, view=view, m=m: e.matmul(PS(pb)[0:m, 0:n], lhsT=view[:, c, jj * 128:jj * 128 + m], rhs=rhs_fn(c),
                                                                                start=(c == 0), stop=(c == kc - 1)),
                             reads=[wname] + rhs_names, writes=[pn(pb)])
                    evac(j, pb)
                    j += 1
                g0 += gc

        def proj_tm(ws, w_ap, K, col0, ncols, lhs_fn, lhs_names, nsub, evac):
            kc = K // 128
            gc_full = min(ncols, (ws.size // kc) // 128 * 128, 512)
            g0 = 0
            while g0 < ncols:
                gc = min(gc_full, ncols - g0)
                view, wname = ws.load(w_ap, K, col0 + g0, gc)
                for ts in range(nsub):
                    pb = nextps()
                    for c in range(kc):
                        S.op("pe", lambda e, c=c, ts=ts, pb=pb, view=view, gc=gc: e.matmul(PS(pb)[:, 0:gc], lhsT=lhs_fn(c, ts), rhs=view[:, c, 0:gc],
                                                                                  start=(c == 0), stop=(c == kc - 1)),
                             reads=[wname] + lhs_names, writes=[pn(pb)])
                    evac(ts, g0, gc, pb)
                g0 += gc

        cpy_i = [0]

        def copy_out(out_ap, out_name, in_ap, in_names):
            cpy_i[0] += 1
            if cpy_i[0] % 2:
                S.op("act", lambda e: e.activation(out=out_ap, in_=in_ap, func=AF.Copy), reads=in_names, writes=[out_name])
            else:
                S.op("dve", lambda e: e.tensor_copy(out=out_ap, in_=in_ap), reads=in_names, writes=[out_name])

        def stage_end():
            S.barrier()
            S.flush()

        with ExitStack() as es:
            set_rot(range(8))
            wkv = sbt(es, "wkv", [128, KC, 2048], BF16)
            for (c0, n, d0) in [(1024, 768, 0), (3096, 512, 768), (1792, 256, 1280), (3608, 512, 1536)]:
                for half in range(2):
                    S.dma("pool", "wkv", lambda e, c0=c0, n=n, d0=d0, half=half: e.dma_start(
                        out=wkv[:, half * 8:(half + 1) * 8, d0:d0 + n],
                        in_=w_in[half * 1024:(half + 1) * 1024, c0:c0 + n].rearrange("(c p) n -> p c n", p=128)), writes=["wkv"])
            for (src_, dst_, rows_) in [(w_in, w_in_b, D), (wo_nsa, wo_nsa_b, 1024), (wo_sb, wo_sb_b, 512), (wo_mem, wo_mem_b, 512),
                                        (w_out, w_out_b, D), (w_gate, w_gate_b, D), (w_up, w_up_b, D), (w_down, w_down_b, FFN)]:
                for r0 in range(0, rows_, 256):
                    S.dma("pool", "wcast", lambda e, src_=src_, dst_=dst_, r0=r0: e.dma_start(out=dst_[r0:r0 + 256, :], in_=src_[r0:r0 + 256, :]), writes=["wb16"])
            hnb = [sbt(es, "hn1", [128, KC, TQ], BF16) for _ in range(2)]
            cosTb = [sbt(es, "cosT", [128, TQ], F32) for _ in range(2)]
            sinTb = [sbt(es, "sinT", [128, TQ], F32) for _ in range(2)]
            nr = NR(es, TQ, "p1nr")
            rms = RMS(es, TQ, "p1rms")
            rope = Rope(es, TQ, "p1rope")
            stg = [sbt(es, "stg", [128, TQ], BF16) for _ in range(3)]
            vst = [sbt(es, "vst", [128, 768], BF16) for _ in range(2)]
            si = 0

            def p1_pre(gt_):
                rms.run(xT_all[:, gt_ * TQ:(gt_ + 1) * TQ], "attn", hnb[gt_ % 2], "hn1_%d" % (gt_ % 2))
                rope.run(pos_all[0:1, gt_ * TQ:(gt_ + 1) * TQ], cosTb[gt_ % 2][:], sinTb[gt_ % 2][:], "p1cs%d" % (gt_ % 2))
            p1_pre(0)
            for gt in range(NGT):
                hn = hnb[gt % 2]
                hname = "hn1_%d" % (gt % 2)
                cosT, sinT, csn = cosTb[gt % 2], sinTb[gt % 2], "p1cs%d" % (gt % 2)
                t0 = gt * TQ
                if gt + 1 < NGT:
                    p1_pre(gt + 1)
                for j in range(10):
                    pb = nextps()
                    for c in range(KC):
                        S.op("pe", lambda e, c=c, j=j, pb=pb, hn=hn: e.matmul(PS(pb)[:, :], lhsT=wkv[:, c, j * 128:(j + 1) * 128], rhs=hn[:, c, :],
                                                                          start=(c == 0), stop=(c == KC - 1)),
                             reads=["wkv", hname], writes=[pn(pb)])
                    sb_ = stg[si % 3]
                    sname = "stg%d" % (si % 3)
                    si += 1
                    if j in (4, 5):
                        nr.run(pb, "ksn", sb_[:], sname, cosT[:], sinT[:], csn)
                        dst = ksT_d[j - 4, :, t0:t0 + TQ]
                    else:
                        copy_out(sb_[:], sname, PS(pb)[:, :], [pn(pb)])
                        if j < 2:
                            dst = kcT_d[j, :, t0:t0 + TQ]
                        elif j < 4:
                            dst = vcT_d[j - 2, :, t0:t0 + TQ]
                        else:
                            dst = sbkT_d[j - 6, :, t0:t0 + TQ]
                    S.dma("sp", "p1st_" + sname, lambda e, dst=dst, sb_=sb_: e.dma_start(out=dst, in_=sb_[:]), reads=[sname], writes=["kvdram"])
                for ts in range(4):
                    vs_ = vst[ts % 2]
                    vname = "vst%d" % (ts % 2)
                    for (c0, n) in [(1280, 256), (1536, 512)]:
                        pb = nextps()
                        for c in range(KC):
                            S.op("pe", lambda e, c=c, pb=pb, hn=hn, ts=ts, c0=c0, n=n: e.matmul(PS(pb)[:, 0:n], lhsT=hn[:, c, ts * 128:(ts + 1) * 128],
                                                                                         rhs=wkv[:, c, c0:c0 + n], start=(c == 0), stop=(c == KC - 1)),
                                 reads=["wkv", hname], writes=[pn(pb)])
                        copy_out(vs_[:, c0 - 1280:c0 - 1280 + n], vname, PS(pb)[:, 0:n], [pn(pb)])
                    S.dma("sp", "p1sv_" + vname, lambda e, vs_=vs_, ts=ts, t0=t0: e.dma_start(out=vtok_d[t0 + ts * 128:t0 + (ts + 1) * 128, :], in_=vs_[:]),
                          reads=[vname], writes=["kvdram"])
            stage_end()

        with ExitStack() as es:
            set_rot(range(8))
            kcs = sbt(es, "kcs", [128, T + 16], BF16)
            w1b = sbt(es, "w1b", [128, 32, 256], BF16)
            w2b = sbt(es, "w2b", [128, 2, 128], BF16)
            peb = sbt(es, "peb", [128, 32], BF16)
            pebias = sbt(es, "pebias", [128, 2], F32)
            hf = sbt(es, "hf", [128, NB], F32)
            h2 = sbt(es, "h2", [128, NB], F32)
            sg = sbt(es, "sg", [128, NB], F32)
            hid = sbt(es, "hid", [128, 2, NB], BF16)
            cosC = sbt(es, "cosC", [128, NCP], F32)
            sinC = sbt(es, "sinC", [128, NCP], F32)
            nrc = NR(es, NB, "cnr")
            ropec = Rope(es, NCP, "crope")
            ropec.run(pos_cmp[0:1, :], cosC[:], sinC[:], "ccs")
            S.op("pool", lambda e: e.memset(kcs[:, T:T + 16], 0.0), writes=["kcs_tail"])
            for kv in range(2):
                w1d, w2d, pename = (w1k, w2k, "pek") if kv == 0 else (w1v, w2v, "pev")
                S.dma("pool", "w1b", lambda e, w1d=w1d: e.dma_start(out=w1b[:], in_=w1d.rearrange("l d f -> d l f")), writes=["w1b"])
                S.dma("pool", "w2b", lambda e, w2d=w2d: e.dma_start(out=w2b[:], in_=w2d.rearrange("(c p) d -> p c d", p=128)), writes=["w2b"])
                S.op("dve", lambda e, pename=pename: e.tensor_copy(out=peb[:], in_=cst[:, CL[pename]:CL[pename] + 32]), reads=["cst"], writes=["peb"])
                for fc in range(2):
                    pb = nextps()
                    for l in range(32):
                        S.op("pe", lambda e, l=l, fc=fc, pb=pb: e.matmul(PS(pb)[:, 0:1], lhsT=w1b[:, l, fc * 128:(fc + 1) * 128], rhs=peb[:, l:l + 1],
                                                                     start=(l == 0), stop=(l == 31)), reads=["w1b", "peb"], writes=[pn(pb)])
                    S.op("dve", lambda e, fc=fc, pb=pb: e.tensor_copy(out=pebias[:, fc:fc + 1], in_=PS(pb)[:, 0:1]), reads=[pn(pb)], writes=["pebias"])
                for g in range(2):
                    src = kcT_d if kv == 0 else vcT_d
                    S.dma("sp", "kcs", lambda e, src=src, g=g: e.dma_start(out=kcs[:, 0:T], in_=src[g, :, :]), reads=["kvdram"], writes=["kcs"])
                    for nt in range(NCP // NB):
                        n0 = nt * NB
                        for fc in range(2):
                            pb = nextps()
                            for l in range(32):
                                a0 = 16 * n0 + l
                                S.op("pe", lambda e, l=l, fc=fc, pb=pb, a0=a0: e.matmul(PS(pb)[:, 0:NB], lhsT=w1b[:, l, fc * 128:(fc + 1) * 128],
                                                                                 rhs=kcs[:, a0:a0 + 16 * (NB - 1) + 1:16], start=(l == 0), stop=(l == 31)),
                                     reads=["w1b", "kcs", "kcs_tail"], writes=[pn(pb)])
                            S.op("act", lambda e, pb=pb, fc=fc: e.activation(out=hf[:], in_=PS(pb)[:, 0:NB], func=AF.Identity, bias=pebias[:, fc:fc + 1]),
                                 reads=[pn(pb), "pebias"], writes=["hf"])
                            S.op("dve", lambda e: e.tensor_tensor(out=h2[:], in0=hf[:], in1=hf[:], op=ALU.mult), reads=["hf"], writes=["h2"])
                            S.op("dve", lambda e: e.tensor_scalar(out=h2[:], in0=h2[:], scalar1=0.044715, scalar2=1.0, op0=ALU.mult, op1=ALU.add),
                                 reads=["h2"], writes=["h2"])
                            S.op("dve", lambda e: e.tensor_tensor(out=h2[:], in0=h2[:], in1=hf[:], op=ALU.mult), reads=["h2", "hf"], writes=["h2"])
                            S.op("act", lambda e: e.activation(out=sg[:], in_=h2[:], func=AF.Sigmoid, scale=float(2.0 * math.sqrt(2.0 / math.pi))),
                                 reads=["h2"], writes=["sg"])
                            S.op("dve", lambda e, fc=fc: e.tensor_tensor(out=hid[:, fc, :], in0=hf[:], in1=sg[:], op=ALU.mult), reads=["hf", "sg"], writes=["hid"])
                        if kv == 0:
                            pb = nextps()
                            for fc in range(2):
                                S.op("pe", lambda e, fc=fc, pb=pb: e.matmul(PS(pb)[:, 0:NB], lhsT=w2b[:, fc, :], rhs=hid[:, fc, :], start=(fc == 0), stop=(fc == 1)),
                                     reads=["w2b", "hid"], writes=[pn(pb)])
                            nrc.run(pb, "kcn", kcTs[:, g, n0:n0 + NB], "kcTs", cosC[:, n0:n0 + NB], sinC[:, n0:n0 + NB], "ccs")
                        else:
                            for ns in range(NB // 128):
                                pb = nextps()
                                for fc in range(2):
                                    S.op("pe", lambda e, fc=fc, pb=pb, ns=ns: e.matmul(PS(pb)[:, 0:128], lhsT=hid[:, fc, ns * 128:(ns + 1) * 128], rhs=w2b[:, fc, :],
                                                                                  start=(fc == 0), stop=(fc == 1)), reads=["w2b", "hid"], writes=[pn(pb)])
                                copy_out(vcs[:, g, n0 // 128 + ns, :], "vcs", PS(pb)[:, 0:128], [pn(pb)])
            stage_end()

        with ExitStack() as es:
            set_rot(range(8))
            hm = sbt(es, "hm", [128, KC, 256], BF16)
            rmsm = RMS(es, 256, "mrms")
            nrm = NR(es, 256, "mnr")
            wsm = WStream(es, "wsm", 2, 4096, cast=True)
            rmsm.run(memT, "memn", hm, "hm")

            def ev_k(j, pb):
                nrm.run(pb, "mkn", kmem[:, j, :], "kmem")
            proj_fm(wsm, w_mem, D, 0, 512, lambda c: hm[:, c, :], ["hm"], 256, ev_k, gcmax=256)

            def ev_v(ts, g0, gc, pb):
                copy_out(vmem[:, ts, g0:g0 + gc], "vmem", PS(pb)[:, 0:gc], [pn(pb)])
            proj_tm(wsm, w_mem, D, 512, 512, lambda c, ts: hm[:, c, ts * 128:(ts + 1) * 128], ["hm"], 2, ev_v)
            stage_end()

        with ExitStack() as es_p2:
            ynsa_b = sbt(es_p2, "ynsa_b", [128, 8, TQ], BF16)
            ysb = sbt(es_p2, "ysb", [128, 4, TQ], BF16)
            ymem = sbt(es_p2, "ymem", [128, 4, TQ], BF16)
            for k in range(NT):
                o0 = k * TQ
                with ExitStack() as es_a:
                    qn = sbt(es_a, "qn", [128, 8, TQ], BF16)
                    qs = sbt(es_a, "qs", [128, 4, TQ], BF16)
                    qm = sbt(es_a, "qm", [128, 4, TQ], BF16)
                    ksTo = sbt(es_a, "ksTo", [128, 2, TQ], BF16)
                    kwTo = sbt(es_a, "kwTo", [128, 2, TQ], BF16)
                    kwTp = sbt(es_a, "kwTp", [128, 2, TQ], BF16)
                    sbkTo = sbt(es_a, "sbkTo", [128, 4, TQ], BF16)
                    vown = sbt(es_a, "vown", [128, 4, 1024], BF16)
                    vprev = sbt(es_a, "vprev", [128, 4, 256], BF16)
                    with ExitStack() as es:
                        set_rot(range(8))
                        hp = sbt(es, "hp", [128, KC, TQ], BF16)
                        hn = sbt(es, "hn", [128, KC, TQ], BF16)
                        cosO = sbt(es, "cosO", [128, TQ], F32)
                        sinO = sbt(es, "sinO", [128, TQ], F32)
                        cosP = sbt(es, "cosP", [128, TQ], F32)
                        sinP = sbt(es, "sinP", [128, TQ], F32)
                        g32 = sbt(es, "g32", [24, TQ], F32)
                        rms2 = RMS(es, TQ, "a1rms")
                        rope2 = Rope(es, TQ, "a1rope")
                        nr2 = NR(es, TQ, "a1nr")
                        ws = WStream(es, "a1ws", 3, 4096)
                        rms2.run(xT_own[:, o0:o0 + TQ], "attn", hn, "hn")
                        rms2.run(xT_prev[:, o0:o0 + TQ], "attn", hp, "hp")
                        rope2.run(pos_own[0:1, o0:o0 + TQ], cosO[:], sinO[:], "cso")
                        rope2.run(pos_prev[0:1, o0:o0 + TQ], cosP[:], sinP[:], "csp")
                        rh = lambda c: hn[:, c, :]
                        rp = lambda c: hp[:, c, :]
                        proj_fm(ws, w_in_b, D, 0, 1024, rh, ["hn"], TQ, lambda j, pb: nr2.run(pb, "qn", qn[:, j, :], "qn", cosO[:], sinO[:], "cso"), gcmax=256)
                        proj_fm(ws, w_in_b, D, 1536, 256, rh, ["hn"], TQ, lambda j, pb: nr2.run(pb, "ksn", ksTo[:, j, :], "ksTo", cosO[:], sinO[:], "cso"), gcmax=256)
                        proj_fm(ws, w_in_b, D, 2048, 256, rh, ["hn"], TQ, lambda j, pb: nr2.run(pb, "kwn", kwTo[:, j, :], "kwTo", cosO[:], sinO[:], "cso"), gcmax=256)
                        proj_fm(ws, w_in_b, D, 2048, 256, rp, ["hp"], TQ, lambda j, pb: nr2.run(pb, "kwn", kwTp[:, j, :], "kwTp", cosP[:], sinP[:], "csp"), gcmax=256)
                        proj_fm(ws, w_in_b, D, 2584, 512, rh, ["hn"], TQ, lambda j, pb: copy_out(qs[:, j, :], "qs", PS(pb)[:, :], [pn(pb)]), gcmax=256)
                        proj_fm(ws, w_in_b, D, 3096, 512, rh, ["hn"], TQ, lambda j, pb: copy_out(sbkTo[:, j, :], "sbkTo", PS(pb)[:, :], [pn(pb)]), gcmax=256)
                        proj_fm(ws, w_in_b, D, 4120, 512, rh, ["hn"], TQ, lambda j, pb: nr2.run(pb, "mqn", qm[:, j, :], "qm"), gcmax=256)

                        def ev_g(j, pb):
                            S.op("act", lambda e: e.activation(out=g32[:], in_=PS(pb)[0:24, :], func=AF.Sigmoid), reads=[pn(pb)], writes=["g32"])
                            S.dma("sp", "gst", lambda e: e.dma_start(out=gates_d, in_=g32[:]), reads=["g32"], writes=["gates_d"])
                        proj_fm(ws, w_in_b, D, 2560, 24, rh, ["hn"], TQ, ev_g)
                        lh = lambda c, ts: hn[:, c, ts * 128:(ts + 1) * 128]
                        lp = lambda c, ts: hp[:, c, ts * 128:(ts + 1) * 128]
                        proj_tm(ws, w_in_b, D, 1792, 256, lh, ["hn"], 4, lambda ts, g0, gc, pb: copy_out(vown[:, ts, 0:256], "vown", PS(pb)[:, 0:256], [pn(pb)]))
                        proj_tm(ws, w_in_b, D, 2304, 256, lh, ["hn"], 4, lambda ts, g0, gc, pb: copy_out(vown[:, ts, 256:512], "vown", PS(pb)[:, 0:256], [pn(pb)]))
                        proj_tm(ws, w_in_b, D, 3608, 512, lh, ["hn"], 4, lambda ts, g0, gc, pb: copy_out(vown[:, ts, 512 + g0:512 + g0 + gc], "vown", PS(pb)[:, 0:gc], [pn(pb)]))
                        proj_tm(ws, w_in_b, D, 2304, 256, lp, ["hp"], 4, lambda ts, g0, gc, pb: copy_out(vprev[:, ts, :], "vprev", PS(pb)[:, 0:256], [pn(pb)]))
                        S.dma("sp", "hnst", lambda e: e.dma_start(out=hn_d, in_=hn[:].rearrange("p c t -> p (c t)")), reads=["hn"], writes=["hn_d"])
                        stage_end()

                    with ExitStack() as es:
                        O_B, SUM_B, U0_B, U1_B = 0, 1, 2, 3
                        set_rot([4, 5, 6, 7])
                        ynsa = sbt(es, "ynsa", [128, 4, TQ], F32)
                        impacc = sbt(es, "impacc", [128, 2, 4, 256], F32)
                        selbT = sbt(es, "selbT", [128, 2, 2, TQ], BF16)
                        ownb = sbt(es, "ownb", [8, 2, TQ], BF16)
                        gbc = [sbt(es, "gbc", [128, 3, TQ], F32) for _ in range(1)]
                        Pb = [sbt(es, "Pb", [128, TQ], BF16) for _ in range(3)]
                        rec = sbt(es, "rec", [128, TQ], F32)
                        tmpf = sbt(es, "tmpf", [128, TQ], F32)
                        kbuf = [sbt(es, "kbuf", [128, TQ], BF16) for _ in range(3)]
                        vbuf = [sbt(es, "vbuf", [128, 4, 128], BF16) for _ in range(3)]
                        cmask = sbt(es, "cmask", [128, NCC, TQ], BF16)
                        rtok = sbt(es, "rtok", [128, 4], F32)
                        tk1 = sbt(es, "tk1", [128, 256], F32)
                        tk2 = sbt(es, "tk2", [128, 256], F32)
                        tk3 = sbt(es, "tk3", [128, 256], F32)
                        tk4 = sbt(es, "tk4", [128, 256], F32)
                        m8 = sbt(es, "m8", [128, 16], F32)
                        alb = sbt(es, "alb", [128, 256], BF16)
                        alT = sbt(es, "alT", [128, 2, TQ], BF16)
                        e32 = [sbt(es, "e32", [128, TQ], F32) for _ in range(2)]
                        ec32 = [sbt(es, "ec32", [128, TQ], F32) for _ in range(2)]
                        Lp = [sbt(es, "Lp", [128, TQ], BF16) for _ in range(3)]
                        Wb = [sbt(es, "Wb", [128, TQ], BF16) for _ in range(2)]
                        Rb = [sbt(es, "Rb", [128, TQ], BF16) for _ in range(2)]
                        ncmp = min(NCC, 2 * (k + 1))
                        gi = [0]

                        def load_gates(h):
                            b = gbc[0]
                            name = "gbc0"
                            gi[0] += 1
                            S.dma("sp", name, lambda e: e.dma_start(out=b[:], in_=gates_d[3 * h:3 * h + 3, :].rearrange("(o r) t -> o r t", o=1).to_broadcast([128, 3, TQ])),
                                  reads=["gates_d"], writes=[name])
                            return b, name

                        for j in range(ncmp):
                            S.op("dve", lambda e, j=j: e.tensor_scalar(out=cmask[:, j, :], in0=j2[:], scalar1=ccol("cthr", k * NCC + j), scalar2=None, op0=ALU.is_gt),
                                 reads=["j2", "cst"], writes=["cmask"])

                        pi = [0]

                        pend = [None]

                        def flush_pending():
                            if pend[0] is None:
                                return
                            ch, P, pname, i, n, extra = pend[0]
                            pend[0] = None
                            S.op("pe", lambda e: e.matmul(PS(O_B)[:, :], lhsT=ch["v"], rhs=P[:], start=(i == 0), stop=(i == n - 1)),
                                 reads=ch["vn"] + [pname], writes=[pn(O_B)])
                            S.op("pe", lambda e: e.matmul(PS(SUM_B)[:, :], lhsT=cmv("ones"), rhs=P[:], start=(i == 0), stop=(i == n - 1)),
                                 reads=["cm", pname], writes=[pn(SUM_B)])
                            if extra:
                                extra(i, P, pname, n)

                        def softmax_chunks(chunks, q_ap, q_names, extra=None, i0=0, n_total=None):
                            n = len(chunks) if n_total is None else n_total
                            for i_, ch in enumerate(chunks):
                                i = i0 + i_
                                sb_ = nextps()
                                masks = ch.get("masks", [])
                                S.op("pe", lambda e: e.matmul(PS(sb_)[:, :], lhsT=ch["kT"], rhs=q_ap, start=True, stop=(len(masks) == 0)),
                                     reads=ch["kn"] + q_names, writes=[pn(sb_)])
                                for mi, (ml, mr, mn) in enumerate(masks):
                                    S.op("pe", lambda e: e.matmul(PS(sb_)[:, :], lhsT=ml, rhs=mr, start=False, stop=(mi == len(masks) - 1)),
                                         reads=mn + ["cm"], writes=[pn(sb_)])
                                P = Pb[pi[0] % 3]
                                pname = "Pb%d" % (pi[0] % 3)
                                pi[0] += 1
                                bias = ch.get("bias")
                                if bias is None:
                                    bias = ccol("zero")
                                S.op("act", lambda e: e.activation(out=P[:], in_=PS(sb_)[:, :], func=AF.Exp, scale=SCALE, bias=bias),
                                     reads=[pn(sb_), "cst"], writes=[pname])
                                flush_pending()
                                pend[0] = (ch, P, pname, i, n, extra)

                        def finalize(out_ap, out_name, gate_ap=None, gate_name=None, accumulate=False):
                            flush_pending()
                            S.op("dve", lambda e: e.tensor_scalar(out=rec[:], in0=PS(SUM_B)[:, :], scalar1=1e-30, scalar2=None, op0=ALU.max), reads=[pn(SUM_B)], writes=["rec"])
                            S.op("dve", lambda e: e.reciprocal(out=rec[:], in_=rec[:]), reads=["rec"], writes=["rec"])
                            if gate_ap is not None:
                                S.op("dve", lambda e: e.tensor_tensor(out=rec[:], in0=rec[:], in1=gate_ap, op=ALU.mult), reads=["rec", gate_name], writes=["rec"])
                            if accumulate:
                                S.op("dve", lambda e: e.tensor_tensor(out=tmpf[:], in0=PS(O_B)[:, :], in1=rec[:], op=ALU.mult), reads=[pn(O_B), "rec"], writes=["tmpf"])
                                S.op("dve", lambda e: e.tensor_tensor(out=out_ap, in0=out_ap, in1=tmpf[:], op=ALU.add), reads=["tmpf", out_name], writes=[out_name])
                            else:
                                S.op("dve", lambda e: e.tensor_tensor(out=out_ap, in0=PS(O_B)[:, :], in1=rec[:], op=ALU.mult), reads=[pn(O_B), "rec"], writes=[out_name])

                        si2 = [0]

                        def stream_tile(kT_src, v_c0, gt):
                            i = si2[0] % 3
                            si2[0] += 1
                            kb, vb = kbuf[i], vbuf[i]
                            S.dma("sp", "kb%d" % i, lambda e: e.dma_start(out=kb[:], in_=kT_src[:, gt * TQ:(gt + 1) * TQ]), reads=["kvdram"], writes=["kbuf%d" % i])
                            S.dma("sp", "vb%d" % i, lambda e: e.dma_start(out=vb[:], in_=vtok_d[gt * TQ:(gt + 1) * TQ, v_c0:v_c0 + 128].rearrange("(ts p) d -> p ts d", p=128)),
                                  reads=["kvdram"], writes=["vbuf%d" % i])
                            return kb, vb, "kbuf%d" % i, "vbuf%d" % i

                        for g in range(2):
                            set_rot([6, 7])
                            for hg in range(4):
                                h = g * 4 + hg
                                gb, gname = load_gates(h)
                                chunks = []
                                for j in range(ncmp):
                                    chunks.append(dict(kT=kcTs[:, g, j * 128:(j + 1) * 128], kn=["kcTs"], v=vcs[:, g, j, :], vn=["vcs"],
                                                       masks=[(cmv("negI"), cmask[:, j, :], ["cmask"])]))

                                def extra(i, P, pname, n):
                                    for ts in range(4):
                                        ub = 2 + ts
                                        o_ = 0
                                        S.op("pe", lambda e, P=P, ts=ts, ub=ub, o_=o_, i=i, n=n: e.matmul(PS(ub)[:, o_:o_ + 257], lhsT=P[:, ts * 128:(ts + 1) * 128],
                                                                                                    rhs=cm[:, ML["OV"] + i * 257:ML["OV"] + (i + 1) * 257],
                                                                                                    start=(i == 0), stop=(i == n - 1)),
                                             reads=[pname, "cm"], writes=[pn(ub)])
                                softmax_chunks(chunks, qn[:, h, :], ["qn"], extra)
                                finalize(ynsa[:, hg, :], "ynsa", gb[:, 0, :], gname)
                                for ts in range(4):
                                    ub = 2 + ts
                                    o_ = 0
                                    S.op("dve", lambda e, ts=ts, ub=ub, o_=o_: e.tensor_scalar(out=rtok[:, ts:ts + 1], in0=PS(ub)[:, o_ + 256:o_ + 257], scalar1=1e-30, scalar2=None, op0=ALU.max),
                                         reads=[pn(ub)], writes=["rtok"])
                                    S.op("dve", lambda e, ts=ts: e.reciprocal(out=rtok[:, ts:ts + 1], in_=rtok[:, ts:ts + 1]), reads=["rtok"], writes=["rtok"])
                                    if hg == 0:
                                        S.op("dve", lambda e, ts=ts, ub=ub, o_=o_: e.tensor_scalar(out=impacc[:, g, ts, :], in0=PS(ub)[:, o_:o_ + 256], scalar1=rtok[:, ts:ts + 1], scalar2=None, op0=ALU.mult),
                                             reads=[pn(ub), "rtok"], writes=["impacc"])
                                    else:
                                        S.op("dve", lambda e, ts=ts, ub=ub, o_=o_: e.scalar_tensor_tensor(out=impacc[:, g, ts, :], in0=PS(ub)[:, o_:o_ + 256], scalar=rtok[:, ts:ts + 1],
                                                                                                       in1=impacc[:, g, ts, :], op0=ALU.mult, op1=ALU.add),
                                             reads=[pn(ub), "rtok", "impacc"], writes=["impacc"])
                            set_rot([2, 3, 4, 5, 6, 7])
                            for ts in range(4):
                                curc = ccol("cur", k * 4 + ts)
                                blk = rowt[:, 0:256]
                                S.op("dve", lambda e, curc=curc: e.tensor_scalar(out=tk1[:], in0=blk, scalar1=curc, scalar2=None, op0=ALU.subtract), reads=["rowt", "cst"], writes=["tk1"])
                                S.op("dve", lambda e: e.tensor_scalar(out=tk2[:], in0=tk1[:], scalar1=0.0, scalar2=None, op0=ALU.is_gt), reads=["tk1"], writes=["tk2"])
                                S.op("dve", lambda e: e.tensor_scalar(out=tk3[:], in0=tk1[:], scalar1=-1.0, scalar2=None, op0=ALU.is_ge), reads=["tk1"], writes=["tk3"])
                                S.op("dve", lambda e: e.tensor_tensor(out=tk3[:], in0=tk3[:], in1=tk2[:], op=ALU.subtract), reads=["tk3", "tk2"], writes=["tk3"])
                                S.op("dve", lambda e: e.tensor_scalar(out=tk4[:], in0=blk, scalar1=0.0, scalar2=None, op0=ALU.is_equal), reads=["rowt"], writes=["tk4"])
                                S.op("dve", lambda e: e.tensor_tensor(out=tk3[:], in0=tk3[:], in1=tk4[:], op=ALU.max), reads=["tk3", "tk4"], writes=["tk3"])
                                S.op("dve", lambda e: e.tensor_tensor(out=tk3[:], in0=tk3[:], in1=tk2[:], op=ALU.subtract), reads=["tk3", "tk2"], writes=["tk3"])
                                S.op("dve", lambda e, ts=ts: e.scalar_tensor_tensor(out=tk1[:], in0=tk3[:], scalar=1.0e4, in1=impacc[:, g, ts, :], op0=ALU.mult, op1=ALU.add),
                                     reads=["tk3", "impacc"], writes=["tk1"])
                                S.op("dve", lambda e: e.max(out=m8[:, 0:8], in_=tk1[:]), reads=["tk1"], writes=["m8a"])
                                S.op("dve", lambda e: e.match_replace(out=tk4[:], in_to_replace=m8[:, 0:8], in_values=tk1[:], imm_value=-3.0e4), reads=["tk1", "m8a"], writes=["tk4"])
                                S.op("dve", lambda e: e.max(out=m8[:, 8:16], in_=tk4[:]), reads=["tk4"], writes=["m8b"])
                                S.op("dve", lambda e: e.tensor_scalar(out=tk4[:], in0=tk1[:], scalar1=m8[:, 15:16], scalar2=None, op0=ALU.is_ge), reads=["tk1", "m8b"], writes=["tk4"])
                                S.op("dve", lambda e: e.tensor_scalar(out=tk2[:], in0=tk2[:], scalar1=-1.0, scalar2=1.0, op0=ALU.mult, op1=ALU.add), reads=["tk2"], writes=["tk2"])
                                S.op("dve", lambda e: e.tensor_tensor(out=alb[:], in0=tk4[:], in1=tk2[:], op=ALU.mult), reads=["tk4", "tk2"], writes=["alb"])
                                for half in range(2):
                                    pb = nextps()
                                    S.op("pe", lambda e, half=half, pb=pb: e.matmul(PS(pb)[:, 0:128], lhsT=alb[:, half * 128:(half + 1) * 128], rhs=cmv("ident"), start=True, stop=True),
                                         reads=["alb", "cm"], writes=[pn(pb)])
                                    copy_out(alT[:, half, ts * 128:(ts + 1) * 128], "alT", PS(pb)[:, 0:128], [pn(pb)])
                            pb = nextps()
                            for half in range(2):
                                osel = cm[:, ML["OwnSel"] + k * 16 + half * 8:ML["OwnSel"] + k * 16 + half * 8 + 8]
                                S.op("pe", lambda e, half=half, pb=pb, osel=osel: e.matmul(PS(pb)[0:8, :], lhsT=osel, rhs=alT[:, half, :], start=(half == 0), stop=(half == 1)),
                                     reads=["alT", "cm"], writes=[pn(pb)])
                            S.op("dve", lambda e, pb=pb: e.tensor_scalar(out=ownb[:, g, :], in0=PS(pb)[0:8, :], scalar1=-1.0, scalar2=-NEG, op0=ALU.add, op1=ALU.mult),
                                 reads=[pn(pb)], writes=["ownb"])
                            for half in range(2):
                                S.op("dve", lambda e, half=half: e.tensor_scalar(out=selbT[:, g, half, :], in0=alT[:, half, :], scalar1=cst[:, CL["_om"] + k * 2 + half:CL["_om"] + k * 2 + half + 1],
                                                                               scalar2=None, op0=ALU.mult), reads=["alT", "cst"], writes=["selbT"])
                                S.op("dve", lambda e, half=half: e.tensor_scalar(out=selbT[:, g, half, :], in0=selbT[:, g, half, :], scalar1=-1.0, scalar2=-NEG, op0=ALU.add, op1=ALU.mult),
                                     reads=["selbT"], writes=["selbT"])
                            for hg in range(4):
                                h = g * 4 + hg
                                gb, gname = load_gates(h)
                                chunks = []
                                for c_ in range(4):
                                    chunks.append(dict(kT=ksTo[:, g, c_ * 128:(c_ + 1) * 128], kn=["ksTo"], v=vown[:, c_, g * 128:(g + 1) * 128], vn=["vown"],
                                                       masks=[(cm[0:8, ML["Eown"] + c_ * 128:ML["Eown"] + (c_ + 1) * 128], ownb[:, g, :], ["ownb"]),
                                                              (cmv("negI"), cm[:, ML["DiagN"] + c_ * 512:ML["DiagN"] + (c_ + 1) * 512], [])]))
                                ntot = 4 + 4 * (8 * k + 8)
                                softmax_chunks(chunks, qn[:, h, :], ["qn"], None, 0, ntot)
                                for gt in range(8 * k + 8):
                                    kb, vb, kname, vname = stream_tile(ksT_d[g], g * 128, gt)
                                    chunks = []
                                    for c_ in range(4):
                                        cg = gt * 4 + c_
                                        chunks.append(dict(kT=kb[:, c_ * 128:(c_ + 1) * 128], kn=[kname], v=vb[:, c_, :], vn=[vname],
                                                           masks=[(cm[:, ML["E64"] + (cg % 64) * 128:ML["E64"] + (cg % 64 + 1) * 128], selbT[:, g, cg // 64, :], ["selbT"])]))
                                    softmax_chunks(chunks, qn[:, h, :], ["qn"], None, 4 + 4 * gt, ntot)
                                finalize(ynsa[:, hg, :], "ynsa", gb[:, 1, :], gname, accumulate=True)
                            for hg in range(4):
                                h = g * 4 + hg
                                gb, gname = load_gates(h)
                                chunks = []
                                for c_ in range(4):
                                    chunks.append(dict(kT=kwTp[:, g, c_ * 128:(c_ + 1) * 128], kn=["kwTp"], v=vprev[:, c_, g * 128:(g + 1) * 128], vn=["vprev"],
                                                       masks=[(cmv("negI"), cm[:, ML["DiagW"] + c_ * 512:ML["DiagW"] + (c_ + 1) * 512], [])], bias=ccol("pv", k)))
                                for c_ in range(4):
                                    chunks.append(dict(kT=kwTo[:, g, c_ * 128:(c_ + 1) * 128], kn=["kwTo"], v=vown[:, c_, 256 + g * 128:256 + (g + 1) * 128], vn=["vown"],
                                                       masks=[(cmv("negI"), cm[:, ML["DiagN"] + c_ * 512:ML["DiagN"] + (c_ + 1) * 512], [])]))
                                softmax_chunks(chunks, qn[:, h, :], ["qn"])
                                finalize(ynsa[:, hg, :], "ynsa", gb[:, 2, :], gname, accumulate=True)
                                copy_out(ynsa_b[:, h, :], "ynsa_b", ynsa[:, hg, :], ["ynsa"])
                        for h in range(4):
                            chunks = [dict(kT=kmem[:, h, ms * 128:(ms + 1) * 128], kn=["kmem"], v=vmem[:, ms, h * 128:(h + 1) * 128], vn=["vmem"]) for ms in range(2)]
                            softmax_chunks(chunks, qm[:, h, :], ["qm"])
                            finalize(ymem[:, h, :], "ymem")
                        for h in range(4):
                            nch = 4 + (8 * k + 8) * 4

                            def sb_gen(h=h):
                                for c_ in (3, 2, 1, 0):
                                    yield dict(kT=sbkTo[:, h, c_ * 128:(c_ + 1) * 128], kn=["sbkTo"], v=vown[:, c_, 512 + h * 128:512 + (h + 1) * 128], vn=["vown"],
                                               diag=cm[:, ML["DiagS"] + c_ * 512:ML["DiagS"] + (c_ + 1) * 512], bias=ccol("zero"))
                                for tl in range(8 * k + 7, -1, -1):
                                    kb, vb, kname, vname = stream_tile(sbkT_d[h], 256 + h * 128, tl)
                                    j_ = tl - 8 * k
                                    bias = ccol("cw", j_) if j_ >= 0 else ccol("zero")
                                    for c_ in (3, 2, 1, 0):
                                        yield dict(kT=kb[:, c_ * 128:(c_ + 1) * 128], kn=[kname], v=vb[:, c_, :], vn=[vname], bias=bias)
                            gen = sb_gen()
                            st = {}
                            Rprev = [None]

                            def stage1(ci):
                                ch = next(gen)
                                zb = nextps()
                                dg = ch.get("diag")
                                S.op("pe", lambda e: e.matmul(PS(zb)[:, :], lhsT=ch["kT"], rhs=qs[:, h, :], start=True, stop=(dg is None)),
                                     reads=ch["kn"] + ["qs"], writes=[pn(zb)])
                                if dg is not None:
                                    S.op("pe", lambda e: e.matmul(PS(zb)[:, :], lhsT=cmv("negI"), rhs=dg, start=False, stop=True), reads=["cm"], writes=[pn(zb)])
                                e_ = e32[ci % 2]
                                en = "e32_%d" % (ci % 2)
                                S.op("act", lambda e: e.activation(out=e_[:], in_=PS(zb)[:, :], func=AF.Exp, scale=SCALE, bias=ch["bias"]),
                                     reads=[pn(zb), "cst"], writes=[en])
                                L_ = Lp[ci % 3]
                                ln_ = "Lp%d" % (ci % 3)
                                S.op("act", lambda e: e.activation(out=L_[:], in_=e_[:], func=AF.Ln, bias=ccol("one")), reads=[en, "cst"], writes=[ln_])
                                st[ci] = dict(ch=ch, e_=e_, en=en, L_=L_, ln_=ln_)

                            def stage2(ci):
                                d_ = st[ci]
                                L_, ln_, e_, en = d_["L_"], d_["ln_"], d_["e_"], d_["en"]
                                cb = nextps()
                                Rp = Rprev[0]
                                S.op("pe", lambda e: e.matmul(PS(cb)[:, :], lhsT=cmv("UIneg"), rhs=L_[:], start=True, stop=(Rp is None)),
                                     reads=[ln_, "cm"], writes=[pn(cb)])
                                if Rp is not None:
                                    S.op("pe", lambda e: e.matmul(PS(cb)[:, :], lhsT=cmv("onesneg"), rhs=Rp[0][:], start=False, stop=True),
                                         reads=[Rp[1], "cm"], writes=[pn(cb)])
                                ec_ = ec32[ci % 2]
                                ecn = "ec32_%d" % (ci % 2)
                                S.op("act", lambda e: e.activation(out=ec_[:], in_=PS(cb)[:, :], func=AF.Exp), reads=[pn(cb)], writes=[ecn])
                                W_ = Wb[ci % 2]
                                wn_ = "Wb%d" % (ci % 2)
                                S.op("dve", lambda e: e.tensor_tensor(out=W_[:], in0=e_[:], in1=ec_[:], op=ALU.mult), reads=[en, ecn], writes=[wn_])
                                Rn = Rb[ci % 2]
                                rn_ = "Rb%d" % (ci % 2)
                                if Rp is None:
                                    S.op("dve", lambda e: e.tensor_copy(out=Rn[:], in_=L_[:]), reads=[ln_], writes=[rn_])
                                else:
                                    S.op("dve", lambda e: e.tensor_tensor(out=Rn[:], in0=Rp[0][:], in1=L_[:], op=ALU.add), reads=[ln_, Rp[1]], writes=[rn_])
                                Rprev[0] = (Rn, rn_)
                                d_["W_"], d_["wn_"] = W_, wn_

                            def stage3(ci):
                                d_ = st.pop(ci)
                                ch, W_, wn_ = d_["ch"], d_["W_"], d_["wn_"]
                                S.op("pe", lambda e: e.matmul(PS(O_B)[:, :], lhsT=ch["v"], rhs=W_[:], start=(ci == 0), stop=(ci == nch - 1)),
                                     reads=ch["vn"] + [wn_], writes=[pn(O_B)])
                            for it in range(nch + 2):
                                if it < nch:
                                    stage1(it)
                                if 1 <= it <= nch:
                                    stage2(it - 1)
                                if it >= 2:
                                    stage3(it - 2)
                            copy_out(ysb[:, h, :], "ysb", PS(O_B)[:, :], [pn(O_B)])
                        stage_end()

                with ExitStack() as es_b:
                    x1 = sbt(es_b, "x1", [128, KC, TQ], F32)
                    hn = sbt(es_b, "hnB", [128, KC, TQ], BF16)
                    S.dma("sp", "hnld", lambda e: e.dma_start(out=hn[:].rearrange("p c t -> p (c t)"), in_=hn_d), reads=["hn_d"], writes=["hn"])
                    with ExitStack() as es:
                        set_rot(range(8))
                        mixed = sbt(es, "mixed", [128, KC, TQ], BF16)
                        sig = sbt(es, "sig", [128, 3, TQ], F32)
                        acc = sbt(es, "acc", [128, TQ], F32)
                        tmpm = sbt(es, "tmpm", [128, TQ], F32)
                        wsg = WStream(es, "b1wg", 3, 2048)
                        wso = WStream(es, "b1wo", 3, 2048)
                        for dc in range(KC):
                            for br in range(3):
                                def ev_s(j, pb, br=br):
                                    S.op("act", lambda e: e.activation(out=sig[:, br, :], in_=PS(pb)[:, :], func=AF.Sigmoid), reads=[pn(pb)], writes=["sig"])
                                proj_fm(wsg, w_in_b, D, 4632 + br * D + dc * 128, 128, lambda c: hn[:, c, :], ["hn"], TQ, ev_s)
                            for br, (wo, K_, y_, yn_) in enumerate([(wo_nsa_b, 1024, ynsa_b, "ynsa_b"), (wo_sb_b, 512, ysb, "ysb"), (wo_mem_b, 512, ymem, "ymem")]):
                                def ev_a(j, pb, br=br):
                                    if br == 0:
                                        S.op("dve", lambda e: e.tensor_tensor(out=acc[:], in0=PS(pb)[:, :], in1=sig[:, 0, :], op=ALU.mult), reads=[pn(pb), "sig"], writes=["acc"])
                                    else:
                                        S.op("dve", lambda e: e.tensor_tensor(out=tmpm[:], in0=PS(pb)[:, :], in1=sig[:, br, :], op=ALU.mult), reads=[pn(pb), "sig"], writes=["tmpm"])
                                        if br == 1:
                                            S.op("dve", lambda e: e.tensor_tensor(out=acc[:], in0=acc[:], in1=tmpm[:], op=ALU.add), reads=["acc", "tmpm"], writes=["acc"])
                                        else:
                                            S.op("dve", lambda e: e.tensor_tensor(out=mixed[:, dc, :], in0=acc[:], in1=tmpm[:], op=ALU.add), reads=["acc", "tmpm"], writes=["mixed"])
                                proj_fm(wso, wo, K_, dc * 128, 128, lambda c, y_=y_: y_[:, c, :], [yn_], TQ, ev_a)
                        xr = [sbt(es, "xr", [128, TQ], F32) for _ in range(2)]

                        def ev_o(j, pb):
                            b = xr[j % 2]
                            S.dma("sp", "xr%d" % (j % 2), lambda e: e.dma_start(out=b[:], in_=xT_own[j * 128:(j + 1) * 128, o0:o0 + TQ]), writes=["xr%d" % (j % 2)])
                            S.op("dve", lambda e: e.tensor_tensor(out=x1[:, j, :], in0=PS(pb)[:, :], in1=b[:], op=ALU.add), reads=[pn(pb), "xr%d" % (j % 2)], writes=["x1"])
                        proj_fm(wsg, w_out_b, D, 0, D, lambda c: mixed[:, c, :], ["mixed"], TQ, ev_o, gcmax=128)
                        stage_end()
                    with ExitStack() as es:
                        set_rot(range(8))
                        act_all = sbt(es, "act_all", [128, FFN // 256, TQ], BF16)
                        sq2 = [sbt(es, "sq2", [128, TQ], BF16) for _ in range(2)]
                        rs2 = sbt(es, "rs2", [128, TQ], F32)
                        sgl = [sbt(es, "sgl", [128, TQ], F32) for _ in range(2)]
                        ost = [sbt(es, "ost", [128, TQ], F32) for _ in range(2)]
                        wsf = WStream(es, "b2wf", 3, 5632)
                        pb0 = nextps()
                        for c in range(KC):
                            S.op("act", lambda e, c=c: e.activation(out=sq2[c % 2][:], in_=x1[:, c, :], func=AF.Square), reads=["x1"], writes=["sq2_%d" % (c % 2)])
                            S.op("pe", lambda e, c=c: e.matmul(PS(pb0)[:, :], lhsT=cmv("ones"), rhs=sq2[c % 2][:], start=(c == 0), stop=(c == KC - 1)),
                                 reads=["sq2_%d" % (c % 2), "cm"], writes=[pn(pb0)])
                        S.op("act", lambda e: e.activation(out=rs2[:], in_=PS(pb0)[:, :], func=AF.Sqrt, scale=1.0 / D, bias=ccol("eps")), reads=[pn(pb0), "cst"], writes=["rs2"])
                        S.op("dve", lambda e: e.reciprocal(out=rs2[:], in_=rs2[:]), reads=["rs2"], writes=["rs2"])
                        for c in range(KC):
                            S.op("dve", lambda e, c=c: e.scalar_tensor_tensor(out=hn[:, c, :], in0=x1[:, c, :], scalar=ccol("ffn", c), in1=rs2[:], op0=ALU.mult, op1=ALU.mult),
                                 reads=["x1", "rs2", "cst"], writes=["hn"])
                        for hf_ in range(2):
                          for fi2 in range(FFN // 512):
                            def ev_g2(j, pb):
                                S.op("act", lambda e: e.activation(out=sgl[j][:], in_=PS(pb)[:, :], func=AF.Silu), reads=[pn(pb)], writes=["sgl%d" % j])
                            proj_fm(wsf, w_gate_b, D, hf_ * 2816 + fi2 * 256, 256, lambda c: hn[:, c, :], ["hn"], TQ, ev_g2)

                            def ev_u(j, pb, fi2=fi2):
                                S.op("dve", lambda e: e.tensor_tensor(out=act_all[:, fi2 * 2 + j, :], in0=PS(pb)[:, :], in1=sgl[j][:], op=ALU.mult), reads=[pn(pb), "sgl%d" % j], writes=["act_all"])
                            proj_fm(wsf, w_up_b, D, hf_ * 2816 + fi2 * 256, 256, lambda c: hn[:, c, :], ["hn"], TQ, ev_u)

                          def ev_d(j, pb, hf_=hf_):
                            if hf_ == 0:
                                S.op("dve", lambda e: e.tensor_tensor(out=x1[:, j, :], in0=PS(pb)[:, :], in1=x1[:, j, :], op=ALU.add), reads=[pn(pb), "x1"], writes=["x1"])
                                return
                            b = ost[j % 2]
                            S.op("dve", lambda e: e.tensor_tensor(out=b[:], in0=PS(pb)[:, :], in1=x1[:, j, :], op=ALU.add), reads=[pn(pb), "x1"], writes=["ost%d" % (j % 2)])
                            S.dma("sp", "ost%d" % (j % 2), lambda e: e.dma_start(out=outT[j * 128:(j + 1) * 128, o0:o0 + TQ], in_=b[:]), reads=["ost%d" % (j % 2)], writes=["outT"])
                          proj_fm(wsf, w_down_b[hf_ * 2816:(hf_ + 1) * 2816, :], 2816, 0, D, lambda c: act_all[:, c, :], ["act_all"], TQ, ev_d, gcmax=256)

                        stage_end()
        S.barrier()
        S.flush()
    return nc


def host_consts(T, core):
    NT = T // (NCORE * TQ)
    NCC = T // 2048
    CL = cst_layout(NT, NCC)
    ML = cm_layout(NT, NCC)
    p = np.arange(128)
    cm = np.zeros((128, ML["_n"]), np.float32)
    cm[p, ML["ident"] + p] = 1.0
    cm[p, ML["negI"] + p] = NEG
    cm[:, ML["ones"]:ML["ones"] + 128] = 1.0
    cm[:, ML["onesneg"]:ML["onesneg"] + 128] = -1.0
    cm[:, ML["UIneg"]:ML["UIneg"] + 128] = -(p[:, None] >= p[None, :]).astype(np.float32)
    for m in range(128):
        if m < 64:
            cm[m + 64, ML["prot"] + m] = -1.0
        else:
            cm[m - 64, ML["prot"] + m] = 1.0
    u = np.arange(8192)
    cm[:, ML["E64"]:ML["E64"] + 8192] = (p[:, None] == (u[None, :] // 64)).astype(np.float32)
    for j in range(NCC):
        n = 128 * j + p
        s = np.arange(256)
        ov = ((n[:, None] >= 4 * s[None, :] - 1) & (n[:, None] <= 4 * s[None, :] + 3) & (n[:, None] < T // 16 - 1)).astype(np.float32)
        cm[:, ML["OV"] + j * 257:ML["OV"] + j * 257 + 256] = ov
        cm[:, ML["OV"] + j * 257 + 256] = (n < T // 16 - 1).astype(np.float32)
    t = np.arange(512)
    for c_ in range(4):
        sg_ = 128 * c_ + p
        cm[:, ML["DiagN"] + c_ * 512:ML["DiagN"] + (c_ + 1) * 512] = (sg_[:, None] > t[None, :]).astype(np.float32)
        cm[:, ML["DiagS"] + c_ * 512:ML["DiagS"] + (c_ + 1) * 512] = (sg_[:, None] >= t[None, :]).astype(np.float32)
        cm[:, ML["DiagW"] + c_ * 512:ML["DiagW"] + (c_ + 1) * 512] = (sg_[:, None] <= t[None, :]).astype(np.float32)
        s_ = np.arange(128)
        for r in range(2):
            cm[2 * c_ + r, ML["Eown"] + c_ * 128 + s_] = (s_ // 64 == r).astype(np.float32)
    for k in range(NT):
        ob0 = 8 * (8 * k + core)
        for half in range(2):
            for b in range(8):
                blk = ob0 + b
                if blk // 128 == half:
                    cm[blk % 128, ML["OwnSel"] + k * 16 + half * 8 + b] = 1.0
    cst = np.zeros((128, CL["_n"]), np.float32)
    cst[:, CL["eps"]] = 1e-6
    cst[:, CL["one"]] = 1.0
    cst[:, CL["halfpi"]] = np.float32(np.pi / 2)
    cst[:, CL["tiny"]] = 1e-30
    inv = (np.float32(1.0) / np.power(np.float32(10000.0), np.arange(0, 128, 2, dtype=np.float32) / np.float32(128))).astype(np.float32)
    cst[:, CL["inv"]] = inv[p % 64]
    for j in range(8):
        cst[:, CL["cw"] + j] = 0.0 if j < core else NEG
    for k in range(NT):
        gt = 8 * k + core
        cst[:, CL["pv"] + k] = NEG if gt == 0 else 0.0
        for ts in range(4):
            cst[:, CL["cur"] + k * 4 + ts] = (gt * 512 + ts * 128 + p) // 64
        for j in range(NCC):
            cst[:, CL["cthr"] + k * NCC + j] = gt * 512 - 2048 * j
        ob0 = 8 * gt
        for half in range(2):
            cst[:, CL["_om"] + k * 2 + half] = ((half * 128 + p) < ob0).astype(np.float32)
    rowt = np.zeros((128, 256), np.float32)
    rowt[:, 0:256] = np.arange(256)[None, :]
    j2 = (16 * p[:, None] + 31 - t[None, :]).astype(np.float32)
    return cst, cm, rowt, j2, CL


_NC_CACHE = {}


def kernel(x, mem, positions, attn_norm, w_in, nsa_q_norm, nsa_kc_norm, nsa_ks_norm, nsa_kw_norm,
           cmp_k_pe, cmp_k_w1, cmp_k_w2, cmp_v_pe, cmp_v_w1, cmp_v_w2, mem_norm, w_mem_kv,
           mem_q_norm, mem_k_norm, w_o_nsa, w_o_sb, w_o_mem, w_out, ffn_norm,
           w_ffn_gate, w_ffn_up, w_ffn_down):
    x = np.asarray(x)
    T = x.shape[1]
    NT = T // (NCORE * TQ)
    NCC = T // 2048
    NCP = NCC * 128
    if T not in _NC_CACHE:
        _NC_CACHE[T] = build(T)
    nc = _NC_CACHE[T]
    f = lambda a: np.ascontiguousarray(np.asarray(a, dtype=np.float32))
    xT = np.ascontiguousarray(np.asarray(x)[0].T)
    pos = np.asarray(positions).astype(np.int32)
    pos_cmp = np.zeros((1, NCP), np.int32)
    pc = pos[0, 31::16]
    pos_cmp[0, :len(pc)] = pc
    common = {
        "xT_all": xT, "memT": np.ascontiguousarray(np.asarray(mem)[0].T), "pos_all": pos, "pos_cmp": pos_cmp,
        "w_in": f(w_in[0]), "w1k": f(cmp_k_w1[0]), "w2k": f(cmp_k_w2[0]), "w1v": f(cmp_v_w1[0]), "w2v": f(cmp_v_w2[0]),
        "w_mem": f(w_mem_kv[0]), "wo_nsa": f(w_o_nsa[0]), "wo_sb": f(w_o_sb[0]), "wo_mem": f(w_o_mem[0]), "w_out": f(w_out[0]),
        "w_gate": f(w_ffn_gate[0]), "w_up": f(w_ffn_up[0]), "w_down": f(w_ffn_down[0]),
    }
    in_maps = []
    for core in range(NCORE):
        cst, cm, rowt, j2, CL = host_consts(T, core)
        for name, arr in [("attn", attn_norm), ("ffn", ffn_norm), ("memn", mem_norm)]:
            cst[:, CL[name]:CL[name] + 16] = np.asarray(arr)[0].reshape(16, 128).T
        for name, arr in [("qn", nsa_q_norm), ("kcn", nsa_kc_norm), ("ksn", nsa_ks_norm), ("kwn", nsa_kw_norm), ("mqn", mem_q_norm), ("mkn", mem_k_norm)]:
            cst[:, CL[name]] = np.asarray(arr)[0]
        cst[:, CL["pek"]:CL["pek"] + 32] = np.asarray(cmp_k_pe)[0].T
        cst[:, CL["pev"]:CL["pev"] + 32] = np.asarray(cmp_v_pe)[0].T
        own = np.zeros((D, NT * TQ), np.float32)
        prev = np.zeros((D, NT * TQ), np.float32)
        pown = np.zeros((1, NT * TQ), np.int32)
        pprev = np.zeros((1, NT * TQ), np.int32)
        for k in range(NT):
            gt = 8 * k + core
            own[:, k * TQ:(k + 1) * TQ] = xT[:, gt * TQ:(gt + 1) * TQ]
            pown[0, k * TQ:(k + 1) * TQ] = pos[0, gt * TQ:(gt + 1) * TQ]
            if gt > 0:
                prev[:, k * TQ:(k + 1) * TQ] = xT[:, (gt - 1) * TQ:gt * TQ]
                pprev[0, k * TQ:(k + 1) * TQ] = pos[0, (gt - 1) * TQ:gt * TQ]
        m = dict(common)
        m.update({"xT_own": own, "xT_prev": prev, "pos_own": pown, "pos_prev": pprev, "cst": cst, "cmat": cm, "rowt": rowt, "j2": j2})
        in_maps.append(m)
    res = run_bass_kernel_spmd(nc, in_maps, core_ids=list(range(NCORE)))
    out = np.zeros((1, T, D), np.float32)
    for core in range(NCORE):
        oT = np.asarray(res.results[core]["outT"])
        for k in range(NT):
            gt = 8 * k + core
            out[0, gt * TQ:(gt + 1) * TQ, :] = oT[:, k * TQ:(k + 1) * TQ].T
    return out
```

```python
import math
from contextlib import ExitStack
import numpy as np
import concourse.bass as bass
import concourse.mybir as mybir
from concourse.bass_utils import run_bass_kernel_spmd

F32 = mybir.dt.float32
BF16 = mybir.dt.bfloat16
I32 = mybir.dt.int32
ALU = mybir.AluOpType
AF = mybir.ActivationFunctionType

D = 2048
KC = 16
NCORE = 8
TQ = 512
FFN = 5632
NEG = -30000.0
SCALE = 128 ** -0.5
MAGIC = 12582912.0
C1 = 6.28125
C2 = float(2 * np.pi - 6.28125)
PI_LO = 3.1415925

COMPUTE_Q = ("pe", "act", "dve", "pool")
ALLQ = ("pe", "act", "dve", "pool", "sp")


class Sched:
    def __init__(self, nc, es):
        self.nc = nc
        self.es = es
        self.q = {k: [] for k in ALLQ}
        self.cnt = {k: 0 for k in COMPUTE_Q}
        self.sems = {k: es.enter_context(nc.semaphore("s_" + k)) for k in COMPUTE_Q}
        self.dcnt = {}
        self.lastw = {}
        self.readers = {}
        self.seen = {k: {} for k in ALLQ}
        self.nops = 0

    def _deps(self, q, reads, writes):
        need = {}

        def add(tok):
            if tok is None:
                return
            k, v = tok
            if need.get(k, 0) < v:
                need[k] = v
        for b in reads:
            add(self.lastw.get(b))
        for b in writes:
            add(self.lastw.get(b))
            for t in self.readers.get(b, ()):
                add(t)
        out = []
        for k, v in need.items():
            if k == q and q == "pe":
                continue
            if self.seen[q].get(k, 0) >= v:
                continue
            self.seen[q][k] = v
            out.append((k, v))
        return out

    def _commit(self, tok, reads, writes):
        for b in reads:
            self.readers.setdefault(b, []).append(tok)
        for b in writes:
            self.lastw[b] = tok
            self.readers[b] = []

    def _rec(self, fn):
        class _R:
            def __getattr__(s, name):
                def f(*a, **kw):
                    s.call = (name, a, kw)
                    return None
                return f
        r = _R()
        fn(r)
        name, a, kw = r.call
        return lambda eng: getattr(eng, name)(*a, **kw)

    def op(self, q, fn, reads=(), writes=(), sig=True):
        fn = self._rec(fn)
        waits = self._deps(q, reads, writes)
        if not sig:
            tok = (q, self.cnt[q] + 1)
            self.q[q].append((waits, fn, None, 0))
            self._commit(tok, reads, writes)
            self.nops += 1
            return tok
        self.cnt[q] += 1
        tok = (q, self.cnt[q])
        self.q[q].append((waits, fn, q, 1))
        self._commit(tok, reads, writes)
        self.nops += 1
        return tok

    def dma(self, q, key, fn, reads=(), writes=()):
        fn = self._rec(fn)
        waits = self._deps(q, reads, writes)
        k = "dma:" + key
        if k not in self.sems:
            self.sems[k] = self.es.enter_context(self.nc.semaphore("d_" + key))
            self.dcnt[k] = 0
        self.dcnt[k] += 16
        tok = (k, self.dcnt[k])
        self.q[q].append((waits, fn, k, 16))
        self._commit(tok, reads, writes)
        self.nops += 1
        return tok

    def barrier(self):
        allw = [(k, v) for k, v in self.cnt.items() if v > 0] + [(k, v) for k, v in self.dcnt.items() if v > 0]
        for q in ALLQ:
            w = [(k, v) for k, v in allw if self.seen[q].get(k, 0) < v]
            for k, v in w:
                self.seen[q][k] = v
            if w:
                self.q[q].append((w, None, None, 0))
        self.lastw = {}
        self.readers = {}

    def flush(self):
        nc = self.nc
        engs = {"pe": "tensor", "act": "scalar", "dve": "vector", "pool": "gpsimd", "sp": "sync"}
        if not any(self.q.values()):
            return
        with nc.Block() as block:
            def mk(items):
                def body(eng):
                    for waits, fn, semk, inc in items:
                        for k, v in waits:
                            eng.wait_ge(self.sems[k], v)
                        if fn is not None:
                            ins = fn(eng)
                            if semk is not None:
                                ins.then_inc(self.sems[semk], inc)
                return body
            for qn, attr in engs.items():
                if self.q[qn]:
                    getattr(block, attr)(mk(self.q[qn]))
        self.q = {k: [] for k in ALLQ}


def cst_layout(NT, NCC):
    o = {}
    c = 0
    for name, n in [("attn", 16), ("ffn", 16), ("memn", 16), ("qn", 1), ("kcn", 1), ("ksn", 1), ("kwn", 1),
                    ("mqn", 1), ("mkn", 1), ("inv", 1), ("eps", 1), ("one", 1), ("zero", 1), ("halfpi", 1),
                    ("cw", 8), ("pv", NT), ("cur", NT * 4), ("cthr", NT * NCC), ("pek", 32), ("pev", 32),
                    ("tiny", 1), ("_om", NT * 2)]:
        o[name] = c
        c += n
    o["_n"] = c
    return o


def cm_layout(NT, NCC):
    o = {}
    c = 0
    for name, n in [("ident", 128), ("negI", 128), ("ones", 128), ("onesneg", 128), ("UIneg", 128), ("prot", 128),
                    ("E64", 8192), ("OV", NCC * 257), ("DiagN", 2048), ("DiagS", 2048), ("DiagW", 2048),
                    ("Eown", 512), ("OwnSel", NT * 16)]:
        o[name] = c
        c += n
    o["_n"] = c
    return o


def build(T):
    NT = T // (NCORE * TQ)
    NGT = T // TQ
    NCC = T // 2048
    NCP = NCC * 128
    NB = min(512, NCP)
    CL = cst_layout(NT, NCC)
    ML = cm_layout(NT, NCC)
    TO = NT * TQ

    nc = bass.Bass("TRN2", target_bir_lowering=False)

    def din(name, shape, dt=F32):
        return nc.dram_tensor(name, list(shape), dt, kind="ExternalInput").ap()

    xT_all = din("xT_all", [D, T])
    xT_own = din("xT_own", [D, TO])
    xT_prev = din("xT_prev", [D, TO])
    memT = din("memT", [D, 256])
    pos_all = din("pos_all", [1, T], I32)
    pos_own = din("pos_own", [1, TO], I32)
    pos_prev = din("pos_prev", [1, TO], I32)
    pos_cmp = din("pos_cmp", [1, NCP], I32)
    w_in = din("w_in", [D, 10776])
    w1k = din("w1k", [32, 128, 256])
    w2k = din("w2k", [256, 128])
    w1v = din("w1v", [32, 128, 256])
    w2v = din("w2v", [256, 128])
    w_mem = din("w_mem", [D, 1024])
    wo_nsa = din("wo_nsa", [1024, D])
    wo_sb = din("wo_sb", [512, D])
    wo_mem = din("wo_mem", [512, D])
    w_out = din("w_out", [D, D])
    w_gate = din("w_gate", [D, FFN])
    w_up = din("w_up", [D, FFN])
    w_down = din("w_down", [FFN, D])
    cst_d = din("cst", [128, CL["_n"]])
    rowt_d = din("rowt", [128, 256])
    j2_d = din("j2", [128, 512])
    cmat_d = din("cmat", [128, ML["_n"]])
    outT = nc.dram_tensor("outT", [D, TO], F32, kind="ExternalOutput").ap()

    kcT_d = nc.dram_tensor("kcT_d", [2, 128, T], BF16).ap()
    vcT_d = nc.dram_tensor("vcT_d", [2, 128, T], BF16).ap()
    ksT_d = nc.dram_tensor("ksT_d", [2, 128, T], BF16).ap()
    sbkT_d = nc.dram_tensor("sbkT_d", [4, 128, T], BF16).ap()
    vtok_d = nc.dram_tensor("vtok_d", [T, 768], BF16).ap()
    gates_d = nc.dram_tensor("gates_d", [24, TQ], F32).ap()
    hn_d = nc.dram_tensor("hn_d", [128, KC * TQ], BF16).ap()
    w_in_b = nc.dram_tensor("w_in_b", [D, 10776], BF16).ap()
    wo_nsa_b = nc.dram_tensor("wo_nsa_b", [1024, D], BF16).ap()
    wo_sb_b = nc.dram_tensor("wo_sb_b", [512, D], BF16).ap()
    wo_mem_b = nc.dram_tensor("wo_mem_b", [512, D], BF16).ap()
    w_out_b = nc.dram_tensor("w_out_b", [D, D], BF16).ap()
    w_gate_b = nc.dram_tensor("w_gate_b", [D, FFN], BF16).ap()
    w_up_b = nc.dram_tensor("w_up_b", [D, FFN], BF16).ap()
    w_down_b = nc.dram_tensor("w_down_b", [FFN, D], BF16).ap()

    with ExitStack() as es_all:
        S = Sched(nc, es_all)
        uid = [0]

        def sbt(es, name, shape, dt):
            uid[0] += 1
            return es.enter_context(nc.sbuf_tensor("%s_%d" % (name, uid[0]), list(shape), dt))

        PSB = [es_all.enter_context(nc.psum_tensor("ps%d" % i, [128, 512], F32)) for i in range(8)]
        rot = {"banks": list(range(8)), "i": 0}

        def set_rot(banks):
            rot["banks"] = list(banks)
            rot["i"] = 0

        def nextps():
            b = rot["banks"][rot["i"] % len(rot["banks"])]
            rot["i"] += 1
            return b

        def PS(b):
            return PSB[b]

        def pn(b):
            return "ps%d" % b

        cst = sbt(es_all, "cst", [128, CL["_n"]], F32)
        rowt = sbt(es_all, "rowt", [128, 256], F32)
        j2 = sbt(es_all, "j2", [128, 512], F32)
        cm = sbt(es_all, "cm", [128, ML["_n"]], BF16)
        kcTs = sbt(es_all, "kcTs", [128, 2, NCP], BF16)
        vcs = sbt(es_all, "vcs", [128, 2, NCC, 128], BF16)
        kmem = sbt(es_all, "kmem", [128, 4, 256], BF16)
        vmem = sbt(es_all, "vmem", [128, 2, 512], BF16)

        def ccol(name, i=0):
            return cst[:, CL[name] + i:CL[name] + i + 1]

        def cmv(name, off=0, n=128):
            return cm[:, ML[name] + off:ML[name] + off + n]

        S.dma("sp", "c0a", lambda e: e.dma_start(out=cst[:], in_=cst_d), writes=["cst"])
        S.dma("sp", "c0b", lambda e: e.dma_start(out=rowt[:], in_=rowt_d), writes=["rowt"])
        S.dma("sp", "c0c", lambda e: e.dma_start(out=j2[:], in_=j2_d), writes=["j2"])
        S.dma("pool", "c1", lambda e: e.dma_start(out=cm[:], in_=cmat_d), writes=["cm"])

        class Rope:
            def __init__(self, es, n, tag):
                self.n, self.tag = n, tag
                self.posi = sbt(es, "posi", [128, n], I32)
                self.ang = sbt(es, "ang", [128, n], F32)
                self.t1 = sbt(es, "rt1", [128, n], F32)
                self.kk = sbt(es, "rkk", [128, n], F32)

            def run(self, pos_ap, cosT, sinT, csname):
                n, tag = self.n, self.tag
                posi, ang, t1, kk = self.posi, self.ang, self.t1, self.kk
                S.dma("sp", tag + "pos", lambda e: e.dma_start(out=posi[:], in_=pos_ap.to_broadcast([128, n])), writes=[tag + "posi"])
                S.op("dve", lambda e: e.tensor_copy(out=t1[:], in_=posi[:]), reads=[tag + "posi"], writes=[tag + "t1"])
                S.op("dve", lambda e: e.tensor_scalar(out=ang[:], in0=t1[:], scalar1=ccol("inv"), scalar2=None, op0=ALU.mult),
                     reads=[tag + "t1", "cst"], writes=[tag + "ang"])
                S.op("dve", lambda e: e.tensor_scalar(out=t1[:], in0=ang[:], scalar1=float(1.0 / (2 * np.pi)), scalar2=MAGIC, op0=ALU.mult, op1=ALU.add),
                     reads=[tag + "ang"], writes=[tag + "t1"])
                S.op("dve", lambda e: e.tensor_scalar(out=kk[:], in0=t1[:], scalar1=MAGIC, scalar2=None, op0=ALU.subtract),
                     reads=[tag + "t1"], writes=[tag + "kk"])
                S.op("dve", lambda e: e.scalar_tensor_tensor(out=t1[:], in0=kk[:], scalar=-C1, in1=ang[:], op0=ALU.mult, op1=ALU.add),
                     reads=[tag + "kk", tag + "ang"], writes=[tag + "t1"])
                S.op("dve", lambda e: e.scalar_tensor_tensor(out=ang[:], in0=kk[:], scalar=-C2, in1=t1[:], op0=ALU.mult, op1=ALU.add),
                     reads=[tag + "kk", tag + "t1"], writes=[tag + "ang"])
                S.op("dve", lambda e: e.tensor_scalar(out=ang[:], in0=ang[:], scalar1=PI_LO, scalar2=-PI_LO, op0=ALU.min, op1=ALU.max),
                     reads=[tag + "ang"], writes=[tag + "ang"])
                S.op("dve", lambda e: e.scalar_tensor_tensor(out=t1[:], in0=ang[:], scalar=-1.0, in1=ang[:], op0=ALU.mult, op1=ALU.max),
                     reads=[tag + "ang"], writes=[tag + "t1"])
                S.op("act", lambda e: e.activation(out=sinT, in_=ang[:], func=AF.Sin), reads=[tag + "ang"], writes=[csname + "sin"])
                S.op("act", lambda e: e.activation(out=cosT, in_=t1[:], func=AF.Sin, scale=-1.0, bias=ccol("halfpi")),
                     reads=[tag + "t1", "cst"], writes=[csname + "cos"])

        class RMS:
            def __init__(self, es, n, tag):
                self.n, self.tag = n, tag
                self.xb = [sbt(es, "xch", [128, n], F32) for _ in range(3)]
                self.sq = [sbt(es, "sqc", [128, n], BF16) for _ in range(2)]
                self.rs = sbt(es, "rstd", [128, n], F32)

            def run(self, src_ap, gname, hn, hn_name):
                n, tag, xb, sq, rs = self.n, self.tag, self.xb, self.sq, self.rs
                pb = nextps()
                for c in range(KC):
                    b = xb[c % 3]
                    S.dma("sp", tag + "x%d" % (c % 3), lambda e, b=b, c=c: e.dma_start(out=b[:], in_=src_ap[c * 128:(c + 1) * 128, :]),
                          writes=[tag + "xb%d" % (c % 3)])
                    S.op("act", lambda e, b=b, c=c: e.activation(out=sq[c % 2][:], in_=b[:], func=AF.Square),
                         reads=[tag + "xb%d" % (c % 3)], writes=[tag + "sq%d" % (c % 2)])
                    S.op("pe", lambda e, c=c: e.matmul(PS(pb)[:, 0:n], lhsT=cmv("ones"), rhs=sq[c % 2][:], start=(c == 0), stop=(c == KC - 1)),
                         reads=[tag + "sq%d" % (c % 2), "cm"], writes=[pn(pb)])
                S.op("act", lambda e: e.activation(out=rs[:], in_=PS(pb)[:, 0:n], func=AF.Sqrt, scale=1.0 / D, bias=ccol("eps")),
                     reads=[pn(pb), "cst"], writes=[tag + "rs"])
                S.op("dve", lambda e: e.reciprocal(out=rs[:], in_=rs[:]), reads=[tag + "rs"], writes=[tag + "rs"])
                for c in range(KC):
                    b = xb[c % 3]
                    S.dma("sp", tag + "x%d" % (c % 3), lambda e, b=b, c=c: e.dma_start(out=b[:], in_=src_ap[c * 128:(c + 1) * 128, :]),
                          writes=[tag + "xb%d" % (c % 3)])
                    S.op("dve", lambda e, b=b, c=c: e.scalar_tensor_tensor(out=hn[:, c, :], in0=b[:], scalar=ccol(gname, c), in1=rs[:],
                                                                          op0=ALU.mult, op1=ALU.mult),
                         reads=[tag + "xb%d" % (c % 3), tag + "rs", "cst"], writes=[hn_name])

        class NR:
            def __init__(self, es, n, tag):
                self.n = n
                self.tag = tag
                self.sq = sbt(es, "nrsq", [128, n], BF16)
                self.r = sbt(es, "nrr", [128, n], F32)
                self.kb = sbt(es, "nrkb", [128, n], BF16)
                self.t1 = sbt(es, "nrt1", [128, n], F32)
                self.t2 = sbt(es, "nrt2", [128, n], F32)

            def run(self, pin, gname, out_ap, out_name, cosT=None, sinT=None, csname=None):
                n, tag = self.n, self.tag
                p2 = nextps()
                S.op("act", lambda e: e.activation(out=self.sq[:], in_=PS(pin)[:, 0:n], func=AF.Square), reads=[pn(pin)], writes=[tag + "sq"])
                S.op("pe", lambda e: e.matmul(PS(p2)[:, 0:n], lhsT=cmv("ones"), rhs=self.sq[:], start=True, stop=True),
                     reads=[tag + "sq", "cm"], writes=[pn(p2)])
                S.op("act", lambda e: e.activation(out=self.r[:], in_=PS(p2)[:, 0:n], func=AF.Sqrt, scale=1.0 / 128, bias=ccol("eps")),
                     reads=[pn(p2), "cst"], writes=[tag + "r"])
                S.op("dve", lambda e: e.reciprocal(out=self.r[:], in_=self.r[:]), reads=[tag + "r"], writes=[tag + "r"])
                if cosT is None:
                    S.op("dve", lambda e: e.scalar_tensor_tensor(out=out_ap, in0=PS(pin)[:, 0:n], scalar=ccol(gname), in1=self.r[:], op0=ALU.mult, op1=ALU.mult),
                         reads=[pn(pin), tag + "r", "cst"], writes=[out_name])
                    return
                S.op("dve", lambda e: e.scalar_tensor_tensor(out=self.kb[:], in0=PS(pin)[:, 0:n], scalar=ccol(gname), in1=self.r[:], op0=ALU.mult, op1=ALU.mult),
                     reads=[pn(pin), tag + "r", "cst"], writes=[tag + "kb"])
                p3 = nextps()
                S.op("pe", lambda e: e.matmul(PS(p3)[:, 0:n], lhsT=cmv("prot"), rhs=self.kb[:], start=True, stop=True),
                     reads=[tag + "kb", "cm"], writes=[pn(p3)])
                S.op("dve", lambda e: e.tensor_tensor(out=self.t1[:], in0=self.kb[:], in1=cosT, op=ALU.mult),
                     reads=[tag + "kb", csname + "cos"], writes=[tag + "t1"])
                S.op("dve", lambda e: e.tensor_tensor(out=self.t2[:], in0=PS(p3)[:, 0:n], in1=sinT, op=ALU.mult),
                     reads=[pn(p3), csname + "sin"], writes=[tag + "t2"])
                S.op("dve", lambda e: e.tensor_tensor(out=out_ap, in0=self.t1[:], in1=self.t2[:], op=ALU.add),
                     reads=[tag + "t1", tag + "t2"], writes=[out_name])

        class WStream:
            def __init__(self, es, tag, nbuf=2, size=5632, cast=False):
                self.bufs = [sbt(es, "wbuf", [128, size], BF16) for _ in range(nbuf)]
                self.tag = tag
                self.i = 0
                self.size = size
                self.cast = cast

            def load(self, w_ap, K, col0, gc):
                kc = K // 128
                assert kc * gc <= self.size
                i = self.i % len(self.bufs)
                self.i += 1
                view = self.bufs[i][:, 0:kc * gc].rearrange("p (c n) -> p c n", c=kc)
                name = self.tag + "w%d" % i
                S.dma("pool" if self.cast else "sp", name, lambda e: e.dma_start(out=view, in_=w_ap[:, col0:col0 + gc].rearrange("(c p) n -> p c n", p=128)),
                      reads=[] if self.cast else ["wb16"], writes=[name])
                return view, name

        def proj_fm(ws, w_ap, K, col0, ncols, rhs_fn, rhs_names, n, evac, gcmax=None):
            kc = K // 128
            gc_full = min(ncols, (ws.size // kc) // 128 * 128)
            if gcmax:
                gc_full = min(gc_full, gcmax)
            j = 0
            g0 = 0
            while g0 < ncols:
                gc = min(gc_full, ncols - g0)
                view, wname = ws.load(w_ap, K, col0 + g0, gc)
                for jj in range((gc + 127) // 128):
                    m = min(128, gc - jj * 128)
                    pb = nextps()
                    for c in range(kc):
                        S.op("pe", lambda e, c=c, jj=jj, pb=pb, view=view, m=m: e.matmul(PS(pb)[0:m, 0:n], lhsT=view[:, c, jj * 128:jj * 128 + m], rhs=rhs_fn(c),
                                                                                start=(c == 0), stop=(c == kc - 1)),
                             reads=[wname] + rhs_names, writes=[pn(pb)], sig=(c == kc - 1))
                    evac(j, pb)
                    j += 1
                g0 += gc

        def proj_tm(ws, w_ap, K, col0, ncols, lhs_fn, lhs_names, nsub, evac):
            kc = K // 128
            gc_full = min(ncols, (ws.size // kc) // 128 * 128, 512)
            g0 = 0
            while g0 < ncols:
                gc = min(gc_full, ncols - g0)
                view, wname = ws.load(w_ap, K, col0 + g0, gc)
                for ts in range(nsub):
                    pb = nextps()
                    for c in range(kc):
                        S.op("pe", lambda e, c=c, ts=ts, pb=pb, view=view, gc=gc: e.matmul(PS(pb)[:, 0:gc], lhsT=lhs_fn(c, ts), rhs=view[:, c, 0:gc],
                                                                                  start=(c == 0), stop=(c == kc - 1)),
                             reads=[wname] + lhs_names, writes=[pn(pb)], sig=(c == kc - 1))
                    evac(ts, g0, gc, pb)
                g0 += gc

        cpy_i = [0]

        def copy_out(out_ap, out_name, in_ap, in_names):
            cpy_i[0] += 1
            if cpy_i[0] % 2:
                S.op("act", lambda e: e.activation(out=out_ap, in_=in_ap, func=AF.Copy), reads=in_names, writes=[out_name])
            else:
                S.op("dve", lambda e: e.tensor_copy(out=out_ap, in_=in_ap), reads=in_names, writes=[out_name])

        def stage_end():
            S.barrier()
            S.flush()

        with ExitStack() as es:
            set_rot(range(8))
            wkv = sbt(es, "wkv", [128, KC, 2048], BF16)
            for (c0, n, d0) in [(1024, 768, 0), (3096, 512, 768), (1792, 256, 1280), (3608, 512, 1536)]:
                for half in range(2):
                    S.dma("pool", "wkv", lambda e, c0=c0, n=n, d0=d0, half=half: e.dma_start(
                        out=wkv[:, half * 8:(half + 1) * 8, d0:d0 + n],
                        in_=w_in[half * 1024:(half + 1) * 1024, c0:c0 + n].rearrange("(c p) n -> p c n", p=128)), writes=["wkv"])
            for (src_, dst_, rows_) in [(w_in, w_in_b, D), (wo_nsa, wo_nsa_b, 1024), (wo_sb, wo_sb_b, 512), (wo_mem, wo_mem_b, 512),
                                        (w_out, w_out_b, D), (w_gate, w_gate_b, D), (w_up, w_up_b, D), (w_down, w_down_b, FFN)]:
                for r0 in range(0, rows_, 256):
                    S.dma("pool", "wcast", lambda e, src_=src_, dst_=dst_, r0=r0: e.dma_start(out=dst_[r0:r0 + 256, :], in_=src_[r0:r0 + 256, :]), writes=["wb16"])
            hnb = [sbt(es, "hn1", [128, KC, TQ], BF16) for _ in range(2)]
            cosTb = [sbt(es, "cosT", [128, TQ], F32) for _ in range(2)]
            sinTb = [sbt(es, "sinT", [128, TQ], F32) for _ in range(2)]
            nr = NR(es, TQ, "p1nr")
            rms = RMS(es, TQ, "p1rms")
            rope = Rope(es, TQ, "p1rope")
            stg = [sbt(es, "stg", [128, TQ], BF16) for _ in range(3)]
            vst = [sbt(es, "vst", [128, 768], BF16) for _ in range(2)]
            si = 0

            def p1_pre(gt_):
                rms.run(xT_all[:, gt_ * TQ:(gt_ + 1) * TQ], "attn", hnb[gt_ % 2], "hn1_%d" % (gt_ % 2))
                rope.run(pos_all[0:1, gt_ * TQ:(gt_ + 1) * TQ], cosTb[gt_ % 2][:], sinTb[gt_ % 2][:], "p1cs%d" % (gt_ % 2))
            p1_pre(0)
            for gt in range(NGT):
                hn = hnb[gt % 2]
                hname = "hn1_%d" % (gt % 2)
                cosT, sinT, csn = cosTb[gt % 2], sinTb[gt % 2], "p1cs%d" % (gt % 2)
                t0 = gt * TQ
                if gt + 1 < NGT:
                    p1_pre(gt + 1)
                for j in range(10):
                    pb = nextps()
                    for c in range(KC):
                        S.op("pe", lambda e, c=c, j=j, pb=pb, hn=hn: e.matmul(PS(pb)[:, :], lhsT=wkv[:, c, j * 128:(j + 1) * 128], rhs=hn[:, c, :],
                                                                          start=(c == 0), stop=(c == KC - 1)),
                             reads=["wkv", hname], writes=[pn(pb)], sig=(c == KC - 1))
                    sb_ = stg[si % 3]
                    sname = "stg%d" % (si % 3)
                    si += 1
                    if j in (4, 5):
                        nr.run(pb, "ksn", sb_[:], sname, cosT[:], sinT[:], csn)
                        dst = ksT_d[j - 4, :, t0:t0 + TQ]
                    else:
                        copy_out(sb_[:], sname, PS(pb)[:, :], [pn(pb)])
                        if j < 2:
                            dst = kcT_d[j, :, t0:t0 + TQ]
                        elif j < 4:
                            dst = vcT_d[j - 2, :, t0:t0 + TQ]
                        else:
                            dst = sbkT_d[j - 6, :, t0:t0 + TQ]
                    S.dma("sp", "p1st_" + sname, lambda e, dst=dst, sb_=sb_: e.dma_start(out=dst, in_=sb_[:]), reads=[sname], writes=["kvdram"])
                for ts in range(4):
                    vs_ = vst[ts % 2]
                    vname = "vst%d" % (ts % 2)
                    for (c0, n) in [(1280, 256), (1536, 512)]:
                        pb = nextps()
                        for c in range(KC):
                            S.op("pe", lambda e, c=c, pb=pb, hn=hn, ts=ts, c0=c0, n=n: e.matmul(PS(pb)[:, 0:n], lhsT=hn[:, c, ts * 128:(ts + 1) * 128],
                                                                                         rhs=wkv[:, c, c0:c0 + n], start=(c == 0), stop=(c == KC - 1)),
                                 reads=["wkv", hname], writes=[pn(pb)], sig=(c == KC - 1))
                        copy_out(vs_[:, c0 - 1280:c0 - 1280 + n], vname, PS(pb)[:, 0:n], [pn(pb)])
                    S.dma("sp", "p1sv_" + vname, lambda e, vs_=vs_, ts=ts, t0=t0: e.dma_start(out=vtok_d[t0 + ts * 128:t0 + (ts + 1) * 128, :], in_=vs_[:]),
                          reads=[vname], writes=["kvdram"])
            stage_end()

        with ExitStack() as es:
            set_rot(range(8))
            kcs = sbt(es, "kcs", [128, T + 16], BF16)
            w1b = sbt(es, "w1b", [128, 32, 256], BF16)
            w2b = sbt(es, "w2b", [128, 2, 128], BF16)
            peb = sbt(es, "peb", [128, 32], BF16)
            pebias = sbt(es, "pebias", [128, 2], F32)
            hf = sbt(es, "hf", [128, NB], F32)
            h2 = sbt(es, "h2", [128, NB], F32)
            sg = sbt(es, "sg", [128, NB], F32)
            hid = sbt(es, "hid", [128, 2, NB], BF16)
            cosC = sbt(es, "cosC", [128, NCP], F32)
            sinC = sbt(es, "sinC", [128, NCP], F32)
            nrc = NR(es, NB, "cnr")
            ropec = Rope(es, NCP, "crope")
            ropec.run(pos_cmp[0:1, :], cosC[:], sinC[:], "ccs")
            S.op("pool", lambda e: e.memset(kcs[:, T:T + 16], 0.0), writes=["kcs_tail"])
            for kv in range(2):
                w1d, w2d, pename = (w1k, w2k, "pek") if kv == 0 else (w1v, w2v, "pev")
                S.dma("pool", "w1b", lambda e, w1d=w1d: e.dma_start(out=w1b[:], in_=w1d.rearrange("l d f -> d l f")), writes=["w1b"])
                S.dma("pool", "w2b", lambda e, w2d=w2d: e.dma_start(out=w2b[:], in_=w2d.rearrange("(c p) d -> p c d", p=128)), writes=["w2b"])
                S.op("dve", lambda e, pename=pename: e.tensor_copy(out=peb[:], in_=cst[:, CL[pename]:CL[pename] + 32]), reads=["cst"], writes=["peb"])
                for fc in range(2):
                    pb = nextps()
                    for l in range(32):
                        S.op("pe", lambda e, l=l, fc=fc, pb=pb: e.matmul(PS(pb)[:, 0:1], lhsT=w1b[:, l, fc * 128:(fc + 1) * 128], rhs=peb[:, l:l + 1],
                                                                     start=(l == 0), stop=(l == 31)), reads=["w1b", "peb"], writes=[pn(pb)])
                    S.op("dve", lambda e, fc=fc, pb=pb: e.tensor_copy(out=pebias[:, fc:fc + 1], in_=PS(pb)[:, 0:1]), reads=[pn(pb)], writes=["pebias"])
                for g in range(2):
                    src = kcT_d if kv == 0 else vcT_d
                    S.dma("sp", "kcs", lambda e, src=src, g=g: e.dma_start(out=kcs[:, 0:T], in_=src[g, :, :]), reads=["kvdram"], writes=["kcs"])
                    for nt in range(NCP // NB):
                        n0 = nt * NB
                        for fc in range(2):
                            pb = nextps()
                            for l in range(32):
                                a0 = 16 * n0 + l
                                S.op("pe", lambda e, l=l, fc=fc, pb=pb, a0=a0: e.matmul(PS(pb)[:, 0:NB], lhsT=w1b[:, l, fc * 128:(fc + 1) * 128],
                                                                                 rhs=kcs[:, a0:a0 + 16 * (NB - 1) + 1:16], start=(l == 0), stop=(l == 31)),
                                     reads=["w1b", "kcs", "kcs_tail"], writes=[pn(pb)])
                            S.op("act", lambda e, pb=pb, fc=fc: e.activation(out=hf[:], in_=PS(pb)[:, 0:NB], func=AF.Identity, bias=pebias[:, fc:fc + 1]),
                                 reads=[pn(pb), "pebias"], writes=["hf"])
                            S.op("dve", lambda e: e.tensor_tensor(out=h2[:], in0=hf[:], in1=hf[:], op=ALU.mult), reads=["hf"], writes=["h2"])
                            S.op("dve", lambda e: e.tensor_scalar(out=h2[:], in0=h2[:], scalar1=0.044715, scalar2=1.0, op0=ALU.mult, op1=ALU.add),
                                 reads=["h2"], writes=["h2"])
                            S.op("dve", lambda e: e.tensor_tensor(out=h2[:], in0=h2[:], in1=hf[:], op=ALU.mult), reads=["h2", "hf"], writes=["h2"])
                            S.op("act", lambda e: e.activation(out=sg[:], in_=h2[:], func=AF.Sigmoid, scale=float(2.0 * math.sqrt(2.0 / math.pi))),
                                 reads=["h2"], writes=["sg"])
                            S.op("dve", lambda e, fc=fc: e.tensor_tensor(out=hid[:, fc, :], in0=hf[:], in1=sg[:], op=ALU.mult), reads=["hf", "sg"], writes=["hid"])
                        if kv == 0:
                            pb = nextps()
                            for fc in range(2):
                                S.op("pe", lambda e, fc=fc, pb=pb: e.matmul(PS(pb)[:, 0:NB], lhsT=w2b[:, fc, :], rhs=hid[:, fc, :], start=(fc == 0), stop=(fc == 1)),
                                     reads=["w2b", "hid"], writes=[pn(pb)])
                            nrc.run(pb, "kcn", kcTs[:, g, n0:n0 + NB], "kcTs", cosC[:, n0:n0 + NB], sinC[:, n0:n0 + NB], "ccs")
                        else:
                            for ns in range(NB // 128):
                                pb = nextps()
                                for fc in range(2):
                                    S.op("pe", lambda e, fc=fc, pb=pb, ns=ns: e.matmul(PS(pb)[:, 0:128], lhsT=hid[:, fc, ns * 128:(ns + 1) * 128], rhs=w2b[:, fc, :],
                                                                                  start=(fc == 0), stop=(fc == 1)), reads=["w2b", "hid"], writes=[pn(pb)])
                                copy_out(vcs[:, g, n0 // 128 + ns, :], "vcs", PS(pb)[:, 0:128], [pn(pb)])
            stage_end()

        with ExitStack() as es:
            set_rot(range(8))
            hm = sbt(es, "hm", [128, KC, 256], BF16)
            rmsm = RMS(es, 256, "mrms")
            nrm = NR(es, 256, "mnr")
            wsm = WStream(es, "wsm", 2, 4096, cast=True)
            rmsm.run(memT, "memn", hm, "hm")

            def ev_k(j, pb):
                nrm.run(pb, "mkn", kmem[:, j, :], "kmem")
            proj_fm(wsm, w_mem, D, 0, 512, lambda c: hm[:, c, :], ["hm"], 256, ev_k, gcmax=256)

            def ev_v(ts, g0, gc, pb):
                copy_out(vmem[:, ts, g0:g0 + gc], "vmem", PS(pb)[:, 0:gc], [pn(pb)])
            proj_tm(wsm, w_mem, D, 512, 512, lambda c, ts: hm[:, c, ts * 128:(ts + 1) * 128], ["hm"], 2, ev_v)
            stage_end()

        with ExitStack() as es_p2:
            ynsa_b = sbt(es_p2, "ynsa_b", [128, 8, TQ], BF16)
            ysb = sbt(es_p2, "ysb", [128, 4, TQ], BF16)
            ymem = sbt(es_p2, "ymem", [128, 4, TQ], BF16)
            for k in range(NT):
                o0 = k * TQ
                with ExitStack() as es_a:
                    qn = sbt(es_a, "qn", [128, 8, TQ], BF16)
                    qs = sbt(es_a, "qs", [128, 4, TQ], BF16)
                    qm = sbt(es_a, "qm", [128, 4, TQ], BF16)
                    ksTo = sbt(es_a, "ksTo", [128, 2, TQ], BF16)
                    kwTo = sbt(es_a, "kwTo", [128, 2, TQ], BF16)
                    kwTp = sbt(es_a, "kwTp", [128, 2, TQ], BF16)
                    sbkTo = sbt(es_a, "sbkTo", [128, 4, TQ], BF16)
                    vown = sbt(es_a, "vown", [128, 4, 1024], BF16)
                    vprev = sbt(es_a, "vprev", [128, 4, 256], BF16)
                    with ExitStack() as es:
                        set_rot(range(8))
                        hp = sbt(es, "hp", [128, KC, TQ], BF16)
                        hn = sbt(es, "hn", [128, KC, TQ], BF16)
                        cosO = sbt(es, "cosO", [128, TQ], F32)
                        sinO = sbt(es, "sinO", [128, TQ], F32)
                        cosP = sbt(es, "cosP", [128, TQ], F32)
                        sinP = sbt(es, "sinP", [128, TQ], F32)
                        g32 = sbt(es, "g32", [24, TQ], F32)
                        rms2 = RMS(es, TQ, "a1rms")
                        rope2 = Rope(es, TQ, "a1rope")
                        nr2 = NR(es, TQ, "a1nr")
                        ws = WStream(es, "a1ws", 3, 4096)
                        rms2.run(xT_own[:, o0:o0 + TQ], "attn", hn, "hn")
                        rms2.run(xT_prev[:, o0:o0 + TQ], "attn", hp, "hp")
                        rope2.run(pos_own[0:1, o0:o0 + TQ], cosO[:], sinO[:], "cso")
                        rope2.run(pos_prev[0:1, o0:o0 + TQ], cosP[:], sinP[:], "csp")
                        rh = lambda c: hn[:, c, :]
                        rp = lambda c: hp[:, c, :]
                        proj_fm(ws, w_in_b, D, 0, 1024, rh, ["hn"], TQ, lambda j, pb: nr2.run(pb, "qn", qn[:, j, :], "qn", cosO[:], sinO[:], "cso"), gcmax=256)
                        proj_fm(ws, w_in_b, D, 1536, 256, rh, ["hn"], TQ, lambda j, pb: nr2.run(pb, "ksn", ksTo[:, j, :], "ksTo", cosO[:], sinO[:], "cso"), gcmax=256)
                        proj_fm(ws, w_in_b, D, 2048, 256, rh, ["hn"], TQ, lambda j, pb: nr2.run(pb, "kwn", kwTo[:, j, :], "kwTo", cosO[:], sinO[:], "cso"), gcmax=256)
                        proj_fm(ws, w_in_b, D, 2048, 256, rp, ["hp"], TQ, lambda j, pb: nr2.run(pb, "kwn", kwTp[:, j, :], "kwTp", cosP[:], sinP[:], "csp"), gcmax=256)
                        proj_fm(ws, w_in_b, D, 2584, 512, rh, ["hn"], TQ, lambda j, pb: copy_out(qs[:, j, :], "qs", PS(pb)[:, :], [pn(pb)]), gcmax=256)
                        proj_fm(ws, w_in_b, D, 3096, 512, rh, ["hn"], TQ, lambda j, pb: copy_out(sbkTo[:, j, :], "sbkTo", PS(pb)[:, :], [pn(pb)]), gcmax=256)
                        proj_fm(ws, w_in_b, D, 4120, 512, rh, ["hn"], TQ, lambda j, pb: nr2.run(pb, "mqn", qm[:, j, :], "qm"), gcmax=256)

                        def ev_g(j, pb):
                            S.op("act", lambda e: e.activation(out=g32[:], in_=PS(pb)[0:24, :], func=AF.Sigmoid), reads=[pn(pb)], writes=["g32"])
                            S.dma("sp", "gst", lambda e: e.dma_start(out=gates_d, in_=g32[:]), reads=["g32"], writes=["gates_d"])
                        proj_fm(ws, w_in_b, D, 2560, 24, rh, ["hn"], TQ, ev_g)
                        lh = lambda c, ts: hn[:, c, ts * 128:(ts + 1) * 128]
                        lp = lambda c, ts: hp[:, c, ts * 128:(ts + 1) * 128]
                        proj_tm(ws, w_in_b, D, 1792, 256, lh, ["hn"], 4, lambda ts, g0, gc, pb: copy_out(vown[:, ts, 0:256], "vown", PS(pb)[:, 0:256], [pn(pb)]))
                        proj_tm(ws, w_in_b, D, 2304, 256, lh, ["hn"], 4, lambda ts, g0, gc, pb: copy_out(vown[:, ts, 256:512], "vown", PS(pb)[:, 0:256], [pn(pb)]))
                        proj_tm(ws, w_in_b, D, 3608, 512, lh, ["hn"], 4, lambda ts, g0, gc, pb: copy_out(vown[:, ts, 512 + g0:512 + g0 + gc], "vown", PS(pb)[:, 0:gc], [pn(pb)]))
                        proj_tm(ws, w_in_b, D, 2304, 256, lp, ["hp"], 4, lambda ts, g0, gc, pb: copy_out(vprev[:, ts, :], "vprev", PS(pb)[:, 0:256], [pn(pb)]))
                        S.dma("sp", "hnst", lambda e: e.dma_start(out=hn_d, in_=hn[:].rearrange("p c t -> p (c t)")), reads=["hn"], writes=["hn_d"])
                        stage_end()

                    with ExitStack() as es:
                        O_B, SUM_B, U0_B, U1_B = 0, 1, 2, 3
                        set_rot([4, 5, 6, 7])
                        ynsa = sbt(es, "ynsa", [128, 4, TQ], F32)
                        impacc = sbt(es, "impacc", [128, 2, 4, 256], F32)
                        selbT = sbt(es, "selbT", [128, 2, 2, TQ], BF16)
                        ownb = sbt(es, "ownb", [8, 2, TQ], BF16)
                        gbc = [sbt(es, "gbc", [128, 3, TQ], F32) for _ in range(1)]
                        Pb = [sbt(es, "Pb", [128, TQ], BF16) for _ in range(3)]
                        rec = sbt(es, "rec", [128, TQ], F32)
                        tmpf = sbt(es, "tmpf", [128, TQ], F32)
                        kbuf = [sbt(es, "kbuf", [128, TQ], BF16) for _ in range(3)]
                        vbuf = [sbt(es, "vbuf", [128, 4, 128], BF16) for _ in range(3)]
                        cmask = sbt(es, "cmask", [128, NCC, TQ], BF16)
                        rtok = sbt(es, "rtok", [128, 4], F32)
                        tk1 = sbt(es, "tk1", [128, 256], F32)
                        tk2 = sbt(es, "tk2", [128, 256], F32)
                        tk3 = sbt(es, "tk3", [128, 256], F32)
                        tk4 = sbt(es, "tk4", [128, 256], F32)
                        m8 = sbt(es, "m8", [128, 16], F32)
                        alb = sbt(es, "alb", [128, 256], BF16)
                        alT = sbt(es, "alT", [128, 2, TQ], BF16)
                        e32 = [sbt(es, "e32", [128, TQ], F32) for _ in range(2)]
                        ec32 = [sbt(es, "ec32", [128, TQ], F32) for _ in range(2)]
                        Lp = [sbt(es, "Lp", [128, TQ], BF16) for _ in range(3)]
                        Wb = [sbt(es, "Wb", [128, TQ], BF16) for _ in range(2)]
                        Rb = [sbt(es, "Rb", [128, TQ], BF16) for _ in range(2)]
                        ncmp = min(NCC, 2 * (k + 1))
                        gi = [0]

                        def load_gates(h):
                            b = gbc[0]
                            name = "gbc0"
                            gi[0] += 1
                            S.dma("sp", name, lambda e: e.dma_start(out=b[:], in_=gates_d[3 * h:3 * h + 3, :].rearrange("(o r) t -> o r t", o=1).to_broadcast([128, 3, TQ])),
                                  reads=["gates_d"], writes=[name])
                            return b, name

                        for j in range(ncmp):
                            S.op("dve", lambda e, j=j: e.tensor_scalar(out=cmask[:, j, :], in0=j2[:], scalar1=ccol("cthr", k * NCC + j), scalar2=None, op0=ALU.is_gt),
                                 reads=["j2", "cst"], writes=["cmask"])

                        pi = [0]

                        pend = [None]

                        def flush_pending():
                            if pend[0] is None:
                                return
                            ch, P, pname, i, n, extra = pend[0]
                            pend[0] = None
                            S.op("pe", lambda e: e.matmul(PS(O_B)[:, :], lhsT=ch["v"], rhs=P[:], start=(i == 0), stop=(i == n - 1)),
                                 reads=ch["vn"] + [pname], writes=[pn(O_B)])
                            S.op("pe", lambda e: e.matmul(PS(SUM_B)[:, :], lhsT=cmv("ones"), rhs=P[:], start=(i == 0), stop=(i == n - 1)),
                                 reads=["cm", pname], writes=[pn(SUM_B)])
                            if extra:
                                extra(i, P, pname, n)

                        def softmax_chunks(chunks, q_ap, q_names, extra=None, i0=0, n_total=None):
                            n = len(chunks) if n_total is None else n_total
                            for i_, ch in enumerate(chunks):
                                i = i0 + i_
                                sb_ = nextps()
                                masks = ch.get("masks", [])
                                S.op("pe", lambda e: e.matmul(PS(sb_)[:, :], lhsT=ch["kT"], rhs=q_ap, start=True, stop=(len(masks) == 0)),
                                     reads=ch["kn"] + q_names, writes=[pn(sb_)])
                                for mi, (ml, mr, mn) in enumerate(masks):
                                    S.op("pe", lambda e: e.matmul(PS(sb_)[:, :], lhsT=ml, rhs=mr, start=False, stop=(mi == len(masks) - 1)),
                                         reads=mn + ["cm"], writes=[pn(sb_)])
                                P = Pb[pi[0] % 3]
                                pname = "Pb%d" % (pi[0] % 3)
                                pi[0] += 1
                                bias = ch.get("bias")
                                if bias is None:
                                    bias = ccol("zero")
                                S.op("act", lambda e: e.activation(out=P[:], in_=PS(sb_)[:, :], func=AF.Exp, scale=SCALE, bias=bias),
                                     reads=[pn(sb_), "cst"], writes=[pname])
                                flush_pending()
                                pend[0] = (ch, P, pname, i, n, extra)

                        def finalize(out_ap, out_name, gate_ap=None, gate_name=None, accumulate=False):
                            flush_pending()
                            S.op("dve", lambda e: e.tensor_scalar(out=rec[:], in0=PS(SUM_B)[:, :], scalar1=1e-30, scalar2=None, op0=ALU.max), reads=[pn(SUM_B)], writes=["rec"])
                            S.op("dve", lambda e: e.reciprocal(out=rec[:], in_=rec[:]), reads=["rec"], writes=["rec"])
                            if gate_ap is not None:
                                S.op("dve", lambda e: e.tensor_tensor(out=rec[:], in0=rec[:], in1=gate_ap, op=ALU.mult), reads=["rec", gate_name], writes=["rec"])
                            if accumulate:
                                S.op("dve", lambda e: e.tensor_tensor(out=tmpf[:], in0=PS(O_B)[:, :], in1=rec[:], op=ALU.mult), reads=[pn(O_B), "rec"], writes=["tmpf"])
                                S.op("dve", lambda e: e.tensor_tensor(out=out_ap, in0=out_ap, in1=tmpf[:], op=ALU.add), reads=["tmpf", out_name], writes=[out_name])
                            else:
                                S.op("dve", lambda e: e.tensor_tensor(out=out_ap, in0=PS(O_B)[:, :], in1=rec[:], op=ALU.mult), reads=[pn(O_B), "rec"], writes=[out_name])

                        si2 = [0]

                        def stream_tile(kT_src, v_c0, gt):
                            i = si2[0] % 3
                            si2[0] += 1
                            kb, vb = kbuf[i], vbuf[i]
                            S.dma("sp", "kb%d" % i, lambda e: e.dma_start(out=kb[:], in_=kT_src[:, gt * TQ:(gt + 1) * TQ]), reads=["kvdram"], writes=["kbuf%d" % i])
                            S.dma("sp", "vb%d" % i, lambda e: e.dma_start(out=vb[:], in_=vtok_d[gt * TQ:(gt + 1) * TQ, v_c0:v_c0 + 128].rearrange("(ts p) d -> p ts d", p=128)),
                                  reads=["kvdram"], writes=["vbuf%d" % i])
                            return kb, vb, "kbuf%d" % i, "vbuf%d" % i

                        for g in range(2):
                            set_rot([6, 7])
                            for hg in range(4):
                                h = g * 4 + hg
                                gb, gname = load_gates(h)
                                chunks = []
                                for j in range(ncmp):
                                    chunks.append(dict(kT=kcTs[:, g, j * 128:(j + 1) * 128], kn=["kcTs"], v=vcs[:, g, j, :], vn=["vcs"],
                                                       masks=[(cmv("negI"), cmask[:, j, :], ["cmask"])]))

                                def extra(i, P, pname, n):
                                    for ts in range(4):
                                        ub = 2 + ts
                                        o_ = 0
                                        S.op("pe", lambda e, P=P, ts=ts, ub=ub, o_=o_, i=i, n=n: e.matmul(PS(ub)[:, o_:o_ + 257], lhsT=P[:, ts * 128:(ts + 1) * 128],
                                                                                                    rhs=cm[:, ML["OV"] + i * 257:ML["OV"] + (i + 1) * 257],
                                                                                                    start=(i == 0), stop=(i == n - 1)),
                                             reads=[pname, "cm"], writes=[pn(ub)])
                                softmax_chunks(chunks, qn[:, h, :], ["qn"], extra)
                                finalize(ynsa[:, hg, :], "ynsa", gb[:, 0, :], gname)
                                for ts in range(4):
                                    ub = 2 + ts
                                    o_ = 0
                                    S.op("dve", lambda e, ts=ts, ub=ub, o_=o_: e.tensor_scalar(out=rtok[:, ts:ts + 1], in0=PS(ub)[:, o_ + 256:o_ + 257], scalar1=1e-30, scalar2=None, op0=ALU.max),
                                         reads=[pn(ub)], writes=["rtok"])
                                    S.op("dve", lambda e, ts=ts: e.reciprocal(out=rtok[:, ts:ts + 1], in_=rtok[:, ts:ts + 1]), reads=["rtok"], writes=["rtok"])
                                    if hg == 0:
                                        S.op("dve", lambda e, ts=ts, ub=ub, o_=o_: e.tensor_scalar(out=impacc[:, g, ts, :], in0=PS(ub)[:, o_:o_ + 256], scalar1=rtok[:, ts:ts + 1], scalar2=None, op0=ALU.mult),
                                             reads=[pn(ub), "rtok"], writes=["impacc"])
                                    else:
                                        S.op("dve", lambda e, ts=ts, ub=ub, o_=o_: e.scalar_tensor_tensor(out=impacc[:, g, ts, :], in0=PS(ub)[:, o_:o_ + 256], scalar=rtok[:, ts:ts + 1],
                                                                                                       in1=impacc[:, g, ts, :], op0=ALU.mult, op1=ALU.add),
                                             reads=[pn(ub), "rtok", "impacc"], writes=["impacc"])
                            set_rot([2, 3, 4, 5, 6, 7])
                            for ts in range(4):
                                curc = ccol("cur", k * 4 + ts)
                                blk = rowt[:, 0:256]
                                S.op("dve", lambda e, curc=curc: e.tensor_scalar(out=tk1[:], in0=blk, scalar1=curc, scalar2=None, op0=ALU.subtract), reads=["rowt", "cst"], writes=["tk1"])
                                S.op("dve", lambda e: e.tensor_scalar(out=tk2[:], in0=tk1[:], scalar1=0.0, scalar2=None, op0=ALU.is_gt), reads=["tk1"], writes=["tk2"])
                                S.op("dve", lambda e: e.tensor_scalar(out=tk3[:], in0=tk1[:], scalar1=-1.0, scalar2=None, op0=ALU.is_ge), reads=["tk1"], writes=["tk3"])
                                S.op("dve", lambda e: e.tensor_tensor(out=tk3[:], in0=tk3[:], in1=tk2[:], op=ALU.subtract), reads=["tk3", "tk2"], writes=["tk3"])
                                S.op("dve", lambda e: e.tensor_scalar(out=tk4[:], in0=blk, scalar1=0.0, scalar2=None, op0=ALU.is_equal), reads=["rowt"], writes=["tk4"])
                                S.op("dve", lambda e: e.tensor_tensor(out=tk3[:], in0=tk3[:], in1=tk4[:], op=ALU.max), reads=["tk3", "tk4"], writes=["tk3"])
                                S.op("dve", lambda e: e.tensor_tensor(out=tk3[:], in0=tk3[:], in1=tk2[:], op=ALU.subtract), reads=["tk3", "tk2"], writes=["tk3"])
                                S.op("dve", lambda e, ts=ts: e.scalar_tensor_tensor(out=tk1[:], in0=tk3[:], scalar=1.0e4, in1=impacc[:, g, ts, :], op0=ALU.mult, op1=ALU.add),
                                     reads=["tk3", "impacc"], writes=["tk1"])
                                S.op("dve", lambda e: e.max(out=m8[:, 0:8], in_=tk1[:]), reads=["tk1"], writes=["m8a"])
                                S.op("dve", lambda e: e.match_replace(out=tk4[:], in_to_replace=m8[:, 0:8], in_values=tk1[:], imm_value=-3.0e4), reads=["tk1", "m8a"], writes=["tk4"])
                                S.op("dve", lambda e: e.max(out=m8[:, 8:16], in_=tk4[:]), reads=["tk4"], writes=["m8b"])
                                S.op("dve", lambda e: e.tensor_scalar(out=tk4[:], in0=tk1[:], scalar1=m8[:, 15:16], scalar2=None, op0=ALU.is_ge), reads=["tk1", "m8b"], writes=["tk4"])
                                S.op("dve", lambda e: e.tensor_scalar(out=tk2[:], in0=tk2[:], scalar1=-1.0, scalar2=1.0, op0=ALU.mult, op1=ALU.add), reads=["tk2"], writes=["tk2"])
                                S.op("dve", lambda e: e.tensor_tensor(out=alb[:], in0=tk4[:], in1=tk2[:], op=ALU.mult), reads=["tk4", "tk2"], writes=["alb"])
                                for half in range(2):
                                    pb = nextps()
                                    S.op("pe", lambda e, half=half, pb=pb: e.matmul(PS(pb)[:, 0:128], lhsT=alb[:, half * 128:(half + 1) * 128], rhs=cmv("ident"), start=True, stop=True),
                                         reads=["alb", "cm"], writes=[pn(pb)])
                                    copy_out(alT[:, half, ts * 128:(ts + 1) * 128], "alT", PS(pb)[:, 0:128], [pn(pb)])
                            pb = nextps()
                            for half in range(2):
                                osel = cm[:, ML["OwnSel"] + k * 16 + half * 8:ML["OwnSel"] + k * 16 + half * 8 + 8]
                                S.op("pe", lambda e, half=half, pb=pb, osel=osel: e.matmul(PS(pb)[0:8, :], lhsT=osel, rhs=alT[:, half, :], start=(half == 0), stop=(half == 1)),
                                     reads=["alT", "cm"], writes=[pn(pb)])
                            S.op("dve", lambda e, pb=pb: e.tensor_scalar(out=ownb[:, g, :], in0=PS(pb)[0:8, :], scalar1=-1.0, scalar2=-NEG, op0=ALU.add, op1=ALU.mult),
                                 reads=[pn(pb)], writes=["ownb"])
                            for half in range(2):
                                S.op("dve", lambda e, half=half: e.tensor_scalar(out=selbT[:, g, half, :], in0=alT[:, half, :], scalar1=cst[:, CL["_om"] + k * 2 + half:CL["_om"] + k * 2 + half + 1],
                                                                               scalar2=None, op0=ALU.mult), reads=["alT", "cst"], writes=["selbT"])
                                S.op("dve", lambda e, half=half: e.tensor_scalar(out=selbT[:, g, half, :], in0=selbT[:, g, half, :], scalar1=-1.0, scalar2=-NEG, op0=ALU.add, op1=ALU.mult),
                                     reads=["selbT"], writes=["selbT"])
                            for hg in range(4):
                                h = g * 4 + hg
                                gb, gname = load_gates(h)
                                chunks = []
                                for c_ in range(4):
                                    chunks.append(dict(kT=ksTo[:, g, c_ * 128:(c_ + 1) * 128], kn=["ksTo"], v=vown[:, c_, g * 128:(g + 1) * 128], vn=["vown"],
                                                       masks=[(cm[0:8, ML["Eown"] + c_ * 128:ML["Eown"] + (c_ + 1) * 128], ownb[:, g, :], ["ownb"]),
                                                              (cmv("negI"), cm[:, ML["DiagN"] + c_ * 512:ML["DiagN"] + (c_ + 1) * 512], [])]))
                                ntot = 4 + 4 * (8 * k + 8)
                                softmax_chunks(chunks, qn[:, h, :], ["qn"], None, 0, ntot)
                                for gt in range(8 * k + 8):
                                    kb, vb, kname, vname = stream_tile(ksT_d[g], g * 128, gt)
                                    chunks = []
                                    for c_ in range(4):
                                        cg = gt * 4 + c_
                                        chunks.append(dict(kT=kb[:, c_ * 128:(c_ + 1) * 128], kn=[kname], v=vb[:, c_, :], vn=[vname],
                                                           masks=[(cm[:, ML["E64"] + (cg % 64) * 128:ML["E64"] + (cg % 64 + 1) * 128], selbT[:, g, cg // 64, :], ["selbT"])]))
                                    softmax_chunks(chunks, qn[:, h, :], ["qn"], None, 4 + 4 * gt, ntot)
                                finalize(ynsa[:, hg, :], "ynsa", gb[:, 1, :], gname, accumulate=True)
                            for hg in range(4):
                                h = g * 4 + hg
                                gb, gname = load_gates(h)
                                chunks = []
                                for c_ in range(4):
                                    chunks.append(dict(kT=kwTp[:, g, c_ * 128:(c_ + 1) * 128], kn=["kwTp"], v=vprev[:, c_, g * 128:(g + 1) * 128], vn=["vprev"],
                                                       masks=[(cmv("negI"), cm[:, ML["DiagW"] + c_ * 512:ML["DiagW"] + (c_ + 1) * 512], [])], bias=ccol("pv", k)))
                                for c_ in range(4):
                                    chunks.append(dict(kT=kwTo[:, g, c_ * 128:(c_ + 1) * 128], kn=["kwTo"], v=vown[:, c_, 256 + g * 128:256 + (g + 1) * 128], vn=["vown"],
                                                       masks=[(cmv("negI"), cm[:, ML["DiagN"] + c_ * 512:ML["DiagN"] + (c_ + 1) * 512], [])]))
                                softmax_chunks(chunks, qn[:, h, :], ["qn"])
                                finalize(ynsa[:, hg, :], "ynsa", gb[:, 2, :], gname, accumulate=True)
                                copy_out(ynsa_b[:, h, :], "ynsa_b", ynsa[:, hg, :], ["ynsa"])
                        for h in range(4):
                            chunks = [dict(kT=kmem[:, h, ms * 128:(ms + 1) * 128], kn=["kmem"], v=vmem[:, ms, h * 128:(h + 1) * 128], vn=["vmem"]) for ms in range(2)]
                            softmax_chunks(chunks, qm[:, h, :], ["qm"])
                            finalize(ymem[:, h, :], "ymem")
                        for h in range(4):
                            nch = 4 + (8 * k + 8) * 4

                            def sb_gen(h=h):
                                for c_ in (3, 2, 1, 0):
                                    yield dict(kT=sbkTo[:, h, c_ * 128:(c_ + 1) * 128], kn=["sbkTo"], v=vown[:, c_, 512 + h * 128:512 + (h + 1) * 128], vn=["vown"],
                                               diag=cm[:, ML["DiagS"] + c_ * 512:ML["DiagS"] + (c_ + 1) * 512], bias=ccol("zero"))
                                for tl in range(8 * k + 7, -1, -1):
                                    kb, vb, kname, vname = stream_tile(sbkT_d[h], 256 + h * 128, tl)
                                    j_ = tl - 8 * k
                                    bias = ccol("cw", j_) if j_ >= 0 else ccol("zero")
                                    for c_ in (3, 2, 1, 0):
                                        yield dict(kT=kb[:, c_ * 128:(c_ + 1) * 128], kn=[kname], v=vb[:, c_, :], vn=[vname], bias=bias)
                            gen = sb_gen()
                            st = {}
                            Rprev = [None]

                            def stage1(ci):
                                ch = next(gen)
                                zb = nextps()
                                dg = ch.get("diag")
                                S.op("pe", lambda e: e.matmul(PS(zb)[:, :], lhsT=ch["kT"], rhs=qs[:, h, :], start=True, stop=(dg is None)),
                                     reads=ch["kn"] + ["qs"], writes=[pn(zb)])
                                if dg is not None:
                                    S.op("pe", lambda e: e.matmul(PS(zb)[:, :], lhsT=cmv("negI"), rhs=dg, start=False, stop=True), reads=["cm"], writes=[pn(zb)])
                                e_ = e32[ci % 2]
                                en = "e32_%d" % (ci % 2)
                                S.op("act", lambda e: e.activation(out=e_[:], in_=PS(zb)[:, :], func=AF.Exp, scale=SCALE, bias=ch["bias"]),
                                     reads=[pn(zb), "cst"], writes=[en])
                                L_ = Lp[ci % 3]
                                ln_ = "Lp%d" % (ci % 3)
                                S.op("act", lambda e: e.activation(out=L_[:], in_=e_[:], func=AF.Ln, bias=ccol("one")), reads=[en, "cst"], writes=[ln_])
                                st[ci] = dict(ch=ch, e_=e_, en=en, L_=L_, ln_=ln_)

                            def stage2(ci):
                                d_ = st[ci]
                                L_, ln_, e_, en = d_["L_"], d_["ln_"], d_["e_"], d_["en"]
                                cb = nextps()
                                Rp = Rprev[0]
                                S.op("pe", lambda e: e.matmul(PS(cb)[:, :], lhsT=cmv("UIneg"), rhs=L_[:], start=True, stop=(Rp is None)),
                                     reads=[ln_, "cm"], writes=[pn(cb)])
                                if Rp is not None:
                                    S.op("pe", lambda e: e.matmul(PS(cb)[:, :], lhsT=cmv("onesneg"), rhs=Rp[0][:], start=False, stop=True),
                                         reads=[Rp[1], "cm"], writes=[pn(cb)])
                                ec_ = ec32[ci % 2]
                                ecn = "ec32_%d" % (ci % 2)
                                S.op("act", lambda e: e.activation(out=ec_[:], in_=PS(cb)[:, :], func=AF.Exp), reads=[pn(cb)], writes=[ecn])
                                W_ = Wb[ci % 2]
                                wn_ = "Wb%d" % (ci % 2)
                                S.op("dve", lambda e: e.tensor_tensor(out=W_[:], in0=e_[:], in1=ec_[:], op=ALU.mult), reads=[en, ecn], writes=[wn_])
                                Rn = Rb[ci % 2]
                                rn_ = "Rb%d" % (ci % 2)
                                if Rp is None:
                                    S.op("dve", lambda e: e.tensor_copy(out=Rn[:], in_=L_[:]), reads=[ln_], writes=[rn_])
                                else:
                                    S.op("dve", lambda e: e.tensor_tensor(out=Rn[:], in0=Rp[0][:], in1=L_[:], op=ALU.add), reads=[ln_, Rp[1]], writes=[rn_])
                                Rprev[0] = (Rn, rn_)
                                d_["W_"], d_["wn_"] = W_, wn_

                            def stage3(ci):
                                d_ = st.pop(ci)
                                ch, W_, wn_ = d_["ch"], d_["W_"], d_["wn_"]
                                S.op("pe", lambda e: e.matmul(PS(O_B)[:, :], lhsT=ch["v"], rhs=W_[:], start=(ci == 0), stop=(ci == nch - 1)),
                                     reads=ch["vn"] + [wn_], writes=[pn(O_B)])
                            for it in range(nch + 2):
                                if it < nch:
                                    stage1(it)
                                if 1 <= it <= nch:
                                    stage2(it - 1)
                                if it >= 2:
                                    stage3(it - 2)
                            copy_out(ysb[:, h, :], "ysb", PS(O_B)[:, :], [pn(O_B)])
                        stage_end()

                with ExitStack() as es_b:
                    x1 = sbt(es_b, "x1", [128, KC, TQ], F32)
                    hn = sbt(es_b, "hnB", [128, KC, TQ], BF16)
                    S.dma("sp", "hnld", lambda e: e.dma_start(out=hn[:].rearrange("p c t -> p (c t)"), in_=hn_d), reads=["hn_d"], writes=["hn"])
                    with ExitStack() as es:
                        set_rot(range(8))
                        mixed = sbt(es, "mixed", [128, KC, TQ], BF16)
                        sig = sbt(es, "sig", [128, 6, TQ], F32)
                        acc = [sbt(es, "acc", [128, TQ], F32) for _ in range(2)]
                        tmpm = [sbt(es, "tmpm", [128, TQ], F32) for _ in range(2)]
                        wsg = WStream(es, "b1wg", 2, 4096)
                        wso = WStream(es, "b1wo", 3, 2048)
                        for dc2 in range(KC // 2):
                            for br in range(3):
                                def ev_s(j, pb, br=br):
                                    S.op("act", lambda e: e.activation(out=sig[:, br * 2 + j, :], in_=PS(pb)[:, :], func=AF.Sigmoid), reads=[pn(pb)], writes=["sig%d" % (br * 2 + j)])
                                proj_fm(wsg, w_in_b, D, 4632 + br * D + dc2 * 256, 256, lambda c: hn[:, c, :], ["hn"], TQ, ev_s)
                            for br, (wo, K_, y_, yn_) in enumerate([(wo_nsa_b, 1024, ynsa_b, "ynsa_b"), (wo_sb_b, 512, ysb, "ysb"), (wo_mem_b, 512, ymem, "ymem")]):
                                def ev_a(j, pb, br=br):
                                    sn = "sig%d" % (br * 2 + j)
                                    if br == 0:
                                        S.op("dve", lambda e: e.tensor_tensor(out=acc[j][:], in0=PS(pb)[:, :], in1=sig[:, j, :], op=ALU.mult), reads=[pn(pb), sn], writes=["acc%d" % j])
                                    else:
                                        S.op("dve", lambda e: e.tensor_tensor(out=tmpm[j][:], in0=PS(pb)[:, :], in1=sig[:, br * 2 + j, :], op=ALU.mult), reads=[pn(pb), sn], writes=["tmpm%d" % j])
                                        if br == 1:
                                            S.op("dve", lambda e: e.tensor_tensor(out=acc[j][:], in0=acc[j][:], in1=tmpm[j][:], op=ALU.add), reads=["acc%d" % j, "tmpm%d" % j], writes=["acc%d" % j])
                                        else:
                                            S.op("dve", lambda e: e.tensor_tensor(out=mixed[:, dc2 * 2 + j, :], in0=acc[j][:], in1=tmpm[j][:], op=ALU.add), reads=["acc%d" % j, "tmpm%d" % j], writes=["mixed"])
                                proj_fm(wso, wo, K_, dc2 * 256, 256, lambda c, y_=y_: y_[:, c, :], [yn_], TQ, ev_a)
                        xr = [sbt(es, "xr", [128, TQ], F32) for _ in range(2)]

                        def ev_o(j, pb):
                            b = xr[j % 2]
                            S.dma("sp", "xr%d" % (j % 2), lambda e: e.dma_start(out=b[:], in_=xT_own[j * 128:(j + 1) * 128, o0:o0 + TQ]), writes=["xr%d" % (j % 2)])
                            S.op("dve", lambda e: e.tensor_tensor(out=x1[:, j, :], in0=PS(pb)[:, :], in1=b[:], op=ALU.add), reads=[pn(pb), "xr%d" % (j % 2)], writes=["x1"])
                        proj_fm(wsg, w_out_b, D, 0, D, lambda c: mixed[:, c, :], ["mixed"], TQ, ev_o, gcmax=256)
                        stage_end()
                    with ExitStack() as es:
                        set_rot(range(8))
                        act_all = sbt(es, "act_all", [128, FFN // 256, TQ], BF16)
                        sq2 = [sbt(es, "sq2", [128, TQ], BF16) for _ in range(2)]
                        rs2 = sbt(es, "rs2", [128, TQ], F32)
                        sgl = [sbt(es, "sgl", [128, TQ], F32) for _ in range(2)]
                        ost = [sbt(es, "ost", [128, TQ], F32) for _ in range(2)]
                        wsf = WStream(es, "b2wf", 3, 5632)
                        pb0 = nextps()
                        for c in range(KC):
                            S.op("act", lambda e, c=c: e.activation(out=sq2[c % 2][:], in_=x1[:, c, :], func=AF.Square), reads=["x1"], writes=["sq2_%d" % (c % 2)])
                            S.op("pe", lambda e, c=c: e.matmul(PS(pb0)[:, :], lhsT=cmv("ones"), rhs=sq2[c % 2][:], start=(c == 0), stop=(c == KC - 1)),
                                 reads=["sq2_%d" % (c % 2), "cm"], writes=[pn(pb0)])
                        S.op("act", lambda e: e.activation(out=rs2[:], in_=PS(pb0)[:, :], func=AF.Sqrt, scale=1.0 / D, bias=ccol("eps")), reads=[pn(pb0), "cst"], writes=["rs2"])
                        S.op("dve", lambda e: e.reciprocal(out=rs2[:], in_=rs2[:]), reads=["rs2"], writes=["rs2"])
                        for c in range(KC):
                            S.op("dve", lambda e, c=c: e.scalar_tensor_tensor(out=hn[:, c, :], in0=x1[:, c, :], scalar=ccol("ffn", c), in1=rs2[:], op0=ALU.mult, op1=ALU.mult),
                                 reads=["x1", "rs2", "cst"], writes=["hn"])
                        for hf_ in range(2):
                          for fi2 in range(FFN // 512):
                            def ev_g2(j, pb):
                                S.op("act", lambda e: e.activation(out=sgl[j][:], in_=PS(pb)[:, :], func=AF.Silu), reads=[pn(pb)], writes=["sgl%d" % j])
                            proj_fm(wsf, w_gate_b, D, hf_ * 2816 + fi2 * 256, 256, lambda c: hn[:, c, :], ["hn"], TQ, ev_g2)

                            def ev_u(j, pb, fi2=fi2):
                                S.op("dve", lambda e: e.tensor_tensor(out=act_all[:, fi2 * 2 + j, :], in0=PS(pb)[:, :], in1=sgl[j][:], op=ALU.mult), reads=[pn(pb), "sgl%d" % j], writes=["act_all"])
                            proj_fm(wsf, w_up_b, D, hf_ * 2816 + fi2 * 256, 256, lambda c: hn[:, c, :], ["hn"], TQ, ev_u)

                          def ev_d(j, pb, hf_=hf_):
                            if hf_ == 0:
                                S.op("dve", lambda e: e.tensor_tensor(out=x1[:, j, :], in0=PS(pb)[:, :], in1=x1[:, j, :], op=ALU.add), reads=[pn(pb), "x1"], writes=["x1"])
                                return
                            b = ost[j % 2]
                            S.op("dve", lambda e: e.tensor_tensor(out=b[:], in0=PS(pb)[:, :], in1=x1[:, j, :], op=ALU.add), reads=[pn(pb), "x1"], writes=["ost%d" % (j % 2)])
                            S.dma("sp", "ost%d" % (j % 2), lambda e: e.dma_start(out=outT[j * 128:(j + 1) * 128, o0:o0 + TQ], in_=b[:]), reads=["ost%d" % (j % 2)], writes=["outT"])
                          proj_fm(wsf, w_down_b[hf_ * 2816:(hf_ + 1) * 2816, :], 2816, 0, D, lambda c: act_all[:, c, :], ["act_all"], TQ, ev_d, gcmax=256)

                        stage_end()
        S.barrier()
        S.flush()
    return nc


def host_consts(T, core):
    NT = T // (NCORE * TQ)
    NCC = T // 2048
    CL = cst_layout(NT, NCC)
    ML = cm_layout(NT, NCC)
    p = np.arange(128)
    cm = np.zeros((128, ML["_n"]), np.float32)
    cm[p, ML["ident"] + p] = 1.0
    cm[p, ML["negI"] + p] = NEG
    cm[:, ML["ones"]:ML["ones"] + 128] = 1.0
    cm[:, ML["onesneg"]:ML["onesneg"] + 128] = -1.0
    cm[:, ML["UIneg"]:ML["UIneg"] + 128] = -(p[:, None] >= p[None, :]).astype(np.float32)
    for m in range(128):
        if m < 64:
            cm[m + 64, ML["prot"] + m] = -1.0
        else:
            cm[m - 64, ML["prot"] + m] = 1.0
    u = np.arange(8192)
    cm[:, ML["E64"]:ML["E64"] + 8192] = (p[:, None] == (u[None, :] // 64)).astype(np.float32)
    for j in range(NCC):
        n = 128 * j + p
        s = np.arange(256)
        ov = ((n[:, None] >= 4 * s[None, :] - 1) & (n[:, None] <= 4 * s[None, :] + 3) & (n[:, None] < T // 16 - 1)).astype(np.float32)
        cm[:, ML["OV"] + j * 257:ML["OV"] + j * 257 + 256] = ov
        cm[:, ML["OV"] + j * 257 + 256] = (n < T // 16 - 1).astype(np.float32)
    t = np.arange(512)
    for c_ in range(4):
        sg_ = 128 * c_ + p
        cm[:, ML["DiagN"] + c_ * 512:ML["DiagN"] + (c_ + 1) * 512] = (sg_[:, None] > t[None, :]).astype(np.float32)
        cm[:, ML["DiagS"] + c_ * 512:ML["DiagS"] + (c_ + 1) * 512] = (sg_[:, None] >= t[None, :]).astype(np.float32)
        cm[:, ML["DiagW"] + c_ * 512:ML["DiagW"] + (c_ + 1) * 512] = (sg_[:, None] <= t[None, :]).astype(np.float32)
        s_ = np.arange(128)
        for r in range(2):
            cm[2 * c_ + r, ML["Eown"] + c_ * 128 + s_] = (s_ // 64 == r).astype(np.float32)
    for k in range(NT):
        ob0 = 8 * (8 * k + core)
        for half in range(2):
            for b in range(8):
                blk = ob0 + b
                if blk // 128 == half:
                    cm[blk % 128, ML["OwnSel"] + k * 16 + half * 8 + b] = 1.0
    cst = np.zeros((128, CL["_n"]), np.float32)
    cst[:, CL["eps"]] = 1e-6
    cst[:, CL["one"]] = 1.0
    cst[:, CL["halfpi"]] = np.float32(np.pi / 2)
    cst[:, CL["tiny"]] = 1e-30
    inv = (np.float32(1.0) / np.power(np.float32(10000.0), np.arange(0, 128, 2, dtype=np.float32) / np.float32(128))).astype(np.float32)
    cst[:, CL["inv"]] = inv[p % 64]
    for j in range(8):
        cst[:, CL["cw"] + j] = 0.0 if j < core else NEG
    for k in range(NT):
        gt = 8 * k + core
        cst[:, CL["pv"] + k] = NEG if gt == 0 else 0.0
        for ts in range(4):
            cst[:, CL["cur"] + k * 4 + ts] = (gt * 512 + ts * 128 + p) // 64
        for j in range(NCC):
            cst[:, CL["cthr"] + k * NCC + j] = gt * 512 - 2048 * j
        ob0 = 8 * gt
        for half in range(2):
            cst[:, CL["_om"] + k * 2 + half] = ((half * 128 + p) < ob0).astype(np.float32)
    rowt = np.zeros((128, 256), np.float32)
    rowt[:, 0:256] = np.arange(256)[None, :]
    j2 = (16 * p[:, None] + 31 - t[None, :]).astype(np.float32)
    return cst, cm, rowt, j2, CL


_NC_CACHE = {}


def kernel(x, mem, positions, attn_norm, w_in, nsa_q_norm, nsa_kc_norm, nsa_ks_norm, nsa_kw_norm,
           cmp_k_pe, cmp_k_w1, cmp_k_w2, cmp_v_pe, cmp_v_w1, cmp_v_w2, mem_norm, w_mem_kv,
           mem_q_norm, mem_k_norm, w_o_nsa, w_o_sb, w_o_mem, w_out, ffn_norm,
           w_ffn_gate, w_ffn_up, w_ffn_down):
    x = np.asarray(x)
    T = x.shape[1]
    NT = T // (NCORE * TQ)
    NCC = T // 2048
    NCP = NCC * 128
    if T not in _NC_CACHE:
        _NC_CACHE[T] = build(T)
    nc = _NC_CACHE[T]
    f = lambda a: np.ascontiguousarray(np.asarray(a, dtype=np.float32))
    xT = np.ascontiguousarray(np.asarray(x)[0].T)
    pos = np.asarray(positions).astype(np.int32)
    pos_cmp = np.zeros((1, NCP), np.int32)
    pc = pos[0, 31::16]
    pos_cmp[0, :len(pc)] = pc
    common = {
        "xT_all": xT, "memT": np.ascontiguousarray(np.asarray(mem)[0].T), "pos_all": pos, "pos_cmp": pos_cmp,
        "w_in": f(w_in[0]), "w1k": f(cmp_k_w1[0]), "w2k": f(cmp_k_w2[0]), "w1v": f(cmp_v_w1[0]), "w2v": f(cmp_v_w2[0]),
        "w_mem": f(w_mem_kv[0]), "wo_nsa": f(w_o_nsa[0]), "wo_sb": f(w_o_sb[0]), "wo_mem": f(w_o_mem[0]), "w_out": f(w_out[0]),
        "w_gate": f(w_ffn_gate[0]), "w_up": f(w_ffn_up[0]), "w_down": f(w_ffn_down[0]),
    }
    in_maps = []
    for core in range(NCORE):
        cst, cm, rowt, j2, CL = host_consts(T, core)
        for name, arr in [("attn", attn_norm), ("ffn", ffn_norm), ("memn", mem_norm)]:
            cst[:, CL[name]:CL[name] + 16] = np.asarray(arr)[0].reshape(16, 128).T
        for name, arr in [("qn", nsa_q_norm), ("kcn", nsa_kc_norm), ("ksn", nsa_ks_norm), ("kwn", nsa_kw_norm), ("mqn", mem_q_norm), ("mkn", mem_k_norm)]:
            cst[:, CL[name]] = np.asarray(arr)[0]
        cst[:, CL["pek"]:CL["pek"] + 32] = np.asarray(cmp_k_pe)[0].T
        cst[:, CL["pev"]:CL["pev"] + 32] = np.asarray(cmp_v_pe)[0].T
        own = np.zeros((D, NT * TQ), np.float32)
        prev = np.zeros((D, NT * TQ), np.float32)
        pown = np.zeros((1, NT * TQ), np.int32)
        pprev = np.zeros((1, NT * TQ), np.int32)
        for k in range(NT):
            gt = 8 * k + core
            own[:, k * TQ:(k + 1) * TQ] = xT[:, gt * TQ:(gt + 1) * TQ]
            pown[0, k * TQ:(k + 1) * TQ] = pos[0, gt * TQ:(gt + 1) * TQ]
            if gt > 0:
                prev[:, k * TQ:(k + 1) * TQ] = xT[:, (gt - 1) * TQ:gt * TQ]
                pprev[0, k * TQ:(k + 1) * TQ] = pos[0, (gt - 1) * TQ:gt * TQ]
        m = dict(common)
        m.update({"xT_own": own, "xT_prev": prev, "pos_own": pown, "pos_prev": pprev, "cst": cst, "cmat": cm, "rowt": rowt, "j2": j2})
        in_maps.append(m)
    res = run_bass_kernel_spmd(nc, in_maps, core_ids=list(range(NCORE)))
    out = np.zeros((1, T, D), np.float32)
    for core in range(NCORE):
        oT = np.asarray(res.results[core]["outT"])
        for k in range(NT):
            gt = 8 * k + core
            out[0, gt * TQ:(gt + 1) * TQ, :] = oT[:, k * TQ:(k + 1) * TQ].T
    return out
```

```python
import math
from contextlib import ExitStack
import numpy as np
import concourse.bass as bass
import concourse.mybir as mybir
from concourse.bass_utils import run_bass_kernel_spmd

F32 = mybir.dt.float32
BF16 = mybir.dt.bfloat16
I32 = mybir.dt.int32
ALU = mybir.AluOpType
AF = mybir.ActivationFunctionType

D = 2048
KC = 16
NCORE = 8
TQ = 512
FFN = 5632
NEG = -30000.0
SCALE = 128 ** -0.5
MAGIC = 12582912.0
C1 = 6.28125
C2 = float(2 * np.pi - 6.28125)
PI_LO = 3.1415925

COMPUTE_Q = ("pe", "act", "dve", "pool")
ALLQ = ("pe", "act", "dve", "pool", "sp")


class Sched:
    def __init__(self, nc, es):
        self.nc = nc
        self.es = es
        self.q = {k: [] for k in ALLQ}
        self.cnt = {k: 0 for k in COMPUTE_Q}
        self.sems = {k: es.enter_context(nc.semaphore("s_" + k)) for k in COMPUTE_Q}
        self.dcnt = {}
        self.lastw = {}
        self.readers = {}
        self.seen = {k: {} for k in ALLQ}
        self.nops = 0

    def _deps(self, q, reads, writes):
        need = {}

        def add(tok):
            if tok is None:
                return
            k, v = tok
            if need.get(k, 0) < v:
                need[k] = v
        for b in reads:
            add(self.lastw.get(b))
        for b in writes:
            add(self.lastw.get(b))
            for t in self.readers.get(b, ()):
                add(t)
        out = []
        for k, v in need.items():
            if k == q and q == "pe":
                continue
            if self.seen[q].get(k, 0) >= v:
                continue
            self.seen[q][k] = v
            out.append((k, v))
        return out

    def _commit(self, tok, reads, writes):
        for b in reads:
            self.readers.setdefault(b, []).append(tok)
        for b in writes:
            self.lastw[b] = tok
            self.readers[b] = []

    def _rec(self, fn):
        class _R:
            def __getattr__(s, name):
                def f(*a, **kw):
                    s.call = (name, a, kw)
                    return None
                return f
        r = _R()
        fn(r)
        name, a, kw = r.call
        return lambda eng: getattr(eng, name)(*a, **kw)

    def op(self, q, fn, reads=(), writes=(), sig=True):
        fn = self._rec(fn)
        waits = self._deps(q, reads, writes)
        if not sig:
            tok = (q, self.cnt[q] + 1)
            self.q[q].append((waits, fn, None, 0))
            self._commit(tok, reads, writes)
            self.nops += 1
            return tok
        self.cnt[q] += 1
        tok = (q, self.cnt[q])
        self.q[q].append((waits, fn, q, 1))
        self._commit(tok, reads, writes)
        self.nops += 1
        return tok

    def dma(self, q, key, fn, reads=(), writes=()):
        fn = self._rec(fn)
        waits = self._deps(q, reads, writes)
        k = "dma:" + key
        if k not in self.sems:
            self.sems[k] = self.es.enter_context(self.nc.semaphore("d_" + key))
            self.dcnt[k] = 0
        self.dcnt[k] += 16
        tok = (k, self.dcnt[k])
        self.q[q].append((waits, fn, k, 16))
        self._commit(tok, reads, writes)
        self.nops += 1
        return tok

    def barrier(self):
        allw = [(k, v) for k, v in self.cnt.items() if v > 0] + [(k, v) for k, v in self.dcnt.items() if v > 0]
        for q in ALLQ:
            w = [(k, v) for k, v in allw if self.seen[q].get(k, 0) < v]
            for k, v in w:
                self.seen[q][k] = v
            if w:
                self.q[q].append((w, None, None, 0))
        self.lastw = {}
        self.readers = {}

    def flush(self):
        nc = self.nc
        engs = {"pe": "tensor", "act": "scalar", "dve": "vector", "pool": "gpsimd", "sp": "sync"}
        if not any(self.q.values()):
            return
        with nc.Block() as block:
            def mk(items):
                def body(eng):
                    for waits, fn, semk, inc in items:
                        for k, v in waits:
                            eng.wait_ge(self.sems[k], v)
                        if fn is not None:
                            ins = fn(eng)
                            if semk is not None:
                                ins.then_inc(self.sems[semk], inc)
                return body
            for qn, attr in engs.items():
                if self.q[qn]:
                    getattr(block, attr)(mk(self.q[qn]))
        self.q = {k: [] for k in ALLQ}


def cst_layout(NT, NCC):
    o = {}
    c = 0
    for name, n in [("attn", 16), ("ffn", 16), ("memn", 16), ("qn", 1), ("kcn", 1), ("ksn", 1), ("kwn", 1),
                    ("mqn", 1), ("mkn", 1), ("inv", 1), ("eps", 1), ("one", 1), ("zero", 1), ("halfpi", 1),
                    ("cw", 8), ("pv", NT), ("cur", NT * 4), ("cthr", NT * NCC), ("pek", 32), ("pev", 32),
                    ("tiny", 1), ("_om", NT * 2)]:
        o[name] = c
        c += n
    o["_n"] = c
    return o


def cm_layout(NT, NCC):
    o = {}
    c = 0
    for name, n in [("ident", 128), ("negI", 128), ("ones", 128), ("onesneg", 128), ("UIneg", 128), ("prot", 128),
                    ("E64", 8192), ("OV", NCC * 257), ("DiagN", 2048), ("DiagS", 2048), ("DiagW", 2048),
                    ("Eown", 512), ("OwnSel", NT * 16)]:
        o[name] = c
        c += n
    o["_n"] = c
    return o


def build(T):
    NT = T // (NCORE * TQ)
    NGT = T // TQ
    NCC = T // 2048
    NCP = NCC * 128
    NB = min(512, NCP)
    CL = cst_layout(NT, NCC)
    ML = cm_layout(NT, NCC)
    TO = NT * TQ

    nc = bass.Bass("TRN2", target_bir_lowering=False)

    def din(name, shape, dt=F32):
        return nc.dram_tensor(name, list(shape), dt, kind="ExternalInput").ap()

    xT_all = din("xT_all", [D, T])
    xT_own = din("xT_own", [D, TO])
    xT_prev = din("xT_prev", [D, TO])
    memT = din("memT", [D, 256])
    pos_all = din("pos_all", [1, T], I32)
    pos_own = din("pos_own", [1, TO], I32)
    pos_prev = din("pos_prev", [1, TO], I32)
    pos_cmp = din("pos_cmp", [1, NCP], I32)
    w_in = din("w_in", [D, 10776])
    w1k = din("w1k", [32, 128, 256])
    w2k = din("w2k", [256, 128])
    w1v = din("w1v", [32, 128, 256])
    w2v = din("w2v", [256, 128])
    w_mem = din("w_mem", [D, 1024])
    wo_nsa = din("wo_nsa", [1024, D])
    wo_sb = din("wo_sb", [512, D])
    wo_mem = din("wo_mem", [512, D])
    w_out = din("w_out", [D, D])
    w_gate = din("w_gate", [D, FFN])
    w_up = din("w_up", [D, FFN])
    w_down = din("w_down", [FFN, D])
    cst_d = din("cst", [128, CL["_n"]])
    rowt_d = din("rowt", [128, 256])
    j2_d = din("j2", [128, 512])
    cmat_d = din("cmat", [128, ML["_n"]])
    outT = nc.dram_tensor("outT", [D, TO], F32, kind="ExternalOutput").ap()

    kcT_d = nc.dram_tensor("kcT_d", [2, 128, T], BF16).ap()
    vcT_d = nc.dram_tensor("vcT_d", [2, 128, T], BF16).ap()
    ksT_d = nc.dram_tensor("ksT_d", [2, 128, T], BF16).ap()
    sbkT_d = nc.dram_tensor("sbkT_d", [4, 128, T], BF16).ap()
    vtok_d = nc.dram_tensor("vtok_d", [T, 768], BF16).ap()
    gates_d = nc.dram_tensor("gates_d", [24, TQ], F32).ap()
    hn_d = nc.dram_tensor("hn_d", [128, KC * TQ], BF16).ap()
    w_in_b = nc.dram_tensor("w_in_b", [D, 10776], BF16).ap()
    wo_nsa_b = nc.dram_tensor("wo_nsa_b", [1024, D], BF16).ap()
    wo_sb_b = nc.dram_tensor("wo_sb_b", [512, D], BF16).ap()
    wo_mem_b = nc.dram_tensor("wo_mem_b", [512, D], BF16).ap()
    w_out_b = nc.dram_tensor("w_out_b", [D, D], BF16).ap()
    w_gate_b = nc.dram_tensor("w_gate_b", [D, FFN], BF16).ap()
    w_up_b = nc.dram_tensor("w_up_b", [D, FFN], BF16).ap()
    w_down_b = nc.dram_tensor("w_down_b", [FFN, D], BF16).ap()

    with ExitStack() as es_all:
        S = Sched(nc, es_all)
        uid = [0]

        def sbt(es, name, shape, dt):
            uid[0] += 1
            return es.enter_context(nc.sbuf_tensor("%s_%d" % (name, uid[0]), list(shape), dt))

        PSB = [es_all.enter_context(nc.psum_tensor("ps%d" % i, [128, 512], F32)) for i in range(8)]
        rot = {"banks": list(range(8)), "i": 0}

        def set_rot(banks):
            rot["banks"] = list(banks)
            rot["i"] = 0

        def nextps():
            b = rot["banks"][rot["i"] % len(rot["banks"])]
            rot["i"] += 1
            return b

        def PS(b):
            return PSB[b]

        def pn(b):
            return "ps%d" % b

        cst = sbt(es_all, "cst", [128, CL["_n"]], F32)
        rowt = sbt(es_all, "rowt", [128, 256], F32)
        j2 = sbt(es_all, "j2", [128, 512], F32)
        cmA = sbt(es_all, "cmA", [128, 768], BF16)
        cmB_ref = [None]

        class _CM:
            def __getitem__(self, idx):
                rows, cols = idx
                a, b = cols.start, cols.stop
                if b <= 768:
                    return cmA[rows, a:b]
                assert a >= 768
                return cmB_ref[0][rows, a - 768:b - 768]
        cm = _CM()
        kcTs = sbt(es_all, "kcTs", [128, 2, NCP], BF16)
        vcs = sbt(es_all, "vcs", [128, 2, NCC, 128], BF16)
        kmem = sbt(es_all, "kmem", [128, 4, 256], BF16)
        vmem = sbt(es_all, "vmem", [128, 2, 512], BF16)

        def ccol(name, i=0):
            return cst[:, CL[name] + i:CL[name] + i + 1]

        def cmv(name, off=0, n=128):
            return cm[:, ML[name] + off:ML[name] + off + n]

        S.dma("sp", "c0a", lambda e: e.dma_start(out=cst[:], in_=cst_d), writes=["cst"])
        S.dma("sp", "c0b", lambda e: e.dma_start(out=rowt[:], in_=rowt_d), writes=["rowt"])
        S.dma("sp", "c0c", lambda e: e.dma_start(out=j2[:], in_=j2_d), writes=["j2"])
        S.dma("pool", "c1", lambda e: e.dma_start(out=cmA[:], in_=cmat_d[:, 0:768]), writes=["cm"])

        class Rope:
            def __init__(self, es, n, tag):
                self.n, self.tag = n, tag
                self.posi = sbt(es, "posi", [128, n], I32)
                self.ang = sbt(es, "ang", [128, n], F32)
                self.t1 = sbt(es, "rt1", [128, n], F32)
                self.kk = sbt(es, "rkk", [128, n], F32)

            def run(self, pos_ap, cosT, sinT, csname):
                n, tag = self.n, self.tag
                posi, ang, t1, kk = self.posi, self.ang, self.t1, self.kk
                S.dma("sp", tag + "pos", lambda e: e.dma_start(out=posi[:], in_=pos_ap.to_broadcast([128, n])), writes=[tag + "posi"])
                S.op("dve", lambda e: e.tensor_copy(out=t1[:], in_=posi[:]), reads=[tag + "posi"], writes=[tag + "t1"])
                S.op("dve", lambda e: e.tensor_scalar(out=ang[:], in0=t1[:], scalar1=ccol("inv"), scalar2=None, op0=ALU.mult),
                     reads=[tag + "t1", "cst"], writes=[tag + "ang"])
                S.op("dve", lambda e: e.tensor_scalar(out=t1[:], in0=ang[:], scalar1=float(1.0 / (2 * np.pi)), scalar2=MAGIC, op0=ALU.mult, op1=ALU.add),
                     reads=[tag + "ang"], writes=[tag + "t1"])
                S.op("dve", lambda e: e.tensor_scalar(out=kk[:], in0=t1[:], scalar1=MAGIC, scalar2=None, op0=ALU.subtract),
                     reads=[tag + "t1"], writes=[tag + "kk"])
                S.op("dve", lambda e: e.scalar_tensor_tensor(out=t1[:], in0=kk[:], scalar=-C1, in1=ang[:], op0=ALU.mult, op1=ALU.add),
                     reads=[tag + "kk", tag + "ang"], writes=[tag + "t1"])
                S.op("dve", lambda e: e.scalar_tensor_tensor(out=ang[:], in0=kk[:], scalar=-C2, in1=t1[:], op0=ALU.mult, op1=ALU.add),
                     reads=[tag + "kk", tag + "t1"], writes=[tag + "ang"])
                S.op("dve", lambda e: e.tensor_scalar(out=ang[:], in0=ang[:], scalar1=PI_LO, scalar2=-PI_LO, op0=ALU.min, op1=ALU.max),
                     reads=[tag + "ang"], writes=[tag + "ang"])
                S.op("dve", lambda e: e.scalar_tensor_tensor(out=t1[:], in0=ang[:], scalar=-1.0, in1=ang[:], op0=ALU.mult, op1=ALU.max),
                     reads=[tag + "ang"], writes=[tag + "t1"])
                S.op("act", lambda e: e.activation(out=sinT, in_=ang[:], func=AF.Sin), reads=[tag + "ang"], writes=[csname + "sin"])
                S.op("act", lambda e: e.activation(out=cosT, in_=t1[:], func=AF.Sin, scale=-1.0, bias=ccol("halfpi")),
                     reads=[tag + "t1", "cst"], writes=[csname + "cos"])

        class RMS:
            def __init__(self, es, n, tag):
                self.n, self.tag = n, tag
                self.xb = [sbt(es, "xch", [128, n], F32) for _ in range(3)]
                self.sq = [sbt(es, "sqc", [128, n], BF16) for _ in range(2)]
                self.rs = sbt(es, "rstd", [128, n], F32)

            def run(self, src_ap, gname, hn, hn_name):
                n, tag, xb, sq, rs = self.n, self.tag, self.xb, self.sq, self.rs
                pb = nextps()
                for c in range(KC):
                    b = xb[c % 3]
                    S.dma("sp", tag + "x%d" % (c % 3), lambda e, b=b, c=c: e.dma_start(out=b[:], in_=src_ap[c * 128:(c + 1) * 128, :]),
                          writes=[tag + "xb%d" % (c % 3)])
                    S.op("act", lambda e, b=b, c=c: e.activation(out=sq[c % 2][:], in_=b[:], func=AF.Square),
                         reads=[tag + "xb%d" % (c % 3)], writes=[tag + "sq%d" % (c % 2)])
                    S.op("pe", lambda e, c=c: e.matmul(PS(pb)[:, 0:n], lhsT=cmv("ones"), rhs=sq[c % 2][:], start=(c == 0), stop=(c == KC - 1)),
                         reads=[tag + "sq%d" % (c % 2), "cm"], writes=[pn(pb)])
                S.op("act", lambda e: e.activation(out=rs[:], in_=PS(pb)[:, 0:n], func=AF.Sqrt, scale=1.0 / D, bias=ccol("eps")),
                     reads=[pn(pb), "cst"], writes=[tag + "rs"])
                S.op("dve", lambda e: e.reciprocal(out=rs[:], in_=rs[:]), reads=[tag + "rs"], writes=[tag + "rs"])
                for c in range(KC):
                    b = xb[c % 3]
                    S.dma("sp", tag + "x%d" % (c % 3), lambda e, b=b, c=c: e.dma_start(out=b[:], in_=src_ap[c * 128:(c + 1) * 128, :]),
                          writes=[tag + "xb%d" % (c % 3)])
                    S.op("dve", lambda e, b=b, c=c: e.scalar_tensor_tensor(out=hn[:, c, :], in0=b[:], scalar=ccol(gname, c), in1=rs[:],
                                                                          op0=ALU.mult, op1=ALU.mult),
                         reads=[tag + "xb%d" % (c % 3), tag + "rs", "cst"], writes=[hn_name])

        class RMSRes:
            def __init__(self, es, n, tag):
                self.n, self.tag = n, tag
                self.xt = sbt(es, "xt", [128, KC, n], F32)
                self.sq = [sbt(es, "sqc", [128, n], BF16) for _ in range(2)]
                self.rs = sbt(es, "rstd", [128, n], F32)

            def run(self, src_ap, gname, hn, hn_name):
                n, tag, xt, sq, rs = self.n, self.tag, self.xt, self.sq, self.rs
                pb = nextps()
                for qd in range(4):
                    S.dma("sp", tag + "q%d" % qd, lambda e, qd=qd: e.dma_start(out=xt[:, 4 * qd:4 * qd + 4, :],
                                                                            in_=src_ap[qd * 512:(qd + 1) * 512, :].rearrange("(c p) t -> p c t", p=128)),
                          writes=[tag + "xq%d" % qd])
                for c in range(KC):
                    S.op("act", lambda e, c=c: e.activation(out=sq[c % 2][:], in_=xt[:, c, :], func=AF.Square),
                         reads=[tag + "xq%d" % (c // 4)], writes=[tag + "sq%d" % (c % 2)])
                    S.op("pe", lambda e, c=c: e.matmul(PS(pb)[:, 0:n], lhsT=cmv("ones"), rhs=sq[c % 2][:], start=(c == 0), stop=(c == KC - 1)),
                         reads=[tag + "sq%d" % (c % 2), "cm"], writes=[pn(pb)])
                S.op("act", lambda e: e.activation(out=rs[:], in_=PS(pb)[:, 0:n], func=AF.Sqrt, scale=1.0 / D, bias=ccol("eps")),
                     reads=[pn(pb), "cst"], writes=[tag + "rs"])
                S.op("dve", lambda e: e.reciprocal(out=rs[:], in_=rs[:]), reads=[tag + "rs"], writes=[tag + "rs"])
                for c in range(KC):
                    S.op("dve", lambda e, c=c: e.scalar_tensor_tensor(out=hn[:, c, :], in0=xt[:, c, :], scalar=ccol(gname, c), in1=rs[:],
                                                                     op0=ALU.mult, op1=ALU.mult),
                         reads=[tag + "xq%d" % (c // 4), tag + "rs", "cst"], writes=[hn_name])

        nrq = []

        def nr_tick():
            ready = []
            for it in nrq:
                it[0] -= 1
            while nrq and nrq[0][0] <= 0:
                ready.append(nrq.pop(0)[1])
            for fn_ in ready:
                fn_()

        def nr_drain():
            while nrq:
                nrq.pop(0)[1]()

        class NR:
            NSET = 2

            def __init__(self, es, n, tag):
                self.n = n
                self.tag = tag
                self.i = 0
                self.sets = []
                for _ in range(self.NSET):
                    self.sets.append(dict(sq=sbt(es, "nrsq", [128, n], BF16), r=sbt(es, "nrr", [128, n], F32), kb=sbt(es, "nrkb", [128, n], BF16),
                                          t1=sbt(es, "nrt1", [128, n], F32), t2=sbt(es, "nrt2", [128, n], F32)))

            def run(self, pin, gname, out_ap, out_name, cosT=None, sinT=None, csname=None, after=None):
                n = self.n
                si_ = self.i % self.NSET
                self.i += 1
                tag = self.tag + "s%d" % si_
                B = self.sets[si_]
                S.op("act", lambda e: e.activation(out=B["sq"][:], in_=PS(pin)[:, 0:n], func=AF.Square), reads=[pn(pin)], writes=[tag + "sq"])

                def stepB():
                    p2 = nextps()
                    S.op("pe", lambda e: e.matmul(PS(p2)[:, 0:n], lhsT=cmv("ones"), rhs=B["sq"][:], start=True, stop=True),
                         reads=[tag + "sq", "cm"], writes=[pn(p2)])
                    S.op("act", lambda e: e.activation(out=B["r"][:], in_=PS(p2)[:, 0:n], func=AF.Sqrt, scale=1.0 / 128, bias=ccol("eps")),
                         reads=[pn(p2), "cst"], writes=[tag + "r"])
                    S.op("dve", lambda e: e.reciprocal(out=B["r"][:], in_=B["r"][:]), reads=[tag + "r"], writes=[tag + "r"])
                    if cosT is None:
                        S.op("dve", lambda e: e.scalar_tensor_tensor(out=out_ap, in0=PS(pin)[:, 0:n], scalar=ccol(gname), in1=B["r"][:], op0=ALU.mult, op1=ALU.mult),
                             reads=[pn(pin), tag + "r", "cst"], writes=[out_name])
                        if after:
                            after()
                        return
                    S.op("dve", lambda e: e.scalar_tensor_tensor(out=B["kb"][:], in0=PS(pin)[:, 0:n], scalar=ccol(gname), in1=B["r"][:], op0=ALU.mult, op1=ALU.mult),
                         reads=[pn(pin), tag + "r", "cst"], writes=[tag + "kb"])
                    nrq.append([1, stepC])

                def stepC():
                    p3 = nextps()
                    S.op("pe", lambda e: e.matmul(PS(p3)[:, 0:n], lhsT=cmv("prot"), rhs=B["kb"][:], start=True, stop=True),
                         reads=[tag + "kb", "cm"], writes=[pn(p3)])
                    S.op("dve", lambda e: e.tensor_tensor(out=B["t1"][:], in0=B["kb"][:], in1=cosT, op=ALU.mult),
                         reads=[tag + "kb", csname + "cos"], writes=[tag + "t1"])
                    S.op("dve", lambda e: e.tensor_tensor(out=B["t2"][:], in0=PS(p3)[:, 0:n], in1=sinT, op=ALU.mult),
                         reads=[pn(p3), csname + "sin"], writes=[tag + "t2"])
                    S.op("dve", lambda e: e.tensor_tensor(out=out_ap, in0=B["t1"][:], in1=B["t2"][:], op=ALU.add),
                         reads=[tag + "t1", tag + "t2"], writes=[out_name])
                    if after:
                        after()
                nrq.append([1, stepB])

        class WStream:
            def __init__(self, es, tag, nbuf=2, size=5632, cast=False):
                self.bufs = [sbt(es, "wbuf", [128, size], BF16) for _ in range(nbuf)]
                self.tag = tag
                self.i = 0
                self.size = size
                self.cast = cast

            def load(self, w_ap, K, col0, gc):
                kc = K // 128
                assert kc * gc <= self.size
                i = self.i % len(self.bufs)
                self.i += 1
                view = self.bufs[i][:, 0:kc * gc].rearrange("p (c n) -> p c n", c=kc)
                name = self.tag + "w%d" % i
                S.dma("pool" if self.cast else "sp", name, lambda e: e.dma_start(out=view, in_=w_ap[:, col0:col0 + gc].rearrange("(c p) n -> p c n", p=128)),
                      reads=[] if self.cast else ["wb16"], writes=[name])
                return view, name

        def proj_fm(ws, w_ap, K, col0, ncols, rhs_fn, rhs_names, n, evac, gcmax=None):
            kc = K // 128
            gc_full = min(ncols, (ws.size // kc) // 128 * 128)
            if gcmax:
                gc_full = min(gc_full, gcmax)
            j = 0
            g0 = 0
            while g0 < ncols:
                gc = min(gc_full, ncols - g0)
                view, wname = ws.load(w_ap, K, col0 + g0, gc)
                for jj in range((gc + 127) // 128):
                    m = min(128, gc - jj * 128)
                    pb = nextps()
                    for c in range(kc):
                        S.op("pe", lambda e, c=c, jj=jj, pb=pb, view=view, m=m: e.matmul(PS(pb)[0:m, 0:n], lhsT=view[:, c, jj * 128:jj * 128 + m], rhs=rhs_fn(c),
                                                                                start=(c == 0), stop=(c == kc - 1)),
                             reads=[wname] + rhs_names, writes=[pn(pb)], sig=(c == kc - 1))
                    nr_tick()
                    evac(j, pb)
                    j += 1
                g0 += gc

        def proj_tm(ws, w_ap, K, col0, ncols, lhs_fn, lhs_names, nsub, evac):
            kc = K // 128
            gc_full = min(ncols, (ws.size // kc) // 128 * 128, 512)
            g0 = 0
            while g0 < ncols:
                gc = min(gc_full, ncols - g0)
                view, wname = ws.load(w_ap, K, col0 + g0, gc)
                for ts in range(nsub):
                    pb = nextps()
                    for c in range(kc):
                        S.op("pe", lambda e, c=c, ts=ts, pb=pb, view=view, gc=gc: e.matmul(PS(pb)[:, 0:gc], lhsT=lhs_fn(c, ts), rhs=view[:, c, 0:gc],
                                                                                  start=(c == 0), stop=(c == kc - 1)),
                             reads=[wname] + lhs_names, writes=[pn(pb)], sig=(c == kc - 1))
                    nr_tick()
                    evac(ts, g0, gc, pb)
                g0 += gc

        cpy_i = [0]

        def copy_out(out_ap, out_name, in_ap, in_names):
            cpy_i[0] += 1
            if cpy_i[0] % 2:
                S.op("act", lambda e: e.activation(out=out_ap, in_=in_ap, func=AF.Copy), reads=in_names, writes=[out_name])
            else:
                S.op("dve", lambda e: e.tensor_copy(out=out_ap, in_=in_ap), reads=in_names, writes=[out_name])

        def stage_end():
            nr_drain()
            S.barrier()
            S.flush()

        with ExitStack() as es:
            set_rot(range(8))
            wkv = sbt(es, "wkv", [128, KC, 2048], BF16)
            for (c0, n, d0) in [(1024, 768, 0), (3096, 512, 768), (1792, 256, 1280), (3608, 512, 1536)]:
                for half in range(2):
                    S.dma("pool", "wkv", lambda e, c0=c0, n=n, d0=d0, half=half: e.dma_start(
                        out=wkv[:, half * 8:(half + 1) * 8, d0:d0 + n],
                        in_=w_in[half * 1024:(half + 1) * 1024, c0:c0 + n].rearrange("(c p) n -> p c n", p=128)), writes=["wkv"])
            for (src_, dst_, rows_) in [(w_in, w_in_b, D), (wo_nsa, wo_nsa_b, 1024), (wo_sb, wo_sb_b, 512), (wo_mem, wo_mem_b, 512),
                                        (w_out, w_out_b, D), (w_gate, w_gate_b, D), (w_up, w_up_b, D), (w_down, w_down_b, FFN)]:
                for r0 in range(0, rows_, 256):
                    S.dma("pool", "wcast", lambda e, src_=src_, dst_=dst_, r0=r0: e.dma_start(out=dst_[r0:r0 + 256, :], in_=src_[r0:r0 + 256, :]), writes=["wb16"])
            hnb = [sbt(es, "hn1", [128, KC, TQ], BF16) for _ in range(2)]
            cosTb = [sbt(es, "cosT", [128, TQ], F32) for _ in range(2)]
            sinTb = [sbt(es, "sinT", [128, TQ], F32) for _ in range(2)]
            nr = NR(es, TQ, "p1nr")
            rms = RMSRes(es, TQ, "p1rms")
            rope = Rope(es, TQ, "p1rope")
            stg = [sbt(es, "stg", [128, TQ], BF16) for _ in range(3)]
            vst = [sbt(es, "vst", [128, 768], BF16) for _ in range(2)]
            si = 0

            def p1_pre(gt_):
                rms.run(xT_all[:, gt_ * TQ:(gt_ + 1) * TQ], "attn", hnb[gt_ % 2], "hn1_%d" % (gt_ % 2))
                rope.run(pos_all[0:1, gt_ * TQ:(gt_ + 1) * TQ], cosTb[gt_ % 2][:], sinTb[gt_ % 2][:], "p1cs%d" % (gt_ % 2))
            p1_pre(0)
            for gt in range(NGT):
                hn = hnb[gt % 2]
                hname = "hn1_%d" % (gt % 2)
                cosT, sinT, csn = cosTb[gt % 2], sinTb[gt % 2], "p1cs%d" % (gt % 2)
                t0 = gt * TQ
                if gt + 1 < NGT:
                    p1_pre(gt + 1)
                for j in range(10):
                    pb = nextps()
                    for c in range(KC):
                        S.op("pe", lambda e, c=c, j=j, pb=pb, hn=hn: e.matmul(PS(pb)[:, :], lhsT=wkv[:, c, j * 128:(j + 1) * 128], rhs=hn[:, c, :],
                                                                          start=(c == 0), stop=(c == KC - 1)),
                             reads=["wkv", hname], writes=[pn(pb)], sig=(c == KC - 1))
                    nr_tick()
                    sb_ = stg[si % 3]
                    sname = "stg%d" % (si % 3)
                    si += 1
                    if j in (4, 5):
                        dst = ksT_d[j - 4, :, t0:t0 + TQ]
                        nr.run(pb, "ksn", sb_[:], sname, cosT[:], sinT[:], csn,
                               after=lambda dst=dst, sb_=sb_, sname=sname: S.dma("sp", "p1st_" + sname, lambda e: e.dma_start(out=dst, in_=sb_[:]), reads=[sname], writes=["kvdram"]))
                        continue
                    copy_out(sb_[:], sname, PS(pb)[:, :], [pn(pb)])
                    if j < 2:
                        dst = kcT_d[j, :, t0:t0 + TQ]
                    elif j < 4:
                        dst = vcT_d[j - 2, :, t0:t0 + TQ]
                    else:
                        dst = sbkT_d[j - 6, :, t0:t0 + TQ]
                    S.dma("sp", "p1st_" + sname, lambda e, dst=dst, sb_=sb_: e.dma_start(out=dst, in_=sb_[:]), reads=[sname], writes=["kvdram"])
                for ts in range(4):
                    vs_ = vst[ts % 2]
                    vname = "vst%d" % (ts % 2)
                    for (c0, n) in [(1280, 256), (1536, 512)]:
                        pb = nextps()
                        for c in range(KC):
                            S.op("pe", lambda e, c=c, pb=pb, hn=hn, ts=ts, c0=c0, n=n: e.matmul(PS(pb)[:, 0:n], lhsT=hn[:, c, ts * 128:(ts + 1) * 128],
                                                                                         rhs=wkv[:, c, c0:c0 + n], start=(c == 0), stop=(c == KC - 1)),
                                 reads=["wkv", hname], writes=[pn(pb)], sig=(c == KC - 1))
                        nr_tick()
                        copy_out(vs_[:, c0 - 1280:c0 - 1280 + n], vname, PS(pb)[:, 0:n], [pn(pb)])
                    S.dma("sp", "p1sv_" + vname, lambda e, vs_=vs_, ts=ts, t0=t0: e.dma_start(out=vtok_d[t0 + ts * 128:t0 + (ts + 1) * 128, :], in_=vs_[:]),
                          reads=[vname], writes=["kvdram"])
            stage_end()

        with ExitStack() as es:
            set_rot(range(8))
            kcs = sbt(es, "kcs", [128, T + 16], BF16)
            w1b = sbt(es, "w1b", [128, 32, 256], BF16)
            w2b = sbt(es, "w2b", [128, 2, 128], BF16)
            peb = sbt(es, "peb", [128, 32], BF16)
            pebias = sbt(es, "pebias", [128, 2], F32)
            hf = sbt(es, "hf", [128, NB], F32)
            h2 = sbt(es, "h2", [128, NB], F32)
            sg = sbt(es, "sg", [128, NB], F32)
            hid = sbt(es, "hid", [128, 2, NB], BF16)
            cosC = sbt(es, "cosC", [128, NCP], F32)
            sinC = sbt(es, "sinC", [128, NCP], F32)
            nrc = NR(es, NB, "cnr")
            ropec = Rope(es, NCP, "crope")
            ropec.run(pos_cmp[0:1, :], cosC[:], sinC[:], "ccs")
            S.op("pool", lambda e: e.memset(kcs[:, T:T + 16], 0.0), writes=["kcs_tail"])
            for kv in range(2):
                w1d, w2d, pename = (w1k, w2k, "pek") if kv == 0 else (w1v, w2v, "pev")
                S.dma("pool", "w1b", lambda e, w1d=w1d: e.dma_start(out=w1b[:], in_=w1d.rearrange("l d f -> d l f")), writes=["w1b"])
                S.dma("pool", "w2b", lambda e, w2d=w2d: e.dma_start(out=w2b[:], in_=w2d.rearrange("(c p) d -> p c d", p=128)), writes=["w2b"])
                S.op("dve", lambda e, pename=pename: e.tensor_copy(out=peb[:], in_=cst[:, CL[pename]:CL[pename] + 32]), reads=["cst"], writes=["peb"])
                for fc in range(2):
                    pb = nextps()
                    for l in range(32):
                        S.op("pe", lambda e, l=l, fc=fc, pb=pb: e.matmul(PS(pb)[:, 0:1], lhsT=w1b[:, l, fc * 128:(fc + 1) * 128], rhs=peb[:, l:l + 1],
                                                                     start=(l == 0), stop=(l == 31)), reads=["w1b", "peb"], writes=[pn(pb)])
                    S.op("dve", lambda e, fc=fc, pb=pb: e.tensor_copy(out=pebias[:, fc:fc + 1], in_=PS(pb)[:, 0:1]), reads=[pn(pb)], writes=["pebias"])
                for g in range(2):
                    src = kcT_d if kv == 0 else vcT_d
                    S.dma("sp", "kcs", lambda e, src=src, g=g: e.dma_start(out=kcs[:, 0:T], in_=src[g, :, :]), reads=["kvdram"], writes=["kcs"])
                    for nt in range(NCP // NB):
                        n0 = nt * NB
                        for fc in range(2):
                            pb = nextps()
                            for l in range(32):
                                a0 = 16 * n0 + l
                                S.op("pe", lambda e, l=l, fc=fc, pb=pb, a0=a0: e.matmul(PS(pb)[:, 0:NB], lhsT=w1b[:, l, fc * 128:(fc + 1) * 128],
                                                                                 rhs=kcs[:, a0:a0 + 16 * (NB - 1) + 1:16], start=(l == 0), stop=(l == 31)),
                                     reads=["w1b", "kcs", "kcs_tail"], writes=[pn(pb)])
                            S.op("act", lambda e, pb=pb, fc=fc: e.activation(out=hf[:], in_=PS(pb)[:, 0:NB], func=AF.Identity, bias=pebias[:, fc:fc + 1]),
                                 reads=[pn(pb), "pebias"], writes=["hf"])
                            S.op("dve", lambda e: e.tensor_tensor(out=h2[:], in0=hf[:], in1=hf[:], op=ALU.mult), reads=["hf"], writes=["h2"])
                            S.op("dve", lambda e: e.tensor_scalar(out=h2[:], in0=h2[:], scalar1=0.044715, scalar2=1.0, op0=ALU.mult, op1=ALU.add),
                                 reads=["h2"], writes=["h2"])
                            S.op("dve", lambda e: e.tensor_tensor(out=h2[:], in0=h2[:], in1=hf[:], op=ALU.mult), reads=["h2", "hf"], writes=["h2"])
                            S.op("act", lambda e: e.activation(out=sg[:], in_=h2[:], func=AF.Sigmoid, scale=float(2.0 * math.sqrt(2.0 / math.pi))),
                                 reads=["h2"], writes=["sg"])
                            S.op("dve", lambda e, fc=fc: e.tensor_tensor(out=hid[:, fc, :], in0=hf[:], in1=sg[:], op=ALU.mult), reads=["hf", "sg"], writes=["hid"])
                        if kv == 0:
                            pb = nextps()
                            for fc in range(2):
                                S.op("pe", lambda e, fc=fc, pb=pb: e.matmul(PS(pb)[:, 0:NB], lhsT=w2b[:, fc, :], rhs=hid[:, fc, :], start=(fc == 0), stop=(fc == 1)),
                                     reads=["w2b", "hid"], writes=[pn(pb)])
                            nrc.run(pb, "kcn", kcTs[:, g, n0:n0 + NB], "kcTs", cosC[:, n0:n0 + NB], sinC[:, n0:n0 + NB], "ccs")
                            nr_drain()
                        else:
                            for ns in range(NB // 128):
                                pb = nextps()
                                for fc in range(2):
                                    S.op("pe", lambda e, fc=fc, pb=pb, ns=ns: e.matmul(PS(pb)[:, 0:128], lhsT=hid[:, fc, ns * 128:(ns + 1) * 128], rhs=w2b[:, fc, :],
                                                                                  start=(fc == 0), stop=(fc == 1)), reads=["w2b", "hid"], writes=[pn(pb)])
                                copy_out(vcs[:, g, n0 // 128 + ns, :], "vcs", PS(pb)[:, 0:128], [pn(pb)])
            stage_end()

        with ExitStack() as es:
            set_rot(range(8))
            hm = sbt(es, "hm", [128, KC, 256], BF16)
            rmsm = RMS(es, 256, "mrms")
            nrm = NR(es, 256, "mnr")
            wsm = WStream(es, "wsm", 2, 4096, cast=True)
            rmsm.run(memT, "memn", hm, "hm")

            def ev_k(j, pb):
                nrm.run(pb, "mkn", kmem[:, j, :], "kmem")
            proj_fm(wsm, w_mem, D, 0, 512, lambda c: hm[:, c, :], ["hm"], 256, ev_k, gcmax=256)

            def ev_v(ts, g0, gc, pb):
                copy_out(vmem[:, ts, g0:g0 + gc], "vmem", PS(pb)[:, 0:gc], [pn(pb)])
            proj_tm(wsm, w_mem, D, 512, 512, lambda c, ts: hm[:, c, ts * 128:(ts + 1) * 128], ["hm"], 2, ev_v)
            stage_end()

        with ExitStack() as es_p2:
            cmB_ref[0] = sbt(es_p2, "cmB", [128, ML["_n"] - 768], BF16)
            S.dma("pool", "c2", lambda e: e.dma_start(out=cmB_ref[0][:], in_=cmat_d[:, 768:ML["_n"]]), writes=["cm"])
            ynsa_b = sbt(es_p2, "ynsa_b", [128, 8, TQ], BF16)
            ysb = sbt(es_p2, "ysb", [128, 4, TQ], BF16)
            ymem = sbt(es_p2, "ymem", [128, 4, TQ], BF16)
            for k in range(NT):
                o0 = k * TQ
                with ExitStack() as es_a:
                    qn = sbt(es_a, "qn", [128, 8, TQ], BF16)
                    qs = sbt(es_a, "qs", [128, 4, TQ], BF16)
                    qm = sbt(es_a, "qm", [128, 4, TQ], BF16)
                    ksTo = sbt(es_a, "ksTo", [128, 2, TQ], BF16)
                    kwTo = sbt(es_a, "kwTo", [128, 2, TQ], BF16)
                    kwTp = sbt(es_a, "kwTp", [128, 2, TQ], BF16)
                    sbkTo = sbt(es_a, "sbkTo", [128, 4, TQ], BF16)
                    vown = sbt(es_a, "vown", [128, 4, 1024], BF16)
                    vprev = sbt(es_a, "vprev", [128, 4, 256], BF16)
                    with ExitStack() as es:
                        set_rot(range(8))
                        hp = sbt(es, "hp", [128, KC, TQ], BF16)
                        hn = sbt(es, "hn", [128, KC, TQ], BF16)
                        cosO = sbt(es, "cosO", [128, TQ], F32)
                        sinO = sbt(es, "sinO", [128, TQ], F32)
                        cosP = sbt(es, "cosP", [128, TQ], F32)
                        sinP = sbt(es, "sinP", [128, TQ], F32)
                        g32 = sbt(es, "g32", [24, TQ], F32)
                        rms2 = RMS(es, TQ, "a1rms")
                        rope2 = Rope(es, TQ, "a1rope")
                        nr2 = NR(es, TQ, "a1nr")
                        ws = WStream(es, "a1ws", 2, 4096)
                        rms2.run(xT_own[:, o0:o0 + TQ], "attn", hn, "hn")
                        rms2.run(xT_prev[:, o0:o0 + TQ], "attn", hp, "hp")
                        rope2.run(pos_own[0:1, o0:o0 + TQ], cosO[:], sinO[:], "cso")
                        rope2.run(pos_prev[0:1, o0:o0 + TQ], cosP[:], sinP[:], "csp")
                        rh = lambda c: hn[:, c, :]
                        rp = lambda c: hp[:, c, :]
                        proj_fm(ws, w_in_b, D, 0, 1024, rh, ["hn"], TQ, lambda j, pb: nr2.run(pb, "qn", qn[:, j, :], "qn", cosO[:], sinO[:], "cso"), gcmax=256)
                        proj_fm(ws, w_in_b, D, 1536, 256, rh, ["hn"], TQ, lambda j, pb: nr2.run(pb, "ksn", ksTo[:, j, :], "ksTo", cosO[:], sinO[:], "cso"), gcmax=256)
                        proj_fm(ws, w_in_b, D, 2048, 256, rh, ["hn"], TQ, lambda j, pb: nr2.run(pb, "kwn", kwTo[:, j, :], "kwTo", cosO[:], sinO[:], "cso"), gcmax=256)
                        proj_fm(ws, w_in_b, D, 2048, 256, rp, ["hp"], TQ, lambda j, pb: nr2.run(pb, "kwn", kwTp[:, j, :], "kwTp", cosP[:], sinP[:], "csp"), gcmax=256)
                        proj_fm(ws, w_in_b, D, 2584, 512, rh, ["hn"], TQ, lambda j, pb: copy_out(qs[:, j, :], "qs", PS(pb)[:, :], [pn(pb)]), gcmax=256)
                        proj_fm(ws, w_in_b, D, 3096, 512, rh, ["hn"], TQ, lambda j, pb: copy_out(sbkTo[:, j, :], "sbkTo", PS(pb)[:, :], [pn(pb)]), gcmax=256)
                        proj_fm(ws, w_in_b, D, 4120, 512, rh, ["hn"], TQ, lambda j, pb: nr2.run(pb, "mqn", qm[:, j, :], "qm"), gcmax=256)

                        def ev_g(j, pb):
                            S.op("act", lambda e: e.activation(out=g32[:], in_=PS(pb)[0:24, :], func=AF.Sigmoid), reads=[pn(pb)], writes=["g32"])
                            S.dma("sp", "gst", lambda e: e.dma_start(out=gates_d, in_=g32[:]), reads=["g32"], writes=["gates_d"])
                        proj_fm(ws, w_in_b, D, 2560, 24, rh, ["hn"], TQ, ev_g)
                        lh = lambda c, ts: hn[:, c, ts * 128:(ts + 1) * 128]
                        lp = lambda c, ts: hp[:, c, ts * 128:(ts + 1) * 128]
                        proj_tm(ws, w_in_b, D, 1792, 256, lh, ["hn"], 4, lambda ts, g0, gc, pb: copy_out(vown[:, ts, 0:256], "vown", PS(pb)[:, 0:256], [pn(pb)]))
                        proj_tm(ws, w_in_b, D, 2304, 256, lh, ["hn"], 4, lambda ts, g0, gc, pb: copy_out(vown[:, ts, 256:512], "vown", PS(pb)[:, 0:256], [pn(pb)]))
                        proj_tm(ws, w_in_b, D, 3608, 512, lh, ["hn"], 4, lambda ts, g0, gc, pb: copy_out(vown[:, ts, 512 + g0:512 + g0 + gc], "vown", PS(pb)[:, 0:gc], [pn(pb)]))
                        proj_tm(ws, w_in_b, D, 2304, 256, lp, ["hp"], 4, lambda ts, g0, gc, pb: copy_out(vprev[:, ts, :], "vprev", PS(pb)[:, 0:256], [pn(pb)]))
                        S.dma("sp", "hnst", lambda e: e.dma_start(out=hn_d, in_=hn[:].rearrange("p c t -> p (c t)")), reads=["hn"], writes=["hn_d"])
                        stage_end()

                    with ExitStack() as es:
                        O_B, SUM_B, U0_B, U1_B = 0, 1, 2, 3
                        set_rot([4, 5, 6, 7])
                        ynsa = sbt(es, "ynsa", [128, 4, TQ], F32)
                        impacc = sbt(es, "impacc", [128, 2, 4, 256], F32)
                        selbT = sbt(es, "selbT", [128, 2, 2, TQ], BF16)
                        ownb = sbt(es, "ownb", [8, 2, TQ], BF16)
                        gbc = [sbt(es, "gbc", [128, 3, TQ], F32) for _ in range(1)]
                        Pb = [sbt(es, "Pb", [128, TQ], BF16) for _ in range(3)]
                        rec = sbt(es, "rec", [128, TQ], F32)
                        tmpf = sbt(es, "tmpf", [128, TQ], F32)
                        kbuf = [sbt(es, "kbuf", [128, TQ], BF16) for _ in range(3)]
                        vbuf = [sbt(es, "vbuf", [128, 4, 128], BF16) for _ in range(3)]
                        cmask = sbt(es, "cmask", [128, NCC, TQ], BF16)
                        rtok = sbt(es, "rtok", [128, 4], F32)
                        tk1 = sbt(es, "tk1", [128, 256], F32)
                        tk2 = sbt(es, "tk2", [128, 256], F32)
                        tk3 = sbt(es, "tk3", [128, 256], F32)
                        tk4 = sbt(es, "tk4", [128, 256], F32)
                        m8 = sbt(es, "m8", [128, 16], F32)
                        alb = sbt(es, "alb", [128, 256], BF16)
                        alT = sbt(es, "alT", [128, 2, TQ], BF16)
                        e32 = [sbt(es, "e32", [128, TQ], F32) for _ in range(2)]
                        ec32 = [sbt(es, "ec32", [128, TQ], F32) for _ in range(2)]
                        Lp = [sbt(es, "Lp", [128, TQ], BF16) for _ in range(3)]
                        Wb = [sbt(es, "Wb", [128, TQ], BF16) for _ in range(2)]
                        Rb = [sbt(es, "Rb", [128, TQ], BF16) for _ in range(2)]
                        ncmp = min(NCC, 2 * (k + 1))
                        gi = [0]

                        def load_gates(h):
                            b = gbc[0]
                            name = "gbc0"
                            gi[0] += 1
                            S.dma("sp", name, lambda e: e.dma_start(out=b[:], in_=gates_d[3 * h:3 * h + 3, :].rearrange("(o r) t -> o r t", o=1).to_broadcast([128, 3, TQ])),
                                  reads=["gates_d"], writes=[name])
                            return b, name

                        for j in range(ncmp):
                            S.op("dve", lambda e, j=j: e.tensor_scalar(out=cmask[:, j, :], in0=j2[:], scalar1=ccol("cthr", k * NCC + j), scalar2=None, op0=ALU.is_gt),
                                 reads=["j2", "cst"], writes=["cmask"])

                        pi = [0]

                        pend = [None]

                        def flush_pending():
                            if pend[0] is None:
                                return
                            ch, P, pname, i, n, extra = pend[0]
                            pend[0] = None
                            S.op("pe", lambda e: e.matmul(PS(O_B)[:, :], lhsT=ch["v"], rhs=P[:], start=(i == 0), stop=(i == n - 1)),
                                 reads=ch["vn"] + [pname], writes=[pn(O_B)])
                            S.op("pe", lambda e: e.matmul(PS(SUM_B)[:, :], lhsT=cmv("ones"), rhs=P[:], start=(i == 0), stop=(i == n - 1)),
                                 reads=["cm", pname], writes=[pn(SUM_B)])
                            if extra:
                                extra(i, P, pname, n)

                        def softmax_chunks(chunks, q_ap, q_names, extra=None, i0=0, n_total=None):
                            n = len(chunks) if n_total is None else n_total
                            for i_, ch in enumerate(chunks):
                                i = i0 + i_
                                sb_ = nextps()
                                masks = ch.get("masks", [])
                                S.op("pe", lambda e: e.matmul(PS(sb_)[:, :], lhsT=ch["kT"], rhs=q_ap, start=True, stop=(len(masks) == 0)),
                                     reads=ch["kn"] + q_names, writes=[pn(sb_)])
                                for mi, (ml, mr, mn) in enumerate(masks):
                                    S.op("pe", lambda e: e.matmul(PS(sb_)[:, :], lhsT=ml, rhs=mr, start=False, stop=(mi == len(masks) - 1)),
                                         reads=mn + ["cm"], writes=[pn(sb_)])
                                P = Pb[pi[0] % 3]
                                pname = "Pb%d" % (pi[0] % 3)
                                pi[0] += 1
                                bias = ch.get("bias")
                                if bias is None:
                                    bias = ccol("zero")
                                S.op("act", lambda e: e.activation(out=P[:], in_=PS(sb_)[:, :], func=AF.Exp, scale=SCALE, bias=bias),
                                     reads=[pn(sb_), "cst"], writes=[pname])
                                flush_pending()
                                pend[0] = (ch, P, pname, i, n, extra)

                        def finalize(out_ap, out_name, gate_ap=None, gate_name=None, accumulate=False):
                            flush_pending()
                            S.op("dve", lambda e: e.tensor_scalar(out=rec[:], in0=PS(SUM_B)[:, :], scalar1=1e-30, scalar2=None, op0=ALU.max), reads=[pn(SUM_B)], writes=["rec"])
                            S.op("dve", lambda e: e.reciprocal(out=rec[:], in_=rec[:]), reads=["rec"], writes=["rec"])
                            if gate_ap is not None:
                                S.op("dve", lambda e: e.tensor_tensor(out=rec[:], in0=rec[:], in1=gate_ap, op=ALU.mult), reads=["rec", gate_name], writes=["rec"])
                            if accumulate:
                                S.op("dve", lambda e: e.tensor_tensor(out=tmpf[:], in0=PS(O_B)[:, :], in1=rec[:], op=ALU.mult), reads=[pn(O_B), "rec"], writes=["tmpf"])
                                S.op("dve", lambda e: e.tensor_tensor(out=out_ap, in0=out_ap, in1=tmpf[:], op=ALU.add), reads=["tmpf", out_name], writes=[out_name])
                            else:
                                S.op("dve", lambda e: e.tensor_tensor(out=out_ap, in0=PS(O_B)[:, :], in1=rec[:], op=ALU.mult), reads=[pn(O_B), "rec"], writes=[out_name])

                        si2 = [0]

                        def stream_tile(kT_src, v_c0, gt):
                            i = si2[0] % 3
                            si2[0] += 1
                            kb, vb = kbuf[i], vbuf[i]
                            S.dma("sp", "kb%d" % i, lambda e: e.dma_start(out=kb[:], in_=kT_src[:, gt * TQ:(gt + 1) * TQ]), reads=["kvdram"], writes=["kbuf%d" % i])
                            S.dma("sp", "vb%d" % i, lambda e: e.dma_start(out=vb[:], in_=vtok_d[gt * TQ:(gt + 1) * TQ, v_c0:v_c0 + 128].rearrange("(ts p) d -> p ts d", p=128)),
                                  reads=["kvdram"], writes=["vbuf%d" % i])
                            return kb, vb, "kbuf%d" % i, "vbuf%d" % i

                        for g in range(2):
                            set_rot([6, 7])
                            for hg in range(4):
                                h = g * 4 + hg
                                gb, gname = load_gates(h)
                                chunks = []
                                for j in range(ncmp):
                                    chunks.append(dict(kT=kcTs[:, g, j * 128:(j + 1) * 128], kn=["kcTs"], v=vcs[:, g, j, :], vn=["vcs"],
                                                       masks=[(cmv("negI"), cmask[:, j, :], ["cmask"])]))

                                def extra(i, P, pname, n):
                                    for ts in range(4):
                                        ub = 2 + ts
                                        o_ = 0
                                        S.op("pe", lambda e, P=P, ts=ts, ub=ub, o_=o_, i=i, n=n: e.matmul(PS(ub)[:, o_:o_ + 257], lhsT=P[:, ts * 128:(ts + 1) * 128],
                                                                                                    rhs=cm[:, ML["OV"] + i * 257:ML["OV"] + (i + 1) * 257],
                                                                                                    start=(i == 0), stop=(i == n - 1)),
                                             reads=[pname, "cm"], writes=[pn(ub)])
                                softmax_chunks(chunks, qn[:, h, :], ["qn"], extra)
                                finalize(ynsa[:, hg, :], "ynsa", gb[:, 0, :], gname)
                                for ts in range(4):
                                    ub = 2 + ts
                                    o_ = 0
                                    S.op("dve", lambda e, ts=ts, ub=ub, o_=o_: e.tensor_scalar(out=rtok[:, ts:ts + 1], in0=PS(ub)[:, o_ + 256:o_ + 257], scalar1=1e-30, scalar2=None, op0=ALU.max),
                                         reads=[pn(ub)], writes=["rtok"])
                                    S.op("dve", lambda e, ts=ts: e.reciprocal(out=rtok[:, ts:ts + 1], in_=rtok[:, ts:ts + 1]), reads=["rtok"], writes=["rtok"])
                                    if hg == 0:
                                        S.op("dve", lambda e, ts=ts, ub=ub, o_=o_: e.tensor_scalar(out=impacc[:, g, ts, :], in0=PS(ub)[:, o_:o_ + 256], scalar1=rtok[:, ts:ts + 1], scalar2=None, op0=ALU.mult),
                                             reads=[pn(ub), "rtok"], writes=["impacc"])
                                    else:
                                        S.op("dve", lambda e, ts=ts, ub=ub, o_=o_: e.scalar_tensor_tensor(out=impacc[:, g, ts, :], in0=PS(ub)[:, o_:o_ + 256], scalar=rtok[:, ts:ts + 1],
                                                                                                       in1=impacc[:, g, ts, :], op0=ALU.mult, op1=ALU.add),
                                             reads=[pn(ub), "rtok", "impacc"], writes=["impacc"])
                            set_rot([2, 3, 4, 5, 6, 7])
                            for ts in range(4):
                                curc = ccol("cur", k * 4 + ts)
                                blk = rowt[:, 0:256]
                                S.op("dve", lambda e, curc=curc: e.tensor_scalar(out=tk1[:], in0=blk, scalar1=curc, scalar2=None, op0=ALU.subtract), reads=["rowt", "cst"], writes=["tk1"])
                                S.op("dve", lambda e: e.tensor_scalar(out=tk2[:], in0=tk1[:], scalar1=0.0, scalar2=None, op0=ALU.is_gt), reads=["tk1"], writes=["tk2"])
                                S.op("dve", lambda e: e.tensor_scalar(out=tk3[:], in0=tk1[:], scalar1=-1.0, scalar2=None, op0=ALU.is_ge), reads=["tk1"], writes=["tk3"])
                                S.op("dve", lambda e: e.tensor_tensor(out=tk3[:], in0=tk3[:], in1=tk2[:], op=ALU.subtract), reads=["tk3", "tk2"], writes=["tk3"])
                                S.op("dve", lambda e: e.tensor_scalar(out=tk4[:], in0=blk, scalar1=0.0, scalar2=None, op0=ALU.is_equal), reads=["rowt"], writes=["tk4"])
                                S.op("dve", lambda e: e.tensor_tensor(out=tk3[:], in0=tk3[:], in1=tk4[:], op=ALU.max), reads=["tk3", "tk4"], writes=["tk3"])
                                S.op("dve", lambda e: e.tensor_tensor(out=tk3[:], in0=tk3[:], in1=tk2[:], op=ALU.subtract), reads=["tk3", "tk2"], writes=["tk3"])
                                S.op("dve", lambda e, ts=ts: e.scalar_tensor_tensor(out=tk1[:], in0=tk3[:], scalar=1.0e4, in1=impacc[:, g, ts, :], op0=ALU.mult, op1=ALU.add),
                                     reads=["tk3", "impacc"], writes=["tk1"])
                                S.op("dve", lambda e: e.max(out=m8[:, 0:8], in_=tk1[:]), reads=["tk1"], writes=["m8a"])
                                S.op("dve", lambda e: e.match_replace(out=tk4[:], in_to_replace=m8[:, 0:8], in_values=tk1[:], imm_value=-3.0e4), reads=["tk1", "m8a"], writes=["tk4"])
                                S.op("dve", lambda e: e.max(out=m8[:, 8:16], in_=tk4[:]), reads=["tk4"], writes=["m8b"])
                                S.op("dve", lambda e: e.tensor_scalar(out=tk4[:], in0=tk1[:], scalar1=m8[:, 15:16], scalar2=None, op0=ALU.is_ge), reads=["tk1", "m8b"], writes=["tk4"])
                                S.op("dve", lambda e: e.tensor_scalar(out=tk2[:], in0=tk2[:], scalar1=-1.0, scalar2=1.0, op0=ALU.mult, op1=ALU.add), reads=["tk2"], writes=["tk2"])
                                S.op("dve", lambda e: e.tensor_tensor(out=alb[:], in0=tk4[:], in1=tk2[:], op=ALU.mult), reads=["tk4", "tk2"], writes=["alb"])
                                for half in range(2):
                                    pb = nextps()
                                    S.op("pe", lambda e, half=half, pb=pb: e.matmul(PS(pb)[:, 0:128], lhsT=alb[:, half * 128:(half + 1) * 128], rhs=cmv("ident"), start=True, stop=True),
                                         reads=["alb", "cm"], writes=[pn(pb)])
                                    copy_out(alT[:, half, ts * 128:(ts + 1) * 128], "alT", PS(pb)[:, 0:128], [pn(pb)])
                            pb = nextps()
                            for half in range(2):
                                osel = cm[:, ML["OwnSel"] + k * 16 + half * 8:ML["OwnSel"] + k * 16 + half * 8 + 8]
                                S.op("pe", lambda e, half=half, pb=pb, osel=osel: e.matmul(PS(pb)[0:8, :], lhsT=osel, rhs=alT[:, half, :], start=(half == 0), stop=(half == 1)),
                                     reads=["alT", "cm"], writes=[pn(pb)])
                            S.op("dve", lambda e, pb=pb: e.tensor_scalar(out=ownb[:, g, :], in0=PS(pb)[0:8, :], scalar1=-1.0, scalar2=-NEG, op0=ALU.add, op1=ALU.mult),
                                 reads=[pn(pb)], writes=["ownb"])
                            for half in range(2):
                                S.op("dve", lambda e, half=half: e.tensor_scalar(out=selbT[:, g, half, :], in0=alT[:, half, :], scalar1=cst[:, CL["_om"] + k * 2 + half:CL["_om"] + k * 2 + half + 1],
                                                                               scalar2=None, op0=ALU.mult), reads=["alT", "cst"], writes=["selbT"])
                                S.op("dve", lambda e, half=half: e.tensor_scalar(out=selbT[:, g, half, :], in0=selbT[:, g, half, :], scalar1=-1.0, scalar2=-NEG, op0=ALU.add, op1=ALU.mult),
                                     reads=["selbT"], writes=["selbT"])
                            for hg in range(4):
                                h = g * 4 + hg
                                gb, gname = load_gates(h)
                                chunks = []
                                for c_ in range(4):
                                    chunks.append(dict(kT=ksTo[:, g, c_ * 128:(c_ + 1) * 128], kn=["ksTo"], v=vown[:, c_, g * 128:(g + 1) * 128], vn=["vown"],
                                                       masks=[(cm[0:8, ML["Eown"] + c_ * 128:ML["Eown"] + (c_ + 1) * 128], ownb[:, g, :], ["ownb"]),
                                                              (cmv("negI"), cm[:, ML["DiagN"] + c_ * 512:ML["DiagN"] + (c_ + 1) * 512], [])]))
                                ntot = 4 + 4 * (8 * k + 8)
                                softmax_chunks(chunks, qn[:, h, :], ["qn"], None, 0, ntot)
                                for gt in range(8 * k + 8):
                                    kb, vb, kname, vname = stream_tile(ksT_d[g], g * 128, gt)
                                    chunks = []
                                    for c_ in range(4):
                                        cg = gt * 4 + c_
                                        chunks.append(dict(kT=kb[:, c_ * 128:(c_ + 1) * 128], kn=[kname], v=vb[:, c_, :], vn=[vname],
                                                           masks=[(cm[:, ML["E64"] + (cg % 64) * 128:ML["E64"] + (cg % 64 + 1) * 128], selbT[:, g, cg // 64, :], ["selbT"])]))
                                    softmax_chunks(chunks, qn[:, h, :], ["qn"], None, 4 + 4 * gt, ntot)
                                finalize(ynsa[:, hg, :], "ynsa", gb[:, 1, :], gname, accumulate=True)
                            for hg in range(4):
                                h = g * 4 + hg
                                gb, gname = load_gates(h)
                                chunks = []
                                for c_ in range(4):
                                    chunks.append(dict(kT=kwTp[:, g, c_ * 128:(c_ + 1) * 128], kn=["kwTp"], v=vprev[:, c_, g * 128:(g + 1) * 128], vn=["vprev"],
                                                       masks=[(cmv("negI"), cm[:, ML["DiagW"] + c_ * 512:ML["DiagW"] + (c_ + 1) * 512], [])], bias=ccol("pv", k)))
                                for c_ in range(4):
                                    chunks.append(dict(kT=kwTo[:, g, c_ * 128:(c_ + 1) * 128], kn=["kwTo"], v=vown[:, c_, 256 + g * 128:256 + (g + 1) * 128], vn=["vown"],
                                                       masks=[(cmv("negI"), cm[:, ML["DiagN"] + c_ * 512:ML["DiagN"] + (c_ + 1) * 512], [])]))
                                softmax_chunks(chunks, qn[:, h, :], ["qn"])
                                finalize(ynsa[:, hg, :], "ynsa", gb[:, 2, :], gname, accumulate=True)
                                copy_out(ynsa_b[:, h, :], "ynsa_b", ynsa[:, hg, :], ["ynsa"])
                        for h in range(4):
                            chunks = [dict(kT=kmem[:, h, ms * 128:(ms + 1) * 128], kn=["kmem"], v=vmem[:, ms, h * 128:(h + 1) * 128], vn=["vmem"]) for ms in range(2)]
                            softmax_chunks(chunks, qm[:, h, :], ["qm"])
                            finalize(ymem[:, h, :], "ymem")
                        for h in range(4):
                            nch = 4 + (8 * k + 8) * 4

                            def sb_gen(h=h):
                                for c_ in (3, 2, 1, 0):
                                    yield dict(kT=sbkTo[:, h, c_ * 128:(c_ + 1) * 128], kn=["sbkTo"], v=vown[:, c_, 512 + h * 128:512 + (h + 1) * 128], vn=["vown"],
                                               diag=cm[:, ML["DiagS"] + c_ * 512:ML["DiagS"] + (c_ + 1) * 512], bias=ccol("zero"))
                                for tl in range(8 * k + 7, -1, -1):
                                    kb, vb, kname, vname = stream_tile(sbkT_d[h], 256 + h * 128, tl)
                                    j_ = tl - 8 * k
                                    bias = ccol("cw", j_) if j_ >= 0 else ccol("zero")
                                    for c_ in (3, 2, 1, 0):
                                        yield dict(kT=kb[:, c_ * 128:(c_ + 1) * 128], kn=[kname], v=vb[:, c_, :], vn=[vname], bias=bias)
                            gen = sb_gen()
                            st = {}
                            Rprev = [None]

                            def stage1(ci):
                                ch = next(gen)
                                zb = nextps()
                                dg = ch.get("diag")
                                S.op("pe", lambda e: e.matmul(PS(zb)[:, :], lhsT=ch["kT"], rhs=qs[:, h, :], start=True, stop=(dg is None)),
                                     reads=ch["kn"] + ["qs"], writes=[pn(zb)])
                                if dg is not None:
                                    S.op("pe", lambda e: e.matmul(PS(zb)[:, :], lhsT=cmv("negI"), rhs=dg, start=False, stop=True), reads=["cm"], writes=[pn(zb)])
                                e_ = e32[ci % 2]
                                en = "e32_%d" % (ci % 2)
                                S.op("act", lambda e: e.activation(out=e_[:], in_=PS(zb)[:, :], func=AF.Exp, scale=SCALE, bias=ch["bias"]),
                                     reads=[pn(zb), "cst"], writes=[en])
                                L_ = Lp[ci % 3]
                                ln_ = "Lp%d" % (ci % 3)
                                S.op("act", lambda e: e.activation(out=L_[:], in_=e_[:], func=AF.Ln, bias=ccol("one")), reads=[en, "cst"], writes=[ln_])
                                st[ci] = dict(ch=ch, e_=e_, en=en, L_=L_, ln_=ln_)

                            def stage2(ci):
                                d_ = st[ci]
                                L_, ln_, e_, en = d_["L_"], d_["ln_"], d_["e_"], d_["en"]
                                cb = nextps()
                                Rp = Rprev[0]
                                S.op("pe", lambda e: e.matmul(PS(cb)[:, :], lhsT=cmv("UIneg"), rhs=L_[:], start=True, stop=(Rp is None)),
                                     reads=[ln_, "cm"], writes=[pn(cb)])
                                if Rp is not None:
                                    S.op("pe", lambda e: e.matmul(PS(cb)[:, :], lhsT=cmv("onesneg"), rhs=Rp[0][:], start=False, stop=True),
                                         reads=[Rp[1], "cm"], writes=[pn(cb)])
                                ec_ = ec32[ci % 2]
                                ecn = "ec32_%d" % (ci % 2)
                                S.op("act", lambda e: e.activation(out=ec_[:], in_=PS(cb)[:, :], func=AF.Exp), reads=[pn(cb)], writes=[ecn])
                                W_ = Wb[ci % 2]
                                wn_ = "Wb%d" % (ci % 2)
                                S.op("dve", lambda e: e.tensor_tensor(out=W_[:], in0=e_[:], in1=ec_[:], op=ALU.mult), reads=[en, ecn], writes=[wn_])
                                Rn = Rb[ci % 2]
                                rn_ = "Rb%d" % (ci % 2)
                                if Rp is None:
                                    S.op("dve", lambda e: e.tensor_copy(out=Rn[:], in_=L_[:]), reads=[ln_], writes=[rn_])
                                else:
                                    S.op("dve", lambda e: e.tensor_tensor(out=Rn[:], in0=Rp[0][:], in1=L_[:], op=ALU.add), reads=[ln_, Rp[1]], writes=[rn_])
                                Rprev[0] = (Rn, rn_)
                                d_["W_"], d_["wn_"] = W_, wn_

                            def stage3(ci):
                                d_ = st.pop(ci)
                                ch, W_, wn_ = d_["ch"], d_["W_"], d_["wn_"]
                                S.op("pe", lambda e: e.matmul(PS(O_B)[:, :], lhsT=ch["v"], rhs=W_[:], start=(ci == 0), stop=(ci == nch - 1)),
                                     reads=ch["vn"] + [wn_], writes=[pn(O_B)])
                            for it in range(nch + 2):
                                if it < nch:
                                    stage1(it)
                                if 1 <= it <= nch:
                                    stage2(it - 1)
                                if it >= 2:
                                    stage3(it - 2)
                            copy_out(ysb[:, h, :], "ysb", PS(O_B)[:, :], [pn(O_B)])
                        stage_end()

                with ExitStack() as es_b:
                    x1 = sbt(es_b, "x1", [128, KC, TQ], F32)
                    hn = sbt(es_b, "hnB", [128, KC, TQ], BF16)
                    S.dma("sp", "hnld", lambda e: e.dma_start(out=hn[:].rearrange("p c t -> p (c t)"), in_=hn_d), reads=["hn_d"], writes=["hn"])
                    with ExitStack() as es:
                        set_rot(range(8))
                        mixed = sbt(es, "mixed", [128, KC, TQ], BF16)
                        sig = sbt(es, "sig", [128, 6, TQ], F32)
                        acc = [sbt(es, "acc", [128, TQ], F32) for _ in range(2)]
                        tmpm = [sbt(es, "tmpm", [128, TQ], F32) for _ in range(2)]
                        wsg = WStream(es, "b1wg", 2, 4096)
                        wso = WStream(es, "b1wo", 3, 2048)
                        for dc2 in range(KC // 2):
                            for br in range(3):
                                def ev_s(j, pb, br=br):
                                    S.op("act", lambda e: e.activation(out=sig[:, br * 2 + j, :], in_=PS(pb)[:, :], func=AF.Sigmoid), reads=[pn(pb)], writes=["sig%d" % (br * 2 + j)])
                                proj_fm(wsg, w_in_b, D, 4632 + br * D + dc2 * 256, 256, lambda c: hn[:, c, :], ["hn"], TQ, ev_s)
                            for br, (wo, K_, y_, yn_) in enumerate([(wo_nsa_b, 1024, ynsa_b, "ynsa_b"), (wo_sb_b, 512, ysb, "ysb"), (wo_mem_b, 512, ymem, "ymem")]):
                                def ev_a(j, pb, br=br):
                                    sn = "sig%d" % (br * 2 + j)
                                    if br == 0:
                                        S.op("dve", lambda e: e.tensor_tensor(out=acc[j][:], in0=PS(pb)[:, :], in1=sig[:, j, :], op=ALU.mult), reads=[pn(pb), sn], writes=["acc%d" % j])
                                    else:
                                        S.op("dve", lambda e: e.tensor_tensor(out=tmpm[j][:], in0=PS(pb)[:, :], in1=sig[:, br * 2 + j, :], op=ALU.mult), reads=[pn(pb), sn], writes=["tmpm%d" % j])
                                        if br == 1:
                                            S.op("dve", lambda e: e.tensor_tensor(out=acc[j][:], in0=acc[j][:], in1=tmpm[j][:], op=ALU.add), reads=["acc%d" % j, "tmpm%d" % j], writes=["acc%d" % j])
                                        else:
                                            S.op("dve", lambda e: e.tensor_tensor(out=mixed[:, dc2 * 2 + j, :], in0=acc[j][:], in1=tmpm[j][:], op=ALU.add), reads=["acc%d" % j, "tmpm%d" % j], writes=["mixed"])
                                proj_fm(wso, wo, K_, dc2 * 256, 256, lambda c, y_=y_: y_[:, c, :], [yn_], TQ, ev_a)
                        xr = [sbt(es, "xr", [128, TQ], F32) for _ in range(2)]

                        def ev_o(j, pb):
                            b = xr[j % 2]
                            S.dma("sp", "xr%d" % (j % 2), lambda e: e.dma_start(out=b[:], in_=xT_own[j * 128:(j + 1) * 128, o0:o0 + TQ]), writes=["xr%d" % (j % 2)])
                            S.op("dve", lambda e: e.tensor_tensor(out=x1[:, j, :], in0=PS(pb)[:, :], in1=b[:], op=ALU.add), reads=[pn(pb), "xr%d" % (j % 2)], writes=["x1"])
                        proj_fm(wsg, w_out_b, D, 0, D, lambda c: mixed[:, c, :], ["mixed"], TQ, ev_o, gcmax=256)
                        stage_end()
                    with ExitStack() as es:
                        set_rot(range(8))
                        act_all = sbt(es, "act_all", [128, FFN // 256, TQ], BF16)
                        sq2 = [sbt(es, "sq2", [128, TQ], BF16) for _ in range(2)]
                        rs2 = sbt(es, "rs2", [128, TQ], F32)
                        sgl = [sbt(es, "sgl", [128, TQ], F32) for _ in range(2)]
                        ost = [sbt(es, "ost", [128, TQ], F32) for _ in range(2)]
                        wsf = WStream(es, "b2wf", 3, 5632)
                        pb0 = nextps()
                        for c in range(KC):
                            S.op("act", lambda e, c=c: e.activation(out=sq2[c % 2][:], in_=x1[:, c, :], func=AF.Square), reads=["x1"], writes=["sq2_%d" % (c % 2)])
                            S.op("pe", lambda e, c=c: e.matmul(PS(pb0)[:, :], lhsT=cmv("ones"), rhs=sq2[c % 2][:], start=(c == 0), stop=(c == KC - 1)),
                                 reads=["sq2_%d" % (c % 2), "cm"], writes=[pn(pb0)])
                        S.op("act", lambda e: e.activation(out=rs2[:], in_=PS(pb0)[:, :], func=AF.Sqrt, scale=1.0 / D, bias=ccol("eps")), reads=[pn(pb0), "cst"], writes=["rs2"])
                        S.op("dve", lambda e: e.reciprocal(out=rs2[:], in_=rs2[:]), reads=["rs2"], writes=["rs2"])
                        for c in range(KC):
                            S.op("dve", lambda e, c=c: e.scalar_tensor_tensor(out=hn[:, c, :], in0=x1[:, c, :], scalar=ccol("ffn", c), in1=rs2[:], op0=ALU.mult, op1=ALU.mult),
                                 reads=["x1", "rs2", "cst"], writes=["hn"])
                        for hf_ in range(2):
                          for fi2 in range(FFN // 512):
                            def ev_g2(j, pb):
                                S.op("act", lambda e: e.activation(out=sgl[j][:], in_=PS(pb)[:, :], func=AF.Silu), reads=[pn(pb)], writes=["sgl%d" % j])
                            proj_fm(wsf, w_gate_b, D, hf_ * 2816 + fi2 * 256, 256, lambda c: hn[:, c, :], ["hn"], TQ, ev_g2)

                            def ev_u(j, pb, fi2=fi2):
                                S.op("dve", lambda e: e.tensor_tensor(out=act_all[:, fi2 * 2 + j, :], in0=PS(pb)[:, :], in1=sgl[j][:], op=ALU.mult), reads=[pn(pb), "sgl%d" % j], writes=["act_all"])
                            proj_fm(wsf, w_up_b, D, hf_ * 2816 + fi2 * 256, 256, lambda c: hn[:, c, :], ["hn"], TQ, ev_u)

                          def ev_d(j, pb, hf_=hf_):
                            if hf_ == 0:
                                S.op("dve", lambda e: e.tensor_tensor(out=x1[:, j, :], in0=PS(pb)[:, :], in1=x1[:, j, :], op=ALU.add), reads=[pn(pb), "x1"], writes=["x1"])
                                return
                            b = ost[j % 2]
                            S.op("dve", lambda e: e.tensor_tensor(out=b[:], in0=PS(pb)[:, :], in1=x1[:, j, :], op=ALU.add), reads=[pn(pb), "x1"], writes=["ost%d" % (j % 2)])
                            S.dma("sp", "ost%d" % (j % 2), lambda e: e.dma_start(out=outT[j * 128:(j + 1) * 128, o0:o0 + TQ], in_=b[:]), reads=["ost%d" % (j % 2)], writes=["outT"])
                          proj_fm(wsf, w_down_b[hf_ * 2816:(hf_ + 1) * 2816, :], 2816, 0, D, lambda c: act_all[:, c, :], ["act_all"], TQ, ev_d, gcmax=256)

                        stage_end()
        S.barrier()
        S.flush()
    return nc


def host_consts(T, core):
    NT = T // (NCORE * TQ)
    NCC = T // 2048
    CL = cst_layout(NT, NCC)
    ML = cm_layout(NT, NCC)
    p = np.arange(128)
    cm = np.zeros((128, ML["_n"]), np.float32)
    cm[p, ML["ident"] + p] = 1.0
    cm[p, ML["negI"] + p] = NEG
    cm[:, ML["ones"]:ML["ones"] + 128] = 1.0
    cm[:, ML["onesneg"]:ML["onesneg"] + 128] = -1.0
    cm[:, ML["UIneg"]:ML["UIneg"] + 128] = -(p[:, None] >= p[None, :]).astype(np.float32)
    for m in range(128):
        if m < 64:
            cm[m + 64, ML["prot"] + m] = -1.0
        else:
            cm[m - 64, ML["prot"] + m] = 1.0
    u = np.arange(8192)
    cm[:, ML["E64"]:ML["E64"] + 8192] = (p[:, None] == (u[None, :] // 64)).astype(np.float32)
    for j in range(NCC):
        n = 128 * j + p
        s = np.arange(256)
        ov = ((n[:, None] >= 4 * s[None, :] - 1) & (n[:, None] <= 4 * s[None, :] + 3) & (n[:, None] < T // 16 - 1)).astype(np.float32)
        cm[:, ML["OV"] + j * 257:ML["OV"] + j * 257 + 256] = ov
        cm[:, ML["OV"] + j * 257 + 256] = (n < T // 16 - 1).astype(np.float32)
    t = np.arange(512)
    for c_ in range(4):
        sg_ = 128 * c_ + p
        cm[:, ML["DiagN"] + c_ * 512:ML["DiagN"] + (c_ + 1) * 512] = (sg_[:, None] > t[None, :]).astype(np.float32)
        cm[:, ML["DiagS"] + c_ * 512:ML["DiagS"] + (c_ + 1) * 512] = (sg_[:, None] >= t[None, :]).astype(np.float32)
        cm[:, ML["DiagW"] + c_ * 512:ML["DiagW"] + (c_ + 1) * 512] = (sg_[:, None] <= t[None, :]).astype(np.float32)
        s_ = np.arange(128)
        for r in range(2):
            cm[2 * c_ + r, ML["Eown"] + c_ * 128 + s_] = (s_ // 64 == r).astype(np.float32)
    for k in range(NT):
        ob0 = 8 * (8 * k + core)
        for half in range(2):
            for b in range(8):
                blk = ob0 + b
                if blk // 128 == half:
                    cm[blk % 128, ML["OwnSel"] + k * 16 + half * 8 + b] = 1.0
    cst = np.zeros((128, CL["_n"]), np.float32)
    cst[:, CL["eps"]] = 1e-6
    cst[:, CL["one"]] = 1.0
    cst[:, CL["halfpi"]] = np.float32(np.pi / 2)
    cst[:, CL["tiny"]] = 1e-30
    inv = (np.float32(1.0) / np.power(np.float32(10000.0), np.arange(0, 128, 2, dtype=np.float32) / np.float32(128))).astype(np.float32)
    cst[:, CL["inv"]] = inv[p % 64]
    for j in range(8):
        cst[:, CL["cw"] + j] = 0.0 if j < core else NEG
    for k in range(NT):
        gt = 8 * k + core
        cst[:, CL["pv"] + k] = NEG if gt == 0 else 0.0
        for ts in range(4):
            cst[:, CL["cur"] + k * 4 + ts] = (gt * 512 + ts * 128 + p) // 64
        for j in range(NCC):
            cst[:, CL["cthr"] + k * NCC + j] = gt * 512 - 2048 * j
        ob0 = 8 * gt
        for half in range(2):
            cst[:, CL["_om"] + k * 2 + half] = ((half * 128 + p) < ob0).astype(np.float32)
    rowt = np.zeros((128, 256), np.float32)
    rowt[:, 0:256] = np.arange(256)[None, :]
    j2 = (16 * p[:, None] + 31 - t[None, :]).astype(np.float32)
    return cst, cm, rowt, j2, CL


_NC_CACHE = {}


def kernel(x, mem, positions, attn_norm, w_in, nsa_q_norm, nsa_kc_norm, nsa_ks_norm, nsa_kw_norm,
           cmp_k_pe, cmp_k_w1, cmp_k_w2, cmp_v_pe, cmp_v_w1, cmp_v_w2, mem_norm, w_mem_kv,
           mem_q_norm, mem_k_norm, w_o_nsa, w_o_sb, w_o_mem, w_out, ffn_norm,
           w_ffn_gate, w_ffn_up, w_ffn_down):
    x = np.asarray(x)
    T = x.shape[1]
    NT = T // (NCORE * TQ)
    NCC = T // 2048
    NCP = NCC * 128
    if T not in _NC_CACHE:
        _NC_CACHE[T] = build(T)
    nc = _NC_CACHE[T]
    f = lambda a: np.ascontiguousarray(np.asarray(a, dtype=np.float32))
    xT = np.ascontiguousarray(np.asarray(x)[0].T)
    pos = np.asarray(positions).astype(np.int32)
    pos_cmp = np.zeros((1, NCP), np.int32)
    pc = pos[0, 31::16]
    pos_cmp[0, :len(pc)] = pc
    common = {
        "xT_all": xT, "memT": np.ascontiguousarray(np.asarray(mem)[0].T), "pos_all": pos, "pos_cmp": pos_cmp,
        "w_in": f(w_in[0]), "w1k": f(cmp_k_w1[0]), "w2k": f(cmp_k_w2[0]), "w1v": f(cmp_v_w1[0]), "w2v": f(cmp_v_w2[0]),
        "w_mem": f(w_mem_kv[0]), "wo_nsa": f(w_o_nsa[0]), "wo_sb": f(w_o_sb[0]), "wo_mem": f(w_o_mem[0]), "w_out": f(w_out[0]),
        "w_gate": f(w_ffn_gate[0]), "w_up": f(w_ffn_up[0]), "w_down": f(w_ffn_down[0]),
    }
    in_maps = []
    for core in range(NCORE):
        cst, cm, rowt, j2, CL = host_consts(T, core)
        for name, arr in [("attn", attn_norm), ("ffn", ffn_norm), ("memn", mem_norm)]:
            cst[:, CL[name]:CL[name] + 16] = np.asarray(arr)[0].reshape(16, 128).T
        for name, arr in [("qn", nsa_q_norm), ("kcn", nsa_kc_norm), ("ksn", nsa_ks_norm), ("kwn", nsa_kw_norm), ("mqn", mem_q_norm), ("mkn", mem_k_norm)]:
            cst[:, CL[name]] = np.asarray(arr)[0]
        cst[:, CL["pek"]:CL["pek"] + 32] = np.asarray(cmp_k_pe)[0].T
        cst[:, CL["pev"]:CL["pev"] + 32] = np.asarray(cmp_v_pe)[0].T
        own = np.zeros((D, NT * TQ), np.float32)
        prev = np.zeros((D, NT * TQ), np.float32)
        pown = np.zeros((1, NT * TQ), np.int32)
        pprev = np.zeros((1, NT * TQ), np.int32)
        for k in range(NT):
            gt = 8 * k + core
            own[:, k * TQ:(k + 1) * TQ] = xT[:, gt * TQ:(gt + 1) * TQ]
            pown[0, k * TQ:(k + 1) * TQ] = pos[0, gt * TQ:(gt + 1) * TQ]
            if gt > 0:
                prev[:, k * TQ:(k + 1) * TQ] = xT[:, (gt - 1) * TQ:gt * TQ]
                pprev[0, k * TQ:(k + 1) * TQ] = pos[0, (gt - 1) * TQ:gt * TQ]
        m = dict(common)
        m.update({"xT_own": own, "xT_prev": prev, "pos_own": pown, "pos_prev": pprev, "cst": cst, "cmat": cm, "rowt": rowt, "j2": j2})
        in_maps.append(m)
    res = run_bass_kernel_spmd(nc, in_maps, core_ids=list(range(NCORE)))
    out = np.zeros((1, T, D), np.float32)
    for core in range(NCORE):
        oT = np.asarray(res.results[core]["outT"])
        for k in range(NT):
            gt = 8 * k + core
            out[0, gt * TQ:(gt + 1) * TQ, :] = oT[:, k * TQ:(k + 1) * TQ].T
    return out
```

```python
import math
from contextlib import ExitStack
import numpy as np
import concourse.bass as bass
import concourse.mybir as mybir
from concourse.bass_utils import run_bass_kernel_spmd

F32 = mybir.dt.float32
BF16 = mybir.dt.bfloat16
I32 = mybir.dt.int32
ALU = mybir.AluOpType
AF = mybir.ActivationFunctionType

D = 2048
KC = 16
NCORE = 8
TQ = 512
FFN = 5632
NEG = -30000.0
SCALE = 128 ** -0.5
MAGIC = 12582912.0
C1 = 6.28125
C2 = float(2 * np.pi - 6.28125)
PI_LO = 3.1415925

COMPUTE_Q = ("pe", "act", "dve", "pool")
ALLQ = ("pe", "act", "dve", "pool", "sp")


class Sched:
    def __init__(self, nc, es):
        self.nc = nc
        self.es = es
        self.q = {k: [] for k in ALLQ}
        self.cnt = {k: 0 for k in COMPUTE_Q}
        self.sems = {k: es.enter_context(nc.semaphore("s_" + k)) for k in COMPUTE_Q}
        self.dcnt = {}
        self.lastw = {}
        self.readers = {}
        self.seen = {k: {} for k in ALLQ}
        self.nops = 0

    def _deps(self, q, reads, writes):
        need = {}

        def add(tok):
            if tok is None:
                return
            k, v = tok
            if need.get(k, 0) < v:
                need[k] = v
        for b in reads:
            add(self.lastw.get(b))
        for b in writes:
            add(self.lastw.get(b))
            for t in self.readers.get(b, ()):
                add(t)
        out = []
        for k, v in need.items():
            if k == q and q == "pe":
                continue
            if self.seen[q].get(k, 0) >= v:
                continue
            self.seen[q][k] = v
            out.append((k, v))
        return out

    def _commit(self, tok, reads, writes):
        for b in reads:
            self.readers.setdefault(b, []).append(tok)
        for b in writes:
            self.lastw[b] = tok
            self.readers[b] = []

    def _rec(self, fn):
        class _R:
            def __getattr__(s, name):
                def f(*a, **kw):
                    s.call = (name, a, kw)
                    return None
                return f
        r = _R()
        fn(r)
        name, a, kw = r.call
        return lambda eng: getattr(eng, name)(*a, **kw)

    def op(self, q, fn, reads=(), writes=(), sig=True):
        fn = self._rec(fn)
        waits = self._deps(q, reads, writes)
        if not sig:
            tok = (q, self.cnt[q] + 1)
            self.q[q].append((waits, fn, None, 0))
            self._commit(tok, reads, writes)
            self.nops += 1
            return tok
        self.cnt[q] += 1
        tok = (q, self.cnt[q])
        self.q[q].append((waits, fn, q, 1))
        self._commit(tok, reads, writes)
        self.nops += 1
        return tok

    def dma(self, q, key, fn, reads=(), writes=()):
        fn = self._rec(fn)
        waits = self._deps(q, reads, writes)
        k = "dma:" + key
        if k not in self.sems:
            self.sems[k] = self.es.enter_context(self.nc.semaphore("d_" + key))
            self.dcnt[k] = 0
        self.dcnt[k] += 16
        tok = (k, self.dcnt[k])
        self.q[q].append((waits, fn, k, 16))
        self._commit(tok, reads, writes)
        self.nops += 1
        return tok

    def barrier(self):
        allw = [(k, v) for k, v in self.cnt.items() if v > 0] + [(k, v) for k, v in self.dcnt.items() if v > 0]
        for q in ALLQ:
            w = [(k, v) for k, v in allw if self.seen[q].get(k, 0) < v]
            for k, v in w:
                self.seen[q][k] = v
            if w:
                self.q[q].append((w, None, None, 0))
        self.lastw = {}
        self.readers = {}

    def flush(self):
        nc = self.nc
        engs = {"pe": "tensor", "act": "scalar", "dve": "vector", "pool": "gpsimd", "sp": "sync"}
        if not any(self.q.values()):
            return
        with nc.Block() as block:
            def mk(items):
                def body(eng):
                    for waits, fn, semk, inc in items:
                        for k, v in waits:
                            eng.wait_ge(self.sems[k], v)
                        if fn is not None:
                            ins = fn(eng)
                            if semk is not None:
                                ins.then_inc(self.sems[semk], inc)
                return body
            for qn, attr in engs.items():
                if self.q[qn]:
                    getattr(block, attr)(mk(self.q[qn]))
        self.q = {k: [] for k in ALLQ}


def cst_layout(NT, NCC):
    o = {}
    c = 0
    for name, n in [("attn", 16), ("ffn", 16), ("memn", 16), ("qn", 1), ("kcn", 1), ("ksn", 1), ("kwn", 1),
                    ("mqn", 1), ("mkn", 1), ("inv", 1), ("eps", 1), ("one", 1), ("zero", 1), ("halfpi", 1),
                    ("cw", 8), ("pv", NT), ("cur", NT * 4), ("cthr", NT * NCC), ("pek", 32), ("pev", 32),
                    ("tiny", 1), ("_om", NT * 2)]:
        o[name] = c
        c += n
    o["_n"] = c
    return o


def cm_layout(NT, NCC):
    o = {}
    c = 0
    for name, n in [("ident", 128), ("negI", 128), ("ones", 128), ("onesneg", 128), ("UIneg", 128), ("prot", 128),
                    ("E64", 8192), ("OV", NCC * 257), ("DiagN", 2048), ("DiagS", 2048), ("DiagW", 2048),
                    ("Eown", 512), ("OwnSel", NT * 16)]:
        o[name] = c
        c += n
    o["_n"] = c
    return o


def build(T):
    NT = T // (NCORE * TQ)
    NGT = T // TQ
    NCC = T // 2048
    NCP = NCC * 128
    NB = min(512, NCP)
    CL = cst_layout(NT, NCC)
    ML = cm_layout(NT, NCC)
    TO = NT * TQ

    nc = bass.Bass("TRN2", target_bir_lowering=False)

    def din(name, shape, dt=F32):
        return nc.dram_tensor(name, list(shape), dt, kind="ExternalInput").ap()

    xT_all = din("xT_all", [D, T])
    xT_own = din("xT_own", [D, TO])
    xT_prev = din("xT_prev", [D, TO])
    memT = din("memT", [D, 256])
    pos_all = din("pos_all", [1, T], I32)
    pos_own = din("pos_own", [1, TO], I32)
    pos_prev = din("pos_prev", [1, TO], I32)
    pos_cmp = din("pos_cmp", [1, NCP], I32)
    w_in = din("w_in", [D, 10776])
    w1k = din("w1k", [32, 128, 256])
    w2k = din("w2k", [256, 128])
    w1v = din("w1v", [32, 128, 256])
    w2v = din("w2v", [256, 128])
    w_mem = din("w_mem", [D, 1024])
    wo_nsa = din("wo_nsa", [1024, D])
    wo_sb = din("wo_sb", [512, D])
    wo_mem = din("wo_mem", [512, D])
    w_out = din("w_out", [D, D])
    w_gate = din("w_gate", [D, FFN])
    w_up = din("w_up", [D, FFN])
    w_down = din("w_down", [FFN, D])
    cst_d = din("cst", [128, CL["_n"]])
    rowt_d = din("rowt", [128, 256])
    j2_d = din("j2", [128, 512])
    cmat_d = din("cmat", [128, ML["_n"]])
    outT = nc.dram_tensor("outT", [D, TO], F32, kind="ExternalOutput").ap()

    kcT_d = nc.dram_tensor("kcT_d", [2, 128, T], BF16).ap()
    vcT_d = nc.dram_tensor("vcT_d", [2, 128, T], BF16).ap()
    ksT_d = nc.dram_tensor("ksT_d", [2, 128, T], BF16).ap()
    sbkT_d = nc.dram_tensor("sbkT_d", [4, 128, T], BF16).ap()
    vtok_d = nc.dram_tensor("vtok_d", [T, 768], BF16).ap()
    gates_d = nc.dram_tensor("gates_d", [24, TQ], F32).ap()
    hn_d = nc.dram_tensor("hn_d", [128, KC * TQ], BF16).ap()
    w_in_b = nc.dram_tensor("w_in_b", [D, 10776], BF16).ap()
    wo_nsa_b = nc.dram_tensor("wo_nsa_b", [1024, D], BF16).ap()
    wo_sb_b = nc.dram_tensor("wo_sb_b", [512, D], BF16).ap()
    wo_mem_b = nc.dram_tensor("wo_mem_b", [512, D], BF16).ap()
    w_out_b = nc.dram_tensor("w_out_b", [D, D], BF16).ap()
    w_gate_b = nc.dram_tensor("w_gate_b", [D, FFN], BF16).ap()
    w_up_b = nc.dram_tensor("w_up_b", [D, FFN], BF16).ap()
    w_down_b = nc.dram_tensor("w_down_b", [FFN, D], BF16).ap()

    with ExitStack() as es_all:
        S = Sched(nc, es_all)
        uid = [0]

        def sbt(es, name, shape, dt):
            uid[0] += 1
            return es.enter_context(nc.sbuf_tensor("%s_%d" % (name, uid[0]), list(shape), dt))

        PSB = [es_all.enter_context(nc.psum_tensor("ps%d" % i, [128, 512], F32)) for i in range(8)]
        rot = {"banks": list(range(8)), "i": 0}

        def set_rot(banks):
            rot["banks"] = list(banks)
            rot["i"] = 0

        def nextps():
            b = rot["banks"][rot["i"] % len(rot["banks"])]
            rot["i"] += 1
            return b

        def PS(b):
            return PSB[b]

        def pn(b):
            return "ps%d" % b

        cst = sbt(es_all, "cst", [128, CL["_n"]], F32)
        rowt = sbt(es_all, "rowt", [128, 256], F32)
        j2 = sbt(es_all, "j2", [128, 512], F32)
        cmA = sbt(es_all, "cmA", [128, 768], BF16)
        cmB_ref = [None]

        class _CM:
            def __getitem__(self, idx):
                rows, cols = idx
                a, b = cols.start, cols.stop
                if b <= 768:
                    return cmA[rows, a:b]
                assert a >= 768
                return cmB_ref[0][rows, a - 768:b - 768]
        cm = _CM()
        kcTs = sbt(es_all, "kcTs", [128, 2, NCP], BF16)
        vcs = sbt(es_all, "vcs", [128, 2, NCC, 128], BF16)
        kmem = sbt(es_all, "kmem", [128, 4, 256], BF16)
        vmem = sbt(es_all, "vmem", [128, 2, 512], BF16)

        def ccol(name, i=0):
            return cst[:, CL[name] + i:CL[name] + i + 1]

        def cmv(name, off=0, n=128):
            return cm[:, ML[name] + off:ML[name] + off + n]

        S.dma("sp", "c0a", lambda e: e.dma_start(out=cst[:], in_=cst_d), writes=["cst"])
        S.dma("sp", "c0b", lambda e: e.dma_start(out=rowt[:], in_=rowt_d), writes=["rowt"])
        S.dma("sp", "c0c", lambda e: e.dma_start(out=j2[:], in_=j2_d), writes=["j2"])
        S.dma("pool", "c1", lambda e: e.dma_start(out=cmA[:], in_=cmat_d[:, 0:768]), writes=["cm"])

        class Rope:
            def __init__(self, es, n, tag):
                self.n, self.tag = n, tag
                self.posi = sbt(es, "posi", [128, n], I32)
                self.ang = sbt(es, "ang", [128, n], F32)
                self.t1 = sbt(es, "rt1", [128, n], F32)
                self.kk = sbt(es, "rkk", [128, n], F32)

            def run(self, pos_ap, cosT, sinT, csname):
                n, tag = self.n, self.tag
                posi, ang, t1, kk = self.posi, self.ang, self.t1, self.kk
                S.dma("sp", tag + "pos", lambda e: e.dma_start(out=posi[:], in_=pos_ap.to_broadcast([128, n])), writes=[tag + "posi"])
                S.op("dve", lambda e: e.tensor_copy(out=t1[:], in_=posi[:]), reads=[tag + "posi"], writes=[tag + "t1"])
                S.op("dve", lambda e: e.tensor_scalar(out=ang[:], in0=t1[:], scalar1=ccol("inv"), scalar2=None, op0=ALU.mult),
                     reads=[tag + "t1", "cst"], writes=[tag + "ang"])
                S.op("dve", lambda e: e.tensor_scalar(out=t1[:], in0=ang[:], scalar1=float(1.0 / (2 * np.pi)), scalar2=MAGIC, op0=ALU.mult, op1=ALU.add),
                     reads=[tag + "ang"], writes=[tag + "t1"])
                S.op("dve", lambda e: e.tensor_scalar(out=kk[:], in0=t1[:], scalar1=MAGIC, scalar2=None, op0=ALU.subtract),
                     reads=[tag + "t1"], writes=[tag + "kk"])
                S.op("dve", lambda e: e.scalar_tensor_tensor(out=t1[:], in0=kk[:], scalar=-C1, in1=ang[:], op0=ALU.mult, op1=ALU.add),
                     reads=[tag + "kk", tag + "ang"], writes=[tag + "t1"])
                S.op("dve", lambda e: e.scalar_tensor_tensor(out=ang[:], in0=kk[:], scalar=-C2, in1=t1[:], op0=ALU.mult, op1=ALU.add),
                     reads=[tag + "kk", tag + "t1"], writes=[tag + "ang"])
                S.op("dve", lambda e: e.tensor_scalar(out=ang[:], in0=ang[:], scalar1=PI_LO, scalar2=-PI_LO, op0=ALU.min, op1=ALU.max),
                     reads=[tag + "ang"], writes=[tag + "ang"])
                S.op("dve", lambda e: e.scalar_tensor_tensor(out=t1[:], in0=ang[:], scalar=-1.0, in1=ang[:], op0=ALU.mult, op1=ALU.max),
                     reads=[tag + "ang"], writes=[tag + "t1"])
                S.op("act", lambda e: e.activation(out=sinT, in_=ang[:], func=AF.Sin), reads=[tag + "ang"], writes=[csname + "sin"])
                S.op("act", lambda e: e.activation(out=cosT, in_=t1[:], func=AF.Sin, scale=-1.0, bias=ccol("halfpi")),
                     reads=[tag + "t1", "cst"], writes=[csname + "cos"])

        class RMS:
            def __init__(self, es, n, tag):
                self.n, self.tag = n, tag
                self.xb = [sbt(es, "xch", [128, n], F32) for _ in range(3)]
                self.sq = [sbt(es, "sqc", [128, n], BF16) for _ in range(2)]
                self.rs = sbt(es, "rstd", [128, n], F32)

            def run(self, src_ap, gname, hn, hn_name):
                n, tag, xb, sq, rs = self.n, self.tag, self.xb, self.sq, self.rs
                pb = nextps()
                for c in range(KC):
                    b = xb[c % 3]
                    S.dma("sp", tag + "x%d" % (c % 3), lambda e, b=b, c=c: e.dma_start(out=b[:], in_=src_ap[c * 128:(c + 1) * 128, :]),
                          writes=[tag + "xb%d" % (c % 3)])
                    S.op("act", lambda e, b=b, c=c: e.activation(out=sq[c % 2][:], in_=b[:], func=AF.Square),
                         reads=[tag + "xb%d" % (c % 3)], writes=[tag + "sq%d" % (c % 2)])
                    S.op("pe", lambda e, c=c: e.matmul(PS(pb)[:, 0:n], lhsT=cmv("ones"), rhs=sq[c % 2][:], start=(c == 0), stop=(c == KC - 1)),
                         reads=[tag + "sq%d" % (c % 2), "cm"], writes=[pn(pb)])
                S.op("act", lambda e: e.activation(out=rs[:], in_=PS(pb)[:, 0:n], func=AF.Sqrt, scale=1.0 / D, bias=ccol("eps")),
                     reads=[pn(pb), "cst"], writes=[tag + "rs"])
                S.op("dve", lambda e: e.reciprocal(out=rs[:], in_=rs[:]), reads=[tag + "rs"], writes=[tag + "rs"])
                for c in range(KC):
                    b = xb[c % 3]
                    S.dma("sp", tag + "x%d" % (c % 3), lambda e, b=b, c=c: e.dma_start(out=b[:], in_=src_ap[c * 128:(c + 1) * 128, :]),
                          writes=[tag + "xb%d" % (c % 3)])
                    S.op("dve", lambda e, b=b, c=c: e.scalar_tensor_tensor(out=hn[:, c, :], in0=b[:], scalar=ccol(gname, c), in1=rs[:],
                                                                          op0=ALU.mult, op1=ALU.mult),
                         reads=[tag + "xb%d" % (c % 3), tag + "rs", "cst"], writes=[hn_name])

        class RMSRes:
            def __init__(self, es, n, tag):
                self.n, self.tag = n, tag
                self.xt = sbt(es, "xt", [128, KC, n], F32)
                self.sq = [sbt(es, "sqc", [128, n], BF16) for _ in range(2)]
                self.rs = sbt(es, "rstd", [128, n], F32)

            def run(self, src_ap, gname, hn, hn_name):
                n, tag, xt, sq, rs = self.n, self.tag, self.xt, self.sq, self.rs
                pb = nextps()
                for qd in range(4):
                    S.dma("sp", tag + "q%d" % qd, lambda e, qd=qd: e.dma_start(out=xt[:, 4 * qd:4 * qd + 4, :],
                                                                            in_=src_ap[qd * 512:(qd + 1) * 512, :].rearrange("(c p) t -> p c t", p=128)),
                          writes=[tag + "xq%d" % qd])
                for c in range(KC):
                    S.op("act", lambda e, c=c: e.activation(out=sq[c % 2][:], in_=xt[:, c, :], func=AF.Square),
                         reads=[tag + "xq%d" % (c // 4)], writes=[tag + "sq%d" % (c % 2)])
                    S.op("pe", lambda e, c=c: e.matmul(PS(pb)[:, 0:n], lhsT=cmv("ones"), rhs=sq[c % 2][:], start=(c == 0), stop=(c == KC - 1)),
                         reads=[tag + "sq%d" % (c % 2), "cm"], writes=[pn(pb)])
                S.op("act", lambda e: e.activation(out=rs[:], in_=PS(pb)[:, 0:n], func=AF.Sqrt, scale=1.0 / D, bias=ccol("eps")),
                     reads=[pn(pb), "cst"], writes=[tag + "rs"])
                S.op("dve", lambda e: e.reciprocal(out=rs[:], in_=rs[:]), reads=[tag + "rs"], writes=[tag + "rs"])
                for c in range(KC):
                    S.op("dve", lambda e, c=c: e.scalar_tensor_tensor(out=hn[:, c, :], in0=xt[:, c, :], scalar=ccol(gname, c), in1=rs[:],
                                                                     op0=ALU.mult, op1=ALU.mult),
                         reads=[tag + "xq%d" % (c // 4), tag + "rs", "cst"], writes=[hn_name])

        nrq = []

        def nr_tick():
            ready = []
            for it in nrq:
                it[0] -= 1
            while nrq and nrq[0][0] <= 0:
                ready.append(nrq.pop(0)[1])
            for fn_ in ready:
                fn_()

        def nr_drain():
            while nrq:
                nrq.pop(0)[1]()

        class NR:
            NSET = 2

            def __init__(self, es, n, tag):
                self.n = n
                self.tag = tag
                self.i = 0
                self.sets = []
                for _ in range(self.NSET):
                    self.sets.append(dict(sq=sbt(es, "nrsq", [128, n], BF16), r=sbt(es, "nrr", [128, n], F32), kb=sbt(es, "nrkb", [128, n], BF16),
                                          t1=sbt(es, "nrt1", [128, n], F32), t2=sbt(es, "nrt2", [128, n], F32)))

            def run(self, pin, gname, out_ap, out_name, cosT=None, sinT=None, csname=None, after=None):
                n = self.n
                si_ = self.i % self.NSET
                self.i += 1
                tag = self.tag + "s%d" % si_
                B = self.sets[si_]
                S.op("act", lambda e: e.activation(out=B["sq"][:], in_=PS(pin)[:, 0:n], func=AF.Square), reads=[pn(pin)], writes=[tag + "sq"])

                def stepB():
                    p2 = nextps()
                    S.op("pe", lambda e: e.matmul(PS(p2)[:, 0:n], lhsT=cmv("ones"), rhs=B["sq"][:], start=True, stop=True),
                         reads=[tag + "sq", "cm"], writes=[pn(p2)])
                    S.op("act", lambda e: e.activation(out=B["r"][:], in_=PS(p2)[:, 0:n], func=AF.Sqrt, scale=1.0 / 128, bias=ccol("eps")),
                         reads=[pn(p2), "cst"], writes=[tag + "r"])
                    S.op("dve", lambda e: e.reciprocal(out=B["r"][:], in_=B["r"][:]), reads=[tag + "r"], writes=[tag + "r"])
                    if cosT is None:
                        S.op("dve", lambda e: e.scalar_tensor_tensor(out=out_ap, in0=PS(pin)[:, 0:n], scalar=ccol(gname), in1=B["r"][:], op0=ALU.mult, op1=ALU.mult),
                             reads=[pn(pin), tag + "r", "cst"], writes=[out_name])
                        if after:
                            after()
                        return
                    S.op("dve", lambda e: e.scalar_tensor_tensor(out=B["kb"][:], in0=PS(pin)[:, 0:n], scalar=ccol(gname), in1=B["r"][:], op0=ALU.mult, op1=ALU.mult),
                         reads=[pn(pin), tag + "r", "cst"], writes=[tag + "kb"])
                    nrq.append([1, stepC])

                def stepC():
                    p3 = nextps()
                    S.op("pe", lambda e: e.matmul(PS(p3)[:, 0:n], lhsT=cmv("prot"), rhs=B["kb"][:], start=True, stop=True),
                         reads=[tag + "kb", "cm"], writes=[pn(p3)])
                    S.op("dve", lambda e: e.tensor_tensor(out=B["t1"][:], in0=B["kb"][:], in1=cosT, op=ALU.mult),
                         reads=[tag + "kb", csname + "cos"], writes=[tag + "t1"])
                    S.op("dve", lambda e: e.tensor_tensor(out=B["t2"][:], in0=PS(p3)[:, 0:n], in1=sinT, op=ALU.mult),
                         reads=[pn(p3), csname + "sin"], writes=[tag + "t2"])
                    S.op("dve", lambda e: e.tensor_tensor(out=out_ap, in0=B["t1"][:], in1=B["t2"][:], op=ALU.add),
                         reads=[tag + "t1", tag + "t2"], writes=[out_name])
                    if after:
                        after()
                nrq.append([1, stepB])

        class WStream:
            def __init__(self, es, tag, nbuf=2, size=5632, cast=False):
                self.bufs = [sbt(es, "wbuf", [128, size], BF16) for _ in range(nbuf)]
                self.tag = tag
                self.i = 0
                self.size = size
                self.cast = cast

            def load(self, w_ap, K, col0, gc):
                kc = K // 128
                assert kc * gc <= self.size
                i = self.i % len(self.bufs)
                self.i += 1
                view = self.bufs[i][:, 0:kc * gc].rearrange("p (c n) -> p c n", c=kc)
                name = self.tag + "w%d" % i
                S.dma("pool" if self.cast else "sp", name, lambda e: e.dma_start(out=view, in_=w_ap[:, col0:col0 + gc].rearrange("(c p) n -> p c n", p=128)),
                      reads=[] if self.cast else ["wb16"], writes=[name])
                return view, name

        def proj_fm(ws, w_ap, K, col0, ncols, rhs_fn, rhs_names, n, evac, gcmax=None):
            kc = K // 128
            gc_full = min(ncols, (ws.size // kc) // 128 * 128)
            if gcmax:
                gc_full = min(gc_full, gcmax)
            j = 0
            g0 = 0
            while g0 < ncols:
                gc = min(gc_full, ncols - g0)
                view, wname = ws.load(w_ap, K, col0 + g0, gc)
                for jj in range((gc + 127) // 128):
                    m = min(128, gc - jj * 128)
                    pb = nextps()
                    for c in range(kc):
                        S.op("pe", lambda e, c=c, jj=jj, pb=pb, view=view, m=m: e.matmul(PS(pb)[0:m, 0:n], lhsT=view[:, c, jj * 128:jj * 128 + m], rhs=rhs_fn(c),
                                                                                start=(c == 0), stop=(c == kc - 1)),
                             reads=[wname] + rhs_names, writes=[pn(pb)], sig=(c == kc - 1))
                    nr_tick()
                    evac(j, pb)
                    j += 1
                g0 += gc

        def proj_tm(ws, w_ap, K, col0, ncols, lhs_fn, lhs_names, nsub, evac):
            kc = K // 128
            gc_full = min(ncols, (ws.size // kc) // 128 * 128, 512)
            g0 = 0
            while g0 < ncols:
                gc = min(gc_full, ncols - g0)
                view, wname = ws.load(w_ap, K, col0 + g0, gc)
                for ts in range(nsub):
                    pb = nextps()
                    for c in range(kc):
                        S.op("pe", lambda e, c=c, ts=ts, pb=pb, view=view, gc=gc: e.matmul(PS(pb)[:, 0:gc], lhsT=lhs_fn(c, ts), rhs=view[:, c, 0:gc],
                                                                                  start=(c == 0), stop=(c == kc - 1)),
                             reads=[wname] + lhs_names, writes=[pn(pb)], sig=(c == kc - 1))
                    nr_tick()
                    evac(ts, g0, gc, pb)
                g0 += gc

        cpy_i = [0]

        def copy_out(out_ap, out_name, in_ap, in_names):
            cpy_i[0] += 1
            if cpy_i[0] % 2:
                S.op("act", lambda e: e.activation(out=out_ap, in_=in_ap, func=AF.Copy), reads=in_names, writes=[out_name])
            else:
                S.op("dve", lambda e: e.tensor_copy(out=out_ap, in_=in_ap), reads=in_names, writes=[out_name])

        def stage_end():
            nr_drain()
            S.barrier()
            S.flush()

        with ExitStack() as es:
            set_rot(range(8))
            wkv = sbt(es, "wkv", [128, KC, 2048], BF16)
            for (c0, n, d0) in [(1024, 768, 0), (3096, 512, 768), (1792, 256, 1280), (3608, 512, 1536)]:
                for half in range(2):
                    S.dma("pool", "wkv", lambda e, c0=c0, n=n, d0=d0, half=half: e.dma_start(
                        out=wkv[:, half * 8:(half + 1) * 8, d0:d0 + n],
                        in_=w_in[half * 1024:(half + 1) * 1024, c0:c0 + n].rearrange("(c p) n -> p c n", p=128)), writes=["wkv"])
            for (src_, dst_, rows_) in [(w_in, w_in_b, D), (wo_nsa, wo_nsa_b, 1024), (wo_sb, wo_sb_b, 512), (wo_mem, wo_mem_b, 512),
                                        (w_out, w_out_b, D), (w_gate, w_gate_b, D), (w_up, w_up_b, D), (w_down, w_down_b, FFN)]:
                for r0 in range(0, rows_, 256):
                    S.dma("pool", "wcast", lambda e, src_=src_, dst_=dst_, r0=r0: e.dma_start(out=dst_[r0:r0 + 256, :], in_=src_[r0:r0 + 256, :]), writes=["wb16"])
            hnb = [sbt(es, "hn1", [128, KC, TQ], BF16) for _ in range(2)]
            cosTb = [sbt(es, "cosT", [128, TQ], F32) for _ in range(2)]
            sinTb = [sbt(es, "sinT", [128, TQ], F32) for _ in range(2)]
            nr = NR(es, TQ, "p1nr")
            rms = RMSRes(es, TQ, "p1rms")
            rope = Rope(es, TQ, "p1rope")
            stg = [sbt(es, "stg", [128, TQ], BF16) for _ in range(3)]
            vst = [sbt(es, "vst", [128, 768], BF16) for _ in range(2)]
            si = 0

            def p1_pre(gt_):
                rms.run(xT_all[:, gt_ * TQ:(gt_ + 1) * TQ], "attn", hnb[gt_ % 2], "hn1_%d" % (gt_ % 2))
                rope.run(pos_all[0:1, gt_ * TQ:(gt_ + 1) * TQ], cosTb[gt_ % 2][:], sinTb[gt_ % 2][:], "p1cs%d" % (gt_ % 2))
            p1_pre(0)
            for gt in range(NGT):
                hn = hnb[gt % 2]
                hname = "hn1_%d" % (gt % 2)
                cosT, sinT, csn = cosTb[gt % 2], sinTb[gt % 2], "p1cs%d" % (gt % 2)
                t0 = gt * TQ
                if gt + 1 < NGT:
                    p1_pre(gt + 1)
                for j in range(10):
                    pb = nextps()
                    for c in range(KC):
                        S.op("pe", lambda e, c=c, j=j, pb=pb, hn=hn: e.matmul(PS(pb)[:, :], lhsT=wkv[:, c, j * 128:(j + 1) * 128], rhs=hn[:, c, :],
                                                                          start=(c == 0), stop=(c == KC - 1)),
                             reads=["wkv", hname], writes=[pn(pb)], sig=(c == KC - 1))
                    nr_tick()
                    sb_ = stg[si % 3]
                    sname = "stg%d" % (si % 3)
                    si += 1
                    if j in (4, 5):
                        dst = ksT_d[j - 4, :, t0:t0 + TQ]
                        nr.run(pb, "ksn", sb_[:], sname, cosT[:], sinT[:], csn,
                               after=lambda dst=dst, sb_=sb_, sname=sname: S.dma("sp", "p1st_" + sname, lambda e: e.dma_start(out=dst, in_=sb_[:]), reads=[sname], writes=["kvdram"]))
                        continue
                    copy_out(sb_[:], sname, PS(pb)[:, :], [pn(pb)])
                    if j < 2:
                        dst = kcT_d[j, :, t0:t0 + TQ]
                    elif j < 4:
                        dst = vcT_d[j - 2, :, t0:t0 + TQ]
                    else:
                        dst = sbkT_d[j - 6, :, t0:t0 + TQ]
                    S.dma("sp", "p1st_" + sname, lambda e, dst=dst, sb_=sb_: e.dma_start(out=dst, in_=sb_[:]), reads=[sname], writes=["kvdram"])
                for ts in range(4):
                    vs_ = vst[ts % 2]
                    vname = "vst%d" % (ts % 2)
                    for (c0, n) in [(1280, 256), (1536, 512)]:
                        pb = nextps()
                        for c in range(KC):
                            S.op("pe", lambda e, c=c, pb=pb, hn=hn, ts=ts, c0=c0, n=n: e.matmul(PS(pb)[:, 0:n], lhsT=hn[:, c, ts * 128:(ts + 1) * 128],
                                                                                         rhs=wkv[:, c, c0:c0 + n], start=(c == 0), stop=(c == KC - 1)),
                                 reads=["wkv", hname], writes=[pn(pb)], sig=(c == KC - 1))
                        nr_tick()
                        copy_out(vs_[:, c0 - 1280:c0 - 1280 + n], vname, PS(pb)[:, 0:n], [pn(pb)])
                    S.dma("sp", "p1sv_" + vname, lambda e, vs_=vs_, ts=ts, t0=t0: e.dma_start(out=vtok_d[t0 + ts * 128:t0 + (ts + 1) * 128, :], in_=vs_[:]),
                          reads=[vname], writes=["kvdram"])
            stage_end()

        with ExitStack() as es:
            set_rot(range(8))
            kcs = sbt(es, "kcs", [128, T + 16], BF16)
            w1b = sbt(es, "w1b", [128, 32, 256], BF16)
            w2b = sbt(es, "w2b", [128, 2, 128], BF16)
            peb = sbt(es, "peb", [128, 32], BF16)
            pebias = sbt(es, "pebias", [128, 2], F32)
            hf = sbt(es, "hf", [128, NB], F32)
            h2 = sbt(es, "h2", [128, NB], F32)
            sg = sbt(es, "sg", [128, NB], F32)
            hid = sbt(es, "hid", [128, 2, NB], BF16)
            cosC = sbt(es, "cosC", [128, NCP], F32)
            sinC = sbt(es, "sinC", [128, NCP], F32)
            nrc = NR(es, NB, "cnr")
            ropec = Rope(es, NCP, "crope")
            ropec.run(pos_cmp[0:1, :], cosC[:], sinC[:], "ccs")
            S.op("pool", lambda e: e.memset(kcs[:, T:T + 16], 0.0), writes=["kcs_tail"])
            for kv in range(2):
                w1d, w2d, pename = (w1k, w2k, "pek") if kv == 0 else (w1v, w2v, "pev")
                S.dma("pool", "w1b", lambda e, w1d=w1d: e.dma_start(out=w1b[:], in_=w1d.rearrange("l d f -> d l f")), writes=["w1b"])
                S.dma("pool", "w2b", lambda e, w2d=w2d: e.dma_start(out=w2b[:], in_=w2d.rearrange("(c p) d -> p c d", p=128)), writes=["w2b"])
                S.op("dve", lambda e, pename=pename: e.tensor_copy(out=peb[:], in_=cst[:, CL[pename]:CL[pename] + 32]), reads=["cst"], writes=["peb"])
                for fc in range(2):
                    pb = nextps()
                    for l in range(32):
                        S.op("pe", lambda e, l=l, fc=fc, pb=pb: e.matmul(PS(pb)[:, 0:1], lhsT=w1b[:, l, fc * 128:(fc + 1) * 128], rhs=peb[:, l:l + 1],
                                                                     start=(l == 0), stop=(l == 31)), reads=["w1b", "peb"], writes=[pn(pb)])
                    S.op("dve", lambda e, fc=fc, pb=pb: e.tensor_copy(out=pebias[:, fc:fc + 1], in_=PS(pb)[:, 0:1]), reads=[pn(pb)], writes=["pebias"])
                for g in range(2):
                    src = kcT_d if kv == 0 else vcT_d
                    S.dma("sp", "kcs", lambda e, src=src, g=g: e.dma_start(out=kcs[:, 0:T], in_=src[g, :, :]), reads=["kvdram"], writes=["kcs"])
                    for nt in range(NCP // NB):
                        n0 = nt * NB
                        for fc in range(2):
                            pb = nextps()
                            for l in range(32):
                                a0 = 16 * n0 + l
                                S.op("pe", lambda e, l=l, fc=fc, pb=pb, a0=a0: e.matmul(PS(pb)[:, 0:NB], lhsT=w1b[:, l, fc * 128:(fc + 1) * 128],
                                                                                 rhs=kcs[:, a0:a0 + 16 * (NB - 1) + 1:16], start=(l == 0), stop=(l == 31)),
                                     reads=["w1b", "kcs", "kcs_tail"], writes=[pn(pb)])
                            S.op("act", lambda e, pb=pb, fc=fc: e.activation(out=hf[:], in_=PS(pb)[:, 0:NB], func=AF.Identity, bias=pebias[:, fc:fc + 1]),
                                 reads=[pn(pb), "pebias"], writes=["hf"])
                            S.op("dve", lambda e: e.tensor_tensor(out=h2[:], in0=hf[:], in1=hf[:], op=ALU.mult), reads=["hf"], writes=["h2"])
                            S.op("dve", lambda e: e.tensor_scalar(out=h2[:], in0=h2[:], scalar1=0.044715, scalar2=1.0, op0=ALU.mult, op1=ALU.add),
                                 reads=["h2"], writes=["h2"])
                            S.op("dve", lambda e: e.tensor_tensor(out=h2[:], in0=h2[:], in1=hf[:], op=ALU.mult), reads=["h2", "hf"], writes=["h2"])
                            S.op("act", lambda e: e.activation(out=sg[:], in_=h2[:], func=AF.Sigmoid, scale=float(2.0 * math.sqrt(2.0 / math.pi))),
                                 reads=["h2"], writes=["sg"])
                            S.op("dve", lambda e, fc=fc: e.tensor_tensor(out=hid[:, fc, :], in0=hf[:], in1=sg[:], op=ALU.mult), reads=["hf", "sg"], writes=["hid"])
                        if kv == 0:
                            pb = nextps()
                            for fc in range(2):
                                S.op("pe", lambda e, fc=fc, pb=pb: e.matmul(PS(pb)[:, 0:NB], lhsT=w2b[:, fc, :], rhs=hid[:, fc, :], start=(fc == 0), stop=(fc == 1)),
                                     reads=["w2b", "hid"], writes=[pn(pb)])
                            nrc.run(pb, "kcn", kcTs[:, g, n0:n0 + NB], "kcTs", cosC[:, n0:n0 + NB], sinC[:, n0:n0 + NB], "ccs")
                            nr_drain()
                        else:
                            for ns in range(NB // 128):
                                pb = nextps()
                                for fc in range(2):
                                    S.op("pe", lambda e, fc=fc, pb=pb, ns=ns: e.matmul(PS(pb)[:, 0:128], lhsT=hid[:, fc, ns * 128:(ns + 1) * 128], rhs=w2b[:, fc, :],
                                                                                  start=(fc == 0), stop=(fc == 1)), reads=["w2b", "hid"], writes=[pn(pb)])
                                copy_out(vcs[:, g, n0 // 128 + ns, :], "vcs", PS(pb)[:, 0:128], [pn(pb)])
            stage_end()

        with ExitStack() as es:
            set_rot(range(8))
            hm = sbt(es, "hm", [128, KC, 256], BF16)
            rmsm = RMS(es, 256, "mrms")
            nrm = NR(es, 256, "mnr")
            wsm = WStream(es, "wsm", 2, 4096, cast=True)
            rmsm.run(memT, "memn", hm, "hm")

            def ev_k(j, pb):
                nrm.run(pb, "mkn", kmem[:, j, :], "kmem")
            proj_fm(wsm, w_mem, D, 0, 512, lambda c: hm[:, c, :], ["hm"], 256, ev_k, gcmax=256)

            def ev_v(ts, g0, gc, pb):
                copy_out(vmem[:, ts, g0:g0 + gc], "vmem", PS(pb)[:, 0:gc], [pn(pb)])
            proj_tm(wsm, w_mem, D, 512, 512, lambda c, ts: hm[:, c, ts * 128:(ts + 1) * 128], ["hm"], 2, ev_v)
            stage_end()

        with ExitStack() as es_p2:
            cmB_ref[0] = sbt(es_p2, "cmB", [128, ML["_n"] - 768], BF16)
            S.dma("pool", "c2", lambda e: e.dma_start(out=cmB_ref[0][:], in_=cmat_d[:, 768:ML["_n"]]), writes=["cm"])
            ynsa_b = sbt(es_p2, "ynsa_b", [128, 8, TQ], BF16)
            ysb = sbt(es_p2, "ysb", [128, 4, TQ], BF16)
            ymem = sbt(es_p2, "ymem", [128, 4, TQ], BF16)
            for k in range(NT):
                o0 = k * TQ
                with ExitStack() as es_a:
                    qn = sbt(es_a, "qn", [128, 8, TQ], BF16)
                    qs = sbt(es_a, "qs", [128, 4, TQ], BF16)
                    qm = sbt(es_a, "qm", [128, 4, TQ], BF16)
                    ksTo = sbt(es_a, "ksTo", [128, 2, TQ], BF16)
                    kwTo = sbt(es_a, "kwTo", [128, 2, TQ], BF16)
                    kwTp = sbt(es_a, "kwTp", [128, 2, TQ], BF16)
                    sbkTo = sbt(es_a, "sbkTo", [128, 4, TQ], BF16)
                    vown = sbt(es_a, "vown", [128, 4, 1024], BF16)
                    vprev = sbt(es_a, "vprev", [128, 4, 256], BF16)
                    with ExitStack() as es:
                        set_rot(range(8))
                        hp = sbt(es, "hp", [128, KC, TQ], BF16)
                        hn = sbt(es, "hn", [128, KC, TQ], BF16)
                        cosO = sbt(es, "cosO", [128, TQ], F32)
                        sinO = sbt(es, "sinO", [128, TQ], F32)
                        cosP = sbt(es, "cosP", [128, TQ], F32)
                        sinP = sbt(es, "sinP", [128, TQ], F32)
                        g32 = sbt(es, "g32", [24, TQ], F32)
                        rms2 = RMS(es, TQ, "a1rms")
                        rope2 = Rope(es, TQ, "a1rope")
                        nr2 = NR(es, TQ, "a1nr")
                        ws = WStream(es, "a1ws", 2, 4096)
                        rms2.run(xT_own[:, o0:o0 + TQ], "attn", hn, "hn")
                        rms2.run(xT_prev[:, o0:o0 + TQ], "attn", hp, "hp")
                        rope2.run(pos_own[0:1, o0:o0 + TQ], cosO[:], sinO[:], "cso")
                        rope2.run(pos_prev[0:1, o0:o0 + TQ], cosP[:], sinP[:], "csp")
                        rh = lambda c: hn[:, c, :]
                        rp = lambda c: hp[:, c, :]
                        proj_fm(ws, w_in_b, D, 0, 1024, rh, ["hn"], TQ, lambda j, pb: nr2.run(pb, "qn", qn[:, j, :], "qn", cosO[:], sinO[:], "cso"), gcmax=256)
                        proj_fm(ws, w_in_b, D, 1536, 256, rh, ["hn"], TQ, lambda j, pb: nr2.run(pb, "ksn", ksTo[:, j, :], "ksTo", cosO[:], sinO[:], "cso"), gcmax=256)
                        proj_fm(ws, w_in_b, D, 2048, 256, rh, ["hn"], TQ, lambda j, pb: nr2.run(pb, "kwn", kwTo[:, j, :], "kwTo", cosO[:], sinO[:], "cso"), gcmax=256)
                        proj_fm(ws, w_in_b, D, 2048, 256, rp, ["hp"], TQ, lambda j, pb: nr2.run(pb, "kwn", kwTp[:, j, :], "kwTp", cosP[:], sinP[:], "csp"), gcmax=256)
                        proj_fm(ws, w_in_b, D, 2584, 512, rh, ["hn"], TQ, lambda j, pb: copy_out(qs[:, j, :], "qs", PS(pb)[:, :], [pn(pb)]), gcmax=256)
                        proj_fm(ws, w_in_b, D, 3096, 512, rh, ["hn"], TQ, lambda j, pb: copy_out(sbkTo[:, j, :], "sbkTo", PS(pb)[:, :], [pn(pb)]), gcmax=256)
                        proj_fm(ws, w_in_b, D, 4120, 512, rh, ["hn"], TQ, lambda j, pb: nr2.run(pb, "mqn", qm[:, j, :], "qm"), gcmax=256)

                        def ev_g(j, pb):
                            S.op("act", lambda e: e.activation(out=g32[:], in_=PS(pb)[0:24, :], func=AF.Sigmoid), reads=[pn(pb)], writes=["g32"])
                            S.dma("sp", "gst", lambda e: e.dma_start(out=gates_d, in_=g32[:]), reads=["g32"], writes=["gates_d"])
                        proj_fm(ws, w_in_b, D, 2560, 24, rh, ["hn"], TQ, ev_g)
                        lh = lambda c, ts: hn[:, c, ts * 128:(ts + 1) * 128]
                        lp = lambda c, ts: hp[:, c, ts * 128:(ts + 1) * 128]
                        proj_tm(ws, w_in_b, D, 1792, 256, lh, ["hn"], 4, lambda ts, g0, gc, pb: copy_out(vown[:, ts, 0:256], "vown", PS(pb)[:, 0:256], [pn(pb)]))
                        proj_tm(ws, w_in_b, D, 2304, 256, lh, ["hn"], 4, lambda ts, g0, gc, pb: copy_out(vown[:, ts, 256:512], "vown", PS(pb)[:, 0:256], [pn(pb)]))
                        proj_tm(ws, w_in_b, D, 3608, 512, lh, ["hn"], 4, lambda ts, g0, gc, pb: copy_out(vown[:, ts, 512 + g0:512 + g0 + gc], "vown", PS(pb)[:, 0:gc], [pn(pb)]))
                        proj_tm(ws, w_in_b, D, 2304, 256, lp, ["hp"], 4, lambda ts, g0, gc, pb: copy_out(vprev[:, ts, :], "vprev", PS(pb)[:, 0:256], [pn(pb)]))
                        S.dma("sp", "hnst", lambda e: e.dma_start(out=hn_d, in_=hn[:].rearrange("p c t -> p (c t)")), reads=["hn"], writes=["hn_d"])
                        stage_end()

                    with ExitStack() as es:
                        O_B, SUM_B, U0_B, U1_B = 0, 1, 2, 3
                        set_rot([4, 5, 6, 7])
                        ynsa = sbt(es, "ynsa", [128, 4, TQ], F32)
                        impacc = sbt(es, "impacc", [128, 2, 4, 256], F32)
                        selbT = sbt(es, "selbT", [128, 2, 2, TQ], BF16)
                        ownb = sbt(es, "ownb", [8, 2, TQ], BF16)
                        gbc = [sbt(es, "gbc", [128, 3, TQ], F32) for _ in range(1)]
                        Pb = [sbt(es, "Pb", [128, TQ], BF16) for _ in range(3)]
                        rec = sbt(es, "rec", [128, TQ], F32)
                        tmpf = sbt(es, "tmpf", [128, TQ], F32)
                        kbuf = [sbt(es, "kbuf", [128, TQ], BF16) for _ in range(6)]
                        vbuf = [sbt(es, "vbuf", [128, 4, 128], BF16) for _ in range(6)]
                        cmask = sbt(es, "cmask", [128, NCC, TQ], BF16)
                        rtok = sbt(es, "rtok", [128, 4], F32)
                        tk1 = sbt(es, "tk1", [128, 256], F32)
                        tk2 = sbt(es, "tk2", [128, 256], F32)
                        tk3 = sbt(es, "tk3", [128, 256], F32)
                        tk4 = sbt(es, "tk4", [128, 256], F32)
                        m8 = sbt(es, "m8", [128, 16], F32)
                        alb = sbt(es, "alb", [128, 256], BF16)
                        alT = sbt(es, "alT", [128, 2, TQ], BF16)
                        e32 = [sbt(es, "e32", [128, TQ], F32) for _ in range(2)]
                        ec32 = [sbt(es, "ec32", [128, TQ], F32) for _ in range(2)]
                        Lp = [sbt(es, "Lp", [128, TQ], BF16) for _ in range(3)]
                        Wb = [sbt(es, "Wb", [128, TQ], BF16) for _ in range(2)]
                        Rb = [sbt(es, "Rb", [128, TQ], BF16) for _ in range(2)]
                        ncmp = min(NCC, 2 * (k + 1))
                        gi = [0]

                        def load_gates(h):
                            b = gbc[0]
                            name = "gbc0"
                            gi[0] += 1
                            S.dma("sp", name, lambda e: e.dma_start(out=b[:], in_=gates_d[3 * h:3 * h + 3, :].rearrange("(o r) t -> o r t", o=1).to_broadcast([128, 3, TQ])),
                                  reads=["gates_d"], writes=[name])
                            return b, name

                        for j in range(ncmp):
                            S.op("dve", lambda e, j=j: e.tensor_scalar(out=cmask[:, j, :], in0=j2[:], scalar1=ccol("cthr", k * NCC + j), scalar2=None, op0=ALU.is_gt),
                                 reads=["j2", "cst"], writes=["cmask"])

                        pi = [0]

                        pend = [None]

                        def flush_pending():
                            if pend[0] is None:
                                return
                            ch, P, pname, i, n, extra = pend[0]
                            pend[0] = None
                            S.op("pe", lambda e: e.matmul(PS(O_B)[:, :], lhsT=ch["v"], rhs=P[:], start=(i == 0), stop=(i == n - 1)),
                                 reads=ch["vn"] + [pname], writes=[pn(O_B)])
                            S.op("pe", lambda e: e.matmul(PS(SUM_B)[:, :], lhsT=cmv("ones"), rhs=P[:], start=(i == 0), stop=(i == n - 1)),
                                 reads=["cm", pname], writes=[pn(SUM_B)])
                            if extra:
                                extra(i, P, pname, n)

                        def softmax_chunks(chunks, q_ap, q_names, extra=None, i0=0, n_total=None):
                            n = len(chunks) if n_total is None else n_total
                            for i_, ch in enumerate(chunks):
                                i = i0 + i_
                                sb_ = nextps()
                                masks = ch.get("masks", [])
                                S.op("pe", lambda e: e.matmul(PS(sb_)[:, :], lhsT=ch["kT"], rhs=q_ap, start=True, stop=(len(masks) == 0)),
                                     reads=ch["kn"] + q_names, writes=[pn(sb_)])
                                for mi, (ml, mr, mn) in enumerate(masks):
                                    S.op("pe", lambda e: e.matmul(PS(sb_)[:, :], lhsT=ml, rhs=mr, start=False, stop=(mi == len(masks) - 1)),
                                         reads=mn + ["cm"], writes=[pn(sb_)])
                                P = Pb[pi[0] % 3]
                                pname = "Pb%d" % (pi[0] % 3)
                                pi[0] += 1
                                bias = ch.get("bias")
                                if bias is None:
                                    bias = ccol("zero")
                                S.op("act", lambda e: e.activation(out=P[:], in_=PS(sb_)[:, :], func=AF.Exp, scale=SCALE, bias=bias),
                                     reads=[pn(sb_), "cst"], writes=[pname])
                                flush_pending()
                                pend[0] = (ch, P, pname, i, n, extra)

                        def finalize(out_ap, out_name, gate_ap=None, gate_name=None, accumulate=False):
                            flush_pending()
                            S.op("dve", lambda e: e.tensor_scalar(out=rec[:], in0=PS(SUM_B)[:, :], scalar1=1e-30, scalar2=None, op0=ALU.max), reads=[pn(SUM_B)], writes=["rec"])
                            S.op("dve", lambda e: e.reciprocal(out=rec[:], in_=rec[:]), reads=["rec"], writes=["rec"])
                            if gate_ap is not None:
                                S.op("dve", lambda e: e.tensor_tensor(out=rec[:], in0=rec[:], in1=gate_ap, op=ALU.mult), reads=["rec", gate_name], writes=["rec"])
                            if accumulate:
                                S.op("dve", lambda e: e.tensor_tensor(out=tmpf[:], in0=PS(O_B)[:, :], in1=rec[:], op=ALU.mult), reads=[pn(O_B), "rec"], writes=["tmpf"])
                                S.op("dve", lambda e: e.tensor_tensor(out=out_ap, in0=out_ap, in1=tmpf[:], op=ALU.add), reads=["tmpf", out_name], writes=[out_name])
                            else:
                                S.op("dve", lambda e: e.tensor_tensor(out=out_ap, in0=PS(O_B)[:, :], in1=rec[:], op=ALU.mult), reads=[pn(O_B), "rec"], writes=[out_name])

                        si2 = [0, 0]

                        def stream_tile(kT_src, v_c0, gt, pool_=0):
                            i = pool_ * 3 + si2[pool_] % 3
                            si2[pool_] += 1
                            kb, vb = kbuf[i], vbuf[i]
                            S.dma("sp", "kb%d" % i, lambda e: e.dma_start(out=kb[:], in_=kT_src[:, gt * TQ:(gt + 1) * TQ]), reads=["kvdram"], writes=["kbuf%d" % i])
                            S.dma("sp", "vb%d" % i, lambda e: e.dma_start(out=vb[:], in_=vtok_d[gt * TQ:(gt + 1) * TQ, v_c0:v_c0 + 128].rearrange("(ts p) d -> p ts d", p=128)),
                                  reads=["kvdram"], writes=["vbuf%d" % i])
                            return kb, vb, "kbuf%d" % i, "vbuf%d" % i

                        SB_O = 7

                        def sb_steps(h):
                            nch = 4 + (8 * k + 8) * 4

                            def sb_gen():
                                for c_ in (3, 2, 1, 0):
                                    yield dict(kT=sbkTo[:, h, c_ * 128:(c_ + 1) * 128], kn=["sbkTo"], v=vown[:, c_, 512 + h * 128:512 + (h + 1) * 128], vn=["vown"],
                                               diag=cm[:, ML["DiagS"] + c_ * 512:ML["DiagS"] + (c_ + 1) * 512], bias=ccol("zero"))
                                for tl in range(8 * k + 7, -1, -1):
                                    kb, vb, kname, vname = stream_tile(sbkT_d[h], 256 + h * 128, tl, 1)
                                    j_ = tl - 8 * k
                                    bias = ccol("cw", j_) if j_ >= 0 else ccol("zero")
                                    for c_ in (3, 2, 1, 0):
                                        yield dict(kT=kb[:, c_ * 128:(c_ + 1) * 128], kn=[kname], v=vb[:, c_, :], vn=[vname], bias=bias)
                            gen = sb_gen()
                            st = {}
                            Rprev = [None]

                            def stage1(ci):
                                ch = next(gen)
                                zb = nextps()
                                dg = ch.get("diag")
                                S.op("pe", lambda e: e.matmul(PS(zb)[:, :], lhsT=ch["kT"], rhs=qs[:, h, :], start=True, stop=(dg is None)),
                                     reads=ch["kn"] + ["qs"], writes=[pn(zb)])
                                if dg is not None:
                                    S.op("pe", lambda e: e.matmul(PS(zb)[:, :], lhsT=cmv("negI"), rhs=dg, start=False, stop=True), reads=["cm"], writes=[pn(zb)])
                                e_ = e32[ci % 2]
                                en = "e32_%d" % (ci % 2)
                                S.op("act", lambda e: e.activation(out=e_[:], in_=PS(zb)[:, :], func=AF.Exp, scale=SCALE, bias=ch["bias"]),
                                     reads=[pn(zb), "cst"], writes=[en])
                                L_ = Lp[ci % 3]
                                ln_ = "Lp%d" % (ci % 3)
                                S.op("act", lambda e: e.activation(out=L_[:], in_=e_[:], func=AF.Ln, bias=ccol("one")), reads=[en, "cst"], writes=[ln_])
                                st[ci] = dict(ch=ch, e_=e_, en=en, L_=L_, ln_=ln_)

                            def stage2(ci):
                                d_ = st[ci]
                                L_, ln_, e_, en = d_["L_"], d_["ln_"], d_["e_"], d_["en"]
                                cb = nextps()
                                Rp = Rprev[0]
                                S.op("pe", lambda e: e.matmul(PS(cb)[:, :], lhsT=cmv("UIneg"), rhs=L_[:], start=True, stop=(Rp is None)),
                                     reads=[ln_, "cm"], writes=[pn(cb)])
                                if Rp is not None:
                                    S.op("pe", lambda e: e.matmul(PS(cb)[:, :], lhsT=cmv("onesneg"), rhs=Rp[0][:], start=False, stop=True),
                                         reads=[Rp[1], "cm"], writes=[pn(cb)])
                                ec_ = ec32[ci % 2]
                                ecn = "ec32_%d" % (ci % 2)
                                S.op("act", lambda e: e.activation(out=ec_[:], in_=PS(cb)[:, :], func=AF.Exp), reads=[pn(cb)], writes=[ecn])
                                W_ = Wb[ci % 2]
                                wn_ = "Wb%d" % (ci % 2)
                                S.op("dve", lambda e: e.tensor_tensor(out=W_[:], in0=e_[:], in1=ec_[:], op=ALU.mult), reads=[en, ecn], writes=[wn_])
                                Rn = Rb[ci % 2]
                                rn_ = "Rb%d" % (ci % 2)
                                if Rp is None:
                                    S.op("dve", lambda e: e.tensor_copy(out=Rn[:], in_=L_[:]), reads=[ln_], writes=[rn_])
                                else:
                                    S.op("dve", lambda e: e.tensor_tensor(out=Rn[:], in0=Rp[0][:], in1=L_[:], op=ALU.add), reads=[ln_, Rp[1]], writes=[rn_])
                                Rprev[0] = (Rn, rn_)
                                d_["W_"], d_["wn_"] = W_, wn_

                            def stage3(ci):
                                d_ = st.pop(ci)
                                ch, W_, wn_ = d_["ch"], d_["W_"], d_["wn_"]
                                S.op("pe", lambda e: e.matmul(PS(SB_O)[:, :], lhsT=ch["v"], rhs=W_[:], start=(ci == 0), stop=(ci == nch - 1)),
                                     reads=ch["vn"] + [wn_], writes=[pn(SB_O)])
                            for it in range(nch + 2):
                                if it < nch:
                                    stage1(it)
                                if 1 <= it <= nch:
                                    stage2(it - 1)
                                if it >= 2:
                                    stage3(it - 2)
                                yield it
                            copy_out(ysb[:, h, :], "ysb", PS(SB_O)[:, :], [pn(SB_O)])

                        for g in range(2):
                            set_rot([6, 7])
                            for hg in range(4):
                                h = g * 4 + hg
                                gb, gname = load_gates(h)
                                chunks = []
                                for j in range(ncmp):
                                    chunks.append(dict(kT=kcTs[:, g, j * 128:(j + 1) * 128], kn=["kcTs"], v=vcs[:, g, j, :], vn=["vcs"],
                                                       masks=[(cmv("negI"), cmask[:, j, :], ["cmask"])]))

                                def extra(i, P, pname, n):
                                    for ts in range(4):
                                        ub = 2 + ts
                                        o_ = 0
                                        S.op("pe", lambda e, P=P, ts=ts, ub=ub, o_=o_, i=i, n=n: e.matmul(PS(ub)[:, o_:o_ + 257], lhsT=P[:, ts * 128:(ts + 1) * 128],
                                                                                                    rhs=cm[:, ML["OV"] + i * 257:ML["OV"] + (i + 1) * 257],
                                                                                                    start=(i == 0), stop=(i == n - 1)),
                                             reads=[pname, "cm"], writes=[pn(ub)])
                                softmax_chunks(chunks, qn[:, h, :], ["qn"], extra)
                                finalize(ynsa[:, hg, :], "ynsa", gb[:, 0, :], gname)
                                for ts in range(4):
                                    ub = 2 + ts
                                    o_ = 0
                                    S.op("dve", lambda e, ts=ts, ub=ub, o_=o_: e.tensor_scalar(out=rtok[:, ts:ts + 1], in0=PS(ub)[:, o_ + 256:o_ + 257], scalar1=1e-30, scalar2=None, op0=ALU.max),
                                         reads=[pn(ub)], writes=["rtok"])
                                    S.op("dve", lambda e, ts=ts: e.reciprocal(out=rtok[:, ts:ts + 1], in_=rtok[:, ts:ts + 1]), reads=["rtok"], writes=["rtok"])
                                    if hg == 0:
                                        S.op("dve", lambda e, ts=ts, ub=ub, o_=o_: e.tensor_scalar(out=impacc[:, g, ts, :], in0=PS(ub)[:, o_:o_ + 256], scalar1=rtok[:, ts:ts + 1], scalar2=None, op0=ALU.mult),
                                             reads=[pn(ub), "rtok"], writes=["impacc"])
                                    else:
                                        S.op("dve", lambda e, ts=ts, ub=ub, o_=o_: e.scalar_tensor_tensor(out=impacc[:, g, ts, :], in0=PS(ub)[:, o_:o_ + 256], scalar=rtok[:, ts:ts + 1],
                                                                                                       in1=impacc[:, g, ts, :], op0=ALU.mult, op1=ALU.add),
                                             reads=[pn(ub), "rtok", "impacc"], writes=["impacc"])
                            set_rot([2, 3, 4, 5, 6, 7])
                            for ts in range(4):
                                curc = ccol("cur", k * 4 + ts)
                                blk = rowt[:, 0:256]
                                S.op("dve", lambda e, curc=curc: e.tensor_scalar(out=tk1[:], in0=blk, scalar1=curc, scalar2=None, op0=ALU.subtract), reads=["rowt", "cst"], writes=["tk1"])
                                S.op("dve", lambda e: e.tensor_scalar(out=tk2[:], in0=tk1[:], scalar1=0.0, scalar2=None, op0=ALU.is_gt), reads=["tk1"], writes=["tk2"])
                                S.op("dve", lambda e: e.tensor_scalar(out=tk3[:], in0=tk1[:], scalar1=-1.0, scalar2=None, op0=ALU.is_ge), reads=["tk1"], writes=["tk3"])
                                S.op("dve", lambda e: e.tensor_tensor(out=tk3[:], in0=tk3[:], in1=tk2[:], op=ALU.subtract), reads=["tk3", "tk2"], writes=["tk3"])
                                S.op("dve", lambda e: e.tensor_scalar(out=tk4[:], in0=blk, scalar1=0.0, scalar2=None, op0=ALU.is_equal), reads=["rowt"], writes=["tk4"])
                                S.op("dve", lambda e: e.tensor_tensor(out=tk3[:], in0=tk3[:], in1=tk4[:], op=ALU.max), reads=["tk3", "tk4"], writes=["tk3"])
                                S.op("dve", lambda e: e.tensor_tensor(out=tk3[:], in0=tk3[:], in1=tk2[:], op=ALU.subtract), reads=["tk3", "tk2"], writes=["tk3"])
                                S.op("dve", lambda e, ts=ts: e.scalar_tensor_tensor(out=tk1[:], in0=tk3[:], scalar=1.0e4, in1=impacc[:, g, ts, :], op0=ALU.mult, op1=ALU.add),
                                     reads=["tk3", "impacc"], writes=["tk1"])
                                S.op("dve", lambda e: e.max(out=m8[:, 0:8], in_=tk1[:]), reads=["tk1"], writes=["m8a"])
                                S.op("dve", lambda e: e.match_replace(out=tk4[:], in_to_replace=m8[:, 0:8], in_values=tk1[:], imm_value=-3.0e4), reads=["tk1", "m8a"], writes=["tk4"])
                                S.op("dve", lambda e: e.max(out=m8[:, 8:16], in_=tk4[:]), reads=["tk4"], writes=["m8b"])
                                S.op("dve", lambda e: e.tensor_scalar(out=tk4[:], in0=tk1[:], scalar1=m8[:, 15:16], scalar2=None, op0=ALU.is_ge), reads=["tk1", "m8b"], writes=["tk4"])
                                S.op("dve", lambda e: e.tensor_scalar(out=tk2[:], in0=tk2[:], scalar1=-1.0, scalar2=1.0, op0=ALU.mult, op1=ALU.add), reads=["tk2"], writes=["tk2"])
                                S.op("dve", lambda e: e.tensor_tensor(out=alb[:], in0=tk4[:], in1=tk2[:], op=ALU.mult), reads=["tk4", "tk2"], writes=["alb"])
                                for half in range(2):
                                    pb = nextps()
                                    S.op("pe", lambda e, half=half, pb=pb: e.matmul(PS(pb)[:, 0:128], lhsT=alb[:, half * 128:(half + 1) * 128], rhs=cmv("ident"), start=True, stop=True),
                                         reads=["alb", "cm"], writes=[pn(pb)])
                                    copy_out(alT[:, half, ts * 128:(ts + 1) * 128], "alT", PS(pb)[:, 0:128], [pn(pb)])
                            pb = nextps()
                            for half in range(2):
                                osel = cm[:, ML["OwnSel"] + k * 16 + half * 8:ML["OwnSel"] + k * 16 + half * 8 + 8]
                                S.op("pe", lambda e, half=half, pb=pb, osel=osel: e.matmul(PS(pb)[0:8, :], lhsT=osel, rhs=alT[:, half, :], start=(half == 0), stop=(half == 1)),
                                     reads=["alT", "cm"], writes=[pn(pb)])
                            S.op("dve", lambda e, pb=pb: e.tensor_scalar(out=ownb[:, g, :], in0=PS(pb)[0:8, :], scalar1=-1.0, scalar2=-NEG, op0=ALU.add, op1=ALU.mult),
                                 reads=[pn(pb)], writes=["ownb"])
                            for half in range(2):
                                S.op("dve", lambda e, half=half: e.tensor_scalar(out=selbT[:, g, half, :], in0=alT[:, half, :], scalar1=cst[:, CL["_om"] + k * 2 + half:CL["_om"] + k * 2 + half + 1],
                                                                               scalar2=None, op0=ALU.mult), reads=["alT", "cst"], writes=["selbT"])
                                S.op("dve", lambda e, half=half: e.tensor_scalar(out=selbT[:, g, half, :], in0=selbT[:, g, half, :], scalar1=-1.0, scalar2=-NEG, op0=ALU.add, op1=ALU.mult),
                                     reads=["selbT"], writes=["selbT"])
                            set_rot([2, 3, 4, 5, 6])
                            import itertools as _it
                            sbit = _it.chain(sb_steps(2 * g), sb_steps(2 * g + 1))

                            def sb_adv(n_):
                                for _ in range(n_):
                                    next(sbit, None)
                            for hg in range(4):
                                h = g * 4 + hg
                                gb, gname = load_gates(h)
                                chunks = []
                                for c_ in range(4):
                                    chunks.append(dict(kT=ksTo[:, g, c_ * 128:(c_ + 1) * 128], kn=["ksTo"], v=vown[:, c_, g * 128:(g + 1) * 128], vn=["vown"],
                                                       masks=[(cm[0:8, ML["Eown"] + c_ * 128:ML["Eown"] + (c_ + 1) * 128], ownb[:, g, :], ["ownb"]),
                                                              (cmv("negI"), cm[:, ML["DiagN"] + c_ * 512:ML["DiagN"] + (c_ + 1) * 512], [])]))
                                ntot = 4 + 4 * (8 * k + 8)
                                softmax_chunks(chunks, qn[:, h, :], ["qn"], None, 0, ntot)
                                sb_adv(2)
                                for gt in range(8 * k + 8):
                                    kb, vb, kname, vname = stream_tile(ksT_d[g], g * 128, gt)
                                    chunks = []
                                    for c_ in range(4):
                                        cg = gt * 4 + c_
                                        chunks.append(dict(kT=kb[:, c_ * 128:(c_ + 1) * 128], kn=[kname], v=vb[:, c_, :], vn=[vname],
                                                           masks=[(cm[:, ML["E64"] + (cg % 64) * 128:ML["E64"] + (cg % 64 + 1) * 128], selbT[:, g, cg // 64, :], ["selbT"])]))
                                    softmax_chunks(chunks, qn[:, h, :], ["qn"], None, 4 + 4 * gt, ntot)
                                    sb_adv(2)
                                finalize(ynsa[:, hg, :], "ynsa", gb[:, 1, :], gname, accumulate=True)
                            for _ in sbit:
                                pass
                            set_rot([2, 3, 4, 5, 6, 7])
                            for hg in range(4):
                                h = g * 4 + hg
                                gb, gname = load_gates(h)
                                chunks = []
                                for c_ in range(4):
                                    chunks.append(dict(kT=kwTp[:, g, c_ * 128:(c_ + 1) * 128], kn=["kwTp"], v=vprev[:, c_, g * 128:(g + 1) * 128], vn=["vprev"],
                                                       masks=[(cmv("negI"), cm[:, ML["DiagW"] + c_ * 512:ML["DiagW"] + (c_ + 1) * 512], [])], bias=ccol("pv", k)))
                                for c_ in range(4):
                                    chunks.append(dict(kT=kwTo[:, g, c_ * 128:(c_ + 1) * 128], kn=["kwTo"], v=vown[:, c_, 256 + g * 128:256 + (g + 1) * 128], vn=["vown"],
                                                       masks=[(cmv("negI"), cm[:, ML["DiagN"] + c_ * 512:ML["DiagN"] + (c_ + 1) * 512], [])]))
                                softmax_chunks(chunks, qn[:, h, :], ["qn"])
                                finalize(ynsa[:, hg, :], "ynsa", gb[:, 2, :], gname, accumulate=True)
                                copy_out(ynsa_b[:, h, :], "ynsa_b", ynsa[:, hg, :], ["ynsa"])
                        for h in range(4):
                            chunks = [dict(kT=kmem[:, h, ms * 128:(ms + 1) * 128], kn=["kmem"], v=vmem[:, ms, h * 128:(h + 1) * 128], vn=["vmem"]) for ms in range(2)]
                            softmax_chunks(chunks, qm[:, h, :], ["qm"])
                            finalize(ymem[:, h, :], "ymem")
                        stage_end()

                with ExitStack() as es_b:
                    x1 = sbt(es_b, "x1", [128, KC, TQ], F32)
                    hn = sbt(es_b, "hnB", [128, KC, TQ], BF16)
                    S.dma("sp", "hnld", lambda e: e.dma_start(out=hn[:].rearrange("p c t -> p (c t)"), in_=hn_d), reads=["hn_d"], writes=["hn"])
                    with ExitStack() as es:
                        set_rot(range(8))
                        mixed = sbt(es, "mixed", [128, KC, TQ], BF16)
                        sig = sbt(es, "sig", [128, 6, TQ], F32)
                        acc = [sbt(es, "acc", [128, TQ], F32) for _ in range(2)]
                        tmpm = [sbt(es, "tmpm", [128, TQ], F32) for _ in range(2)]
                        wsg = WStream(es, "b1wg", 2, 4096)
                        wso = WStream(es, "b1wo", 3, 2048)
                        for dc2 in range(KC // 2):
                            for br in range(3):
                                def ev_s(j, pb, br=br):
                                    S.op("act", lambda e: e.activation(out=sig[:, br * 2 + j, :], in_=PS(pb)[:, :], func=AF.Sigmoid), reads=[pn(pb)], writes=["sig%d" % (br * 2 + j)])
                                proj_fm(wsg, w_in_b, D, 4632 + br * D + dc2 * 256, 256, lambda c: hn[:, c, :], ["hn"], TQ, ev_s)
                            for br, (wo, K_, y_, yn_) in enumerate([(wo_nsa_b, 1024, ynsa_b, "ynsa_b"), (wo_sb_b, 512, ysb, "ysb"), (wo_mem_b, 512, ymem, "ymem")]):
                                def ev_a(j, pb, br=br):
                                    sn = "sig%d" % (br * 2 + j)
                                    if br == 0:
                                        S.op("dve", lambda e: e.tensor_tensor(out=acc[j][:], in0=PS(pb)[:, :], in1=sig[:, j, :], op=ALU.mult), reads=[pn(pb), sn], writes=["acc%d" % j])
                                    else:
                                        S.op("dve", lambda e: e.tensor_tensor(out=tmpm[j][:], in0=PS(pb)[:, :], in1=sig[:, br * 2 + j, :], op=ALU.mult), reads=[pn(pb), sn], writes=["tmpm%d" % j])
                                        if br == 1:
                                            S.op("dve", lambda e: e.tensor_tensor(out=acc[j][:], in0=acc[j][:], in1=tmpm[j][:], op=ALU.add), reads=["acc%d" % j, "tmpm%d" % j], writes=["acc%d" % j])
                                        else:
                                            S.op("dve", lambda e: e.tensor_tensor(out=mixed[:, dc2 * 2 + j, :], in0=acc[j][:], in1=tmpm[j][:], op=ALU.add), reads=["acc%d" % j, "tmpm%d" % j], writes=["mixed"])
                                proj_fm(wso, wo, K_, dc2 * 256, 256, lambda c, y_=y_: y_[:, c, :], [yn_], TQ, ev_a)
                        xr = [sbt(es, "xr", [128, TQ], F32) for _ in range(2)]

                        def ev_o(j, pb):
                            b = xr[j % 2]
                            S.dma("sp", "xr%d" % (j % 2), lambda e: e.dma_start(out=b[:], in_=xT_own[j * 128:(j + 1) * 128, o0:o0 + TQ]), writes=["xr%d" % (j % 2)])
                            S.op("dve", lambda e: e.tensor_tensor(out=x1[:, j, :], in0=PS(pb)[:, :], in1=b[:], op=ALU.add), reads=[pn(pb), "xr%d" % (j % 2)], writes=["x1"])
                        proj_fm(wsg, w_out_b, D, 0, D, lambda c: mixed[:, c, :], ["mixed"], TQ, ev_o, gcmax=256)
                        stage_end()
                    with ExitStack() as es:
                        set_rot(range(8))
                        act_all = sbt(es, "act_all", [128, FFN // 256, TQ], BF16)
                        sq2 = [sbt(es, "sq2", [128, TQ], BF16) for _ in range(2)]
                        rs2 = sbt(es, "rs2", [128, TQ], F32)
                        sgl = [sbt(es, "sgl", [128, TQ], F32) for _ in range(2)]
                        ost = [sbt(es, "ost", [128, TQ], F32) for _ in range(2)]
                        wsf = WStream(es, "b2wf", 3, 5632)
                        pb0 = nextps()
                        for c in range(KC):
                            S.op("act", lambda e, c=c: e.activation(out=sq2[c % 2][:], in_=x1[:, c, :], func=AF.Square), reads=["x1"], writes=["sq2_%d" % (c % 2)])
                            S.op("pe", lambda e, c=c: e.matmul(PS(pb0)[:, :], lhsT=cmv("ones"), rhs=sq2[c % 2][:], start=(c == 0), stop=(c == KC - 1)),
                                 reads=["sq2_%d" % (c % 2), "cm"], writes=[pn(pb0)])
                        S.op("act", lambda e: e.activation(out=rs2[:], in_=PS(pb0)[:, :], func=AF.Sqrt, scale=1.0 / D, bias=ccol("eps")), reads=[pn(pb0), "cst"], writes=["rs2"])
                        S.op("dve", lambda e: e.reciprocal(out=rs2[:], in_=rs2[:]), reads=["rs2"], writes=["rs2"])
                        for c in range(KC):
                            S.op("dve", lambda e, c=c: e.scalar_tensor_tensor(out=hn[:, c, :], in0=x1[:, c, :], scalar=ccol("ffn", c), in1=rs2[:], op0=ALU.mult, op1=ALU.mult),
                                 reads=["x1", "rs2", "cst"], writes=["hn"])
                        for hf_ in range(2):
                          for fi2 in range(FFN // 512):
                            def ev_g2(j, pb):
                                S.op("act", lambda e: e.activation(out=sgl[j][:], in_=PS(pb)[:, :], func=AF.Silu), reads=[pn(pb)], writes=["sgl%d" % j])
                            proj_fm(wsf, w_gate_b, D, hf_ * 2816 + fi2 * 256, 256, lambda c: hn[:, c, :], ["hn"], TQ, ev_g2)

                            def ev_u(j, pb, fi2=fi2):
                                S.op("dve", lambda e: e.tensor_tensor(out=act_all[:, fi2 * 2 + j, :], in0=PS(pb)[:, :], in1=sgl[j][:], op=ALU.mult), reads=[pn(pb), "sgl%d" % j], writes=["act_all"])
                            proj_fm(wsf, w_up_b, D, hf_ * 2816 + fi2 * 256, 256, lambda c: hn[:, c, :], ["hn"], TQ, ev_u)

                          def ev_d(j, pb, hf_=hf_):
                            if hf_ == 0:
                                S.op("dve", lambda e: e.tensor_tensor(out=x1[:, j, :], in0=PS(pb)[:, :], in1=x1[:, j, :], op=ALU.add), reads=[pn(pb), "x1"], writes=["x1"])
                                return
                            b = ost[j % 2]
                            S.op("dve", lambda e: e.tensor_tensor(out=b[:], in0=PS(pb)[:, :], in1=x1[:, j, :], op=ALU.add), reads=[pn(pb), "x1"], writes=["ost%d" % (j % 2)])
                            S.dma("sp", "ost%d" % (j % 2), lambda e: e.dma_start(out=outT[j * 128:(j + 1) * 128, o0:o0 + TQ], in_=b[:]), reads=["ost%d" % (j % 2)], writes=["outT"])
                          proj_fm(wsf, w_down_b[hf_ * 2816:(hf_ + 1) * 2816, :], 2816, 0, D, lambda c: act_all[:, c, :], ["act_all"], TQ, ev_d, gcmax=256)

                        stage_end()
        S.barrier()
        S.flush()
    return nc


def host_consts(T, core):
    NT = T // (NCORE * TQ)
    NCC = T // 2048
    CL = cst_layout(NT, NCC)
    ML = cm_layout(NT, NCC)
    p = np.arange(128)
    cm = np.zeros((128, ML["_n"]), np.float32)
    cm[p, ML["ident"] + p] = 1.0
    cm[p, ML["negI"] + p] = NEG
    cm[:, ML["ones"]:ML["ones"] + 128] = 1.0
    cm[:, ML["onesneg"]:ML["onesneg"] + 128] = -1.0
    cm[:, ML["UIneg"]:ML["UIneg"] + 128] = -(p[:, None] >= p[None, :]).astype(np.float32)
    for m in range(128):
        if m < 64:
            cm[m + 64, ML["prot"] + m] = -1.0
        else:
            cm[m - 64, ML["prot"] + m] = 1.0
    u = np.arange(8192)
    cm[:, ML["E64"]:ML["E64"] + 8192] = (p[:, None] == (u[None, :] // 64)).astype(np.float32)
    for j in range(NCC):
        n = 128 * j + p
        s = np.arange(256)
        ov = ((n[:, None] >= 4 * s[None, :] - 1) & (n[:, None] <= 4 * s[None, :] + 3) & (n[:, None] < T // 16 - 1)).astype(np.float32)
        cm[:, ML["OV"] + j * 257:ML["OV"] + j * 257 + 256] = ov
        cm[:, ML["OV"] + j * 257 + 256] = (n < T // 16 - 1).astype(np.float32)
    t = np.arange(512)
    for c_ in range(4):
        sg_ = 128 * c_ + p
        cm[:, ML["DiagN"] + c_ * 512:ML["DiagN"] + (c_ + 1) * 512] = (sg_[:, None] > t[None, :]).astype(np.float32)
        cm[:, ML["DiagS"] + c_ * 512:ML["DiagS"] + (c_ + 1) * 512] = (sg_[:, None] >= t[None, :]).astype(np.float32)
        cm[:, ML["DiagW"] + c_ * 512:ML["DiagW"] + (c_ + 1) * 512] = (sg_[:, None] <= t[None, :]).astype(np.float32)
        s_ = np.arange(128)
        for r in range(2):
            cm[2 * c_ + r, ML["Eown"] + c_ * 128 + s_] = (s_ // 64 == r).astype(np.float32)
    for k in range(NT):
        ob0 = 8 * (8 * k + core)
        for half in range(2):
            for b in range(8):
                blk = ob0 + b
                if blk // 128 == half:
                    cm[blk % 128, ML["OwnSel"] + k * 16 + half * 8 + b] = 1.0
    cst = np.zeros((128, CL["_n"]), np.float32)
    cst[:, CL["eps"]] = 1e-6
    cst[:, CL["one"]] = 1.0
    cst[:, CL["halfpi"]] = np.float32(np.pi / 2)
    cst[:, CL["tiny"]] = 1e-30
    inv = (np.float32(1.0) / np.power(np.float32(10000.0), np.arange(0, 128, 2, dtype=np.float32) / np.float32(128))).astype(np.float32)
    cst[:, CL["inv"]] = inv[p % 64]
    for j in range(8):
        cst[:, CL["cw"] + j] = 0.0 if j < core else NEG
    for k in range(NT):
        gt = 8 * k + core
        cst[:, CL["pv"] + k] = NEG if gt == 0 else 0.0
        for ts in range(4):
            cst[:, CL["cur"] + k * 4 + ts] = (gt * 512 + ts * 128 + p) // 64
        for j in range(NCC):
            cst[:, CL["cthr"] + k * NCC + j] = gt * 512 - 2048 * j
        ob0 = 8 * gt
        for half in range(2):
            cst[:, CL["_om"] + k * 2 + half] = ((half * 128 + p) < ob0).astype(np.float32)
    rowt = np.zeros((128, 256), np.float32)
    rowt[:, 0:256] = np.arange(256)[None, :]
    j2 = (16 * p[:, None] + 31 - t[None, :]).astype(np.float32)
    return cst, cm, rowt, j2, CL


_NC_CACHE = {}


def kernel(x, mem, positions, attn_norm, w_in, nsa_q_norm, nsa_kc_norm, nsa_ks_norm, nsa_kw_norm,
           cmp_k_pe, cmp_k_w1, cmp_k_w2, cmp_v_pe, cmp_v_w1, cmp_v_w2, mem_norm, w_mem_kv,
           mem_q_norm, mem_k_norm, w_o_nsa, w_o_sb, w_o_mem, w_out, ffn_norm,
           w_ffn_gate, w_ffn_up, w_ffn_down):
    x = np.asarray(x)
    T = x.shape[1]
    NT = T // (NCORE * TQ)
    NCC = T // 2048
    NCP = NCC * 128
    if T not in _NC_CACHE:
        _NC_CACHE[T] = build(T)
    nc = _NC_CACHE[T]
    f = lambda a: np.ascontiguousarray(np.asarray(a, dtype=np.float32))
    xT = np.ascontiguousarray(np.asarray(x)[0].T)
    pos = np.asarray(positions).astype(np.int32)
    pos_cmp = np.zeros((1, NCP), np.int32)
    pc = pos[0, 31::16]
    pos_cmp[0, :len(pc)] = pc
    common = {
        "xT_all": xT, "memT": np.ascontiguousarray(np.asarray(mem)[0].T), "pos_all": pos, "pos_cmp": pos_cmp,
        "w_in": f(w_in[0]), "w1k": f(cmp_k_w1[0]), "w2k": f(cmp_k_w2[0]), "w1v": f(cmp_v_w1[0]), "w2v": f(cmp_v_w2[0]),
        "w_mem": f(w_mem_kv[0]), "wo_nsa": f(w_o_nsa[0]), "wo_sb": f(w_o_sb[0]), "wo_mem": f(w_o_mem[0]), "w_out": f(w_out[0]),
        "w_gate": f(w_ffn_gate[0]), "w_up": f(w_ffn_up[0]), "w_down": f(w_ffn_down[0]),
    }
    in_maps = []
    for core in range(NCORE):
        cst, cm, rowt, j2, CL = host_consts(T, core)
        for name, arr in [("attn", attn_norm), ("ffn", ffn_norm), ("memn", mem_norm)]:
            cst[:, CL[name]:CL[name] + 16] = np.asarray(arr)[0].reshape(16, 128).T
        for name, arr in [("qn", nsa_q_norm), ("kcn", nsa_kc_norm), ("ksn", nsa_ks_norm), ("kwn", nsa_kw_norm), ("mqn", mem_q_norm), ("mkn", mem_k_norm)]:
            cst[:, CL[name]] = np.asarray(arr)[0]
        cst[:, CL["pek"]:CL["pek"] + 32] = np.asarray(cmp_k_pe)[0].T
        cst[:, CL["pev"]:CL["pev"] + 32] = np.asarray(cmp_v_pe)[0].T
        own = np.zeros((D, NT * TQ), np.float32)
        prev = np.zeros((D, NT * TQ), np.float32)
        pown = np.zeros((1, NT * TQ), np.int32)
        pprev = np.zeros((1, NT * TQ), np.int32)
        for k in range(NT):
            gt = 8 * k + core
            own[:, k * TQ:(k + 1) * TQ] = xT[:, gt * TQ:(gt + 1) * TQ]
            pown[0, k * TQ:(k + 1) * TQ] = pos[0, gt * TQ:(gt + 1) * TQ]
            if gt > 0:
                prev[:, k * TQ:(k + 1) * TQ] = xT[:, (gt - 1) * TQ:gt * TQ]
                pprev[0, k * TQ:(k + 1) * TQ] = pos[0, (gt - 1) * TQ:gt * TQ]
        m = dict(common)
        m.update({"xT_own": own, "xT_prev": prev, "pos_own": pown, "pos_prev": pprev, "cst": cst, "cmat": cm, "rowt": rowt, "j2": j2})
        in_maps.append(m)
    res = run_bass_kernel_spmd(nc, in_maps, core_ids=list(range(NCORE)))
    out = np.zeros((1, T, D), np.float32)
    for core in range(NCORE):
        oT = np.asarray(res.results[core]["outT"])
        for k in range(NT):
            gt = 8 * k + core
            out[0, gt * TQ:(gt + 1) * TQ, :] = oT[:, k * TQ:(k + 1) * TQ].T
    return out
```

```python
import math
from contextlib import ExitStack
import numpy as np
import concourse.bass as bass
import concourse.mybir as mybir
from concourse.bass_utils import run_bass_kernel_spmd

F32 = mybir.dt.float32
BF16 = mybir.dt.bfloat16
I32 = mybir.dt.int32
ALU = mybir.AluOpType
AF = mybir.ActivationFunctionType

D = 2048
KC = 16
NCORE = 8
TQ = 512
FFN = 5632
NEG = -30000.0
SCALE = 128 ** -0.5
MAGIC = 12582912.0
C1 = 6.28125
C2 = float(2 * np.pi - 6.28125)
PI_LO = 3.1415925

COMPUTE_Q = ("pe", "act", "dve", "pool")
ALLQ = ("pe", "act", "dve", "pool", "sp")


class Sched:
    def __init__(self, nc, es):
        self.nc = nc
        self.es = es
        self.q = {k: [] for k in ALLQ}
        self.cnt = {k: 0 for k in COMPUTE_Q}
        self.sems = {k: es.enter_context(nc.semaphore("s_" + k)) for k in COMPUTE_Q}
        self.dcnt = {}
        self.lastw = {}
        self.readers = {}
        self.seen = {k: {} for k in ALLQ}
        self.nops = 0

    def _deps(self, q, reads, writes):
        need = {}

        def add(tok):
            if tok is None:
                return
            k, v = tok
            if need.get(k, 0) < v:
                need[k] = v
        for b in reads:
            add(self.lastw.get(b))
        for b in writes:
            add(self.lastw.get(b))
            for t in self.readers.get(b, ()):
                add(t)
        out = []
        for k, v in need.items():
            if k == q and q == "pe":
                continue
            if self.seen[q].get(k, 0) >= v:
                continue
            self.seen[q][k] = v
            out.append((k, v))
        return out

    def _commit(self, tok, reads, writes):
        for b in reads:
            self.readers.setdefault(b, []).append(tok)
        for b in writes:
            self.lastw[b] = tok
            self.readers[b] = []

    def _rec(self, fn):
        class _R:
            def __getattr__(s, name):
                def f(*a, **kw):
                    s.call = (name, a, kw)
                    return None
                return f
        r = _R()
        fn(r)
        name, a, kw = r.call
        return lambda eng: getattr(eng, name)(*a, **kw)

    def op(self, q, fn, reads=(), writes=(), sig=True):
        fn = self._rec(fn)
        waits = self._deps(q, reads, writes)
        if not sig:
            tok = (q, self.cnt[q] + 1)
            self.q[q].append((waits, fn, None, 0))
            self._commit(tok, reads, writes)
            self.nops += 1
            return tok
        self.cnt[q] += 1
        tok = (q, self.cnt[q])
        self.q[q].append((waits, fn, q, 1))
        self._commit(tok, reads, writes)
        self.nops += 1
        return tok

    def dma(self, q, key, fn, reads=(), writes=()):
        fn = self._rec(fn)
        waits = self._deps(q, reads, writes)
        k = "dma:" + key
        if k not in self.sems:
            self.sems[k] = self.es.enter_context(self.nc.semaphore("d_" + key))
            self.dcnt[k] = 0
        self.dcnt[k] += 16
        tok = (k, self.dcnt[k])
        self.q[q].append((waits, fn, k, 16))
        self._commit(tok, reads, writes)
        self.nops += 1
        return tok

    def barrier(self):
        allw = [(k, v) for k, v in self.cnt.items() if v > 0] + [(k, v) for k, v in self.dcnt.items() if v > 0]
        for q in ALLQ:
            w = [(k, v) for k, v in allw if self.seen[q].get(k, 0) < v]
            for k, v in w:
                self.seen[q][k] = v
            if w:
                self.q[q].append((w, None, None, 0))
        self.lastw = {}
        self.readers = {}

    def flush(self):
        nc = self.nc
        engs = {"pe": "tensor", "act": "scalar", "dve": "vector", "pool": "gpsimd", "sp": "sync"}
        if not any(self.q.values()):
            return
        with nc.Block() as block:
            def mk(items):
                def body(eng):
                    for waits, fn, semk, inc in items:
                        for k, v in waits:
                            eng.wait_ge(self.sems[k], v)
                        if fn is not None:
                            ins = fn(eng)
                            if semk is not None:
                                ins.then_inc(self.sems[semk], inc)
                return body
            for qn, attr in engs.items():
                if self.q[qn]:
                    getattr(block, attr)(mk(self.q[qn]))
        self.q = {k: [] for k in ALLQ}


def cst_layout(NT, NCC):
    o = {}
    c = 0
    for name, n in [("attn", 16), ("ffn", 16), ("memn", 16), ("qn", 1), ("kcn", 1), ("ksn", 1), ("kwn", 1),
                    ("mqn", 1), ("mkn", 1), ("inv", 1), ("eps", 1), ("one", 1), ("zero", 1), ("halfpi", 1),
                    ("cw", 8), ("pv", NT), ("cur", NT * 4), ("cthr", NT * NCC), ("pek", 32), ("pev", 32),
                    ("tiny", 1), ("_om", NT * 2)]:
        o[name] = c
        c += n
    o["_n"] = c
    return o


def cm_layout(NT, NCC):
    o = {}
    c = 0
    for name, n in [("ident", 128), ("negI", 128), ("ones", 128), ("onesneg", 128), ("UIneg", 128), ("prot", 128),
                    ("E64", 8192), ("OV", NCC * 257), ("DiagN", 2048), ("DiagS", 2048), ("DiagW", 2048),
                    ("Eown", 512), ("OwnSel", NT * 16)]:
        o[name] = c
        c += n
    o["_n"] = c
    return o


def build(T):
    NT = T // (NCORE * TQ)
    NGT = T // TQ
    NCC = T // 2048
    NCP = NCC * 128
    NB = min(512, NCP)
    CL = cst_layout(NT, NCC)
    ML = cm_layout(NT, NCC)
    TO = NT * TQ

    nc = bass.Bass("TRN2", target_bir_lowering=False)

    def din(name, shape, dt=F32):
        return nc.dram_tensor(name, list(shape), dt, kind="ExternalInput").ap()

    xT_all = din("xT_all", [D, T])
    xT_own = din("xT_own", [D, TO])
    xT_prev = din("xT_prev", [D, TO])
    memT = din("memT", [D, 256])
    pos_all = din("pos_all", [1, T], I32)
    pos_own = din("pos_own", [1, TO], I32)
    pos_prev = din("pos_prev", [1, TO], I32)
    pos_cmp = din("pos_cmp", [1, NCP], I32)
    w_in = din("w_in", [D, 10776])
    w1k = din("w1k", [32, 128, 256])
    w2k = din("w2k", [256, 128])
    w1v = din("w1v", [32, 128, 256])
    w2v = din("w2v", [256, 128])
    w_mem = din("w_mem", [D, 1024])
    wo_nsa = din("wo_nsa", [1024, D])
    wo_sb = din("wo_sb", [512, D])
    wo_mem = din("wo_mem", [512, D])
    w_out = din("w_out", [D, D])
    w_gate = din("w_gate", [D, FFN])
    w_up = din("w_up", [D, FFN])
    w_down = din("w_down", [FFN, D])
    cst_d = din("cst", [128, CL["_n"]])
    rowt_d = din("rowt", [128, 256])
    j2_d = din("j2", [128, 512])
    cmat_d = din("cmat", [128, ML["_n"]])
    outT = nc.dram_tensor("outT", [D, TO], F32, kind="ExternalOutput").ap()

    kcT_d = nc.dram_tensor("kcT_d", [2, 128, T], BF16).ap()
    vcT_d = nc.dram_tensor("vcT_d", [2, 128, T], BF16).ap()
    ksT_d = nc.dram_tensor("ksT_d", [2, 128, T], BF16).ap()
    sbkT_d = nc.dram_tensor("sbkT_d", [4, 128, T], BF16).ap()
    vtok_d = nc.dram_tensor("vtok_d", [T, 768], BF16).ap()
    gates_d = nc.dram_tensor("gates_d", [24, TQ], F32).ap()
    hn_d = nc.dram_tensor("hn_d", [128, KC * TQ], BF16).ap()
    w_in_b = nc.dram_tensor("w_in_b", [D, 10776], BF16).ap()
    wo_nsa_b = nc.dram_tensor("wo_nsa_b", [1024, D], BF16).ap()
    wo_sb_b = nc.dram_tensor("wo_sb_b", [512, D], BF16).ap()
    wo_mem_b = nc.dram_tensor("wo_mem_b", [512, D], BF16).ap()
    w_out_b = nc.dram_tensor("w_out_b", [D, D], BF16).ap()
    w_gate_b = nc.dram_tensor("w_gate_b", [D, FFN], BF16).ap()
    w_up_b = nc.dram_tensor("w_up_b", [D, FFN], BF16).ap()
    w_down_b = nc.dram_tensor("w_down_b", [FFN, D], BF16).ap()

    with ExitStack() as es_all:
        S = Sched(nc, es_all)
        uid = [0]

        def sbt(es, name, shape, dt):
            uid[0] += 1
            return es.enter_context(nc.sbuf_tensor("%s_%d" % (name, uid[0]), list(shape), dt))

        PSB = [es_all.enter_context(nc.psum_tensor("ps%d" % i, [128, 512], F32)) for i in range(8)]
        rot = {"banks": list(range(8)), "i": 0}

        def set_rot(banks):
            rot["banks"] = list(banks)
            rot["i"] = 0

        def nextps():
            b = rot["banks"][rot["i"] % len(rot["banks"])]
            rot["i"] += 1
            return b

        def PS(b):
            return PSB[b]

        def pn(b):
            return "ps%d" % b

        cst = sbt(es_all, "cst", [128, CL["_n"]], F32)
        rowt = sbt(es_all, "rowt", [128, 256], F32)
        j2 = sbt(es_all, "j2", [128, 512], F32)
        cmA = sbt(es_all, "cmA", [128, 768], BF16)
        cmB_ref = [None]

        class _CM:
            def __getitem__(self, idx):
                rows, cols = idx
                a, b = cols.start, cols.stop
                if b <= 768:
                    return cmA[rows, a:b]
                assert a >= 768
                return cmB_ref[0][rows, a - 768:b - 768]
        cm = _CM()
        kcTs = sbt(es_all, "kcTs", [128, 2, NCP], BF16)
        vcs = sbt(es_all, "vcs", [128, 2, NCC, 128], BF16)
        kmem = sbt(es_all, "kmem", [128, 4, 256], BF16)
        vmem = sbt(es_all, "vmem", [128, 2, 512], BF16)

        def ccol(name, i=0):
            return cst[:, CL[name] + i:CL[name] + i + 1]

        def cmv(name, off=0, n=128):
            return cm[:, ML[name] + off:ML[name] + off + n]

        S.dma("sp", "c0a", lambda e: e.dma_start(out=cst[:], in_=cst_d), writes=["cst"])
        S.dma("sp", "c0b", lambda e: e.dma_start(out=rowt[:], in_=rowt_d), writes=["rowt"])
        S.dma("sp", "c0c", lambda e: e.dma_start(out=j2[:], in_=j2_d), writes=["j2"])
        S.dma("pool", "c1", lambda e: e.dma_start(out=cmA[:], in_=cmat_d[:, 0:768]), writes=["cm"])

        class Rope:
            def __init__(self, es, n, tag):
                self.n, self.tag = n, tag
                self.posi = sbt(es, "posi", [128, n], I32)
                self.ang = sbt(es, "ang", [128, n], F32)
                self.t1 = sbt(es, "rt1", [128, n], F32)
                self.kk = sbt(es, "rkk", [128, n], F32)

            def run(self, pos_ap, cosT, sinT, csname):
                n, tag = self.n, self.tag
                posi, ang, t1, kk = self.posi, self.ang, self.t1, self.kk
                S.dma("sp", tag + "pos", lambda e: e.dma_start(out=posi[:], in_=pos_ap.to_broadcast([128, n])), writes=[tag + "posi"])
                S.op("dve", lambda e: e.tensor_copy(out=t1[:], in_=posi[:]), reads=[tag + "posi"], writes=[tag + "t1"])
                S.op("dve", lambda e: e.tensor_scalar(out=ang[:], in0=t1[:], scalar1=ccol("inv"), scalar2=None, op0=ALU.mult),
                     reads=[tag + "t1", "cst"], writes=[tag + "ang"])
                S.op("dve", lambda e: e.tensor_scalar(out=t1[:], in0=ang[:], scalar1=float(1.0 / (2 * np.pi)), scalar2=MAGIC, op0=ALU.mult, op1=ALU.add),
                     reads=[tag + "ang"], writes=[tag + "t1"])
                S.op("dve", lambda e: e.tensor_scalar(out=kk[:], in0=t1[:], scalar1=MAGIC, scalar2=None, op0=ALU.subtract),
                     reads=[tag + "t1"], writes=[tag + "kk"])
                S.op("dve", lambda e: e.scalar_tensor_tensor(out=t1[:], in0=kk[:], scalar=-C1, in1=ang[:], op0=ALU.mult, op1=ALU.add),
                     reads=[tag + "kk", tag + "ang"], writes=[tag + "t1"])
                S.op("dve", lambda e: e.scalar_tensor_tensor(out=ang[:], in0=kk[:], scalar=-C2, in1=t1[:], op0=ALU.mult, op1=ALU.add),
                     reads=[tag + "kk", tag + "t1"], writes=[tag + "ang"])
                S.op("dve", lambda e: e.tensor_scalar(out=ang[:], in0=ang[:], scalar1=PI_LO, scalar2=-PI_LO, op0=ALU.min, op1=ALU.max),
                     reads=[tag + "ang"], writes=[tag + "ang"])
                S.op("dve", lambda e: e.scalar_tensor_tensor(out=t1[:], in0=ang[:], scalar=-1.0, in1=ang[:], op0=ALU.mult, op1=ALU.max),
                     reads=[tag + "ang"], writes=[tag + "t1"])
                S.op("act", lambda e: e.activation(out=sinT, in_=ang[:], func=AF.Sin), reads=[tag + "ang"], writes=[csname + "sin"])
                S.op("act", lambda e: e.activation(out=cosT, in_=t1[:], func=AF.Sin, scale=-1.0, bias=ccol("halfpi")),
                     reads=[tag + "t1", "cst"], writes=[csname + "cos"])

        class RMS:
            def __init__(self, es, n, tag):
                self.n, self.tag = n, tag
                self.xb = [sbt(es, "xch", [128, n], F32) for _ in range(3)]
                self.sq = [sbt(es, "sqc", [128, n], BF16) for _ in range(2)]
                self.rs = sbt(es, "rstd", [128, n], F32)

            def run(self, src_ap, gname, hn, hn_name):
                n, tag, xb, sq, rs = self.n, self.tag, self.xb, self.sq, self.rs
                pb = nextps()
                for c in range(KC):
                    b = xb[c % 3]
                    S.dma("sp", tag + "x%d" % (c % 3), lambda e, b=b, c=c: e.dma_start(out=b[:], in_=src_ap[c * 128:(c + 1) * 128, :]),
                          writes=[tag + "xb%d" % (c % 3)])
                    S.op("act", lambda e, b=b, c=c: e.activation(out=sq[c % 2][:], in_=b[:], func=AF.Square),
                         reads=[tag + "xb%d" % (c % 3)], writes=[tag + "sq%d" % (c % 2)])
                    S.op("pe", lambda e, c=c: e.matmul(PS(pb)[:, 0:n], lhsT=cmv("ones"), rhs=sq[c % 2][:], start=(c == 0), stop=(c == KC - 1)),
                         reads=[tag + "sq%d" % (c % 2), "cm"], writes=[pn(pb)])
                S.op("act", lambda e: e.activation(out=rs[:], in_=PS(pb)[:, 0:n], func=AF.Sqrt, scale=1.0 / D, bias=ccol("eps")),
                     reads=[pn(pb), "cst"], writes=[tag + "rs"])
                S.op("dve", lambda e: e.reciprocal(out=rs[:], in_=rs[:]), reads=[tag + "rs"], writes=[tag + "rs"])
                for c in range(KC):
                    b = xb[c % 3]
                    S.dma("sp", tag + "x%d" % (c % 3), lambda e, b=b, c=c: e.dma_start(out=b[:], in_=src_ap[c * 128:(c + 1) * 128, :]),
                          writes=[tag + "xb%d" % (c % 3)])
                    S.op("dve", lambda e, b=b, c=c: e.scalar_tensor_tensor(out=hn[:, c, :], in0=b[:], scalar=ccol(gname, c), in1=rs[:],
                                                                          op0=ALU.mult, op1=ALU.mult),
                         reads=[tag + "xb%d" % (c % 3), tag + "rs", "cst"], writes=[hn_name])

        class RMSRes:
            def __init__(self, es, n, tag):
                self.n, self.tag = n, tag
                self.xt = sbt(es, "xt", [128, KC, n], F32)
                self.sq = [sbt(es, "sqc", [128, n], BF16) for _ in range(2)]
                self.rs = sbt(es, "rstd", [128, n], F32)

            def run(self, src_ap, gname, hn, hn_name):
                n, tag, xt, sq, rs = self.n, self.tag, self.xt, self.sq, self.rs
                pb = nextps()
                for qd in range(4):
                    S.dma("sp", tag + "q%d" % qd, lambda e, qd=qd: e.dma_start(out=xt[:, 4 * qd:4 * qd + 4, :],
                                                                            in_=src_ap[qd * 512:(qd + 1) * 512, :].rearrange("(c p) t -> p c t", p=128)),
                          writes=[tag + "xq%d" % qd])
                for c in range(KC):
                    S.op("act", lambda e, c=c: e.activation(out=sq[c % 2][:], in_=xt[:, c, :], func=AF.Square),
                         reads=[tag + "xq%d" % (c // 4)], writes=[tag + "sq%d" % (c % 2)])
                    S.op("pe", lambda e, c=c: e.matmul(PS(pb)[:, 0:n], lhsT=cmv("ones"), rhs=sq[c % 2][:], start=(c == 0), stop=(c == KC - 1)),
                         reads=[tag + "sq%d" % (c % 2), "cm"], writes=[pn(pb)])
                S.op("act", lambda e: e.activation(out=rs[:], in_=PS(pb)[:, 0:n], func=AF.Sqrt, scale=1.0 / D, bias=ccol("eps")),
                     reads=[pn(pb), "cst"], writes=[tag + "rs"])
                S.op("dve", lambda e: e.reciprocal(out=rs[:], in_=rs[:]), reads=[tag + "rs"], writes=[tag + "rs"])
                for c in range(KC):
                    S.op("dve", lambda e, c=c: e.scalar_tensor_tensor(out=hn[:, c, :], in0=xt[:, c, :], scalar=ccol(gname, c), in1=rs[:],
                                                                     op0=ALU.mult, op1=ALU.mult),
                         reads=[tag + "xq%d" % (c // 4), tag + "rs", "cst"], writes=[hn_name])

        nrq = []

        def nr_tick():
            ready = []
            for it in nrq:
                it[0] -= 1
            while nrq and nrq[0][0] <= 0:
                ready.append(nrq.pop(0)[1])
            for fn_ in ready:
                fn_()

        def nr_drain():
            while nrq:
                nrq.pop(0)[1]()

        class NR:
            NSET = 2

            def __init__(self, es, n, tag):
                self.n = n
                self.tag = tag
                self.i = 0
                self.sets = []
                for _ in range(self.NSET):
                    self.sets.append(dict(sq=sbt(es, "nrsq", [128, n], BF16), r=sbt(es, "nrr", [128, n], F32), kb=sbt(es, "nrkb", [128, n], BF16),
                                          t1=sbt(es, "nrt1", [128, n], F32), t2=sbt(es, "nrt2", [128, n], F32)))

            def run(self, pin, gname, out_ap, out_name, cosT=None, sinT=None, csname=None, after=None):
                n = self.n
                si_ = self.i % self.NSET
                self.i += 1
                tag = self.tag + "s%d" % si_
                B = self.sets[si_]
                S.op("act", lambda e: e.activation(out=B["sq"][:], in_=PS(pin)[:, 0:n], func=AF.Square), reads=[pn(pin)], writes=[tag + "sq"])

                def stepB():
                    p2 = nextps()
                    S.op("pe", lambda e: e.matmul(PS(p2)[:, 0:n], lhsT=cmv("ones"), rhs=B["sq"][:], start=True, stop=True),
                         reads=[tag + "sq", "cm"], writes=[pn(p2)])
                    S.op("act", lambda e: e.activation(out=B["r"][:], in_=PS(p2)[:, 0:n], func=AF.Sqrt, scale=1.0 / 128, bias=ccol("eps")),
                         reads=[pn(p2), "cst"], writes=[tag + "r"])
                    S.op("dve", lambda e: e.reciprocal(out=B["r"][:], in_=B["r"][:]), reads=[tag + "r"], writes=[tag + "r"])
                    if cosT is None:
                        S.op("dve", lambda e: e.scalar_tensor_tensor(out=out_ap, in0=PS(pin)[:, 0:n], scalar=ccol(gname), in1=B["r"][:], op0=ALU.mult, op1=ALU.mult),
                             reads=[pn(pin), tag + "r", "cst"], writes=[out_name])
                        if after:
                            after()
                        return
                    S.op("dve", lambda e: e.scalar_tensor_tensor(out=B["kb"][:], in0=PS(pin)[:, 0:n], scalar=ccol(gname), in1=B["r"][:], op0=ALU.mult, op1=ALU.mult),
                         reads=[pn(pin), tag + "r", "cst"], writes=[tag + "kb"])
                    nrq.append([1, stepC])

                def stepC():
                    p3 = nextps()
                    S.op("pe", lambda e: e.matmul(PS(p3)[:, 0:n], lhsT=cmv("prot"), rhs=B["kb"][:], start=True, stop=True),
                         reads=[tag + "kb", "cm"], writes=[pn(p3)])
                    S.op("dve", lambda e: e.tensor_tensor(out=B["t1"][:], in0=B["kb"][:], in1=cosT, op=ALU.mult),
                         reads=[tag + "kb", csname + "cos"], writes=[tag + "t1"])
                    S.op("dve", lambda e: e.tensor_tensor(out=B["t2"][:], in0=PS(p3)[:, 0:n], in1=sinT, op=ALU.mult),
                         reads=[pn(p3), csname + "sin"], writes=[tag + "t2"])
                    S.op("dve", lambda e: e.tensor_tensor(out=out_ap, in0=B["t1"][:], in1=B["t2"][:], op=ALU.add),
                         reads=[tag + "t1", tag + "t2"], writes=[out_name])
                    if after:
                        after()
                nrq.append([1, stepB])

        class WStream:
            def __init__(self, es, tag, nbuf=2, size=5632, cast=False):
                self.bufs = [sbt(es, "wbuf", [128, size], BF16) for _ in range(nbuf)]
                self.tag = tag
                self.i = 0
                self.size = size
                self.cast = cast

            def load(self, w_ap, K, col0, gc):
                kc = K // 128
                assert kc * gc <= self.size
                i = self.i % len(self.bufs)
                self.i += 1
                view = self.bufs[i][:, 0:kc * gc].rearrange("p (c n) -> p c n", c=kc)
                name = self.tag + "w%d" % i
                S.dma("pool" if self.cast else "sp", name, lambda e: e.dma_start(out=view, in_=w_ap[:, col0:col0 + gc].rearrange("(c p) n -> p c n", p=128)),
                      reads=[] if self.cast else ["wb16"], writes=[name])
                return view, name

        def proj_fm(ws, w_ap, K, col0, ncols, rhs_fn, rhs_names, n, evac, gcmax=None):
            kc = K // 128
            gc_full = min(ncols, (ws.size // kc) // 128 * 128)
            if gcmax:
                gc_full = min(gc_full, gcmax)
            j = 0
            g0 = 0
            while g0 < ncols:
                gc = min(gc_full, ncols - g0)
                view, wname = ws.load(w_ap, K, col0 + g0, gc)
                for jj in range((gc + 127) // 128):
                    m = min(128, gc - jj * 128)
                    pb = nextps()
                    for c in range(kc):
                        S.op("pe", lambda e, c=c, jj=jj, pb=pb, view=view, m=m: e.matmul(PS(pb)[0:m, 0:n], lhsT=view[:, c, jj * 128:jj * 128 + m], rhs=rhs_fn(c),
                                                                                start=(c == 0), stop=(c == kc - 1)),
                             reads=[wname] + rhs_names, writes=[pn(pb)], sig=(c == kc - 1))
                    nr_tick()
                    evac(j, pb)
                    j += 1
                g0 += gc

        def proj_tm(ws, w_ap, K, col0, ncols, lhs_fn, lhs_names, nsub, evac):
            kc = K // 128
            gc_full = min(ncols, (ws.size // kc) // 128 * 128, 512)
            g0 = 0
            while g0 < ncols:
                gc = min(gc_full, ncols - g0)
                view, wname = ws.load(w_ap, K, col0 + g0, gc)
                for ts in range(nsub):
                    pb = nextps()
                    for c in range(kc):
                        S.op("pe", lambda e, c=c, ts=ts, pb=pb, view=view, gc=gc: e.matmul(PS(pb)[:, 0:gc], lhsT=lhs_fn(c, ts), rhs=view[:, c, 0:gc],
                                                                                  start=(c == 0), stop=(c == kc - 1)),
                             reads=[wname] + lhs_names, writes=[pn(pb)], sig=(c == kc - 1))
                    nr_tick()
                    evac(ts, g0, gc, pb)
                g0 += gc

        cpy_i = [0]

        def copy_out(out_ap, out_name, in_ap, in_names):
            cpy_i[0] += 1
            if cpy_i[0] % 2:
                S.op("act", lambda e: e.activation(out=out_ap, in_=in_ap, func=AF.Copy), reads=in_names, writes=[out_name])
            else:
                S.op("dve", lambda e: e.tensor_copy(out=out_ap, in_=in_ap), reads=in_names, writes=[out_name])

        def stage_end():
            nr_drain()
            S.barrier()
            S.flush()

        with ExitStack() as es:
            set_rot(range(8))
            wkv = sbt(es, "wkv", [128, KC, 2048], BF16)
            for (c0, n, d0) in [(1024, 768, 0), (3096, 512, 768), (1792, 256, 1280), (3608, 512, 1536)]:
                for half in range(2):
                    S.dma("pool", "wkv", lambda e, c0=c0, n=n, d0=d0, half=half: e.dma_start(
                        out=wkv[:, half * 8:(half + 1) * 8, d0:d0 + n],
                        in_=w_in[half * 1024:(half + 1) * 1024, c0:c0 + n].rearrange("(c p) n -> p c n", p=128)), writes=["wkv"])
            for (src_, dst_, rows_) in [(w_in, w_in_b, D), (wo_nsa, wo_nsa_b, 1024), (wo_sb, wo_sb_b, 512), (wo_mem, wo_mem_b, 512),
                                        (w_out, w_out_b, D), (w_gate, w_gate_b, D), (w_up, w_up_b, D), (w_down, w_down_b, FFN)]:
                for r0 in range(0, rows_, 256):
                    S.dma("pool", "wcast", lambda e, src_=src_, dst_=dst_, r0=r0: e.dma_start(out=dst_[r0:r0 + 256, :], in_=src_[r0:r0 + 256, :]), writes=["wb16"])
            hnb = [sbt(es, "hn1", [128, KC, TQ], BF16) for _ in range(2)]
            cosTb = [sbt(es, "cosT", [128, TQ], F32) for _ in range(2)]
            sinTb = [sbt(es, "sinT", [128, TQ], F32) for _ in range(2)]
            nr = NR(es, TQ, "p1nr")
            rms = RMSRes(es, TQ, "p1rms")
            rope = Rope(es, TQ, "p1rope")
            stg = [sbt(es, "stg", [128, TQ], BF16) for _ in range(3)]
            vst = [sbt(es, "vst", [128, 768], BF16) for _ in range(2)]
            si = 0

            def p1_pre(gt_):
                rms.run(xT_all[:, gt_ * TQ:(gt_ + 1) * TQ], "attn", hnb[gt_ % 2], "hn1_%d" % (gt_ % 2))
                rope.run(pos_all[0:1, gt_ * TQ:(gt_ + 1) * TQ], cosTb[gt_ % 2][:], sinTb[gt_ % 2][:], "p1cs%d" % (gt_ % 2))
            p1_pre(0)
            for gt in range(NGT):
                hn = hnb[gt % 2]
                hname = "hn1_%d" % (gt % 2)
                cosT, sinT, csn = cosTb[gt % 2], sinTb[gt % 2], "p1cs%d" % (gt % 2)
                t0 = gt * TQ
                if gt + 1 < NGT:
                    p1_pre(gt + 1)
                for j in range(10):
                    pb = nextps()
                    for c in range(KC):
                        S.op("pe", lambda e, c=c, j=j, pb=pb, hn=hn: e.matmul(PS(pb)[:, :], lhsT=wkv[:, c, j * 128:(j + 1) * 128], rhs=hn[:, c, :],
                                                                          start=(c == 0), stop=(c == KC - 1)),
                             reads=["wkv", hname], writes=[pn(pb)], sig=(c == KC - 1))
                    nr_tick()
                    sb_ = stg[si % 3]
                    sname = "stg%d" % (si % 3)
                    si += 1
                    if j in (4, 5):
                        dst = ksT_d[j - 4, :, t0:t0 + TQ]
                        nr.run(pb, "ksn", sb_[:], sname, cosT[:], sinT[:], csn,
                               after=lambda dst=dst, sb_=sb_, sname=sname: S.dma("sp", "p1st_" + sname, lambda e: e.dma_start(out=dst, in_=sb_[:]), reads=[sname], writes=["kvdram"]))
                        continue
                    copy_out(sb_[:], sname, PS(pb)[:, :], [pn(pb)])
                    if j < 2:
                        dst = kcT_d[j, :, t0:t0 + TQ]
                    elif j < 4:
                        dst = vcT_d[j - 2, :, t0:t0 + TQ]
                    else:
                        dst = sbkT_d[j - 6, :, t0:t0 + TQ]
                    S.dma("sp", "p1st_" + sname, lambda e, dst=dst, sb_=sb_: e.dma_start(out=dst, in_=sb_[:]), reads=[sname], writes=["kvdram"])
                for ts in range(4):
                    vs_ = vst[ts % 2]
                    vname = "vst%d" % (ts % 2)
                    for (c0, n) in [(1280, 256), (1536, 512)]:
                        pb = nextps()
                        for c in range(KC):
                            S.op("pe", lambda e, c=c, pb=pb, hn=hn, ts=ts, c0=c0, n=n: e.matmul(PS(pb)[:, 0:n], lhsT=hn[:, c, ts * 128:(ts + 1) * 128],
                                                                                         rhs=wkv[:, c, c0:c0 + n], start=(c == 0), stop=(c == KC - 1)),
                                 reads=["wkv", hname], writes=[pn(pb)], sig=(c == KC - 1))
                        nr_tick()
                        copy_out(vs_[:, c0 - 1280:c0 - 1280 + n], vname, PS(pb)[:, 0:n], [pn(pb)])
                    S.dma("sp", "p1sv_" + vname, lambda e, vs_=vs_, ts=ts, t0=t0: e.dma_start(out=vtok_d[t0 + ts * 128:t0 + (ts + 1) * 128, :], in_=vs_[:]),
                          reads=[vname], writes=["kvdram"])
            stage_end()

        with ExitStack() as es:
            set_rot(range(8))
            kcs = sbt(es, "kcs", [128, T + 16], BF16)
            w1b = sbt(es, "w1b", [128, 32, 256], BF16)
            w2b = sbt(es, "w2b", [128, 2, 128], BF16)
            peb = sbt(es, "peb", [128, 32], BF16)
            pebias = sbt(es, "pebias", [128, 2], F32)
            hf = sbt(es, "hf", [128, NB], F32)
            h2 = sbt(es, "h2", [128, NB], F32)
            sg = sbt(es, "sg", [128, NB], F32)
            hid = sbt(es, "hid", [128, 2, NB], BF16)
            cosC = sbt(es, "cosC", [128, NCP], F32)
            sinC = sbt(es, "sinC", [128, NCP], F32)
            nrc = NR(es, NB, "cnr")
            ropec = Rope(es, NCP, "crope")
            ropec.run(pos_cmp[0:1, :], cosC[:], sinC[:], "ccs")
            S.op("pool", lambda e: e.memset(kcs[:, T:T + 16], 0.0), writes=["kcs_tail"])
            for kv in range(2):
                w1d, w2d, pename = (w1k, w2k, "pek") if kv == 0 else (w1v, w2v, "pev")
                S.dma("pool", "w1b", lambda e, w1d=w1d: e.dma_start(out=w1b[:], in_=w1d.rearrange("l d f -> d l f")), writes=["w1b"])
                S.dma("pool", "w2b", lambda e, w2d=w2d: e.dma_start(out=w2b[:], in_=w2d.rearrange("(c p) d -> p c d", p=128)), writes=["w2b"])
                S.op("dve", lambda e, pename=pename: e.tensor_copy(out=peb[:], in_=cst[:, CL[pename]:CL[pename] + 32]), reads=["cst"], writes=["peb"])
                for fc in range(2):
                    pb = nextps()
                    for l in range(32):
                        S.op("pe", lambda e, l=l, fc=fc, pb=pb: e.matmul(PS(pb)[:, 0:1], lhsT=w1b[:, l, fc * 128:(fc + 1) * 128], rhs=peb[:, l:l + 1],
                                                                     start=(l == 0), stop=(l == 31)), reads=["w1b", "peb"], writes=[pn(pb)])
                    S.op("dve", lambda e, fc=fc, pb=pb: e.tensor_copy(out=pebias[:, fc:fc + 1], in_=PS(pb)[:, 0:1]), reads=[pn(pb)], writes=["pebias"])
                for g in range(2):
                    src = kcT_d if kv == 0 else vcT_d
                    S.dma("sp", "kcs", lambda e, src=src, g=g: e.dma_start(out=kcs[:, 0:T], in_=src[g, :, :]), reads=["kvdram"], writes=["kcs"])
                    for nt in range(NCP // NB):
                        n0 = nt * NB
                        for fc in range(2):
                            pb = nextps()
                            for l in range(32):
                                a0 = 16 * n0 + l
                                S.op("pe", lambda e, l=l, fc=fc, pb=pb, a0=a0: e.matmul(PS(pb)[:, 0:NB], lhsT=w1b[:, l, fc * 128:(fc + 1) * 128],
                                                                                 rhs=kcs[:, a0:a0 + 16 * (NB - 1) + 1:16], start=(l == 0), stop=(l == 31)),
                                     reads=["w1b", "kcs", "kcs_tail"], writes=[pn(pb)])
                            S.op("act", lambda e, pb=pb, fc=fc: e.activation(out=hf[:], in_=PS(pb)[:, 0:NB], func=AF.Identity, bias=pebias[:, fc:fc + 1]),
                                 reads=[pn(pb), "pebias"], writes=["hf"])
                            S.op("dve", lambda e: e.tensor_tensor(out=h2[:], in0=hf[:], in1=hf[:], op=ALU.mult), reads=["hf"], writes=["h2"])
                            S.op("dve", lambda e: e.tensor_scalar(out=h2[:], in0=h2[:], scalar1=0.044715, scalar2=1.0, op0=ALU.mult, op1=ALU.add),
                                 reads=["h2"], writes=["h2"])
                            S.op("dve", lambda e: e.tensor_tensor(out=h2[:], in0=h2[:], in1=hf[:], op=ALU.mult), reads=["h2", "hf"], writes=["h2"])
                            S.op("act", lambda e: e.activation(out=sg[:], in_=h2[:], func=AF.Sigmoid, scale=float(2.0 * math.sqrt(2.0 / math.pi))),
                                 reads=["h2"], writes=["sg"])
                            S.op("dve", lambda e, fc=fc: e.tensor_tensor(out=hid[:, fc, :], in0=hf[:], in1=sg[:], op=ALU.mult), reads=["hf", "sg"], writes=["hid"])
                        if kv == 0:
                            pb = nextps()
                            for fc in range(2):
                                S.op("pe", lambda e, fc=fc, pb=pb: e.matmul(PS(pb)[:, 0:NB], lhsT=w2b[:, fc, :], rhs=hid[:, fc, :], start=(fc == 0), stop=(fc == 1)),
                                     reads=["w2b", "hid"], writes=[pn(pb)])
                            nrc.run(pb, "kcn", kcTs[:, g, n0:n0 + NB], "kcTs", cosC[:, n0:n0 + NB], sinC[:, n0:n0 + NB], "ccs")
                            nr_drain()
                        else:
                            for ns in range(NB // 128):
                                pb = nextps()
                                for fc in range(2):
                                    S.op("pe", lambda e, fc=fc, pb=pb, ns=ns: e.matmul(PS(pb)[:, 0:128], lhsT=hid[:, fc, ns * 128:(ns + 1) * 128], rhs=w2b[:, fc, :],
                                                                                  start=(fc == 0), stop=(fc == 1)), reads=["w2b", "hid"], writes=[pn(pb)])
                                copy_out(vcs[:, g, n0 // 128 + ns, :], "vcs", PS(pb)[:, 0:128], [pn(pb)])
            stage_end()

        with ExitStack() as es:
            set_rot(range(8))
            hm = sbt(es, "hm", [128, KC, 256], BF16)
            rmsm = RMS(es, 256, "mrms")
            nrm = NR(es, 256, "mnr")
            wsm = WStream(es, "wsm", 2, 4096, cast=True)
            rmsm.run(memT, "memn", hm, "hm")

            def ev_k(j, pb):
                nrm.run(pb, "mkn", kmem[:, j, :], "kmem")
            proj_fm(wsm, w_mem, D, 0, 512, lambda c: hm[:, c, :], ["hm"], 256, ev_k, gcmax=256)

            def ev_v(ts, g0, gc, pb):
                copy_out(vmem[:, ts, g0:g0 + gc], "vmem", PS(pb)[:, 0:gc], [pn(pb)])
            proj_tm(wsm, w_mem, D, 512, 512, lambda c, ts: hm[:, c, ts * 128:(ts + 1) * 128], ["hm"], 2, ev_v)
            stage_end()

        with ExitStack() as es_p2:
            cmB_ref[0] = sbt(es_p2, "cmB", [128, ML["_n"] - 768], BF16)
            S.dma("pool", "c2", lambda e: e.dma_start(out=cmB_ref[0][:], in_=cmat_d[:, 768:ML["_n"]]), writes=["cm"])
            ynsa_b = sbt(es_p2, "ynsa_b", [128, 8, TQ], BF16)
            ysb = sbt(es_p2, "ysb", [128, 4, TQ], BF16)
            ymem = sbt(es_p2, "ymem", [128, 4, TQ], BF16)
            for k in range(NT):
                o0 = k * TQ
                with ExitStack() as es_a:
                    qn = sbt(es_a, "qn", [128, 8, TQ], BF16)
                    qs = sbt(es_a, "qs", [128, 4, TQ], BF16)
                    qm = sbt(es_a, "qm", [128, 4, TQ], BF16)
                    ksTo = sbt(es_a, "ksTo", [128, 2, TQ], BF16)
                    kwTo = sbt(es_a, "kwTo", [128, 2, TQ], BF16)
                    kwTp = sbt(es_a, "kwTp", [128, 2, TQ], BF16)
                    sbkTo = sbt(es_a, "sbkTo", [128, 4, TQ], BF16)
                    vown = sbt(es_a, "vown", [128, 4, 1024], BF16)
                    vprev = sbt(es_a, "vprev", [128, 4, 256], BF16)
                    with ExitStack() as es:
                        set_rot(range(8))
                        hp = sbt(es, "hp", [128, KC, TQ], BF16)
                        hn = sbt(es, "hn", [128, KC, TQ], BF16)
                        cosO = sbt(es, "cosO", [128, TQ], F32)
                        sinO = sbt(es, "sinO", [128, TQ], F32)
                        cosP = sbt(es, "cosP", [128, TQ], F32)
                        sinP = sbt(es, "sinP", [128, TQ], F32)
                        g32 = sbt(es, "g32", [24, TQ], F32)
                        rms2 = RMS(es, TQ, "a1rms")
                        rope2 = Rope(es, TQ, "a1rope")
                        nr2 = NR(es, TQ, "a1nr")
                        ws = WStream(es, "a1ws", 2, 4096)
                        rms2.run(xT_own[:, o0:o0 + TQ], "attn", hn, "hn")
                        rms2.run(xT_prev[:, o0:o0 + TQ], "attn", hp, "hp")
                        rope2.run(pos_own[0:1, o0:o0 + TQ], cosO[:], sinO[:], "cso")
                        rope2.run(pos_prev[0:1, o0:o0 + TQ], cosP[:], sinP[:], "csp")
                        rh = lambda c: hn[:, c, :]
                        rp = lambda c: hp[:, c, :]
                        proj_fm(ws, w_in_b, D, 0, 1024, rh, ["hn"], TQ, lambda j, pb: nr2.run(pb, "qn", qn[:, j, :], "qn", cosO[:], sinO[:], "cso"), gcmax=256)
                        proj_fm(ws, w_in_b, D, 1536, 256, rh, ["hn"], TQ, lambda j, pb: nr2.run(pb, "ksn", ksTo[:, j, :], "ksTo", cosO[:], sinO[:], "cso"), gcmax=256)
                        proj_fm(ws, w_in_b, D, 2048, 256, rh, ["hn"], TQ, lambda j, pb: nr2.run(pb, "kwn", kwTo[:, j, :], "kwTo", cosO[:], sinO[:], "cso"), gcmax=256)
                        proj_fm(ws, w_in_b, D, 2048, 256, rp, ["hp"], TQ, lambda j, pb: nr2.run(pb, "kwn", kwTp[:, j, :], "kwTp", cosP[:], sinP[:], "csp"), gcmax=256)
                        proj_fm(ws, w_in_b, D, 2584, 512, rh, ["hn"], TQ, lambda j, pb: copy_out(qs[:, j, :], "qs", PS(pb)[:, :], [pn(pb)]), gcmax=256)
                        proj_fm(ws, w_in_b, D, 3096, 512, rh, ["hn"], TQ, lambda j, pb: copy_out(sbkTo[:, j, :], "sbkTo", PS(pb)[:, :], [pn(pb)]), gcmax=256)
                        proj_fm(ws, w_in_b, D, 4120, 512, rh, ["hn"], TQ, lambda j, pb: nr2.run(pb, "mqn", qm[:, j, :], "qm"), gcmax=256)

                        def ev_g(j, pb):
                            S.op("act", lambda e: e.activation(out=g32[:], in_=PS(pb)[0:24, :], func=AF.Sigmoid), reads=[pn(pb)], writes=["g32"])
                            S.dma("sp", "gst", lambda e: e.dma_start(out=gates_d, in_=g32[:]), reads=["g32"], writes=["gates_d"])
                        proj_fm(ws, w_in_b, D, 2560, 24, rh, ["hn"], TQ, ev_g)
                        lh = lambda c, ts: hn[:, c, ts * 128:(ts + 1) * 128]
                        lp = lambda c, ts: hp[:, c, ts * 128:(ts + 1) * 128]
                        proj_tm(ws, w_in_b, D, 1792, 256, lh, ["hn"], 4, lambda ts, g0, gc, pb: copy_out(vown[:, ts, 0:256], "vown", PS(pb)[:, 0:256], [pn(pb)]))
                        proj_tm(ws, w_in_b, D, 2304, 256, lh, ["hn"], 4, lambda ts, g0, gc, pb: copy_out(vown[:, ts, 256:512], "vown", PS(pb)[:, 0:256], [pn(pb)]))
                        proj_tm(ws, w_in_b, D, 3608, 512, lh, ["hn"], 4, lambda ts, g0, gc, pb: copy_out(vown[:, ts, 512 + g0:512 + g0 + gc], "vown", PS(pb)[:, 0:gc], [pn(pb)]))
                        proj_tm(ws, w_in_b, D, 2304, 256, lp, ["hp"], 4, lambda ts, g0, gc, pb: copy_out(vprev[:, ts, :], "vprev", PS(pb)[:, 0:256], [pn(pb)]))
                        S.dma("sp", "hnst", lambda e: e.dma_start(out=hn_d, in_=hn[:].rearrange("p c t -> p (c t)")), reads=["hn"], writes=["hn_d"])
                        stage_end()

                    with ExitStack() as es:
                        O_B, SUM_B, U0_B, U1_B = 0, 1, 2, 3
                        set_rot([4, 5, 6, 7])
                        ynsa = sbt(es, "ynsa", [128, 4, TQ], F32)
                        impacc = sbt(es, "impacc", [128, 2, 4, 256], F32)
                        selbT = sbt(es, "selbT", [128, 2, 2, TQ], BF16)
                        ownb = sbt(es, "ownb", [8, 2, TQ], BF16)
                        gbc = [sbt(es, "gbc", [128, 3, TQ], F32) for _ in range(1)]
                        Pb = [sbt(es, "Pb", [128, TQ], BF16) for _ in range(3)]
                        rec = sbt(es, "rec", [128, TQ], F32)
                        tmpf = sbt(es, "tmpf", [128, TQ], F32)
                        kbuf = [sbt(es, "kbuf", [128, TQ], BF16) for _ in range(6)]
                        vbuf = [sbt(es, "vbuf", [128, 4, 128], BF16) for _ in range(6)]
                        cmask = sbt(es, "cmask", [128, NCC, TQ], BF16)
                        rtok = sbt(es, "rtok", [128, 4], F32)
                        tk1 = sbt(es, "tk1", [128, 256], F32)
                        tk2 = sbt(es, "tk2", [128, 256], F32)
                        tk3 = sbt(es, "tk3", [128, 256], F32)
                        tk4 = sbt(es, "tk4", [128, 256], F32)
                        m8 = sbt(es, "m8", [128, 16], F32)
                        alb = sbt(es, "alb", [128, 256], BF16)
                        alT = sbt(es, "alT", [128, 2, TQ], BF16)
                        e32 = [sbt(es, "e32", [128, TQ], F32) for _ in range(2)]
                        ec32 = [sbt(es, "ec32", [128, TQ], F32) for _ in range(2)]
                        Lp = [sbt(es, "Lp", [128, TQ], BF16) for _ in range(3)]
                        Wb = [sbt(es, "Wb", [128, TQ], BF16) for _ in range(2)]
                        Rb = [sbt(es, "Rb", [128, TQ], BF16) for _ in range(2)]
                        ncmp = min(NCC, 2 * (k + 1))
                        gi = [0]

                        def load_gates(h):
                            b = gbc[0]
                            name = "gbc0"
                            gi[0] += 1
                            S.dma("sp", name, lambda e: e.dma_start(out=b[:], in_=gates_d[3 * h:3 * h + 3, :].rearrange("(o r) t -> o r t", o=1).to_broadcast([128, 3, TQ])),
                                  reads=["gates_d"], writes=[name])
                            return b, name

                        for j in range(ncmp):
                            S.op("dve", lambda e, j=j: e.tensor_scalar(out=cmask[:, j, :], in0=j2[:], scalar1=ccol("cthr", k * NCC + j), scalar2=None, op0=ALU.is_gt),
                                 reads=["j2", "cst"], writes=["cmask"])

                        pi = [0]

                        pend = [None]

                        def flush_pending():
                            if pend[0] is None:
                                return
                            ch, P, pname, i, n, extra = pend[0]
                            pend[0] = None
                            S.op("pe", lambda e: e.matmul(PS(O_B)[:, :], lhsT=ch["v"], rhs=P[:], start=(i == 0), stop=(i == n - 1)),
                                 reads=ch["vn"] + [pname], writes=[pn(O_B)])
                            S.op("pe", lambda e: e.matmul(PS(SUM_B)[:, :], lhsT=cmv("ones"), rhs=P[:], start=(i == 0), stop=(i == n - 1)),
                                 reads=["cm", pname], writes=[pn(SUM_B)])
                            if extra:
                                extra(i, P, pname, n)

                        def softmax_chunks(chunks, q_ap, q_names, extra=None, i0=0, n_total=None):
                            n = len(chunks) if n_total is None else n_total
                            for i_, ch in enumerate(chunks):
                                i = i0 + i_
                                sb_ = nextps()
                                masks = ch.get("masks", [])
                                S.op("pe", lambda e: e.matmul(PS(sb_)[:, :], lhsT=ch["kT"], rhs=q_ap, start=True, stop=(len(masks) == 0)),
                                     reads=ch["kn"] + q_names, writes=[pn(sb_)])
                                for mi, (ml, mr, mn) in enumerate(masks):
                                    S.op("pe", lambda e: e.matmul(PS(sb_)[:, :], lhsT=ml, rhs=mr, start=False, stop=(mi == len(masks) - 1)),
                                         reads=mn + ["cm"], writes=[pn(sb_)])
                                P = Pb[pi[0] % 3]
                                pname = "Pb%d" % (pi[0] % 3)
                                pi[0] += 1
                                bias = ch.get("bias")
                                if bias is None:
                                    S.op("act", lambda e: e.activation(out=P[:], in_=PS(sb_)[:, :], func=AF.Exp, scale=SCALE),
                                         reads=[pn(sb_)], writes=[pname])
                                else:
                                    S.op("act", lambda e: e.activation(out=P[:], in_=PS(sb_)[:, :], func=AF.Exp, scale=SCALE, bias=bias),
                                         reads=[pn(sb_), "cst"], writes=[pname])
                                flush_pending()
                                pend[0] = (ch, P, pname, i, n, extra)

                        def finalize(out_ap, out_name, gate_ap=None, gate_name=None, accumulate=False):
                            flush_pending()
                            S.op("dve", lambda e: e.tensor_scalar(out=rec[:], in0=PS(SUM_B)[:, :], scalar1=1e-30, scalar2=None, op0=ALU.max), reads=[pn(SUM_B)], writes=["rec"])
                            S.op("dve", lambda e: e.reciprocal(out=rec[:], in_=rec[:]), reads=["rec"], writes=["rec"])
                            if gate_ap is not None:
                                S.op("dve", lambda e: e.tensor_tensor(out=rec[:], in0=rec[:], in1=gate_ap, op=ALU.mult), reads=["rec", gate_name], writes=["rec"])
                            if accumulate:
                                S.op("dve", lambda e: e.tensor_tensor(out=tmpf[:], in0=PS(O_B)[:, :], in1=rec[:], op=ALU.mult), reads=[pn(O_B), "rec"], writes=["tmpf"])
                                S.op("dve", lambda e: e.tensor_tensor(out=out_ap, in0=out_ap, in1=tmpf[:], op=ALU.add), reads=["tmpf", out_name], writes=[out_name])
                            else:
                                S.op("dve", lambda e: e.tensor_tensor(out=out_ap, in0=PS(O_B)[:, :], in1=rec[:], op=ALU.mult), reads=[pn(O_B), "rec"], writes=[out_name])

                        si2 = [0, 0]

                        def stream_tile(kT_src, v_c0, gt, pool_=0):
                            i = pool_ * 3 + si2[pool_] % 3
                            si2[pool_] += 1
                            kb, vb = kbuf[i], vbuf[i]
                            S.dma("sp", "kb%d" % i, lambda e: e.dma_start(out=kb[:], in_=kT_src[:, gt * TQ:(gt + 1) * TQ]), reads=["kvdram"], writes=["kbuf%d" % i])
                            S.dma("sp", "vb%d" % i, lambda e: e.dma_start(out=vb[:], in_=vtok_d[gt * TQ:(gt + 1) * TQ, v_c0:v_c0 + 128].rearrange("(ts p) d -> p ts d", p=128)),
                                  reads=["kvdram"], writes=["vbuf%d" % i])
                            return kb, vb, "kbuf%d" % i, "vbuf%d" % i

                        SB_O = 7

                        def sb_steps(h):
                            nch = 4 + (8 * k + 8) * 4

                            def sb_gen():
                                for c_ in (3, 2, 1, 0):
                                    yield dict(kT=sbkTo[:, h, c_ * 128:(c_ + 1) * 128], kn=["sbkTo"], v=vown[:, c_, 512 + h * 128:512 + (h + 1) * 128], vn=["vown"],
                                               diag=cm[:, ML["DiagS"] + c_ * 512:ML["DiagS"] + (c_ + 1) * 512], bias=None)
                                for tl in range(8 * k + 7, -1, -1):
                                    kb, vb, kname, vname = stream_tile(sbkT_d[h], 256 + h * 128, tl, 1)
                                    j_ = tl - 8 * k
                                    bias = ccol("cw", j_) if j_ >= 0 else None
                                    for c_ in (3, 2, 1, 0):
                                        yield dict(kT=kb[:, c_ * 128:(c_ + 1) * 128], kn=[kname], v=vb[:, c_, :], vn=[vname], bias=bias)
                            gen = sb_gen()
                            st = {}
                            Rprev = [None]

                            def stage1(ci):
                                ch = next(gen)
                                zb = nextps()
                                dg = ch.get("diag")
                                S.op("pe", lambda e: e.matmul(PS(zb)[:, :], lhsT=ch["kT"], rhs=qs[:, h, :], start=True, stop=(dg is None)),
                                     reads=ch["kn"] + ["qs"], writes=[pn(zb)])
                                if dg is not None:
                                    S.op("pe", lambda e: e.matmul(PS(zb)[:, :], lhsT=cmv("negI"), rhs=dg, start=False, stop=True), reads=["cm"], writes=[pn(zb)])
                                e_ = e32[ci % 2]
                                en = "e32_%d" % (ci % 2)
                                if ch["bias"] is None:
                                    S.op("act", lambda e: e.activation(out=e_[:], in_=PS(zb)[:, :], func=AF.Exp, scale=SCALE), reads=[pn(zb)], writes=[en])
                                else:
                                    S.op("act", lambda e: e.activation(out=e_[:], in_=PS(zb)[:, :], func=AF.Exp, scale=SCALE, bias=ch["bias"]),
                                         reads=[pn(zb), "cst"], writes=[en])
                                L_ = Lp[ci % 3]
                                ln_ = "Lp%d" % (ci % 3)
                                S.op("act", lambda e: e.activation(out=L_[:], in_=e_[:], func=AF.Ln, bias=1.0), reads=[en], writes=[ln_])
                                st[ci] = dict(ch=ch, e_=e_, en=en, L_=L_, ln_=ln_)

                            def stage2(ci):
                                d_ = st[ci]
                                L_, ln_, e_, en = d_["L_"], d_["ln_"], d_["e_"], d_["en"]
                                cb = nextps()
                                Rp = Rprev[0]
                                S.op("pe", lambda e: e.matmul(PS(cb)[:, :], lhsT=cmv("UIneg"), rhs=L_[:], start=True, stop=(Rp is None)),
                                     reads=[ln_, "cm"], writes=[pn(cb)])
                                if Rp is not None:
                                    S.op("pe", lambda e: e.matmul(PS(cb)[:, :], lhsT=cmv("onesneg"), rhs=Rp[0][:], start=False, stop=True),
                                         reads=[Rp[1], "cm"], writes=[pn(cb)])
                                ec_ = ec32[ci % 2]
                                ecn = "ec32_%d" % (ci % 2)
                                S.op("act", lambda e: e.activation(out=ec_[:], in_=PS(cb)[:, :], func=AF.Exp), reads=[pn(cb)], writes=[ecn])
                                W_ = Wb[ci % 2]
                                wn_ = "Wb%d" % (ci % 2)
                                S.op("dve", lambda e: e.tensor_tensor(out=W_[:], in0=e_[:], in1=ec_[:], op=ALU.mult), reads=[en, ecn], writes=[wn_])
                                Rn = Rb[ci % 2]
                                rn_ = "Rb%d" % (ci % 2)
                                if Rp is None:
                                    S.op("dve", lambda e: e.tensor_copy(out=Rn[:], in_=L_[:]), reads=[ln_], writes=[rn_])
                                else:
                                    S.op("dve", lambda e: e.tensor_tensor(out=Rn[:], in0=Rp[0][:], in1=L_[:], op=ALU.add), reads=[ln_, Rp[1]], writes=[rn_])
                                Rprev[0] = (Rn, rn_)
                                d_["W_"], d_["wn_"] = W_, wn_

                            def stage3(ci):
                                d_ = st.pop(ci)
                                ch, W_, wn_ = d_["ch"], d_["W_"], d_["wn_"]
                                S.op("pe", lambda e: e.matmul(PS(SB_O)[:, :], lhsT=ch["v"], rhs=W_[:], start=(ci == 0), stop=(ci == nch - 1)),
                                     reads=ch["vn"] + [wn_], writes=[pn(SB_O)])
                            for it in range(nch + 2):
                                if it < nch:
                                    stage1(it)
                                if 1 <= it <= nch:
                                    stage2(it - 1)
                                if it >= 2:
                                    stage3(it - 2)
                                yield it
                            copy_out(ysb[:, h, :], "ysb", PS(SB_O)[:, :], [pn(SB_O)])

                        for g in range(2):
                            set_rot([6, 7])
                            for hg in range(4):
                                h = g * 4 + hg
                                gb, gname = load_gates(h)
                                chunks = []
                                for j in range(ncmp):
                                    chunks.append(dict(kT=kcTs[:, g, j * 128:(j + 1) * 128], kn=["kcTs"], v=vcs[:, g, j, :], vn=["vcs"],
                                                       masks=[(cmv("negI"), cmask[:, j, :], ["cmask"])]))

                                def extra(i, P, pname, n):
                                    for ts in range(4):
                                        ub = 2 + ts
                                        o_ = 0
                                        S.op("pe", lambda e, P=P, ts=ts, ub=ub, o_=o_, i=i, n=n: e.matmul(PS(ub)[:, o_:o_ + 257], lhsT=P[:, ts * 128:(ts + 1) * 128],
                                                                                                    rhs=cm[:, ML["OV"] + i * 257:ML["OV"] + (i + 1) * 257],
                                                                                                    start=(i == 0), stop=(i == n - 1)),
                                             reads=[pname, "cm"], writes=[pn(ub)])
                                softmax_chunks(chunks, qn[:, h, :], ["qn"], extra)
                                finalize(ynsa[:, hg, :], "ynsa", gb[:, 0, :], gname)
                                for ts in range(4):
                                    ub = 2 + ts
                                    o_ = 0
                                    S.op("dve", lambda e, ts=ts, ub=ub, o_=o_: e.tensor_scalar(out=rtok[:, ts:ts + 1], in0=PS(ub)[:, o_ + 256:o_ + 257], scalar1=1e-30, scalar2=None, op0=ALU.max),
                                         reads=[pn(ub)], writes=["rtok"])
                                    S.op("dve", lambda e, ts=ts: e.reciprocal(out=rtok[:, ts:ts + 1], in_=rtok[:, ts:ts + 1]), reads=["rtok"], writes=["rtok"])
                                    if hg == 0:
                                        S.op("dve", lambda e, ts=ts, ub=ub, o_=o_: e.tensor_scalar(out=impacc[:, g, ts, :], in0=PS(ub)[:, o_:o_ + 256], scalar1=rtok[:, ts:ts + 1], scalar2=None, op0=ALU.mult),
                                             reads=[pn(ub), "rtok"], writes=["impacc"])
                                    else:
                                        S.op("dve", lambda e, ts=ts, ub=ub, o_=o_: e.scalar_tensor_tensor(out=impacc[:, g, ts, :], in0=PS(ub)[:, o_:o_ + 256], scalar=rtok[:, ts:ts + 1],
                                                                                                       in1=impacc[:, g, ts, :], op0=ALU.mult, op1=ALU.add),
                                             reads=[pn(ub), "rtok", "impacc"], writes=["impacc"])
                            set_rot([2, 3, 4, 5, 6, 7])
                            for ts in range(4):
                                curc = ccol("cur", k * 4 + ts)
                                blk = rowt[:, 0:256]
                                S.op("dve", lambda e, curc=curc: e.tensor_scalar(out=tk1[:], in0=blk, scalar1=curc, scalar2=None, op0=ALU.subtract), reads=["rowt", "cst"], writes=["tk1"])
                                S.op("dve", lambda e: e.tensor_scalar(out=tk2[:], in0=tk1[:], scalar1=0.0, scalar2=None, op0=ALU.is_gt), reads=["tk1"], writes=["tk2"])
                                S.op("dve", lambda e: e.tensor_scalar(out=tk3[:], in0=tk1[:], scalar1=-1.0, scalar2=None, op0=ALU.is_ge), reads=["tk1"], writes=["tk3"])
                                S.op("dve", lambda e: e.tensor_tensor(out=tk3[:], in0=tk3[:], in1=tk2[:], op=ALU.subtract), reads=["tk3", "tk2"], writes=["tk3"])
                                S.op("dve", lambda e: e.tensor_scalar(out=tk4[:], in0=blk, scalar1=0.0, scalar2=None, op0=ALU.is_equal), reads=["rowt"], writes=["tk4"])
                                S.op("dve", lambda e: e.tensor_tensor(out=tk3[:], in0=tk3[:], in1=tk4[:], op=ALU.max), reads=["tk3", "tk4"], writes=["tk3"])
                                S.op("dve", lambda e: e.tensor_tensor(out=tk3[:], in0=tk3[:], in1=tk2[:], op=ALU.subtract), reads=["tk3", "tk2"], writes=["tk3"])
                                S.op("dve", lambda e, ts=ts: e.scalar_tensor_tensor(out=tk1[:], in0=tk3[:], scalar=1.0e4, in1=impacc[:, g, ts, :], op0=ALU.mult, op1=ALU.add),
                                     reads=["tk3", "impacc"], writes=["tk1"])
                                S.op("dve", lambda e: e.max(out=m8[:, 0:8], in_=tk1[:]), reads=["tk1"], writes=["m8a"])
                                S.op("dve", lambda e: e.match_replace(out=tk4[:], in_to_replace=m8[:, 0:8], in_values=tk1[:], imm_value=-3.0e4), reads=["tk1", "m8a"], writes=["tk4"])
                                S.op("dve", lambda e: e.max(out=m8[:, 8:16], in_=tk4[:]), reads=["tk4"], writes=["m8b"])
                                S.op("dve", lambda e: e.tensor_scalar(out=tk4[:], in0=tk1[:], scalar1=m8[:, 15:16], scalar2=None, op0=ALU.is_ge), reads=["tk1", "m8b"], writes=["tk4"])
                                S.op("dve", lambda e: e.tensor_scalar(out=tk2[:], in0=tk2[:], scalar1=-1.0, scalar2=1.0, op0=ALU.mult, op1=ALU.add), reads=["tk2"], writes=["tk2"])
                                S.op("dve", lambda e: e.tensor_tensor(out=alb[:], in0=tk4[:], in1=tk2[:], op=ALU.mult), reads=["tk4", "tk2"], writes=["alb"])
                                for half in range(2):
                                    pb = nextps()
                                    S.op("pe", lambda e, half=half, pb=pb: e.matmul(PS(pb)[:, 0:128], lhsT=alb[:, half * 128:(half + 1) * 128], rhs=cmv("ident"), start=True, stop=True),
                                         reads=["alb", "cm"], writes=[pn(pb)])
                                    copy_out(alT[:, half, ts * 128:(ts + 1) * 128], "alT", PS(pb)[:, 0:128], [pn(pb)])
                            pb = nextps()
                            for half in range(2):
                                osel = cm[:, ML["OwnSel"] + k * 16 + half * 8:ML["OwnSel"] + k * 16 + half * 8 + 8]
                                S.op("pe", lambda e, half=half, pb=pb, osel=osel: e.matmul(PS(pb)[0:8, :], lhsT=osel, rhs=alT[:, half, :], start=(half == 0), stop=(half == 1)),
                                     reads=["alT", "cm"], writes=[pn(pb)])
                            S.op("dve", lambda e, pb=pb: e.tensor_scalar(out=ownb[:, g, :], in0=PS(pb)[0:8, :], scalar1=-1.0, scalar2=-NEG, op0=ALU.add, op1=ALU.mult),
                                 reads=[pn(pb)], writes=["ownb"])
                            for half in range(2):
                                S.op("dve", lambda e, half=half: e.tensor_scalar(out=selbT[:, g, half, :], in0=alT[:, half, :], scalar1=cst[:, CL["_om"] + k * 2 + half:CL["_om"] + k * 2 + half + 1],
                                                                               scalar2=None, op0=ALU.mult), reads=["alT", "cst"], writes=["selbT"])
                                S.op("dve", lambda e, half=half: e.tensor_scalar(out=selbT[:, g, half, :], in0=selbT[:, g, half, :], scalar1=-1.0, scalar2=-NEG, op0=ALU.add, op1=ALU.mult),
                                     reads=["selbT"], writes=["selbT"])
                            set_rot([2, 3, 4, 5, 6])
                            import itertools as _it
                            sbit = _it.chain(sb_steps(2 * g), sb_steps(2 * g + 1))

                            def sb_adv(n_):
                                for _ in range(n_):
                                    next(sbit, None)
                            for hg in range(4):
                                h = g * 4 + hg
                                gb, gname = load_gates(h)
                                chunks = []
                                for c_ in range(4):
                                    chunks.append(dict(kT=ksTo[:, g, c_ * 128:(c_ + 1) * 128], kn=["ksTo"], v=vown[:, c_, g * 128:(g + 1) * 128], vn=["vown"],
                                                       masks=[(cm[0:8, ML["Eown"] + c_ * 128:ML["Eown"] + (c_ + 1) * 128], ownb[:, g, :], ["ownb"]),
                                                              (cmv("negI"), cm[:, ML["DiagN"] + c_ * 512:ML["DiagN"] + (c_ + 1) * 512], [])]))
                                ntot = 4 + 4 * (8 * k + 8)
                                softmax_chunks(chunks, qn[:, h, :], ["qn"], None, 0, ntot)
                                sb_adv(2)
                                for gt in range(8 * k + 8):
                                    kb, vb, kname, vname = stream_tile(ksT_d[g], g * 128, gt)
                                    chunks = []
                                    for c_ in range(4):
                                        cg = gt * 4 + c_
                                        chunks.append(dict(kT=kb[:, c_ * 128:(c_ + 1) * 128], kn=[kname], v=vb[:, c_, :], vn=[vname],
                                                           masks=[(cm[:, ML["E64"] + (cg % 64) * 128:ML["E64"] + (cg % 64 + 1) * 128], selbT[:, g, cg // 64, :], ["selbT"])]))
                                    softmax_chunks(chunks, qn[:, h, :], ["qn"], None, 4 + 4 * gt, ntot)
                                    sb_adv(2)
                                finalize(ynsa[:, hg, :], "ynsa", gb[:, 1, :], gname, accumulate=True)
                            for _ in sbit:
                                pass
                            set_rot([2, 3, 4, 5, 6, 7])
                            for hg in range(4):
                                h = g * 4 + hg
                                gb, gname = load_gates(h)
                                chunks = []
                                for c_ in range(4):
                                    chunks.append(dict(kT=kwTp[:, g, c_ * 128:(c_ + 1) * 128], kn=["kwTp"], v=vprev[:, c_, g * 128:(g + 1) * 128], vn=["vprev"],
                                                       masks=[(cmv("negI"), cm[:, ML["DiagW"] + c_ * 512:ML["DiagW"] + (c_ + 1) * 512], [])], bias=ccol("pv", k)))
                                for c_ in range(4):
                                    chunks.append(dict(kT=kwTo[:, g, c_ * 128:(c_ + 1) * 128], kn=["kwTo"], v=vown[:, c_, 256 + g * 128:256 + (g + 1) * 128], vn=["vown"],
                                                       masks=[(cmv("negI"), cm[:, ML["DiagN"] + c_ * 512:ML["DiagN"] + (c_ + 1) * 512], [])]))
                                softmax_chunks(chunks, qn[:, h, :], ["qn"])
                                finalize(ynsa[:, hg, :], "ynsa", gb[:, 2, :], gname, accumulate=True)
                                copy_out(ynsa_b[:, h, :], "ynsa_b", ynsa[:, hg, :], ["ynsa"])
                        for h in range(4):
                            chunks = [dict(kT=kmem[:, h, ms * 128:(ms + 1) * 128], kn=["kmem"], v=vmem[:, ms, h * 128:(h + 1) * 128], vn=["vmem"]) for ms in range(2)]
                            softmax_chunks(chunks, qm[:, h, :], ["qm"])
                            finalize(ymem[:, h, :], "ymem")
                        stage_end()

                with ExitStack() as es_b:
                    x1 = sbt(es_b, "x1", [128, KC, TQ], F32)
                    hn = sbt(es_b, "hnB", [128, KC, TQ], BF16)
                    S.dma("sp", "hnld", lambda e: e.dma_start(out=hn[:].rearrange("p c t -> p (c t)"), in_=hn_d), reads=["hn_d"], writes=["hn"])
                    with ExitStack() as es:
                        set_rot(range(8))
                        mixed = sbt(es, "mixed", [128, KC, TQ], BF16)
                        sig = sbt(es, "sig", [128, 6, TQ], F32)
                        acc = [sbt(es, "acc", [128, TQ], F32) for _ in range(2)]
                        tmpm = [sbt(es, "tmpm", [128, TQ], F32) for _ in range(2)]
                        wsg = WStream(es, "b1wg", 2, 4096)
                        wso = WStream(es, "b1wo", 3, 2048)
                        for dc2 in range(KC // 2):
                            for br in range(3):
                                def ev_s(j, pb, br=br):
                                    S.op("act", lambda e: e.activation(out=sig[:, br * 2 + j, :], in_=PS(pb)[:, :], func=AF.Sigmoid), reads=[pn(pb)], writes=["sig%d" % (br * 2 + j)])
                                proj_fm(wsg, w_in_b, D, 4632 + br * D + dc2 * 256, 256, lambda c: hn[:, c, :], ["hn"], TQ, ev_s)
                            for br, (wo, K_, y_, yn_) in enumerate([(wo_nsa_b, 1024, ynsa_b, "ynsa_b"), (wo_sb_b, 512, ysb, "ysb"), (wo_mem_b, 512, ymem, "ymem")]):
                                def ev_a(j, pb, br=br):
                                    sn = "sig%d" % (br * 2 + j)
                                    if br == 0:
                                        S.op("dve", lambda e: e.tensor_tensor(out=acc[j][:], in0=PS(pb)[:, :], in1=sig[:, j, :], op=ALU.mult), reads=[pn(pb), sn], writes=["acc%d" % j])
                                    else:
                                        S.op("dve", lambda e: e.tensor_tensor(out=tmpm[j][:], in0=PS(pb)[:, :], in1=sig[:, br * 2 + j, :], op=ALU.mult), reads=[pn(pb), sn], writes=["tmpm%d" % j])
                                        if br == 1:
                                            S.op("dve", lambda e: e.tensor_tensor(out=acc[j][:], in0=acc[j][:], in1=tmpm[j][:], op=ALU.add), reads=["acc%d" % j, "tmpm%d" % j], writes=["acc%d" % j])
                                        else:
                                            S.op("dve", lambda e: e.tensor_tensor(out=mixed[:, dc2 * 2 + j, :], in0=acc[j][:], in1=tmpm[j][:], op=ALU.add), reads=["acc%d" % j, "tmpm%d" % j], writes=["mixed"])
                                proj_fm(wso, wo, K_, dc2 * 256, 256, lambda c, y_=y_: y_[:, c, :], [yn_], TQ, ev_a)
                        xr = [sbt(es, "xr", [128, TQ], F32) for _ in range(2)]

                        def ev_o(j, pb):
                            b = xr[j % 2]
                            S.dma("sp", "xr%d" % (j % 2), lambda e: e.dma_start(out=b[:], in_=xT_own[j * 128:(j + 1) * 128, o0:o0 + TQ]), writes=["xr%d" % (j % 2)])
                            S.op("dve", lambda e: e.tensor_tensor(out=x1[:, j, :], in0=PS(pb)[:, :], in1=b[:], op=ALU.add), reads=[pn(pb), "xr%d" % (j % 2)], writes=["x1"])
                        proj_fm(wsg, w_out_b, D, 0, D, lambda c: mixed[:, c, :], ["mixed"], TQ, ev_o, gcmax=256)
                        stage_end()
                    with ExitStack() as es:
                        set_rot(range(8))
                        act_all = sbt(es, "act_all", [128, FFN // 256, TQ], BF16)
                        sq2 = [sbt(es, "sq2", [128, TQ], BF16) for _ in range(2)]
                        rs2 = sbt(es, "rs2", [128, TQ], F32)
                        sgl = [sbt(es, "sgl", [128, TQ], F32) for _ in range(2)]
                        ost = [sbt(es, "ost", [128, TQ], F32) for _ in range(2)]
                        wsf = WStream(es, "b2wf", 3, 5632)
                        pb0 = nextps()
                        for c in range(KC):
                            S.op("act", lambda e, c=c: e.activation(out=sq2[c % 2][:], in_=x1[:, c, :], func=AF.Square), reads=["x1"], writes=["sq2_%d" % (c % 2)])
                            S.op("pe", lambda e, c=c: e.matmul(PS(pb0)[:, :], lhsT=cmv("ones"), rhs=sq2[c % 2][:], start=(c == 0), stop=(c == KC - 1)),
                                 reads=["sq2_%d" % (c % 2), "cm"], writes=[pn(pb0)])
                        S.op("act", lambda e: e.activation(out=rs2[:], in_=PS(pb0)[:, :], func=AF.Sqrt, scale=1.0 / D, bias=ccol("eps")), reads=[pn(pb0), "cst"], writes=["rs2"])
                        S.op("dve", lambda e: e.reciprocal(out=rs2[:], in_=rs2[:]), reads=["rs2"], writes=["rs2"])
                        for c in range(KC):
                            S.op("dve", lambda e, c=c: e.scalar_tensor_tensor(out=hn[:, c, :], in0=x1[:, c, :], scalar=ccol("ffn", c), in1=rs2[:], op0=ALU.mult, op1=ALU.mult),
                                 reads=["x1", "rs2", "cst"], writes=["hn"])
                        for hf_ in range(2):
                          for fi2 in range(FFN // 512):
                            def ev_g2(j, pb):
                                S.op("act", lambda e: e.activation(out=sgl[j][:], in_=PS(pb)[:, :], func=AF.Silu), reads=[pn(pb)], writes=["sgl%d" % j])
                            proj_fm(wsf, w_gate_b, D, hf_ * 2816 + fi2 * 256, 256, lambda c: hn[:, c, :], ["hn"], TQ, ev_g2)

                            def ev_u(j, pb, fi2=fi2):
                                S.op("dve", lambda e: e.tensor_tensor(out=act_all[:, fi2 * 2 + j, :], in0=PS(pb)[:, :], in1=sgl[j][:], op=ALU.mult), reads=[pn(pb), "sgl%d" % j], writes=["act_all"])
                            proj_fm(wsf, w_up_b, D, hf_ * 2816 + fi2 * 256, 256, lambda c: hn[:, c, :], ["hn"], TQ, ev_u)

                          def ev_d(j, pb, hf_=hf_):
                            if hf_ == 0:
                                S.op("dve", lambda e: e.tensor_tensor(out=x1[:, j, :], in0=PS(pb)[:, :], in1=x1[:, j, :], op=ALU.add), reads=[pn(pb), "x1"], writes=["x1"])
                                return
                            b = ost[j % 2]
                            S.op("dve", lambda e: e.tensor_tensor(out=b[:], in0=PS(pb)[:, :], in1=x1[:, j, :], op=ALU.add), reads=[pn(pb), "x1"], writes=["ost%d" % (j % 2)])
                            S.dma("sp", "ost%d" % (j % 2), lambda e: e.dma_start(out=outT[j * 128:(j + 1) * 128, o0:o0 + TQ], in_=b[:]), reads=["ost%d" % (j % 2)], writes=["outT"])
                          proj_fm(wsf, w_down_b[hf_ * 2816:(hf_ + 1) * 2816, :], 2816, 0, D, lambda c: act_all[:, c, :], ["act_all"], TQ, ev_d, gcmax=256)

                        stage_end()
        S.barrier()
        S.flush()
    return nc


def host_consts(T, core):
    NT = T // (NCORE * TQ)
    NCC = T // 2048
    CL = cst_layout(NT, NCC)
    ML = cm_layout(NT, NCC)
    p = np.arange(128)
    cm = np.zeros((128, ML["_n"]), np.float32)
    cm[p, ML["ident"] + p] = 1.0
    cm[p, ML["negI"] + p] = NEG
    cm[:, ML["ones"]:ML["ones"] + 128] = 1.0
    cm[:, ML["onesneg"]:ML["onesneg"] + 128] = -1.0
    cm[:, ML["UIneg"]:ML["UIneg"] + 128] = -(p[:, None] >= p[None, :]).astype(np.float32)
    for m in range(128):
        if m < 64:
            cm[m + 64, ML["prot"] + m] = -1.0
        else:
            cm[m - 64, ML["prot"] + m] = 1.0
    u = np.arange(8192)
    cm[:, ML["E64"]:ML["E64"] + 8192] = (p[:, None] == (u[None, :] // 64)).astype(np.float32)
    for j in range(NCC):
        n = 128 * j + p
        s = np.arange(256)
        ov = ((n[:, None] >= 4 * s[None, :] - 1) & (n[:, None] <= 4 * s[None, :] + 3) & (n[:, None] < T // 16 - 1)).astype(np.float32)
        cm[:, ML["OV"] + j * 257:ML["OV"] + j * 257 + 256] = ov
        cm[:, ML["OV"] + j * 257 + 256] = (n < T // 16 - 1).astype(np.float32)
    t = np.arange(512)
    for c_ in range(4):
        sg_ = 128 * c_ + p
        cm[:, ML["DiagN"] + c_ * 512:ML["DiagN"] + (c_ + 1) * 512] = (sg_[:, None] > t[None, :]).astype(np.float32)
        cm[:, ML["DiagS"] + c_ * 512:ML["DiagS"] + (c_ + 1) * 512] = (sg_[:, None] >= t[None, :]).astype(np.float32)
        cm[:, ML["DiagW"] + c_ * 512:ML["DiagW"] + (c_ + 1) * 512] = (sg_[:, None] <= t[None, :]).astype(np.float32)
        s_ = np.arange(128)
        for r in range(2):
            cm[2 * c_ + r, ML["Eown"] + c_ * 128 + s_] = (s_ // 64 == r).astype(np.float32)
    for k in range(NT):
        ob0 = 8 * (8 * k + core)
        for half in range(2):
            for b in range(8):
                blk = ob0 + b
                if blk // 128 == half:
                    cm[blk % 128, ML["OwnSel"] + k * 16 + half * 8 + b] = 1.0
    cst = np.zeros((128, CL["_n"]), np.float32)
    cst[:, CL["eps"]] = 1e-6
    cst[:, CL["one"]] = 1.0
    cst[:, CL["halfpi"]] = np.float32(np.pi / 2)
    cst[:, CL["tiny"]] = 1e-30
    inv = (np.float32(1.0) / np.power(np.float32(10000.0), np.arange(0, 128, 2, dtype=np.float32) / np.float32(128))).astype(np.float32)
    cst[:, CL["inv"]] = inv[p % 64]
    for j in range(8):
        cst[:, CL["cw"] + j] = 0.0 if j < core else NEG
    for k in range(NT):
        gt = 8 * k + core
        cst[:, CL["pv"] + k] = NEG if gt == 0 else 0.0
        for ts in range(4):
            cst[:, CL["cur"] + k * 4 + ts] = (gt * 512 + ts * 128 + p) // 64
        for j in range(NCC):
            cst[:, CL["cthr"] + k * NCC + j] = gt * 512 - 2048 * j
        ob0 = 8 * gt
        for half in range(2):
            cst[:, CL["_om"] + k * 2 + half] = ((half * 128 + p) < ob0).astype(np.float32)
    rowt = np.zeros((128, 256), np.float32)
    rowt[:, 0:256] = np.arange(256)[None, :]
    j2 = (16 * p[:, None] + 31 - t[None, :]).astype(np.float32)
    return cst, cm, rowt, j2, CL


_NC_CACHE = {}


def kernel(x, mem, positions, attn_norm, w_in, nsa_q_norm, nsa_kc_norm, nsa_ks_norm, nsa_kw_norm,
           cmp_k_pe, cmp_k_w1, cmp_k_w2, cmp_v_pe, cmp_v_w1, cmp_v_w2, mem_norm, w_mem_kv,
           mem_q_norm, mem_k_norm, w_o_nsa, w_o_sb, w_o_mem, w_out, ffn_norm,
           w_ffn_gate, w_ffn_up, w_ffn_down):
    x = np.asarray(x)
    T = x.shape[1]
    NT = T // (NCORE * TQ)
    NCC = T // 2048
    NCP = NCC * 128
    if T not in _NC_CACHE:
        _NC_CACHE[T] = build(T)
    nc = _NC_CACHE[T]
    f = lambda a: np.ascontiguousarray(np.asarray(a, dtype=np.float32))
    xT = np.ascontiguousarray(np.asarray(x)[0].T)
    pos = np.asarray(positions).astype(np.int32)
    pos_cmp = np.zeros((1, NCP), np.int32)
    pc = pos[0, 31::16]
    pos_cmp[0, :len(pc)] = pc
    common = {
        "xT_all": xT, "memT": np.ascontiguousarray(np.asarray(mem)[0].T), "pos_all": pos, "pos_cmp": pos_cmp,
        "w_in": f(w_in[0]), "w1k": f(cmp_k_w1[0]), "w2k": f(cmp_k_w2[0]), "w1v": f(cmp_v_w1[0]), "w2v": f(cmp_v_w2[0]),
        "w_mem": f(w_mem_kv[0]), "wo_nsa": f(w_o_nsa[0]), "wo_sb": f(w_o_sb[0]), "wo_mem": f(w_o_mem[0]), "w_out": f(w_out[0]),
        "w_gate": f(w_ffn_gate[0]), "w_up": f(w_ffn_up[0]), "w_down": f(w_ffn_down[0]),
    }
    in_maps = []
    for core in range(NCORE):
        cst, cm, rowt, j2, CL = host_consts(T, core)
        for name, arr in [("attn", attn_norm), ("ffn", ffn_norm), ("memn", mem_norm)]:
            cst[:, CL[name]:CL[name] + 16] = np.asarray(arr)[0].reshape(16, 128).T
        for name, arr in [("qn", nsa_q_norm), ("kcn", nsa_kc_norm), ("ksn", nsa_ks_norm), ("kwn", nsa_kw_norm), ("mqn", mem_q_norm), ("mkn", mem_k_norm)]:
            cst[:, CL[name]] = np.asarray(arr)[0]
        cst[:, CL["pek"]:CL["pek"] + 32] = np.asarray(cmp_k_pe)[0].T
        cst[:, CL["pev"]:CL["pev"] + 32] = np.asarray(cmp_v_pe)[0].T
        own = np.zeros((D, NT * TQ), np.float32)
        prev = np.zeros((D, NT * TQ), np.float32)
        pown = np.zeros((1, NT * TQ), np.int32)
        pprev = np.zeros((1, NT * TQ), np.int32)
        for k in range(NT):
            gt = 8 * k + core
            own[:, k * TQ:(k + 1) * TQ] = xT[:, gt * TQ:(gt + 1) * TQ]
            pown[0, k * TQ:(k + 1) * TQ] = pos[0, gt * TQ:(gt + 1) * TQ]
            if gt > 0:
                prev[:, k * TQ:(k + 1) * TQ] = xT[:, (gt - 1) * TQ:gt * TQ]
                pprev[0, k * TQ:(k + 1) * TQ] = pos[0, (gt - 1) * TQ:gt * TQ]
        m = dict(common)
        m.update({"xT_own": own, "xT_prev": prev, "pos_own": pown, "pos_prev": pprev, "cst": cst, "cmat": cm, "rowt": rowt, "j2": j2})
        in_maps.append(m)
    res = run_bass_kernel_spmd(nc, in_maps, core_ids=list(range(NCORE)))
    out = np.zeros((1, T, D), np.float32)
    for core in range(NCORE):
        oT = np.asarray(res.results[core]["outT"])
        for k in range(NT):
            gt = 8 * k + core
            out[0, gt * TQ:(gt + 1) * TQ, :] = oT[:, k * TQ:(k + 1) * TQ].T
    return out
```
